# Optimizing a Trainium2 kernel written in Bass

```python
import math
import jax
import jax.numpy as jnp
from jax import lax
import numpy as np

D_MODEL = 1024
BATCH = 8
SEQ = 4096
DEPTH = 2

GRID_W = 64
CTX_LEN = 256
HEAD_DIM = 64
NA_HEADS = 6
NA_WIN_R = 8
NA_WIN_C = 16
NA_QCB = 16
NA_KCB = 32
WG_HEADS = 6
WG_KV_HEADS = 2
WG_WINDOW = 128
WG_BLOCK = 128
SC_CH = 256
SC_GROUPS = 4
CONV_W = 3
D_FF = 2816
ROPE_BASE = 10000.0
EPS = 1e-6
NEG = -1e30

NA_W = NA_HEADS * HEAD_DIM
WG_W = WG_HEADS * HEAD_DIM
WG_KV_W = WG_KV_HEADS * HEAD_DIM
MIX_W = NA_W + WG_W + SC_CH
IN_SPLITS = (NA_W, NA_W, NA_W, WG_W, WG_KV_W, WG_KV_W, SC_CH, SC_CH, SC_CH)
IN_W = sum(IN_SPLITS)

kernel_name = 'hybrid_natten_swa_shortconv_dit_block'


def _rmsnorm(x, g):
    xf = x.astype(jnp.float32)
    y = xf * lax.rsqrt(jnp.mean(xf * xf, axis=-1, keepdims=True) + EPS)
    return (y * g.astype(jnp.float32)).astype(x.dtype)


def _modulate(h, shift, scale):
    return h * (1 + scale) + shift


def _dwconv(x, w):
    L = x.shape[1]
    pad = CONV_W // 2
    xp = jnp.pad(x, ((0, 0), (pad, CONV_W - 1 - pad), (0, 0)))
    out = xp[:, 0:L] * w[0]
    for k in range(1, CONV_W):
        out = out + xp[:, k:k + L] * w[k]
    return out


def _axial_rope(x):
    L = x.shape[1]
    t = jnp.arange(L)
    row = (t // GRID_W).astype(jnp.float32)
    col = (t % GRID_W).astype(jnp.float32)
    half = HEAD_DIM // 2
    n_freq = half // 2
    inv = ROPE_BASE ** (-jnp.arange(n_freq, dtype=jnp.float32) / n_freq)
    ang = jnp.concatenate([row[:, None] * inv, col[:, None] * inv], axis=-1)
    cos = jnp.cos(ang)[None, :, None, :]
    sin = jnp.sin(ang)[None, :, None, :]
    xf = x.astype(jnp.float32)
    x1, x2 = xf[..., :half], xf[..., half:]
    return jnp.concatenate([x1 * cos - x2 * sin, x2 * cos + x1 * sin], axis=-1).astype(x.dtype)


def _ctx_attention(q, k, v, sink=None):
    B, C, Hq, dh = q.shape
    Hkv = k.shape[2]
    G = Hq // Hkv
    qg = q.reshape(B, C, Hkv, G, dh)
    s = jnp.einsum('bqkgd,bckd->bkgqc', qg, k).astype(jnp.float32) * (1.0 / math.sqrt(dh))
    if sink is not None:
        s_sink = jnp.broadcast_to(sink.astype(jnp.float32).reshape(1, Hkv, G, 1, 1), s.shape[:-1] + (1,))
        s = jnp.concatenate([s, s_sink], axis=-1)
    p = jax.nn.softmax(s, axis=-1)[..., :C].astype(v.dtype)
    o = jnp.einsum('bkgqc,bckd->bqkgd', p, v)
    return o.reshape(B, C, Hq * dh)


def _neighbourhood_attention(q, k, v, kc, vc, rpb):
    B, S, H, dh = q.shape
    rows = S // GRID_W
    wr = min(NA_WIN_R, rows)
    ncb = GRID_W // NA_QCB
    nk = wr * NA_KCB
    scale = 1.0 / math.sqrt(dh)
    qg = q.reshape(B, rows, GRID_W, H, dh)
    kg = k.reshape(B, rows, GRID_W, H, dh)
    vg = v.reshape(B, rows, GRID_W, H, dh)
    qcols = np.arange(GRID_W).reshape(ncb, NA_QCB)
    c0 = np.clip(qcols - NA_WIN_C // 2, 0, GRID_W - NA_WIN_C)
    kcols = np.clip(np.arange(ncb) * NA_QCB - NA_WIN_C // 2, 0, GRID_W - NA_KCB)[:, None] + np.arange(NA_KCB)
    col_ok = (kcols[:, None, :] >= c0[:, :, None]) & (kcols[:, None, :] < c0[:, :, None] + NA_WIN_C)
    mask = np.broadcast_to(col_ok[:, :, None, :], (ncb, NA_QCB, wr, NA_KCB)).reshape(ncb, NA_QCB, nk)
    dcol = np.clip(kcols[:, None, :] - qcols[:, :, None] + NA_WIN_C - 1, 0, 2 * NA_WIN_C - 2)

    def row_block(r):
        r0 = jnp.clip(r - wr // 2, 0, rows - wr)
        kr = lax.dynamic_slice_in_dim(kg, r0, wr, axis=1)[:, :, kcols]
        vr = lax.dynamic_slice_in_dim(vg, r0, wr, axis=1)[:, :, kcols]
        kr = jnp.transpose(kr, (0, 2, 1, 3, 4, 5)).reshape(B, ncb, nk, H, dh)
        vr = jnp.transpose(vr, (0, 2, 1, 3, 4, 5)).reshape(B, ncb, nk, H, dh)
        qr = lax.dynamic_index_in_dim(qg, r, axis=1, keepdims=False).reshape(B, ncb, NA_QCB, H, dh)
        drow = r0 + jnp.arange(wr) - r + NA_WIN_R - 1
        bias = rpb[:, drow[None, None, :, None], dcol[:, :, None, :]].reshape(H, ncb, NA_QCB, nk).astype(jnp.float32)
        s_loc = jnp.einsum('bjqhd,bjkhd->bhjqk', qr, kr).astype(jnp.float32) * scale + bias
        s_loc = jnp.where(mask, s_loc, NEG)
        s_ctx = jnp.einsum('bjqhd,bchd->bhjqc', qr, kc).astype(jnp.float32) * scale
        p = jax.nn.softmax(jnp.concatenate([s_loc, s_ctx], axis=-1), axis=-1).astype(v.dtype)
        o = (jnp.einsum('bhjqk,bjkhd->bjqhd', p[..., :nk], vr)
             + jnp.einsum('bhjqc,bchd->bjqhd', p[..., nk:], vc))
        return o.reshape(B, GRID_W, H * dh)

    out = lax.map(row_block, jnp.arange(rows))
    return jnp.transpose(out, (1, 0, 2, 3)).reshape(B, S, H * dh)


def _window_gqa(q, k, v, kc, vc, sink):
    B, S, Hq, dh = q.shape
    Hkv = k.shape[2]
    G = Hq // Hkv
    C = kc.shape[1]
    nb = S // WG_BLOCK
    span = 3 * WG_BLOCK
    scale = 1.0 / math.sqrt(dh)
    qb = q.reshape(B, nb, WG_BLOCK, Hkv, G, dh)
    pad = ((0, 0), (WG_BLOCK, WG_BLOCK), (0, 0), (0, 0))
    kp = jnp.pad(k, pad)
    vp = jnp.pad(v, pad)
    s_sink = jnp.broadcast_to(sink.astype(jnp.float32).reshape(1, Hkv, G, 1, 1), (B, Hkv, G, WG_BLOCK, 1))

    def band_block(i):
        qi = lax.dynamic_index_in_dim(qb, i, axis=1, keepdims=False)
        ki = lax.dynamic_slice_in_dim(kp, i * WG_BLOCK, span, axis=1)
        vi = lax.dynamic_slice_in_dim(vp, i * WG_BLOCK, span, axis=1)
        qpos = i * WG_BLOCK + jnp.arange(WG_BLOCK)
        kpos = (i - 1) * WG_BLOCK + jnp.arange(span)
        ok = (jnp.abs(qpos[:, None] - kpos[None, :]) <= WG_WINDOW) & (kpos >= 0)[None, :] & (kpos < S)[None, :]
        s_loc = jnp.einsum('bqkgd,bnkd->bkgqn', qi, ki).astype(jnp.float32) * scale
        s_loc = jnp.where(ok, s_loc, NEG)
        s_ctx = jnp.einsum('bqkgd,bckd->bkgqc', qi, kc).astype(jnp.float32) * scale
        p = jax.nn.softmax(jnp.concatenate([s_loc, s_ctx, s_sink], axis=-1), axis=-1).astype(v.dtype)
        o = (jnp.einsum('bkgqn,bnkd->bqkgd', p[..., :span], vi)
             + jnp.einsum('bkgqc,bckd->bqkgd', p[..., span:span + C], vc))
        return o.reshape(B, WG_BLOCK, Hq * dh)

    out = lax.map(band_block, jnp.arange(nb))
    return jnp.transpose(out, (1, 0, 2, 3)).reshape(B, S, Hq * dh)


def _mixer_inputs(h, w_in, qn_a, kn_a, qn_b, kn_b):
    B, L, _ = h.shape
    p = h @ w_in
    cuts = [int(i) for i in np.cumsum(IN_SPLITS)[:-1]]
    qa, ka, va, qb, kb, vb, u, bg, cg = jnp.split(p, cuts, axis=-1)
    hd = lambda t: t.reshape(B, L, -1, HEAD_DIM)
    qa = _rmsnorm(hd(qa), qn_a)
    ka = _rmsnorm(hd(ka), kn_a)
    qb = _rmsnorm(hd(qb), qn_b)
    kb = _rmsnorm(hd(kb), kn_b)
    return qa, ka, hd(va), qb, kb, hd(vb), u, bg, cg


def _conv_ffn(h, w_up, conv_ffn, w_down):
    a, g = jnp.split(_dwconv(h @ w_up, conv_ffn), 2, axis=-1)
    return (jax.nn.silu(a) * g) @ w_down


def _layer(xl, xc, mod_l, mod_c, g_attn, w_in, qn_a, kn_a, qn_b, kn_b, rpb_a, sink_b, conv_c, w_o,
           g_ffn, w_up, conv_ffn, w_down, update_ctx):
    sh1, sc1, gt1, sh2, sc2, gt2 = jnp.split(mod_l[:, None, :], 6, axis=-1)
    csh1, csc1, cgt1, csh2, csc2, cgt2 = jnp.split(mod_c, 6, axis=-1)
    hl = _modulate(_rmsnorm(xl, g_attn), sh1, sc1)
    hc = _modulate(_rmsnorm(xc, g_attn), csh1, csc1)
    qa, ka, va, qb, kb, vb, u, bg, cg = _mixer_inputs(hl, w_in, qn_a, kn_a, qn_b, kn_b)
    qa_c, ka_c, va_c, qb_c, kb_c, vb_c, u_c, bg_c, cg_c = _mixer_inputs(hc, w_in, qn_a, kn_a, qn_b, kn_b)
    qb = _axial_rope(qb)
    kb = _axial_rope(kb)
    ya = _neighbourhood_attention(qa, ka, va, ka_c, va_c, rpb_a)
    yb = _window_gqa(qb, kb, vb, kb_c, vb_c, sink_b)
    yc = bg * _dwconv(cg * u, conv_c)
    xl = xl + gt1 * (jnp.concatenate([ya, yb, yc], axis=-1) @ w_o)
    if update_ctx:
        ya_c = _ctx_attention(qa_c, ka_c, va_c)
        yb_c = _ctx_attention(qb_c, kb_c, vb_c, sink_b)
        yc_c = bg_c * _dwconv(cg_c * u_c, conv_c)
        xc = xc + cgt1 * (jnp.concatenate([ya_c, yb_c, yc_c], axis=-1) @ w_o)
    xl = xl + gt2 * _conv_ffn(_modulate(_rmsnorm(xl, g_ffn), sh2, sc2), w_up, conv_ffn, w_down)
    if update_ctx:
        xc = xc + cgt2 * _conv_ffn(_modulate(_rmsnorm(xc, g_ffn), csh2, csc2), w_up, conv_ffn, w_down)
    return xl, xc


def setup_inputs(seed: int = 0) -> dict:
    key = jax.random.key(seed)
    ks = jax.random.split(key, 20)
    nrm = lambda k, shape, s: jax.random.normal(k, shape, jnp.float32) * s
    return {
        'x': nrm(ks[0], (BATCH, SEQ, D_MODEL), 1.0),
        'c': nrm(ks[1], (BATCH, D_MODEL), 1.0),
        'ctx': nrm(ks[2], (BATCH, CTX_LEN, D_MODEL), 1.0),
        'c_ctx': nrm(ks[3], (D_MODEL,), 1.0),
        'w_ada': nrm(ks[4], (DEPTH, D_MODEL, 6 * D_MODEL), 0.5 * D_MODEL ** -0.5),
        'b_ada': nrm(ks[5], (DEPTH, 6 * D_MODEL), 0.02),
        'g_attn': 1.0 + nrm(ks[6], (DEPTH, D_MODEL), 0.01),
        'w_in': nrm(ks[7], (DEPTH, D_MODEL, IN_W), D_MODEL ** -0.5),
        'qn_a': 1.0 + nrm(ks[8], (DEPTH, HEAD_DIM), 0.01),
        'kn_a': 1.0 + nrm(ks[9], (DEPTH, HEAD_DIM), 0.01),
        'qn_b': 1.0 + nrm(ks[10], (DEPTH, HEAD_DIM), 0.01),
        'kn_b': 1.0 + nrm(ks[11], (DEPTH, HEAD_DIM), 0.01),
        'rpb_a': nrm(ks[12], (DEPTH, NA_HEADS, 2 * NA_WIN_R - 1, 2 * NA_WIN_C - 1), 0.1),
        'sink_b': nrm(ks[13], (DEPTH, WG_HEADS), 0.5),
        'conv_c': nrm(ks[14], (DEPTH, CONV_W, SC_CH), CONV_W ** -0.5),
        'w_o': nrm(ks[15], (DEPTH, MIX_W, D_MODEL), MIX_W ** -0.5),
        'g_ffn': 1.0 + nrm(ks[16], (DEPTH, D_MODEL), 0.01),
        'w_up': nrm(ks[17], (DEPTH, D_MODEL, 2 * D_FF), D_MODEL ** -0.5),
        'conv_ffn': nrm(ks[18], (DEPTH, CONV_W, 2 * D_FF), CONV_W ** -0.5),
        'w_down': nrm(ks[19], (DEPTH, D_FF, D_MODEL), D_FF ** -0.5),
    }


def reference(x, c, ctx, c_ctx, w_ada, b_ada, g_attn, w_in, qn_a, kn_a, qn_b, kn_b, rpb_a, sink_b,
              conv_c, w_o, g_ffn, w_up, conv_ffn, w_down):
    xl, xc = x, ctx
    sc = jax.nn.silu(c)
    scc = jax.nn.silu(c_ctx)
    for l in range(DEPTH):
        mod_l = sc @ w_ada[l] + b_ada[l]
        mod_c = scc @ w_ada[l] + b_ada[l]
        xl, xc = _layer(xl, xc, mod_l, mod_c, g_attn[l], w_in[l], qn_a[l], kn_a[l], qn_b[l], kn_b[l],
                        rpb_a[l], sink_b[l], conv_c[l], w_o[l], g_ffn[l], w_up[l], conv_ffn[l], w_down[l],
                        update_ctx=(l < DEPTH - 1))
    return xl
```

```python
import numpy as np
import ml_dtypes
import concourse.bass as bass
import concourse.mybir as mybir
from concourse.bass_utils import run_bass_kernel_spmd

F32 = mybir.dt.float32
BF16 = mybir.dt.bfloat16
AF = mybir.ActivationFunctionType
ALU = mybir.AluOpType

D = 1024
S = 4096
CT = 256
TT = S + CT
L = 2
DFF = 2816
INW = 2560
EPS = 1e-6
NEGM = -30000.0
NCORES = 8

ROWCFG = [(5, 4), (5, 5), (5, 6), (0, 0), (0, 1), (15, 14), (15, 15)]
NTILE = 6 * 7 * 4


class Buf:
    __slots__ = ("name",)

    def __init__(self, name):
        self.name = name


class Op:
    __slots__ = ("eng", "fn", "deps", "dma", "sig", "sem", "val", "inc")

    def __init__(self, eng, fn, dma):
        self.eng = eng
        self.fn = fn
        self.deps = []
        self.dma = dma
        self.sig = dma
        self.sem = None
        self.val = 0
        self.inc = 1


class Sched:
    NDMA = 12
    ENGS = ["pe", "act", "dve", "pool", "sp"]

    def __init__(self, nc, stack):
        self.nc = nc
        self.csem = {e: stack.enter_context(nc.semaphore("c_" + e)) for e in self.ENGS}
        self.dsem = {e: [stack.enter_context(nc.semaphore("d_%s_%d" % (e, i))) for i in range(self.NDMA)]
                     for e in ("sp", "pool")}
        self.cnt = {e: 0 for e in self.ENGS}
        self.dcnt = {e: 0 for e in self.dsem}
        self.dtot = {}
        self.seen = {e: {} for e in self.ENGS}
        self.nphase = 0
        self.reset()

    def reset(self):
        self.ops = []
        self.last_w = {}
        self.readers = {}
        self.dma_hist = {}

    def add(self, eng, fn, r=(), w=(), dma=False):
        op = Op(eng, fn, dma)
        deps = {}
        for b in r:
            lw = self.last_w.get(b)
            if lw is not None:
                deps[id(lw)] = (lw, 0)
        for b in w:
            lw = self.last_w.get(b)
            if lw is not None and id(lw) not in deps:
                deps[id(lw)] = (lw, 1)
            for rd in self.readers.get(b, ()):
                if id(rd) not in deps:
                    deps[id(rd)] = (rd, 1)
        for p, kind in deps.values():
            if (not p.dma) and (not dma) and p.eng == eng and kind == 1 and eng == "pe":
                continue
            op.deps.append(p)
            p.sig = True
        if dma:
            h = self.dma_hist.setdefault(eng, [])
            if len(h) >= self.NDMA:
                op.deps.append(h[len(h) - self.NDMA])
            h.append(op)
        for b in r:
            self.readers.setdefault(b, []).append(op)
        for b in w:
            self.last_w[b] = op
            self.readers[b] = []
        self.ops.append(op)
        return op

    def emit_phase(self):
        nc = self.nc
        per = {e: [o for o in self.ops if o.eng == e] for e in self.ENGS}
        bar = [(self.csem[e], self.cnt[e]) for e in self.ENGS if self.cnt[e] > 0]
        for e in self.dsem:
            for s in self.dsem[e]:
                if self.dtot.get(id(s), 0) > 0:
                    bar.append((s, self.dtot[id(s)]))
        for e in self.ENGS:
            comp = [o for o in per[e] if not o.dma]
            if comp:
                comp[-1].sig = True
        for op in self.ops:
            if op.dma:
                i = self.dcnt[op.eng]
                self.dcnt[op.eng] = i + 1
                sm = self.dsem[op.eng][i % self.NDMA]
                t = self.dtot.get(id(sm), 0) + 16
                self.dtot[id(sm)] = t
                op.sem, op.val, op.inc = sm, t, 16
            elif op.sig:
                self.cnt[op.eng] += 1
                op.sem, op.val, op.inc = self.csem[op.eng], self.cnt[op.eng], 1
        first = self.nphase == 0
        self.nphase += 1

        def run(e, eng):
            seen = self.seen[e]
            if not first:
                for sm, v in bar:
                    if seen.get(id(sm), 0) < v:
                        eng.wait_ge(sm, v)
                        seen[id(sm)] = v
            for op in per[e]:
                for p in op.deps:
                    k = id(p.sem)
                    if seen.get(k, 0) < p.val:
                        eng.wait_ge(p.sem, p.val)
                        seen[k] = p.val
                ins = op.fn(eng)
                if op.sig:
                    ins.then_inc(op.sem, op.inc)

        with nc.Block() as block:
            @block.tensor
            def _(eng):
                run("pe", eng)

            @block.scalar
            def _(eng):
                run("act", eng)

            @block.vector
            def _(eng):
                run("dve", eng)

            @block.gpsimd
            def _(eng):
                run("pool", eng)

            @block.sync
            def _(eng):
                run("sp", eng)
        self.reset()

    def emit_final(self):
        nc = self.nc
        bar = [(self.csem[e], self.cnt[e]) for e in self.ENGS if self.cnt[e] > 0]
        for e in self.dsem:
            for s in self.dsem[e]:
                if self.dtot.get(id(s), 0) > 0:
                    bar.append((s, self.dtot[id(s)]))
        with nc.Block() as block:
            @block.sync
            def _(eng):
                for sm, v in bar:
                    eng.wait_ge(sm, v)


class Tn:
    def __init__(self, t, name):
        self.t = t
        self.b = Buf(name)

    def __getitem__(self, k):
        return self.t[k]


class K:
    def __init__(self, nc, stack, dbg=None):
        self.nc = nc
        self.st = stack
        self.s = Sched(nc, stack)
        self.dbg = dbg or set()
        self.gst = stack

    def sb(self, name, shape, dt):
        self.nn = getattr(self, "nn", 0) + 1
        t = self.st.enter_context(self.nc.sbuf_tensor("s%d_%s" % (self.nn, name), list(shape), dt))
        return Tn(t, name)

    def ps(self, name, dt=F32, cols=512):
        self.nn = getattr(self, "nn", 0) + 1
        t = self.st.enter_context(self.nc.psum_tensor("p%d_%s" % (self.nn, name), [128, cols], dt))
        return Tn(t, name)

    def dram(self, name, shape, dt, kind="Internal"):
        if name in self.dbg:
            kind = "ExternalOutput"
        if name in getattr(self, "dbg_in", ()):
            kind = "ExternalInput"
        t = self.nc.dram_tensor(name, list(shape), dt, kind=kind)
        d = Tn(t.ap(), name)
        d.h = t
        return d

    def dma(self, out, in_, r=(), w=(), q="sp", **kw):
        return self.s.add(q, lambda e: e.dma_start(out=out, in_=in_, **kw), r=r, w=w, dma=True)

    def mm(self, out, lhsT, rhs, start, stop=True, r=(), w=()):
        return self.s.add(
            "pe",
            lambda e: e.matmul(out, lhsT, rhs, start=start, stop=stop, skip_group_check=True),
            r=r, w=w)

    def tr(self, out, in_, ident, r=(), w=()):
        return self.s.add("pe", lambda e: e.transpose(out, in_, ident), r=r, w=w)

    def act(self, out, in_, func, r=(), w=(), eng="act", **kw):
        return self.s.add(eng, lambda e: e.activation(out=out, in_=in_, func=func, **kw), r=r, w=w)

    def tt(self, eng, out, in0, in1, op, r=(), w=()):
        return self.s.add(eng, lambda e: e.tensor_tensor(out=out, in0=in0, in1=in1, op=op), r=r, w=w)

    def ts(self, eng, out, in0, s1, s2, op0, op1=None, r=(), w=()):
        if op1 is None:
            return self.s.add(eng, lambda e: e.tensor_scalar(out=out, in0=in0, scalar1=s1, scalar2=None, op0=op0), r=r, w=w)
        return self.s.add(eng, lambda e: e.tensor_scalar(out=out, in0=in0, scalar1=s1, scalar2=s2, op0=op0, op1=op1), r=r, w=w)

    def stt(self, eng, out, in0, scalar, in1, op0, op1, r=(), w=()):
        return self.s.add(eng, lambda e: e.scalar_tensor_tensor(out=out, in0=in0, scalar=scalar, in1=in1, op0=op0, op1=op1), r=r, w=w)

    def cp(self, eng, out, in_, r=(), w=()):
        if eng == "act":
            return self.s.add(eng, lambda e: e.copy(out=out, in_=in_), r=r, w=w)
        return self.s.add(eng, lambda e: e.tensor_copy(out=out, in_=in_), r=r, w=w)

    def recip(self, out, in_, r=(), w=()):
        return self.s.add("dve", lambda e: e.reciprocal(out=out, in_=in_), r=r, w=w)

    def rsqrt(self, out, in_, epsb, r=(), w=(), inw=()):
        self.act(out, in_, AF.Ln, r=list(r) + [epsb.b], w=list(inw) + list(w), bias=epsb[:, 0:1])
        return self.act(out, out, AF.Exp, r=list(w), w=list(w), scale=-0.5)

    def memset(self, eng, ap, val, w=()):
        return self.s.add(eng, lambda e: e.memset(ap, val), w=w)


def _na_index():
    kr_in = np.arange(128) // 32
    kc_in = np.arange(128) % 32
    r_in = np.arange(64) // 16
    c_in = np.arange(64) % 16
    drow = np.zeros((7, 128, 64), np.int64)
    rok = np.zeros((7, 128, 64), bool)
    for i, (a, b) in enumerate(ROWCFG):
        r = 4 * a + r_in[None, :]
        kr = 4 * b + kr_in[:, None]
        r0 = np.clip(r - 4, 0, 56)
        rok[i] = (kr >= r0) & (kr < r0 + 8)
        drow[i] = np.clip(kr - r + 7, 0, 14)
    dcol = np.zeros((4, 128, 64), np.int64)
    cok = np.zeros((4, 128, 64), bool)
    for i, j in enumerate((0, 1, 2, 3)):
        kc0 = int(np.clip(16 * j - 8, 0, 32))
        c = 16 * j + c_in[None, :]
        kc = kc0 + kc_in[:, None]
        c0 = np.clip(c - 8, 0, 48)
        cok[i] = (kc >= c0) & (kc < c0 + 16)
        dcol[i] = np.clip(kc - c + 15, 0, 30)
    return drow, rok, dcol, cok


def _consts():
    c = {}
    c["identb"] = np.eye(128, dtype=np.float32).astype(ml_dtypes.bfloat16)
    c["identf"] = np.eye(128, dtype=np.float32)
    bm = np.zeros((128, 128), np.float32)
    bm[:64, :64] = 1.0 / 64
    bm[64:, 64:] = 1.0 / 64
    c["bm"] = bm.astype(ml_dtypes.bfloat16)
    pm = np.zeros((128, 128), np.float32)
    for m in range(128):
        k = m + 32 if (m % 64) < 32 else m - 32
        pm[k, m] = 1.0
    c["pm"] = pm
    t = np.arange(S)
    row = (t // 64).astype(np.float32)
    col = (t % 64).astype(np.float32)
    inv = (np.float32(10000.0) ** (-np.arange(16, dtype=np.float32) / np.float32(16))).astype(np.float32)
    ang = np.concatenate([row[:, None] * inv, col[:, None] * inv], axis=-1).astype(np.float32)
    cos = np.cos(ang).astype(np.float32)
    sin = np.sin(ang).astype(np.float32)
    d = np.arange(128) % 64
    cosT = cos[:, d % 32].T
    sgn = np.where(d < 32, -1.0, 1.0).astype(np.float32)
    sinT = sin[:, d % 32].T * sgn[:, None]
    c["rope"] = np.ascontiguousarray(np.stack([cosT, sinT], axis=1)).astype(np.float32)
    ki = np.arange(128)[:, None]
    qi = np.arange(128)[None, :]
    mprev = np.where(qi <= ki, 1.0, 0.0)
    mnext = np.where(ki <= qi, 1.0, 0.0)
    c["wgm"] = np.stack([np.ones_like(mprev), mprev, mnext], axis=1).astype(np.float32).astype(ml_dtypes.bfloat16)
    return c


def _core_inputs(b, inp, consts, shared=None):
    f = lambda a: np.ascontiguousarray(np.asarray(a, dtype=np.float32))
    m = {}
    m["x"] = f(inp["x"][b])
    m["ctx"] = f(inp["ctx"][b])
    cvec = np.stack([np.asarray(inp["c"][b]), np.asarray(inp["c_ctx"])], 0)
    m["cvt"] = f(cvec.reshape(2, 8, 128).transpose(2, 1, 0))
    if shared is not None:
        m.update(shared)
        return m
    m["w_ada"] = f(inp["w_ada"])
    m["b_ada"] = f(inp["b_ada"])
    gt = lambda g: np.asarray(g).reshape(L, 8, 128).transpose(2, 0, 1)
    m["gT"] = f(np.stack([gt(inp["g_attn"]), gt(inp["g_ffn"])], axis=2))
    m["w_in"] = f(inp["w_in"])
    qkg = np.stack([np.asarray(inp[k]) for k in ("qn_a", "kn_a", "qn_b", "kn_b")], axis=-1)
    m["qkg"] = f(np.concatenate([qkg, qkg], axis=1).transpose(1, 0, 2))
    drow, rok, dcol, cok = _na_index()
    rpb = np.asarray(inp["rpb_a"], dtype=np.float32)
    g = rpb[:, :, drow[:, None], dcol[None, :]]
    ok = (rok[:, None] & cok[None, :])[None, None]
    nab = np.where(ok, g, np.float32(NEGM)).astype(np.float32)
    m["nab"] = f(nab.transpose(4, 0, 1, 2, 3, 5).reshape(128, L, NTILE * 64))
    m["sink"] = f(np.broadcast_to(np.asarray(inp["sink_b"])[None], (128, L, 6)))
    m["convc"] = f(np.asarray(inp["conv_c"]).reshape(L, 3, 2, 128).transpose(3, 0, 2, 1))
    m["w_o"] = f(inp["w_o"])
    m["w_up"] = f(inp["w_up"])
    m["convf"] = f(np.asarray(inp["conv_ffn"]).reshape(L, 3, 44, 128).transpose(3, 0, 2, 1))
    m["w_down"] = f(inp["w_down"])
    m.update(consts)
    return m


from contextlib import ExitStack, contextmanager


@contextmanager
def phase(k):
    old = k.st
    with ExitStack() as st:
        k.st = st
        yield
        k.s.emit_phase()
    k.st = old


def fence(k, eng, r, w):
    d = k.dummy
    return k.s.add(eng, lambda e: e.memset(d[0:1, 0:1], 0.0), r=r, w=list(w) + [d.b])


def wload_cast(k, src, nkc, segs, nsplit=4):
    subs = {}
    step = (nkc + nsplit - 1) // nsplit
    for (s0, s1, dst, d0) in segs:
        for k0 in range(0, nkc, step):
            k1 = min(nkc, k0 + step)
            sb_ = Buf("sub")
            subs.setdefault(id(dst), (dst, []))[1].append(sb_)
            k.dma(dst[:, k0:k1, d0:d0 + (s1 - s0)], src[:, k0:k1, s0:s1], w=[sb_], q="pool")
    for dst, bl in subs.values():
        fence(k, "pool", r=bl, w=[dst.b])


class WSegs:
    def __init__(self):
        self.rng = []

    def bufs(self, c0, c1):
        return [b for (d0, d1, b) in self.rng if d0 < c1 and c0 < d1]


def wload_segs(k, src, nkc, dst, segs):
    ws = WSegs()
    for (s0, s1, d0) in segs:
        b = Buf("wseg")
        k.dma(dst[:, 0:nkc, d0:d0 + (s1 - s0)], src[:, :, s0:s1], w=[b], q="pool")
        ws.rng.append((d0, d0 + (s1 - s0), b))
    return ws


def wload(k, stg, src, nkc, ncols, segs, engs=("dve", "pool", "act"), blk=256, func=None):
    subs = {}
    ci = 0
    for bi, c0 in enumerate(range(0, ncols, blk)):
        c1 = min(ncols, c0 + blk)
        sg = stg[bi % len(stg)]
        k.dma(sg[:, 0:nkc, 0:c1 - c0], src[:, :, c0:c1], w=[sg.b])
        for (s0, s1, dst, d0) in segs:
            lo, hi = max(s0, c0), min(s1, c1)
            if lo >= hi:
                continue
            sb_ = Buf("sub")
            subs.setdefault(id(dst), (dst, []))[1].append(sb_)
            if func is None:
                k.cp(engs[ci % len(engs)], dst[:, 0:nkc, d0 + lo - s0:d0 + hi - s0], sg[:, 0:nkc, lo - c0:hi - c0],
                     r=[sg.b], w=[sb_])
            else:
                k.act(dst[:, 0:nkc, d0 + lo - s0:d0 + hi - s0], sg[:, 0:nkc, lo - c0:hi - c0], func, r=[sg.b], w=[sb_])
            ci += 1
    for dst, bl in subs.values():
        fence(k, "pool", r=bl, w=[dst.b])


class G:
    pass


def build(nlayers=L, dbg=(), stop_after=None, dbg_in=(), only=None):
    nc = bass.Bass("TRN2", target_bir_lowering=False)
    gst = ExitStack()
    with gst:
        k = K(nc, gst, set(dbg))
        k.dbg_in = set(dbg_in)
        g = G()
        din = {}

        def inp(name, shape, dt=F32):
            din[name] = nc.dram_tensor(name, list(shape), dt, kind="ExternalInput").ap()

        inp("x", [S, D]); inp("ctx", [CT, D]); inp("cvt", [128, 8, 2])
        inp("w_ada", [L, D, 6 * D]); inp("b_ada", [L, 6 * D]); inp("gT", [128, L, 2, 8])
        inp("w_in", [L, D, INW]); inp("qkg", [128, L, 4]); inp("nab", [128, L, NTILE * 64])
        inp("sink", [128, L, 6]); inp("convc", [128, L, 2, 3]); inp("w_o", [L, D, D])
        inp("w_up", [L, D, 2 * DFF]); inp("convf", [128, L, 44, 3]); inp("w_down", [L, DFF, D])
        inp("identb", [128, 128], BF16); inp("identf", [128, 128]); inp("bm", [128, 128], BF16)
        inp("pm", [128, 128]); inp("rope", [128, 2, S]); inp("wgm", [128, 3, 128], BF16)
        g.din = din
        g.out = nc.dram_tensor("out", [S, D], F32, kind="ExternalOutput").ap()
        g.modrow = k.dram("modrow", [2, L * 6 * D], F32)
        g.QK = k.dram("QK", [11, 128, TT], BF16)
        g.VA = k.dram("VA", [TT, 6, 128], BF16)
        g.VB = k.dram("VB", [TT, 2, 128], BF16)
        g.CU = k.dram("CU", [2, 128, TT], F32)
        g.BG = k.dram("BG", [2, 128, TT], F32)
        g.XN = k.dram("XN", [TT, D], F32)
        g.XP = k.dram("XP", [TT, D], F32)
        g.XM = k.dram("XM", [TT, D], F32)
        g.HT2 = k.dram("HT2", [8, 128, TT], BF16)
        g.identb = k.sb("identb", [128, 128], BF16)
        g.identf = k.sb("identf", [128, 128], F32)
        g.bm = k.sb("bm", [128, 128], BF16)
        g.pm = k.sb("pm", [128, 128], F32)
        g.wgm = k.sb("wgm", [128, 3, 128], BF16)
        g.modT = k.sb("modT", [128, L, 96], F32)
        g.AB = k.sb("AB", [128, L * 2 * 4, 8], F32)
        g.gT = k.sb("gT", [128, L, 2, 8], F32)
        g.qkg = k.sb("qkg", [128, L, 4], F32)
        g.sink = k.sb("sink", [128, L, 6], F32)
        g.convc = k.sb("convc", [128, L, 2, 3], F32)
        g.convf = k.sb("convf", [128, L, 44, 3], F32)
        k.dummy = k.sb("dummy", [128, 4], F32)
        g.epsb = k.sb("epsb", [128, 1], F32)

        p0_mods(k, g)
        if stop_after == "p0":
            k.s.emit_final()
            return nc
        for l in range(nlayers):
            if only is None or "p1" in only:
                p1_inproj(k, g, l)
            if stop_after == "p1":
                break
            if only is None or "p2" in only:
                p2_mixers(k, g, l)
            if stop_after == "p2":
                break
            if only is None or "p3" in only:
                p3_ffn(k, g, l, 0)
                p3_ffn(k, g, l, 1)
        k.s.emit_final()
    return nc


def abv(g, l, var, which):
    return g.AB[:, (l * 2 + var) * 4 + which, :]


def p0_mods(k, g):
    din = g.din
    with phase(k):
        for nm in ("identb", "identf", "bm", "pm", "wgm", "gT", "qkg", "sink", "convc", "convf"):
            t = getattr(g, nm)
            k.dma(t[:], din[nm], w=[t.b])
        k.memset("dve", g.epsb[:], EPS, w=[g.epsb.b])
        for gi in (0, 2):
            k.ts("dve", g.qkg[:, :, gi:gi + 1], g.qkg[:, :, gi:gi + 1], 0.125, None, ALU.mult, r=[g.qkg.b], w=[g.qkg.b])
        cvt = k.sb("cvt", [128, 8, 2], F32)
        sct = k.sb("sct", [128, 8, 2], F32)
        k.dma(cvt[:], din["cvt"], w=[cvt.b])
        k.act(sct[:], cvt[:], AF.Silu, r=[cvt.b], w=[sct.b])
        NB0 = 6
        wst = [k.sb("wst%d" % i, [128, 8, 512], F32) for i in range(NB0)]
        bad = [k.sb("bad%d" % i, [2, 512], F32) for i in range(NB0)]
        mrow = [k.sb("mrow%d" % i, [2, 512], F32) for i in range(NB0)]
        pmm = [k.ps("p0m%d" % i) for i in range(NB0)]
        pT = k.ps("p0T")
        chunks = [(l_, n_) for l_ in range(L) for n_ in range(12)]

        def p0_load(ci):
            l_, n_ = chunks[ci]
            i = ci % NB0
            wv = din["w_ada"][l_].rearrange("(kc p) n -> p kc n", p=128)
            k.dma(wst[i][:], wv[:, :, n_ * 512:(n_ + 1) * 512], w=[wst[i].b])
            for r_ in range(2):
                k.dma(bad[i][r_:r_ + 1, :], din["b_ada"][l_:l_ + 1, n_ * 512:(n_ + 1) * 512], w=[bad[i].b])

        PF = NB0 - 2
        for ci in range(PF):
            p0_load(ci)
        for l in range(L):
            for n in range(12):
                ci = l * 12 + n
                i = ci % NB0
                if ci + PF < len(chunks):
                    p0_load(ci + PF)
                for kc in range(8):
                    k.mm(pmm[i][0:2, :], sct[:, kc, :], wst[i][:, kc, :], start=(kc == 0),
                         r=[sct.b, wst[i].b], w=[pmm[i].b])
                k.tt("dve", mrow[i][:], pmm[i][0:2, :], bad[i][:], ALU.add, r=[bad[i].b], w=[pmm[i].b, mrow[i].b])
                k.dma(g.modrow[:, l * 6144 + n * 512:l * 6144 + (n + 1) * 512], mrow[i][:], r=[mrow[i].b])
                for j in range(4):
                    idx = n * 4 + j
                    k.mm(pT[:, idx * 2:idx * 2 + 2], mrow[i][0:2, j * 128:(j + 1) * 128], g.identf[0:2, 0:2],
                         start=(idx == 0), r=[mrow[i].b, g.identf.b], w=[pT.b])
            k.cp("dve", g.modT[:, l, :], pT[:, 0:96], w=[pT.b, g.modT.b])
            mv = g.modT[:, l, :].rearrange("p (c v) -> p c v", v=2)
            for var in range(2):
                k.stt("dve", abv(g, l, var, 0), mv[:, 8:16, var], 1.0, g.gT[:, l, 0, :], ALU.add, ALU.mult,
                      r=[g.modT.b, g.gT.b], w=[g.AB.b])
                k.cp("dve", abv(g, l, var, 1), mv[:, 0:8, var], r=[g.modT.b], w=[g.AB.b])
                k.stt("dve", abv(g, l, var, 2), mv[:, 32:40, var], 1.0, g.gT[:, l, 1, :], ALU.add, ALU.mult,
                      r=[g.modT.b, g.gT.b], w=[g.AB.b])
                k.cp("dve", abv(g, l, var, 3), mv[:, 24:32, var], r=[g.modT.b], w=[g.AB.b])


def norm_rows(k, xt, s, ss, junk, r=()):
    k.act(junk[:], xt, AF.Square, r=list(r), w=[junk.b, ss.b], accum_out=ss[:, s:s + 1], scale=1.0 / 32.0)


W1_QA, W1_KA, W1_QB, W1_KBD, W1_U, W1_CG, W1_BG, W1_V = 0, 384, 768, 1152, 1408, 1664, 1920, 2176
W1_N = 2688


def p1_inproj(k, g, l):
    din = g.din
    with phase(k):
        W = k.sb("w1", [128, 8, W1_N], BF16)
        src = din["w_in"][l].rearrange("(kc p) n -> p kc n", p=128)
        segs = [(0, 128, W1_QA), (128, 384, W1_QA + 128), (384, 768, W1_KA), (1152, 1536, W1_QB),
                (1536, 1600, W1_KBD), (1536, 1600, W1_KBD + 64),
                (1600, 1664, W1_KBD + 128), (1600, 1664, W1_KBD + 192),
                (1792, 2048, W1_U), (2304, 2560, W1_CG), (2048, 2304, W1_BG),
                (768, 1152, W1_V), (1664, 1792, W1_V + 384)]
        WS = wload_segs(k, src, 8, W, segs)

        xt = [k.sb("xt%d" % i, [128, 4, D], F32) for i in range(2)]
        cs = [k.sb("cs%d" % i, [128, 2, 512], F32) for i in range(2)]
        xn = [k.sb("xn%d" % i, [128, 4, D], BF16) for i in range(2)]
        junk = k.sb("junk", [128, D], BF16)
        ss = [k.sb("ss%d" % i, [128, 4], F32) for i in range(2)]
        rs = [k.sb("rs%d" % i, [128, 4], F32) for i in range(2)]
        hTs = [k.sb("hT%d" % i, [128, 8, 512], BF16) for i in range(2)]
        sq = [k.sb("sq%d" % i, [128, 512], BF16) for i in range(3)]
        rstd = [k.sb("rstd%d" % i, [128, 512], F32) for i in range(3)]
        qn = [k.sb("qn%d" % i, [128, 512], F32) for i in range(3)]
        t1 = [k.sb("t1%d" % i, [128, 512], F32) for i in range(3)]
        t2 = [k.sb("t2%d" % i, [128, 512], F32) for i in range(3)]
        ob = [k.sb("ob%d" % i, [128, 512], BF16) for i in range(3)]
        usb = [k.sb("usb%d" % i, [128, 512], F32) for i in range(2)]
        of = [k.sb("of%d" % i, [128, 512], F32) for i in range(3)]
        vt = [k.sb("vt%d" % i, [128, 8, 128], BF16) for i in range(2)]
        for v_ in vt:
            k.memset("pool", v_[:, :, 64:128], 1.0, w=[v_.b])
        pT = [k.ps("pT%d" % i, BF16, 1024) for i in range(2)]
        pq = [k.ps("pq%d" % i) for i in range(4)]
        pmn = k.ps("pmn")
        pr = k.ps("pr")

        tiles = [(i * 512, 512, 0) for i in range(8)] + [(S, CT, 1)]
        cnt = {"ob": 0, "of": 0, "pq": 0, "a": 0, "vt": 0, "pT": 0}

        def load(ti):
            tok0, n, var = tiles[ti]
            nsub = n // 128
            b = ti % 2
            if var == 0:
                srcx = (din["x"] if l == 0 else g.XM[:])[tok0:tok0 + n, :]
                rd = [] if l == 0 else [g.XM.b]
            else:
                srcx = din["ctx"] if l == 0 else g.XM[S:S + CT, :]
                rd = [] if l == 0 else [g.XM.b]
            k.dma(xt[b][:, 0:nsub, :], srcx.rearrange("(s p) f -> p s f", p=128), w=[xt[b].b])

        def load_cs(ti):
            tok0, n, var = tiles[ti]
            b = ti % 2
            if var == 0:
                k.dma(cs[b][:, :, 0:n], din["rope"][:, :, tok0:tok0 + n], w=[cs[b].b])

        def norm(ti):
            tok0, n, var = tiles[ti]
            nsub = n // 128
            b = ti % 2
            xn_ = xn[b]
            for s in range(nsub):
                norm_rows(k, xt[b][:, s, :], s, ss[b], junk, r=[xt[b].b])
            k.rsqrt(rs[b][:, 0:nsub], ss[b][:, 0:nsub], g.epsb, r=[ss[b].b], w=[rs[b].b])
            for s in range(nsub):
                if s % 2 == 0:
                    k.ts("dve", xn_[:, s, :], xt[b][:, s, :], rs[b][:, s:s + 1], None, ALU.mult,
                         r=[xt[b].b, rs[b].b], w=[xn_.b])
                else:
                    k.act(xn_[:, s, :], xt[b][:, s, :], AF.Copy, r=[xt[b].b, rs[b].b], w=[xn_.b], scale=rs[b][:, s:s + 1])

        def trans(ti):
            tok0, n, var = tiles[ti]
            nsub = n // 128
            xn_ = xn[ti % 2]
            hT_ = hTs[ti % 2]
            for kc in range(8):
                p = pT[cnt["pT"] % 2]
                cnt["pT"] += 1
                for s in range(nsub):
                    k.tr(p[:, s * 128:(s + 1) * 128], xn_[:, s, kc * 128:(kc + 1) * 128], g.identb[:],
                         r=[xn_.b, g.identb.b], w=[p.b])
                k.act(hT_[:, kc, 0:n], p[:, 0:n], AF.Identity, r=[g.AB.b], w=[p.b, hT_.b],
                      scale=abv(g, l, var, 0)[:, kc:kc + 1], bias=abv(g, l, var, 1)[:, kc:kc + 1])

        load(0)
        load_cs(0)
        load(1)
        norm(0)
        trans(0)
        chunks = []

        def add_chunk(A, B=None, C=None, pre=None):
            chunks.append((A, B, C, pre))

        def make_tile(ti):
            tok0, n, var = tiles[ti]
            nsub = n // 128
            b = ti % 2
            hT = hTs[b]

            def proj(wc0):
                p = pq[cnt["pq"] % 4]
                cnt["pq"] += 1
                wb = WS.bufs(wc0, wc0 + 128)
                for kc in range(8):
                    k.mm(p[:, 0:n], W[:, kc, wc0:wc0 + 128], hT[:, kc, 0:n], start=(kc == 0), r=wb + [hT.b], w=[p.b])
                return p

            def qk_chunk(wc0, gi, rope, qkidx, pre=None):
                cell = {}

                def A():
                    p = proj(wc0)
                    a = cnt["a"] % 3
                    cnt["a"] += 1
                    cell["p"], cell["a"] = p, a
                    k.act(sq[a][:, 0:n], p[:, 0:n], AF.Square, w=[p.b, sq[a].b])

                def B():
                    p, a = cell["p"], cell["a"]
                    k.mm(pmn[:, 0:n], g.bm[:], sq[a][:, 0:n], start=True, r=[g.bm.b, sq[a].b], w=[pmn.b])
                    k.rsqrt(rstd[a][:, 0:n], pmn[:, 0:n], g.epsb, w=[rstd[a].b], inw=[pmn.b])
                    if not rope:
                        o = ob[cnt["ob"] % 3]
                        cnt["ob"] += 1
                        k.stt("dve", o[:, 0:n], p[:, 0:n], g.qkg[:, l, gi:gi + 1], rstd[a][:, 0:n], ALU.mult, ALU.mult,
                              r=[rstd[a].b, g.qkg.b], w=[p.b, o.b])
                        k.dma(g.QK[qkidx, :, tok0:tok0 + n], o[:, 0:n], r=[o.b])
                    else:
                        k.stt("dve", qn[a][:, 0:n], p[:, 0:n], g.qkg[:, l, gi:gi + 1], rstd[a][:, 0:n], ALU.mult, ALU.mult,
                              r=[rstd[a].b, g.qkg.b], w=[p.b, qn[a].b])

                def C():
                    a = cell["a"]
                    o = ob[cnt["ob"] % 3]
                    cnt["ob"] += 1
                    k.mm(pr[:, 0:n], g.pm[:], qn[a][:, 0:n], start=True, r=[g.pm.b, qn[a].b], w=[pr.b])
                    k.tt("pool", t1[a][:, 0:n], qn[a][:, 0:n], cs[b][:, 0, 0:n], ALU.mult, r=[qn[a].b, cs[b].b], w=[t1[a].b])
                    k.tt("dve", t2[a][:, 0:n], pr[:, 0:n], cs[b][:, 1, 0:n], ALU.mult, r=[cs[b].b], w=[pr.b, t2[a].b])
                    k.tt("pool", o[:, 0:n], t1[a][:, 0:n], t2[a][:, 0:n], ALU.add, r=[t1[a].b, t2[a].b], w=[o.b])
                    k.dma(g.QK[qkidx, :, tok0:tok0 + n], o[:, 0:n], r=[o.b])

                add_chunk(A, B, C if rope else None, pre)

            def tile_pre():
                if ti + 1 < len(tiles):
                    norm(ti + 1)
                if ti + 2 < len(tiles):
                    load(ti + 2)

            def mid_pre():
                if ti + 1 < len(tiles):
                    trans(ti + 1)
                    load_cs(ti + 1)

            for c in range(3):
                qk_chunk(W1_QA + c * 128, 0, False, c, pre=tile_pre if c == 0 else None)
            for c in range(3):
                qk_chunk(W1_KA + c * 128, 1, False, 3 + c)
            for c in range(3):
                qk_chunk(W1_QB + c * 128, 2, var == 0, 6 + c, pre=mid_pre if c == 0 else None)
            for c in range(2):
                qk_chunk(W1_KBD + c * 128, 3, var == 0, 9 + c)
            for c in range(2):
                ucell = {}

                def A_u(c=c, ucell=ucell):
                    pu = proj(W1_U + c * 128)
                    a = cnt["u"] % 2
                    cnt["u"] += 1
                    ucell["a"] = a
                    k.cp("act", usb[a][:, 0:n], pu[:, 0:n], w=[pu.b, usb[a].b])

                def A_cg(c=c, ucell=ucell):
                    a = ucell["a"]
                    pc = proj(W1_CG + c * 128)
                    o = of[cnt["of"] % 3]
                    cnt["of"] += 1
                    k.tt("dve", o[:, 0:n], pc[:, 0:n], usb[a][:, 0:n], ALU.mult, r=[usb[a].b], w=[pc.b, o.b])
                    k.dma(g.CU[c, :, tok0:tok0 + n], o[:, 0:n], r=[o.b])

                def A_bg(c=c):
                    pb = proj(W1_BG + c * 128)
                    o = of[cnt["of"] % 3]
                    cnt["of"] += 1
                    k.cp("act", o[:, 0:n], pb[:, 0:n], w=[pb.b, o.b])
                    k.dma(g.BG[c, :, tok0:tok0 + n], o[:, 0:n], r=[o.b])

                add_chunk(A_u)
                add_chunk(A_cg)
                add_chunk(A_bg)
            for s in range(nsub):
                def A_v(s=s):
                    pv_ = pq[cnt["pq"] % 4]
                    cnt["pq"] += 1
                    wb = WS.bufs(W1_V, W1_V + 512)
                    for kc in range(8):
                        k.mm(pv_[:, :], hT[:, kc, s * 128:(s + 1) * 128], W[:, kc, W1_V:W1_V + 512], start=(kc == 0),
                             r=wb + [hT.b], w=[pv_.b])
                    v = vt[cnt["vt"] % 2]
                    cnt["vt"] += 1
                    k.cp("act" if s % 2 == 0 else "dve", v[:, :, 0:64], pv_[:, :].rearrange("p (h d) -> p h d", d=64),
                         w=[pv_.b, v.b])
                    k.dma(g.VA[tok0 + s * 128:tok0 + (s + 1) * 128, :, :], v[:, 0:6, :], r=[v.b])
                    k.dma(g.VB[tok0 + s * 128:tok0 + (s + 1) * 128, :, :], v[:, 6:8, :], r=[v.b])

                add_chunk(A_v)

        cnt["u"] = 0
        for ti in range(len(tiles)):
            make_tile(ti)
        nch = len(chunks)
        for i in range(nch + 2):
            if i < nch:
                A, B, C, pre = chunks[i]
                if pre is not None:
                    pre()
                A()
            if 0 <= i - 1 < nch and chunks[i - 1][1] is not None:
                chunks[i - 1][1]()
            if 0 <= i - 2 < nch and chunks[i - 2][2] is not None:
                chunks[i - 2][2]()

def bcast_row(dt_, row, off, n):
    ncols = dt_.t.shape[1]
    return bass.AP(dt_.h, row * ncols + off, [[0, 128], [1, n]])


def p3_tiles(l):
    t = []
    s0 = 0
    while s0 < S:
        n = min(510, S - s0)
        t.append((s0, n, 0))
        s0 += n
    if l == 0:
        t.append((S, CT, 1))
    return t


def p3_ffn(k, g, l, hf):
    din = g.din
    HC = 11
    with phase(k):
        WU = k.sb("wu", [128, 8, 2 * HC * 128], BF16)
        WD = k.sb("wd", [128, HC, D], BF16)
        srcu = din["w_up"][l].rearrange("(kc p) n -> p kc n", p=128)
        a0 = hf * HC * 128
        CG = [(0, 1), (1, 2), (2, 4), (4, 7), (7, HC)]
        usegs = []
        for (c0, c1) in CG:
            usegs.append((a0 + c0 * 128, a0 + c1 * 128, c0 * 128))
            usegs.append((DFF + a0 + c0 * 128, DFF + a0 + c1 * 128, HC * 128 + c0 * 128))
        WUS = wload_segs(k, srcu, 8, WU, usegs)
        srcd = din["w_down"][l][a0:a0 + HC * 128, :].rearrange("(hc p) n -> p hc n", p=128)
        WDS = wload_segs(k, srcd, HC, WD, [(0, 512, 0), (512, 1024, 512)])
        gtb = [k.sb("gtb%d" % v, [128, D], F32) for v in range(2)]
        for v in range(2 if l == 0 else 1):
            k.dma(gtb[v][:], bcast_row(g.modrow, v, l * 6144 + 5 * D, D), w=[gtb[v].b])
        ht = [k.sb("ht%d" % i, [128, 8, 512], BF16) for i in range(2)]
        actT = [k.sb("actT%d" % i, [128, HC, 512], BF16) for i in range(2)]
        t1 = [k.sb("t1%d" % i, [128, 512], F32) for i in range(2)]
        t2 = [k.sb("t2%d" % i, [128, 512], F32) for i in range(2)]
        ca = [k.sb("ca%d" % i, [128, 512], F32) for i in range(2)]
        cg = [k.sb("cg%d" % i, [128, 512], F32) for i in range(2)]
        sa = [k.sb("sa%d" % i, [128, 512], F32) for i in range(2)]
        xt = [k.sb("xt%d" % i, [128, D], F32) for i in range(8)]
        xo = [k.sb("xo%d" % i, [128, D], F32) for i in range(2)]
        tmp = [k.sb("tmp%d" % i, [128, 512], F32) for i in range(2)]
        pa = [k.ps("pa%d" % i) for i in range(2)]
        pg = [k.ps("pg%d" % i) for i in range(2)]
        po = [k.ps("po%d" % i) for i in range(3)]
        xsrc = g.XN if hf == 0 else g.XP
        tiles = p3_tiles(l)
        cnt = {"x": 0, "o": 0, "po": 0, "c": 0, "tmp": 0}

        def load(ti):
            s0, n, var = tiles[ti]
            h = ht[ti % 2]
            lo, hi = (0, S) if var == 0 else (S, S + CT)
            a, b_ = max(s0 - 1, lo), min(s0 + n + 1, hi)
            c0 = a - (s0 - 1)
            if c0 > 0:
                k.memset("pool", h[:, :, 0:c0], 0.0, w=[h.b])
            if b_ < s0 + n + 1:
                k.memset("pool", h[:, :, n + 1:n + 2], 0.0, w=[h.b])
            k.dma(h[:, :, c0:c0 + (b_ - a)], g.HT2[:, :, a:b_].rearrange("c p t -> p c t"), w=[h.b])

        def load_x(ti):
            s0, n, var = tiles[ti]
            nsub = (n + 127) // 128
            bufs = []
            for j in range(nsub):
                m = min(128, n - j * 128)
                r0 = s0 + j * 128
                x_ = xt[cnt["x"] % 8]
                cnt["x"] += 1
                k.dma(x_[0:m, :], xsrc[r0:r0 + m, :], w=[x_.b])
                bufs.append(x_)
            return bufs

        def up_chunk(ti, c):
            s0, n, var = tiles[ti]
            h = ht[ti % 2]
            aT = actT[ti % 2]
            cols = n + 2
            i = cnt["c"] % 2
            cnt["c"] += 1
            for (pp, wc0) in ((pa[i], c * 128), (pg[i], HC * 128 + c * 128)):
                wb = WUS.bufs(wc0, wc0 + 128)
                for kc in range(8):
                    k.mm(pp[:, 0:cols], WU[:, kc, wc0:wc0 + 128], h[:, kc, 0:cols], start=(kc == 0),
                         r=wb + [h.b], w=[pp.b])
            for (pp, dst, ci) in ((pa[i], ca[i], hf * HC + c), (pg[i], cg[i], 22 + hf * HC + c)):
                w3 = g.convf[:, l, ci, :]
                k.act(t1[i][:, 0:n], pp[:, 1:n + 1], AF.Copy, r=[g.convf.b], w=[pp.b, t1[i].b], scale=w3[:, 1:2])
                k.stt("dve", t2[i][:, 0:n], pp[:, 0:n], w3[:, 0:1], t1[i][:, 0:n], ALU.mult, ALU.add,
                      r=[g.convf.b, t1[i].b], w=[pp.b, t2[i].b])
                k.stt("dve", dst[:, 0:n], pp[:, 2:n + 2], w3[:, 2:3], t2[i][:, 0:n], ALU.mult, ALU.add,
                      r=[g.convf.b, t2[i].b], w=[pp.b, dst.b])
            k.act(sa[i][:, 0:n], ca[i][:, 0:n], AF.Silu, r=[ca[i].b], w=[sa[i].b])
            k.tt("pool", aT[:, c, 0:n], sa[i][:, 0:n], cg[i][:, 0:n], ALU.mult, r=[sa[i].b, cg[i].b], w=[aT.b])

        def down(ti, xbufs):
            s0, n, var = tiles[ti]
            aT = actT[ti % 2]
            nsub = (n + 127) // 128
            for j in range(nsub):
                m = min(128, n - j * 128)
                r0 = s0 + j * 128
                x_ = xbufs[j]
                o_ = xo[cnt["o"] % 2]
                cnt["o"] += 1
                for hh in range(2):
                    p_ = po[cnt["po"] % 3]
                    cnt["po"] += 1
                    wb = WDS.bufs(hh * 512, (hh + 1) * 512)
                    for hc in range(HC):
                        k.mm(p_[0:m, :], aT[:, hc, j * 128:j * 128 + m], WD[:, hc, hh * 512:(hh + 1) * 512],
                             start=(hc == 0), r=[aT.b] + wb, w=[p_.b])
                    t_ = tmp[cnt["tmp"] % 2]
                    cnt["tmp"] += 1
                    k.tt("dve", t_[0:m, :], p_[0:m, :], gtb[var][0:m, hh * 512:(hh + 1) * 512], ALU.mult,
                         r=[gtb[var].b], w=[p_.b, t_.b])
                    k.tt("pool", o_[0:m, hh * 512:(hh + 1) * 512], x_[0:m, hh * 512:(hh + 1) * 512], t_[0:m, :], ALU.add,
                         r=[x_.b, t_.b], w=[o_.b])
                if hf == 0:
                    dst = g.XP[r0:r0 + m, :]
                elif l == L - 1:
                    dst = g.out[r0:r0 + m, :]
                else:
                    dst = g.XM[r0:r0 + m, :]
                k.dma(dst, o_[0:m, :], r=[o_.b])

        NPRE = 2
        load(0)
        for c in range(HC):
            up_chunk(0, c)
        for ti in range(len(tiles)):
            if ti + 1 < len(tiles):
                load(ti + 1)
            xbufs = load_x(ti)
            if ti + 1 < len(tiles):
                for c in range(NPRE):
                    up_chunk(ti + 1, c)
            down(ti, xbufs)
            if ti + 1 < len(tiles):
                for c in range(NPRE, HC):
                    up_chunk(ti + 1, c)

WARM_N = 0
WARM_EVERY = 1
KC0 = (0, 8, 24, 32)


def na_rowcfgs(a):
    if a == 0:
        return [(0, 3), (1, 4)]
    if a == 15:
        return [(14, 5), (15, 6)]
    return [(a - 1, 0), (a, 1), (a + 1, 2)]


def p2_mixers(k, g, l):
    din = g.din
    N = 256
    with phase(k):
        WO = k.sb("wo", [128, 8, D], BF16)
        NAB = k.sb("nab", [128, 8, NTILE * 8], BF16)
        stg = [k.sb("stg%d" % i, [128, 8, 128], F32) for i in range(2)]
        def load_p2_weights():
            wload(k, stg, din["nab"][:, l, :].rearrange("p (a c) -> p a c", a=8), 8, NTILE * 8, [(0, NTILE * 8, NAB, 0)],
                  blk=128, func=AF.Exp)
            wload_cast(k, din["w_o"][l].rearrange("(kc p) n -> p kc n", p=128), 8, [(0, D, WO, 0)])
        nabf = NAB[:, :, :].rearrange("p a c -> p (a c)")
        nvar = 2 if l == 0 else 1
        gtb = [k.sb("gtb%d" % v, [128, D], F32) for v in range(nvar)]
        for v in range(nvar):
            k.dma(gtb[v][:], bcast_row(g.modrow, v, l * 6144 + 2 * D, D), w=[gtb[v].b])
        es = k.sb("es", [128, 6], F32)
        k.act(es[:], g.sink[:, l, :], AF.Exp, r=[g.sink.b], w=[es.b])
        KAc = k.sb("kac", [128, 3, CT], BF16)
        KBc = k.sb("kbc", [128, 2, CT], BF16)
        k.dma(KAc[:], g.QK[3:6, :, S:S + CT].rearrange("c p t -> p c t"), w=[KAc.b])
        k.dma(KBc[:], g.QK[9:11, :, S:S + CT].rearrange("c p t -> p c t"), w=[KBc.b])
        VAc = [k.sb("vac%d" % i, [128, 6, 128], BF16) for i in range(2)]
        VBc = [k.sb("vbc%d" % i, [128, 2, 128], BF16) for i in range(2)]
        for ct in range(2):
            k.dma(VAc[ct][:], g.VA[S + ct * 128:S + (ct + 1) * 128, :, :], w=[VAc[ct].b])
            k.dma(VBc[ct][:], g.VB[S + ct * 128:S + (ct + 1) * 128, :, :], w=[VBc[ct].b])
        KAn = [k.sb("kan%d" % i, [128, 3, 256], BF16) for i in range(2)]
        KAg = [[k.sb("kag%d_%d" % (i, j), [128, 3, 128], BF16) for j in range(4)] for i in range(4)]
        VAr = [[k.sb("var%d_%d" % (i, j), [128, 6, 128], BF16) for j in range(4)] for i in range(4)]
        KBr = [k.sb("kbr%d" % i, [128, 2, 128], BF16) for i in range(6)]
        VBr = [k.sb("vbr%d" % i, [128, 2, 128], BF16) for i in range(6)]

        def load_group(b):
            if b < 0 or b > 15:
                return
            sl = b % 4
            t0 = b * 256
            kn = KAn[b % 2]
            k.dma(kn[:], g.QK[3:6, :, t0:t0 + 256].rearrange("c p t -> p c t"), w=[kn.b])
            for j in range(4):
                for c in range(3):
                    k.cp("pool", KAg[sl][j][:, c, :].rearrange("p (r x) -> p r x", x=32),
                         kn[:, c, :].rearrange("p (r x) -> p r x", x=64)[:, :, KC0[j]:KC0[j] + 32],
                         r=[kn.b], w=[KAg[sl][j].b])
                for kr in range(4):
                    r0 = t0 + kr * 64 + KC0[j]
                    k.dma(VAr[sl][j][kr * 32:(kr + 1) * 32, :, :], g.VA[r0:r0 + 32, :, :], w=[VAr[sl][j].b])

        def load_kt(kt):
            if kt < 0 or kt > 31:
                return
            sl = kt % 6
            t0 = kt * 128
            k.dma(KBr[sl][:], g.QK[9:11, :, t0:t0 + 128].rearrange("c p t -> p c t"), w=[KBr[sl].b])
            k.dma(VBr[sl][:], g.VB[t0:t0 + 128, :, :], w=[VBr[sl].b])

        q = [k.sb("q%d" % i, [128, 6, N], BF16) for i in range(2)]
        cu = [k.sb("cu%d" % i, [128, 2, N + 2], F32) for i in range(2)]
        bg = [k.sb("bg%d" % i, [128, 2, N], F32) for i in range(2)]
        P = [k.sb("P%d" % i, [128, 512], BF16) for i in range(6)]
        YT = k.sb("YT", [128, 8, N], BF16)
        rc = [k.sb("rc%d" % i, [128, N], F32) for i in range(2)]
        c1 = [k.sb("c1%d" % i, [128, N], F32) for i in range(2)]
        c2 = [k.sb("c2%d" % i, [128, N], F32) for i in range(2)]
        xt = [k.sb("xt%d" % i, [128, D], F32) for i in range(4)]
        xo = [k.sb("xo%d" % i, [128, D], F32) for i in range(2)]
        tmp = [k.sb("tmp%d" % i, [128, 512], F32) for i in range(2)]
        xn = k.sb("xn", [128, 2, D], BF16)
        junk = k.sb("junk", [128, D], BF16)
        ss = k.sb("ss", [128, 2], F32)
        rs = k.sb("rs", [128, 2], F32)
        h2o = k.sb("h2o", [128, 8, N], BF16)
        pS = [k.ps("pS%d" % i) for i in range(4)]
        pO = [k.ps("pO%d" % i) for i in range(2)]
        po = [k.ps("po%d" % i) for i in range(1)]
        pT = k.ps("pT", BF16, 1024)
        cnt = {"pS": 0, "pO": 0, "P": 0, "rc": 0, "x": 0, "o": 0, "po": 0, "tmp": 0, "c": 0}

        tiles = [(i * N, N, 0) for i in range(S // N)] + ([(S, CT, 1)] if l == 0 else [])

        def load_tile(ti):
            tok0, n, var = tiles[ti]
            b_ = ti % 2
            k.dma(q[b_][:, 0:3, 0:n], g.QK[0:3, :, tok0:tok0 + n].rearrange("c p t -> p c t"), w=[q[b_].b])
            k.dma(q[b_][:, 3:6, 0:n], g.QK[6:9, :, tok0:tok0 + n].rearrange("c p t -> p c t"), w=[q[b_].b])
            lo, hi = (0, S) if var == 0 else (S, S + CT)
            a, e = max(tok0 - 1, lo), min(tok0 + n + 1, hi)
            c0 = a - (tok0 - 1)
            if c0 > 0:
                k.memset("pool", cu[b_][:, :, 0:1], 0.0, w=[cu[b_].b])
            if e < tok0 + n + 1:
                k.memset("pool", cu[b_][:, :, n + 1:n + 2], 0.0, w=[cu[b_].b])
            k.dma(cu[b_][:, :, c0:c0 + (e - a)], g.CU[:, :, a:e].rearrange("c p t -> p c t"), w=[cu[b_].b])
            k.dma(bg[b_][:, :, 0:n], g.BG[:, :, tok0:tok0 + n].rearrange("c p t -> p c t"), w=[bg[b_].b])

        def next_ps(name, arr):
            p = arr[cnt[name] % len(arr)]
            cnt[name] += 1
            return p

        def ctx_part(qh, qbuf, Kc, kch, pb, Vc, vh, n, pOut):
            for ct in range(2):
                ps_ = next_ps("pS", pS)
                k.mm(ps_[:, 0:n], Kc[pb:pb + 64, kch, ct * 128:(ct + 1) * 128], qh, start=True,
                     r=[Kc.b, qbuf], w=[ps_.b])
                pp = next_ps("P", P)
                k.act(pp[:, 0:n], ps_[:, 0:n], AF.Exp, w=[ps_.b, pp.b])
                k.mm(pOut[:, 0:n], Vc[ct][:, vh, :], pp[:, 0:n], start=(ct == 0), r=[Vc[ct].b, pp.b], w=[pOut.b])

        wmask = {}

        def get_mask(pattern):
            if pattern not in wmask:
                t = k.sb("wm%d" % len(wmask), [128, len(pattern) * 128], BF16)
                for i, mk in enumerate(pattern):
                    k.cp("pool", t[:, i * 128:(i + 1) * 128], g.wgm[:, mk, :], r=[g.wgm.b], w=[t.b])
                wmask[pattern] = t
            return wmask[pattern]

        def blk(ap):
            return ap.rearrange("p (j r c) -> p j r c", j=4, r=4, c=16)

        def finalize(pOut, n, h_extra, dst, blocked):
            r_ = rc[cnt["rc"] % 2]
            cnt["rc"] += 1
            if h_extra is None:
                k.act(r_[64:128, 0:n], pOut[64:128, 0:n], AF.Ln, w=[pOut.b, r_.b])
            else:
                k.act(r_[64:128, 0:n], pOut[64:128, 0:n], AF.Ln, r=[es.b], w=[pOut.b, r_.b],
                      bias=es[64:128, h_extra:h_extra + 1])
            k.act(r_[64:128, 0:n], r_[64:128, 0:n], AF.Exp, r=[r_.b], w=[r_.b], scale=-1.0)
            if blocked:
                k.tt("dve", dst.rearrange("p (r j c) -> p j r c", r=4, j=4, c=16), blk(pOut[0:64, 0:n]),
                     blk(r_[64:128, 0:n]), ALU.mult, r=[r_.b], w=[pOut.b, YT.b])
            else:
                k.tt("dve", dst, pOut[0:64, 0:n], r_[64:128, 0:n], ALU.mult, r=[r_.b], w=[pOut.b, YT.b])

        class U:
            __slots__ = ("pre", "qk", "ex", "pv", "fin", "post")

            def __init__(self):
                self.pre = []
                self.qk = self.ex = self.pv = self.fin = None
                self.post = []

        def make_tile_units(ti):
            tok0, n, var = tiles[ti]
            b_ = ti % 2
            a = ti
            qq = q[b_]
            units = []

            def ctx_units(qh, Kc, kch, pb, Vc, vh, pO_cell, blocked=False):
                u = U()
                cell = {}

                def qk(cell=cell):
                    ps_ = next_ps("pS", pS)
                    cell["ps"] = ps_
                    for ct in range(2):
                        k.mm(ps_[:, ct * n:(ct + 1) * n], Kc[pb:pb + 64, kch, ct * 128:(ct + 1) * 128], qh, start=(ct == 0),
                             r=[Kc.b, qq.b], w=[ps_.b])

                def ex(cell=cell):
                    pp = next_ps("P", P)
                    cell["pp"] = pp
                    if blocked:
                        for ct in range(2):
                            k.act(pp[:, ct * n:(ct + 1) * n].rearrange("p (j r c) -> p r j c", j=4, r=4, c=16),
                                  cell["ps"][:, ct * n:(ct + 1) * n].rearrange("p (r j c) -> p r j c", r=4, j=4, c=16),
                                  AF.Exp, w=[cell["ps"].b, pp.b])
                    else:
                        k.act(pp[:, 0:2 * n], cell["ps"][:, 0:2 * n], AF.Exp, w=[cell["ps"].b, pp.b])

                def pv(cell=cell):
                    pO_cell["p"] = next_ps("pO", pO)
                    pOut = pO_cell["p"]
                    pp = cell["pp"]
                    for ct in range(2):
                        k.mm(pOut[:, 0:n], Vc[ct][:, vh, :], pp[:, ct * n:(ct + 1) * n], start=(ct == 0),
                             r=[Vc[ct].b, pp.b], w=[pOut.b])

                u.qk, u.ex, u.pv = qk, ex, pv
                units.append(u)

            for h in range(6):
                ch, pb = h // 2, 64 * (h % 2)
                qh = qq[pb:pb + 64, ch, 0:n]
                pO_cell = {}
                ctx_units(qh, KAc, ch, pb, VAc, h, pO_cell, blocked=(var == 0))
                if var == 0:
                    q3 = qq[pb:pb + 64, ch, :].rearrange("p (r c) -> p r c", c=64)
                    slots = []
                    for (b, rcfg) in na_rowcfgs(a):
                        for j in range(4):
                            slots.append((b, j, (h * 7 + rcfg) * 4 + j))
                    for s0 in range(0, len(slots), 8):
                        grp = slots[s0:s0 + 8]
                        u = U()
                        cell = {}

                        def qk(grp=grp, cell=cell, q3=q3, pb=pb, ch=ch):
                            ps_ = next_ps("pS", pS)
                            cell["ps"] = ps_
                            for i, (b, j, tix) in enumerate(grp):
                                kt_ = KAg[b % 4][j]
                                k.mm(ps_[:, i * 64:(i + 1) * 64], kt_[pb:pb + 64, ch, :], q3[:, :, 16 * j:16 * j + 16],
                                     start=(i == 0), r=[kt_.b, qq.b], w=[ps_.b])

                        def ex(grp=grp, cell=cell):
                            pp = next_ps("P", P)
                            cell["pp"] = pp
                            w_ = len(grp) * 64
                            k.act(pp[:, 0:w_], cell["ps"][:, 0:w_], AF.Exp, w=[cell["ps"].b, pp.b])
                            t0_ = grp[0][2] * 64
                            k.tt("dve", pp[:, 0:w_], pp[:, 0:w_], nabf[:, t0_:t0_ + w_], ALU.mult, r=[pp.b, NAB.b], w=[pp.b])

                        def pv(grp=grp, cell=cell, pO_cell=pO_cell, h=h):
                            pOut = pO_cell["p"]
                            pp = cell["pp"]
                            for i, (b, j, tix) in enumerate(grp):
                                vt_ = VAr[b % 4][j]
                                k.mm(pOut[:, j * 64:(j + 1) * 64], vt_[:, h, :], pp[:, i * 64:(i + 1) * 64],
                                     start=False, r=[vt_.b, pp.b], w=[pOut.b])

                        u.qk, u.ex, u.pv = qk, ex, pv
                        units.append(u)
                units[-1].fin = (lambda pO_cell=pO_cell, pb=pb, ch=ch: finalize(pO_cell["p"], n, None, YT[pb:pb + 64, ch, 0:n],
                                                                                    var == 0))
            for h in range(6):
                ch, pb, kv = 3 + h // 2, 64 * (h % 2), h // 3
                qh = qq[pb:pb + 64, ch, 0:n]
                pO_cell = {}
                ctx_units(qh, KBc, kv, pb, VBc, kv, pO_cell)
                if var == 0:
                    slots = []
                    for t in range(2):
                        i_ = 2 * a + t
                        for kt in (i_ - 1, i_, i_ + 1):
                            if 0 <= kt <= 31:
                                slots.append((t, kt, 0 if kt == i_ else (1 if kt < i_ else 2)))
                    for s0 in range(0, len(slots), 4):
                        grp = slots[s0:s0 + 4]
                        u = U()
                        cell = {}

                        def qk(grp=grp, cell=cell, pb=pb, ch=ch, kv=kv):
                            ps_ = next_ps("pS", pS)
                            cell["ps"] = ps_
                            for i, (t, kt, mk) in enumerate(grp):
                                kt_ = KBr[kt % 6]
                                k.mm(ps_[:, i * 128:(i + 1) * 128], kt_[pb:pb + 64, kv, :],
                                     qq[pb:pb + 64, ch, t * 128:(t + 1) * 128],
                                     start=(i == 0), r=[kt_.b, qq.b], w=[ps_.b])

                        def ex(grp=grp, cell=cell):
                            pp = next_ps("P", P)
                            cell["pp"] = pp
                            w_ = len(grp) * 128
                            k.act(pp[:, 0:w_], cell["ps"][:, 0:w_], AF.Exp, w=[cell["ps"].b, pp.b])
                            pat = tuple(mk for (_, _, mk) in grp)
                            if any(pat):
                                mt = get_mask(pat)
                                k.tt("dve", pp[:, 0:w_], pp[:, 0:w_], mt[:, 0:w_], ALU.mult, r=[pp.b, mt.b], w=[pp.b])

                        def pv(grp=grp, cell=cell, pO_cell=pO_cell, kv=kv):
                            pOut = pO_cell["p"]
                            pp = cell["pp"]
                            for i, (t, kt, mk) in enumerate(grp):
                                vt_ = VBr[kt % 6]
                                k.mm(pOut[:, t * 128:(t + 1) * 128], vt_[:, kv, :], pp[:, i * 128:(i + 1) * 128],
                                     start=False, r=[vt_.b, pp.b], w=[pOut.b])

                        u.qk, u.ex, u.pv = qk, ex, pv
                        units.append(u)
                units[-1].fin = (lambda pO_cell=pO_cell, pb=pb, ch=ch, h=h: finalize(pO_cell["p"], n, h, YT[pb:pb + 64, ch, 0:n],
                                                                                         False))

            xcell = {}

            def prefetch():
                if var == 0:
                    xsrc = (din["x"] if l == 0 else g.XM[:])[tok0:tok0 + n, :]
                else:
                    xsrc = din["ctx"]
                xcell["b"] = []
                for s in range(n // 128):
                    x_ = xt[cnt["x"] % 4]
                    cnt["x"] += 1
                    k.dma(x_[:], xsrc[s * 128:(s + 1) * 128, :], w=[x_.b])
                    xcell["b"].append(x_)
                if ti + 1 < len(tiles):
                    load_tile(ti + 1)
                if var == 0:
                    load_group(a + 2)
                    load_kt(2 * a + 3); load_kt(2 * a + 4)

            def conv():
                for c in range(2):
                    i = cnt["c"] % 2
                    cnt["c"] += 1
                    w3 = g.convc[:, l, c, :]
                    cuc = cu[b_]
                    k.act(c1[i][:, 0:n], cuc[:, c, 1:n + 1], AF.Copy, r=[cuc.b, g.convc.b], w=[c1[i].b], scale=w3[:, 1:2])
                    k.stt("dve", c2[i][:, 0:n], cuc[:, c, 0:n], w3[:, 0:1], c1[i][:, 0:n], ALU.mult, ALU.add,
                          r=[cuc.b, c1[i].b, g.convc.b], w=[c2[i].b])
                    k.stt("dve", c1[i][:, 0:n], cuc[:, c, 2:n + 2], w3[:, 2:3], c2[i][:, 0:n], ALU.mult, ALU.add,
                          r=[cuc.b, c2[i].b, g.convc.b], w=[c1[i].b])
                    k.tt("pool", YT[:, 6 + c, 0:n], c1[i][:, 0:n], bg[b_][:, c, 0:n], ALU.mult,
                         r=[c1[i].b, bg[b_].b], w=[YT.b])

            def wo_epi():
                nsub = n // 128
                for s in range(nsub):
                    x_ = xcell["b"][s]
                    o_ = xo[cnt["o"] % 2]
                    cnt["o"] += 1
                    for hh in range(2):
                        p_ = next_ps("po", po)
                        for kc in range(8):
                            k.mm(p_[:, :], YT[:, kc, s * 128:(s + 1) * 128], WO[:, kc, hh * 512:(hh + 1) * 512],
                                 start=(kc == 0), r=[YT.b, WO.b], w=[p_.b])
                        t_ = tmp[cnt["tmp"] % 2]
                        cnt["tmp"] += 1
                        k.tt("dve", t_[:], p_[:, :], gtb[var][:, hh * 512:(hh + 1) * 512], ALU.mult,
                             r=[gtb[var].b], w=[p_.b, t_.b])
                        k.tt("pool", o_[:, hh * 512:(hh + 1) * 512], x_[:, hh * 512:(hh + 1) * 512], t_[:], ALU.add,
                             r=[x_.b, t_.b], w=[o_.b])
                    k.dma(g.XN[tok0 + s * 128:tok0 + (s + 1) * 128, :], o_[:], r=[o_.b])
                    norm_rows(k, o_[:], s, ss, junk, r=[o_.b])
                    k.rsqrt(rs[:, s:s + 1], ss[:, s:s + 1], g.epsb, r=[ss.b], w=[rs.b])
                    k.ts("dve", xn[:, s, :], o_[:], rs[:, s:s + 1], None, ALU.mult, r=[o_.b, rs.b], w=[xn.b])

            def transposes():
                nsub = n // 128
                for kc in range(8):
                    for s in range(nsub):
                        k.tr(pT[:, s * 128:(s + 1) * 128], xn[:, s, kc * 128:(kc + 1) * 128], g.identb[:],
                             r=[xn.b, g.identb.b], w=[pT.b])
                    k.act(h2o[:, kc, 0:n], pT[:, 0:n], AF.Identity, r=[g.AB.b], w=[pT.b, h2o.b],
                          scale=abv(g, l, var, 2)[:, kc:kc + 1], bias=abv(g, l, var, 3)[:, kc:kc + 1])
                k.dma(g.HT2[:, :, tok0:tok0 + n].rearrange("c p t -> p c t"), h2o[:, :, 0:n], r=[h2o.b])

            def warm():
                pw = pT.t.bitcast(F32)
                for i in range(WARM_N):
                    k.mm(pw[:, 0:512], WO[:, i % 8, 0:128], WO[:, (i + 1) % 8, 0:512], start=True, r=[WO.b], w=[pT.b])

            if WARM_N and (ti % WARM_EVERY == 0):
                units[0].pre.append(warm)
            units[min(7, len(units) - 1)].pre.append(prefetch)
            units[min(8, len(units) - 1)].pre.append(conv)
            units[-1].post.append(wo_epi)
            return units, transposes

        load_tile(0)
        load_group(0)
        for kt in range(0, 3):
            load_kt(kt)
        load_p2_weights()
        load_group(1)
        allu = []
        pending_tr = None
        for ti in range(len(tiles)):
            us, trf = make_tile_units(ti)
            if pending_tr is not None:
                us[min(12, len(us) - 1)].pre.append(pending_tr)
            pending_tr = trf
            allu.extend(us)
        SK = 4
        FD = 1
        due = {}
        for i in range(len(allu) + SK + FD + 1):
            if i < len(allu):
                u = allu[i]
                for f in u.pre:
                    f()
                u.qk()
                u.ex()
            j = i - SK
            if 0 <= j < len(allu):
                u = allu[j]
                u.pv()
                fl = []
                if u.fin is not None:
                    fl.append(u.fin)
                fl.extend(u.post)
                if fl:
                    due.setdefault(i + FD, []).extend(fl)
            for f in due.pop(i, []):
                f()
        assert not due
        pending_tr()

def _shared_inputs(inp, consts):
    m = _core_inputs(0, inp, consts)
    for kx in ("x", "ctx", "cvt"):
        m.pop(kx)
    return m


def kernel(**inputs):
    consts = _consts()
    shared = _shared_inputs(inputs, consts)
    in_maps = [_core_inputs(b, inputs, consts, shared) for b in range(NCORES)]
    nc = build()
    res = run_bass_kernel_spmd(nc, in_maps, core_ids=list(range(NCORES)))
    return np.stack([np.asarray(r["out"], dtype=np.float32) for r in res.results], axis=0)
```

```python
import numpy as np
import ml_dtypes
import concourse.bass as bass
import concourse.mybir as mybir
from concourse.bass_utils import run_bass_kernel_spmd

F32 = mybir.dt.float32
BF16 = mybir.dt.bfloat16
AF = mybir.ActivationFunctionType
ALU = mybir.AluOpType

D = 1024
S = 4096
CT = 256
TT = S + CT
L = 2
DFF = 2816
INW = 2560
EPS = 1e-6
NEGM = -30000.0
NCORES = 8

ROWCFG = [(5, 4), (5, 5), (5, 6), (0, 0), (0, 1), (15, 14), (15, 15)]
NTILE = 6 * 7 * 4


class Buf:
    __slots__ = ("name",)

    def __init__(self, name):
        self.name = name


class Op:
    __slots__ = ("eng", "fn", "deps", "dma", "sig", "sem", "val", "inc")

    def __init__(self, eng, fn, dma):
        self.eng = eng
        self.fn = fn
        self.deps = []
        self.dma = dma
        self.sig = dma
        self.sem = None
        self.val = 0
        self.inc = 1


class Sched:
    NDMA = 12
    ENGS = ["pe", "act", "dve", "pool", "sp"]

    def __init__(self, nc, stack):
        self.nc = nc
        self.csem = {e: stack.enter_context(nc.semaphore("c_" + e)) for e in self.ENGS}
        self.dsem = {e: [stack.enter_context(nc.semaphore("d_%s_%d" % (e, i))) for i in range(self.NDMA)]
                     for e in ("sp", "pool")}
        self.cnt = {e: 0 for e in self.ENGS}
        self.dcnt = {e: 0 for e in self.dsem}
        self.dtot = {}
        self.seen = {e: {} for e in self.ENGS}
        self.nphase = 0
        self.reset()

    def reset(self):
        self.ops = []
        self.last_w = {}
        self.readers = {}
        self.dma_hist = {}

    def add(self, eng, fn, r=(), w=(), dma=False):
        op = Op(eng, fn, dma)
        deps = {}
        for b in r:
            lw = self.last_w.get(b)
            if lw is not None:
                deps[id(lw)] = (lw, 0)
        for b in w:
            lw = self.last_w.get(b)
            if lw is not None and id(lw) not in deps:
                deps[id(lw)] = (lw, 1)
            for rd in self.readers.get(b, ()):
                if id(rd) not in deps:
                    deps[id(rd)] = (rd, 1)
        for p, kind in deps.values():
            if (not p.dma) and (not dma) and p.eng == eng and kind == 1 and eng == "pe":
                continue
            op.deps.append(p)
            p.sig = True
        if dma:
            h = self.dma_hist.setdefault(eng, [])
            if len(h) >= self.NDMA:
                op.deps.append(h[len(h) - self.NDMA])
            h.append(op)
        for b in r:
            self.readers.setdefault(b, []).append(op)
        for b in w:
            self.last_w[b] = op
            self.readers[b] = []
        self.ops.append(op)
        return op

    def emit_phase(self):
        nc = self.nc
        per = {e: [o for o in self.ops if o.eng == e] for e in self.ENGS}
        bar = [(self.csem[e], self.cnt[e]) for e in self.ENGS if self.cnt[e] > 0]
        for e in self.dsem:
            for s in self.dsem[e]:
                if self.dtot.get(id(s), 0) > 0:
                    bar.append((s, self.dtot[id(s)]))
        for e in self.ENGS:
            comp = [o for o in per[e] if not o.dma]
            if comp:
                comp[-1].sig = True
        for op in self.ops:
            if op.dma:
                i = self.dcnt[op.eng]
                self.dcnt[op.eng] = i + 1
                sm = self.dsem[op.eng][i % self.NDMA]
                t = self.dtot.get(id(sm), 0) + 16
                self.dtot[id(sm)] = t
                op.sem, op.val, op.inc = sm, t, 16
            elif op.sig:
                self.cnt[op.eng] += 1
                op.sem, op.val, op.inc = self.csem[op.eng], self.cnt[op.eng], 1
        first = self.nphase == 0
        self.nphase += 1

        def run(e, eng):
            seen = self.seen[e]
            if not first:
                for sm, v in bar:
                    if seen.get(id(sm), 0) < v:
                        eng.wait_ge(sm, v)
                        seen[id(sm)] = v
            for op in per[e]:
                for p in op.deps:
                    k = id(p.sem)
                    if seen.get(k, 0) < p.val:
                        eng.wait_ge(p.sem, p.val)
                        seen[k] = p.val
                ins = op.fn(eng)
                if op.sig:
                    ins.then_inc(op.sem, op.inc)

        with nc.Block() as block:
            @block.tensor
            def _(eng):
                run("pe", eng)

            @block.scalar
            def _(eng):
                run("act", eng)

            @block.vector
            def _(eng):
                run("dve", eng)

            @block.gpsimd
            def _(eng):
                run("pool", eng)

            @block.sync
            def _(eng):
                run("sp", eng)
        self.reset()

    def emit_final(self):
        nc = self.nc
        bar = [(self.csem[e], self.cnt[e]) for e in self.ENGS if self.cnt[e] > 0]
        for e in self.dsem:
            for s in self.dsem[e]:
                if self.dtot.get(id(s), 0) > 0:
                    bar.append((s, self.dtot[id(s)]))
        with nc.Block() as block:
            @block.sync
            def _(eng):
                for sm, v in bar:
                    eng.wait_ge(sm, v)


class Tn:
    def __init__(self, t, name):
        self.t = t
        self.b = Buf(name)

    def __getitem__(self, k):
        return self.t[k]


class K:
    def __init__(self, nc, stack, dbg=None):
        self.nc = nc
        self.st = stack
        self.s = Sched(nc, stack)
        self.dbg = dbg or set()
        self.gst = stack

    def sb(self, name, shape, dt):
        self.nn = getattr(self, "nn", 0) + 1
        t = self.st.enter_context(self.nc.sbuf_tensor("s%d_%s" % (self.nn, name), list(shape), dt))
        return Tn(t, name)

    def ps(self, name, dt=F32, cols=512):
        self.nn = getattr(self, "nn", 0) + 1
        t = self.st.enter_context(self.nc.psum_tensor("p%d_%s" % (self.nn, name), [128, cols], dt))
        return Tn(t, name)

    def dram(self, name, shape, dt, kind="Internal"):
        if name in self.dbg:
            kind = "ExternalOutput"
        if name in getattr(self, "dbg_in", ()):
            kind = "ExternalInput"
        t = self.nc.dram_tensor(name, list(shape), dt, kind=kind)
        d = Tn(t.ap(), name)
        d.h = t
        return d

    def dma(self, out, in_, r=(), w=(), q="sp", **kw):
        return self.s.add(q, lambda e: e.dma_start(out=out, in_=in_, **kw), r=r, w=w, dma=True)

    def mm(self, out, lhsT, rhs, start, stop=True, r=(), w=()):
        return self.s.add(
            "pe",
            lambda e: e.matmul(out, lhsT, rhs, start=start, stop=stop, skip_group_check=True),
            r=r, w=w)

    def tr(self, out, in_, ident, r=(), w=()):
        return self.s.add("pe", lambda e: e.transpose(out, in_, ident), r=r, w=w)

    def act(self, out, in_, func, r=(), w=(), eng="act", **kw):
        return self.s.add(eng, lambda e: e.activation(out=out, in_=in_, func=func, **kw), r=r, w=w)

    def tt(self, eng, out, in0, in1, op, r=(), w=()):
        return self.s.add(eng, lambda e: e.tensor_tensor(out=out, in0=in0, in1=in1, op=op), r=r, w=w)

    def ts(self, eng, out, in0, s1, s2, op0, op1=None, r=(), w=()):
        if op1 is None:
            return self.s.add(eng, lambda e: e.tensor_scalar(out=out, in0=in0, scalar1=s1, scalar2=None, op0=op0), r=r, w=w)
        return self.s.add(eng, lambda e: e.tensor_scalar(out=out, in0=in0, scalar1=s1, scalar2=s2, op0=op0, op1=op1), r=r, w=w)

    def stt(self, eng, out, in0, scalar, in1, op0, op1, r=(), w=()):
        return self.s.add(eng, lambda e: e.scalar_tensor_tensor(out=out, in0=in0, scalar=scalar, in1=in1, op0=op0, op1=op1), r=r, w=w)

    def cp(self, eng, out, in_, r=(), w=()):
        if eng == "act":
            return self.s.add(eng, lambda e: e.copy(out=out, in_=in_), r=r, w=w)
        return self.s.add(eng, lambda e: e.tensor_copy(out=out, in_=in_), r=r, w=w)

    def recip(self, out, in_, r=(), w=()):
        return self.s.add("dve", lambda e: e.reciprocal(out=out, in_=in_), r=r, w=w)

    def rsqrt(self, out, in_, epsb, r=(), w=(), inw=()):
        self.act(out, in_, AF.Ln, r=list(r) + [epsb.b], w=list(inw) + list(w), bias=epsb[:, 0:1])
        return self.act(out, out, AF.Exp, r=list(w), w=list(w), scale=-0.5)

    def memset(self, eng, ap, val, w=()):
        return self.s.add(eng, lambda e: e.memset(ap, val), w=w)


def _na_index():
    kr_in = np.arange(128) // 32
    kc_in = np.arange(128) % 32
    r_in = np.arange(64) // 16
    c_in = np.arange(64) % 16
    drow = np.zeros((7, 128, 64), np.int64)
    rok = np.zeros((7, 128, 64), bool)
    for i, (a, b) in enumerate(ROWCFG):
        r = 4 * a + r_in[None, :]
        kr = 4 * b + kr_in[:, None]
        r0 = np.clip(r - 4, 0, 56)
        rok[i] = (kr >= r0) & (kr < r0 + 8)
        drow[i] = np.clip(kr - r + 7, 0, 14)
    dcol = np.zeros((4, 128, 64), np.int64)
    cok = np.zeros((4, 128, 64), bool)
    for i, j in enumerate((0, 1, 2, 3)):
        kc0 = int(np.clip(16 * j - 8, 0, 32))
        c = 16 * j + c_in[None, :]
        kc = kc0 + kc_in[:, None]
        c0 = np.clip(c - 8, 0, 48)
        cok[i] = (kc >= c0) & (kc < c0 + 16)
        dcol[i] = np.clip(kc - c + 15, 0, 30)
    return drow, rok, dcol, cok


def _consts():
    c = {}
    c["identb"] = np.eye(128, dtype=np.float32).astype(ml_dtypes.bfloat16)
    c["identf"] = np.eye(128, dtype=np.float32)
    bm = np.zeros((128, 128), np.float32)
    bm[:64, :64] = 1.0 / 64
    bm[64:, 64:] = 1.0 / 64
    c["bm"] = bm.astype(ml_dtypes.bfloat16)
    pm = np.zeros((128, 128), np.float32)
    for m in range(128):
        k = m + 32 if (m % 64) < 32 else m - 32
        pm[k, m] = 1.0
    c["pm"] = pm
    t = np.arange(S)
    row = (t // 64).astype(np.float32)
    col = (t % 64).astype(np.float32)
    inv = (np.float32(10000.0) ** (-np.arange(16, dtype=np.float32) / np.float32(16))).astype(np.float32)
    ang = np.concatenate([row[:, None] * inv, col[:, None] * inv], axis=-1).astype(np.float32)
    cos = np.cos(ang).astype(np.float32)
    sin = np.sin(ang).astype(np.float32)
    d = np.arange(128) % 64
    cosT = cos[:, d % 32].T
    sgn = np.where(d < 32, -1.0, 1.0).astype(np.float32)
    sinT = sin[:, d % 32].T * sgn[:, None]
    c["rope"] = np.ascontiguousarray(np.stack([cosT, sinT], axis=1)).astype(np.float32)
    ki = np.arange(128)[:, None]
    qi = np.arange(128)[None, :]
    mprev = np.where(qi <= ki, 1.0, 0.0)
    mnext = np.where(ki <= qi, 1.0, 0.0)
    c["wgm"] = np.stack([np.ones_like(mprev), mprev, mnext], axis=1).astype(np.float32).astype(ml_dtypes.bfloat16)
    return c


def _core_inputs(b, inp, consts, shared=None):
    f = lambda a: np.ascontiguousarray(np.asarray(a, dtype=np.float32))
    m = {}
    m["x"] = f(inp["x"][b])
    m["ctx"] = f(inp["ctx"][b])
    cvec = np.stack([np.asarray(inp["c"][b]), np.asarray(inp["c_ctx"])], 0)
    m["cvt"] = f(cvec.reshape(2, 8, 128).transpose(2, 1, 0))
    if shared is not None:
        m.update(shared)
        return m
    m["w_ada"] = f(inp["w_ada"])
    m["b_ada"] = f(inp["b_ada"])
    gt = lambda g: np.asarray(g).reshape(L, 8, 128).transpose(2, 0, 1)
    m["gT"] = f(np.stack([gt(inp["g_attn"]), gt(inp["g_ffn"])], axis=2))
    m["w_in"] = f(inp["w_in"])
    qkg = np.stack([np.asarray(inp[k]) for k in ("qn_a", "kn_a", "qn_b", "kn_b")], axis=-1)
    m["qkg"] = f(np.concatenate([qkg, qkg], axis=1).transpose(1, 0, 2))
    drow, rok, dcol, cok = _na_index()
    rpb = np.asarray(inp["rpb_a"], dtype=np.float32)
    g = rpb[:, :, drow[:, None], dcol[None, :]]
    ok = (rok[:, None] & cok[None, :])[None, None]
    nab = np.where(ok, g, np.float32(NEGM)).astype(np.float32)
    m["nab"] = f(nab.transpose(4, 0, 1, 2, 3, 5).reshape(128, L, NTILE * 64))
    m["sink"] = f(np.broadcast_to(np.asarray(inp["sink_b"])[None], (128, L, 6)))
    m["convc"] = f(np.asarray(inp["conv_c"]).reshape(L, 3, 2, 128).transpose(3, 0, 2, 1))
    m["w_o"] = f(inp["w_o"])
    m["w_up"] = f(inp["w_up"])
    m["convf"] = f(np.asarray(inp["conv_ffn"]).reshape(L, 3, 44, 128).transpose(3, 0, 2, 1))
    m["w_down"] = f(inp["w_down"])
    m.update(consts)
    return m


from contextlib import ExitStack, contextmanager


@contextmanager
def phase(k):
    old = k.st
    with ExitStack() as st:
        k.st = st
        yield
        k.s.emit_phase()
    k.st = old


def fence(k, eng, r, w):
    d = k.dummy
    return k.s.add(eng, lambda e: e.memset(d[0:1, 0:1], 0.0), r=r, w=list(w) + [d.b])


def wload_cast(k, src, nkc, segs, nsplit=4):
    subs = {}
    step = (nkc + nsplit - 1) // nsplit
    for (s0, s1, dst, d0) in segs:
        for k0 in range(0, nkc, step):
            k1 = min(nkc, k0 + step)
            sb_ = Buf("sub")
            subs.setdefault(id(dst), (dst, []))[1].append(sb_)
            k.dma(dst[:, k0:k1, d0:d0 + (s1 - s0)], src[:, k0:k1, s0:s1], w=[sb_], q="pool")
    for dst, bl in subs.values():
        fence(k, "pool", r=bl, w=[dst.b])


class WSegs:
    def __init__(self):
        self.rng = []

    def bufs(self, c0, c1):
        return [b for (d0, d1, b) in self.rng if d0 < c1 and c0 < d1]


def wload_segs(k, src, nkc, dst, segs):
    ws = WSegs()
    for (s0, s1, d0) in segs:
        b = Buf("wseg")
        k.dma(dst[:, 0:nkc, d0:d0 + (s1 - s0)], src[:, :, s0:s1], w=[b], q="pool")
        ws.rng.append((d0, d0 + (s1 - s0), b))
    return ws


def wload(k, stg, src, nkc, ncols, segs, engs=("dve", "pool", "act"), blk=256, func=None):
    subs = {}
    ci = 0
    for bi, c0 in enumerate(range(0, ncols, blk)):
        c1 = min(ncols, c0 + blk)
        sg = stg[bi % len(stg)]
        k.dma(sg[:, 0:nkc, 0:c1 - c0], src[:, :, c0:c1], w=[sg.b])
        for (s0, s1, dst, d0) in segs:
            lo, hi = max(s0, c0), min(s1, c1)
            if lo >= hi:
                continue
            sb_ = Buf("sub")
            subs.setdefault(id(dst), (dst, []))[1].append(sb_)
            if func is None:
                k.cp(engs[ci % len(engs)], dst[:, 0:nkc, d0 + lo - s0:d0 + hi - s0], sg[:, 0:nkc, lo - c0:hi - c0],
                     r=[sg.b], w=[sb_])
            else:
                k.act(dst[:, 0:nkc, d0 + lo - s0:d0 + hi - s0], sg[:, 0:nkc, lo - c0:hi - c0], func, r=[sg.b], w=[sb_])
            ci += 1
    for dst, bl in subs.values():
        fence(k, "pool", r=bl, w=[dst.b])


class G:
    pass


def build(nlayers=L, dbg=(), stop_after=None, dbg_in=(), only=None):
    nc = bass.Bass("TRN2", target_bir_lowering=False)
    gst = ExitStack()
    with gst:
        k = K(nc, gst, set(dbg))
        k.dbg_in = set(dbg_in)
        g = G()
        din = {}

        def inp(name, shape, dt=F32):
            din[name] = nc.dram_tensor(name, list(shape), dt, kind="ExternalInput").ap()

        inp("x", [S, D]); inp("ctx", [CT, D]); inp("cvt", [128, 8, 2])
        inp("w_ada", [L, D, 6 * D]); inp("b_ada", [L, 6 * D]); inp("gT", [128, L, 2, 8])
        inp("w_in", [L, D, INW]); inp("qkg", [128, L, 4]); inp("nab", [128, L, NTILE * 64])
        inp("sink", [128, L, 6]); inp("convc", [128, L, 2, 3]); inp("w_o", [L, D, D])
        inp("w_up", [L, D, 2 * DFF]); inp("convf", [128, L, 44, 3]); inp("w_down", [L, DFF, D])
        inp("identb", [128, 128], BF16); inp("identf", [128, 128]); inp("bm", [128, 128], BF16)
        inp("pm", [128, 128]); inp("rope", [128, 2, S]); inp("wgm", [128, 3, 128], BF16)
        g.din = din
        g.out = nc.dram_tensor("out", [S, D], F32, kind="ExternalOutput").ap()
        g.modrow = k.dram("modrow", [2, L * 6 * D], F32)
        g.QK = k.dram("QK", [11, 128, TT], BF16)
        g.VA = k.dram("VA", [TT, 6, 128], BF16)
        g.VB = k.dram("VB", [TT, 2, 128], BF16)
        g.CU = k.dram("CU", [2, 128, TT], F32)
        g.BG = k.dram("BG", [2, 128, TT], F32)
        g.XN = k.dram("XN", [TT, D], F32)
        g.XP = k.dram("XP", [TT, D], F32)
        g.XM = k.dram("XM", [TT, D], F32)
        g.HT2 = k.dram("HT2", [8, 128, TT], BF16)
        g.identb = k.sb("identb", [128, 128], BF16)
        g.identf = k.sb("identf", [128, 128], F32)
        g.bm = k.sb("bm", [128, 128], BF16)
        g.pm = k.sb("pm", [128, 128], F32)
        g.wgm = k.sb("wgm", [128, 3, 128], BF16)
        g.modT = k.sb("modT", [128, L, 96], F32)
        g.AB = k.sb("AB", [128, L * 2 * 4, 8], F32)
        g.gT = k.sb("gT", [128, L, 2, 8], F32)
        g.qkg = k.sb("qkg", [128, L, 4], F32)
        g.sink = k.sb("sink", [128, L, 6], F32)
        g.convc = k.sb("convc", [128, L, 2, 3], F32)
        g.convf = k.sb("convf", [128, L, 44, 3], F32)
        k.dummy = k.sb("dummy", [128, 4], F32)
        g.epsb = k.sb("epsb", [128, 1], F32)

        p0_mods(k, g)
        if stop_after == "p0":
            k.s.emit_final()
            return nc
        for l in range(nlayers):
            if only is None or "p1" in only:
                p1_inproj(k, g, l)
            if stop_after == "p1":
                break
            if only is None or "p2" in only:
                p2_mixers(k, g, l)
            if stop_after == "p2":
                break
            if only is None or "p3" in only:
                p3_ffn(k, g, l, 0)
                p3_ffn(k, g, l, 1)
        k.s.emit_final()
    return nc


def abv(g, l, var, which):
    return g.AB[:, (l * 2 + var) * 4 + which, :]


def p0_mods(k, g):
    din = g.din
    with phase(k):
        for nm in ("identb", "identf", "bm", "pm", "wgm", "gT", "qkg", "sink", "convc", "convf"):
            t = getattr(g, nm)
            k.dma(t[:], din[nm], w=[t.b])
        k.memset("dve", g.epsb[:], EPS, w=[g.epsb.b])
        for gi in (0, 2):
            k.ts("dve", g.qkg[:, :, gi:gi + 1], g.qkg[:, :, gi:gi + 1], 0.125, None, ALU.mult, r=[g.qkg.b], w=[g.qkg.b])
        cvt = k.sb("cvt", [128, 8, 2], F32)
        sct = k.sb("sct", [128, 8, 2], F32)
        k.dma(cvt[:], din["cvt"], w=[cvt.b])
        k.act(sct[:], cvt[:], AF.Silu, r=[cvt.b], w=[sct.b])
        NB0 = 6
        wst = [k.sb("wst%d" % i, [128, 8, 512], F32) for i in range(NB0)]
        bad = [k.sb("bad%d" % i, [2, 512], F32) for i in range(NB0)]
        mrow = [k.sb("mrow%d" % i, [2, 512], F32) for i in range(NB0)]
        pmm = [k.ps("p0m%d" % i) for i in range(NB0)]
        pT = k.ps("p0T")
        chunks = [(l_, n_) for l_ in range(L) for n_ in range(12)]

        def p0_load(ci):
            l_, n_ = chunks[ci]
            i = ci % NB0
            wv = din["w_ada"][l_].rearrange("(kc p) n -> p kc n", p=128)
            k.dma(wst[i][:], wv[:, :, n_ * 512:(n_ + 1) * 512], w=[wst[i].b])
            for r_ in range(2):
                k.dma(bad[i][r_:r_ + 1, :], din["b_ada"][l_:l_ + 1, n_ * 512:(n_ + 1) * 512], w=[bad[i].b])

        PF = NB0 - 2
        for ci in range(PF):
            p0_load(ci)
        for l in range(L):
            for n in range(12):
                ci = l * 12 + n
                i = ci % NB0
                if ci + PF < len(chunks):
                    p0_load(ci + PF)
                for kc in range(8):
                    k.mm(pmm[i][0:2, :], sct[:, kc, :], wst[i][:, kc, :], start=(kc == 0),
                         r=[sct.b, wst[i].b], w=[pmm[i].b])
                k.tt("dve", mrow[i][:], pmm[i][0:2, :], bad[i][:], ALU.add, r=[bad[i].b], w=[pmm[i].b, mrow[i].b])
                k.dma(g.modrow[:, l * 6144 + n * 512:l * 6144 + (n + 1) * 512], mrow[i][:], r=[mrow[i].b])
                for j in range(4):
                    idx = n * 4 + j
                    k.mm(pT[:, idx * 2:idx * 2 + 2], mrow[i][0:2, j * 128:(j + 1) * 128], g.identf[0:2, 0:2],
                         start=(idx == 0), r=[mrow[i].b, g.identf.b], w=[pT.b])
            k.cp("dve", g.modT[:, l, :], pT[:, 0:96], w=[pT.b, g.modT.b])
            mv = g.modT[:, l, :].rearrange("p (c v) -> p c v", v=2)
            for var in range(2):
                k.stt("dve", abv(g, l, var, 0), mv[:, 8:16, var], 1.0, g.gT[:, l, 0, :], ALU.add, ALU.mult,
                      r=[g.modT.b, g.gT.b], w=[g.AB.b])
                k.cp("dve", abv(g, l, var, 1), mv[:, 0:8, var], r=[g.modT.b], w=[g.AB.b])
                k.stt("dve", abv(g, l, var, 2), mv[:, 32:40, var], 1.0, g.gT[:, l, 1, :], ALU.add, ALU.mult,
                      r=[g.modT.b, g.gT.b], w=[g.AB.b])
                k.cp("dve", abv(g, l, var, 3), mv[:, 24:32, var], r=[g.modT.b], w=[g.AB.b])


def norm_rows(k, xt, s, ss, junk, r=()):
    k.act(junk[:], xt, AF.Square, r=list(r), w=[junk.b, ss.b], accum_out=ss[:, s:s + 1], scale=1.0 / 32.0)


W1_QA, W1_KA, W1_QB, W1_KBD, W1_U, W1_CG, W1_BG, W1_V = 0, 384, 768, 1152, 1408, 1664, 1920, 2176
W1_N = 2688


def p1_inproj(k, g, l):
    din = g.din
    with phase(k):
        W = k.sb("w1", [128, 8, W1_N], BF16)
        src = din["w_in"][l].rearrange("(kc p) n -> p kc n", p=128)
        segs = [(0, 128, W1_QA), (128, 384, W1_QA + 128), (384, 768, W1_KA), (1152, 1536, W1_QB),
                (1536, 1600, W1_KBD), (1536, 1600, W1_KBD + 64),
                (1600, 1664, W1_KBD + 128), (1600, 1664, W1_KBD + 192),
                (1792, 2048, W1_U), (2304, 2560, W1_CG), (2048, 2304, W1_BG),
                (768, 1152, W1_V), (1664, 1792, W1_V + 384)]
        WS = wload_segs(k, src, 8, W, segs)

        xt = [k.sb("xt%d" % i, [128, 4, D], F32) for i in range(2)]
        cs = [k.sb("cs%d" % i, [128, 2, 512], F32) for i in range(2)]
        xn = [k.sb("xn%d" % i, [128, 4, D], BF16) for i in range(2)]
        junk = k.sb("junk", [128, D], BF16)
        ss = [k.sb("ss%d" % i, [128, 4], F32) for i in range(2)]
        rs = [k.sb("rs%d" % i, [128, 4], F32) for i in range(2)]
        hTs = [k.sb("hT%d" % i, [128, 8, 512], BF16) for i in range(2)]
        sq = [k.sb("sq%d" % i, [128, 512], BF16) for i in range(3)]
        rstd = [k.sb("rstd%d" % i, [128, 512], F32) for i in range(3)]
        qn = [k.sb("qn%d" % i, [128, 512], F32) for i in range(3)]
        t1 = [k.sb("t1%d" % i, [128, 512], F32) for i in range(3)]
        t2 = [k.sb("t2%d" % i, [128, 512], F32) for i in range(3)]
        ob = [k.sb("ob%d" % i, [128, 512], BF16) for i in range(3)]
        usb = [k.sb("usb%d" % i, [128, 512], F32) for i in range(2)]
        of = [k.sb("of%d" % i, [128, 512], F32) for i in range(3)]
        vt = [k.sb("vt%d" % i, [128, 8, 128], BF16) for i in range(2)]
        for v_ in vt:
            k.memset("pool", v_[:, :, 64:128], 1.0, w=[v_.b])
        pT = [k.ps("pT%d" % i, BF16, 1024) for i in range(2)]
        pq = [k.ps("pq%d" % i) for i in range(4)]
        pmn = k.ps("pmn")
        pr = k.ps("pr")

        tiles = [(i * 512, 512, 0) for i in range(8)] + [(S, CT, 1)]
        cnt = {"ob": 0, "of": 0, "pq": 0, "a": 0, "vt": 0, "pT": 0}

        def load(ti):
            tok0, n, var = tiles[ti]
            nsub = n // 128
            b = ti % 2
            if var == 0:
                srcx = (din["x"] if l == 0 else g.XM[:])[tok0:tok0 + n, :]
                rd = [] if l == 0 else [g.XM.b]
            else:
                srcx = din["ctx"] if l == 0 else g.XM[S:S + CT, :]
                rd = [] if l == 0 else [g.XM.b]
            k.dma(xt[b][:, 0:nsub, :], srcx.rearrange("(s p) f -> p s f", p=128), w=[xt[b].b])

        def load_cs(ti):
            tok0, n, var = tiles[ti]
            b = ti % 2
            if var == 0:
                k.dma(cs[b][:, :, 0:n], din["rope"][:, :, tok0:tok0 + n], w=[cs[b].b])

        def norm(ti):
            tok0, n, var = tiles[ti]
            nsub = n // 128
            b = ti % 2
            xn_ = xn[b]
            for s in range(nsub):
                norm_rows(k, xt[b][:, s, :], s, ss[b], junk, r=[xt[b].b])
            k.rsqrt(rs[b][:, 0:nsub], ss[b][:, 0:nsub], g.epsb, r=[ss[b].b], w=[rs[b].b])
            for s in range(nsub):
                if s % 2 == 0:
                    k.ts("dve", xn_[:, s, :], xt[b][:, s, :], rs[b][:, s:s + 1], None, ALU.mult,
                         r=[xt[b].b, rs[b].b], w=[xn_.b])
                else:
                    k.act(xn_[:, s, :], xt[b][:, s, :], AF.Copy, r=[xt[b].b, rs[b].b], w=[xn_.b], scale=rs[b][:, s:s + 1])

        def trans(ti):
            tok0, n, var = tiles[ti]
            nsub = n // 128
            xn_ = xn[ti % 2]
            hT_ = hTs[ti % 2]
            for kc in range(8):
                p = pT[cnt["pT"] % 2]
                cnt["pT"] += 1
                for s in range(nsub):
                    k.tr(p[:, s * 128:(s + 1) * 128], xn_[:, s, kc * 128:(kc + 1) * 128], g.identb[:],
                         r=[xn_.b, g.identb.b], w=[p.b])
                k.act(hT_[:, kc, 0:n], p[:, 0:n], AF.Identity, r=[g.AB.b], w=[p.b, hT_.b],
                      scale=abv(g, l, var, 0)[:, kc:kc + 1], bias=abv(g, l, var, 1)[:, kc:kc + 1])

        load(0)
        load_cs(0)
        load(1)
        norm(0)
        trans(0)
        chunks = []

        def add_chunk(A, B=None, C=None, pre=None):
            chunks.append((A, B, C, pre))

        def make_tile(ti):
            tok0, n, var = tiles[ti]
            nsub = n // 128
            b = ti % 2
            hT = hTs[b]

            def proj(wc0):
                p = pq[cnt["pq"] % 4]
                cnt["pq"] += 1
                wb = WS.bufs(wc0, wc0 + 128)
                for kc in range(8):
                    k.mm(p[:, 0:n], W[:, kc, wc0:wc0 + 128], hT[:, kc, 0:n], start=(kc == 0), r=wb + [hT.b], w=[p.b])
                return p

            def qk_chunk(wc0, gi, rope, qkidx, pre=None):
                cell = {}

                def A():
                    p = proj(wc0)
                    a = cnt["a"] % 3
                    cnt["a"] += 1
                    cell["p"], cell["a"] = p, a
                    k.act(sq[a][:, 0:n], p[:, 0:n], AF.Square, w=[p.b, sq[a].b])

                def B():
                    p, a = cell["p"], cell["a"]
                    k.mm(pmn[:, 0:n], g.bm[:], sq[a][:, 0:n], start=True, r=[g.bm.b, sq[a].b], w=[pmn.b])
                    k.rsqrt(rstd[a][:, 0:n], pmn[:, 0:n], g.epsb, w=[rstd[a].b], inw=[pmn.b])
                    if not rope:
                        o = ob[cnt["ob"] % 3]
                        cnt["ob"] += 1
                        k.stt("dve", o[:, 0:n], p[:, 0:n], g.qkg[:, l, gi:gi + 1], rstd[a][:, 0:n], ALU.mult, ALU.mult,
                              r=[rstd[a].b, g.qkg.b], w=[p.b, o.b])
                        k.dma(g.QK[qkidx, :, tok0:tok0 + n], o[:, 0:n], r=[o.b])
                    else:
                        k.stt("dve", qn[a][:, 0:n], p[:, 0:n], g.qkg[:, l, gi:gi + 1], rstd[a][:, 0:n], ALU.mult, ALU.mult,
                              r=[rstd[a].b, g.qkg.b], w=[p.b, qn[a].b])

                def C():
                    a = cell["a"]
                    o = ob[cnt["ob"] % 3]
                    cnt["ob"] += 1
                    k.mm(pr[:, 0:n], g.pm[:], qn[a][:, 0:n], start=True, r=[g.pm.b, qn[a].b], w=[pr.b])
                    k.tt("pool", t1[a][:, 0:n], qn[a][:, 0:n], cs[b][:, 0, 0:n], ALU.mult, r=[qn[a].b, cs[b].b], w=[t1[a].b])
                    k.tt("dve", t2[a][:, 0:n], pr[:, 0:n], cs[b][:, 1, 0:n], ALU.mult, r=[cs[b].b], w=[pr.b, t2[a].b])
                    k.tt("pool", o[:, 0:n], t1[a][:, 0:n], t2[a][:, 0:n], ALU.add, r=[t1[a].b, t2[a].b], w=[o.b])
                    k.dma(g.QK[qkidx, :, tok0:tok0 + n], o[:, 0:n], r=[o.b])

                add_chunk(A, B, C if rope else None, pre)

            def tile_pre():
                if ti + 1 < len(tiles):
                    norm(ti + 1)
                if ti + 2 < len(tiles):
                    load(ti + 2)

            def mid_pre():
                if ti + 1 < len(tiles):
                    trans(ti + 1)
                    load_cs(ti + 1)

            for c in range(3):
                qk_chunk(W1_QA + c * 128, 0, False, c, pre=tile_pre if c == 0 else None)
            for c in range(3):
                qk_chunk(W1_KA + c * 128, 1, False, 3 + c)
            for c in range(3):
                qk_chunk(W1_QB + c * 128, 2, var == 0, 6 + c, pre=mid_pre if c == 0 else None)
            for c in range(2):
                qk_chunk(W1_KBD + c * 128, 3, var == 0, 9 + c)
            for c in range(2):
                ucell = {}

                def A_u(c=c, ucell=ucell):
                    pu = proj(W1_U + c * 128)
                    a = cnt["u"] % 2
                    cnt["u"] += 1
                    ucell["a"] = a
                    k.cp("act", usb[a][:, 0:n], pu[:, 0:n], w=[pu.b, usb[a].b])

                def A_cg(c=c, ucell=ucell):
                    a = ucell["a"]
                    pc = proj(W1_CG + c * 128)
                    o = of[cnt["of"] % 3]
                    cnt["of"] += 1
                    k.tt("dve", o[:, 0:n], pc[:, 0:n], usb[a][:, 0:n], ALU.mult, r=[usb[a].b], w=[pc.b, o.b])
                    k.dma(g.CU[c, :, tok0:tok0 + n], o[:, 0:n], r=[o.b])

                def A_bg(c=c):
                    pb = proj(W1_BG + c * 128)
                    o = of[cnt["of"] % 3]
                    cnt["of"] += 1
                    k.cp("act", o[:, 0:n], pb[:, 0:n], w=[pb.b, o.b])
                    k.dma(g.BG[c, :, tok0:tok0 + n], o[:, 0:n], r=[o.b])

                add_chunk(A_u)
                add_chunk(A_cg)
                add_chunk(A_bg)
            for s in range(nsub):
                def A_v(s=s):
                    pv_ = pq[cnt["pq"] % 4]
                    cnt["pq"] += 1
                    wb = WS.bufs(W1_V, W1_V + 512)
                    for kc in range(8):
                        k.mm(pv_[:, :], hT[:, kc, s * 128:(s + 1) * 128], W[:, kc, W1_V:W1_V + 512], start=(kc == 0),
                             r=wb + [hT.b], w=[pv_.b])
                    v = vt[cnt["vt"] % 2]
                    cnt["vt"] += 1
                    k.cp("act" if s % 2 == 0 else "dve", v[:, :, 0:64], pv_[:, :].rearrange("p (h d) -> p h d", d=64),
                         w=[pv_.b, v.b])
                    k.dma(g.VA[tok0 + s * 128:tok0 + (s + 1) * 128, :, :], v[:, 0:6, :], r=[v.b])
                    k.dma(g.VB[tok0 + s * 128:tok0 + (s + 1) * 128, :, :], v[:, 6:8, :], r=[v.b])

                add_chunk(A_v)

        cnt["u"] = 0
        for ti in range(len(tiles)):
            make_tile(ti)
        nch = len(chunks)
        for i in range(nch + 2):
            if i < nch:
                A, B, C, pre = chunks[i]
                if pre is not None:
                    pre()
                A()
            if 0 <= i - 1 < nch and chunks[i - 1][1] is not None:
                chunks[i - 1][1]()
            if 0 <= i - 2 < nch and chunks[i - 2][2] is not None:
                chunks[i - 2][2]()

def bcast_row(dt_, row, off, n):
    ncols = dt_.t.shape[1]
    return bass.AP(dt_.h, row * ncols + off, [[0, 128], [1, n]])


def p3_tiles(l):
    t = []
    s0 = 0
    while s0 < S:
        n = min(510, S - s0)
        t.append((s0, n, 0))
        s0 += n
    if l == 0:
        t.append((S, CT, 1))
    return t


def p3_ffn(k, g, l, hf):
    din = g.din
    HC = 11
    with phase(k):
        WU = k.sb("wu", [128, 8, 2 * HC * 128], BF16)
        WD = k.sb("wd", [128, HC, D], BF16)
        srcu = din["w_up"][l].rearrange("(kc p) n -> p kc n", p=128)
        a0 = hf * HC * 128
        CG = [(0, 1), (1, 2), (2, 4), (4, 7), (7, HC)]
        usegs = []
        for (c0, c1) in CG:
            usegs.append((a0 + c0 * 128, a0 + c1 * 128, c0 * 128))
            usegs.append((DFF + a0 + c0 * 128, DFF + a0 + c1 * 128, HC * 128 + c0 * 128))
        WUS = wload_segs(k, srcu, 8, WU, usegs)
        srcd = din["w_down"][l][a0:a0 + HC * 128, :].rearrange("(hc p) n -> p hc n", p=128)
        WDS = wload_segs(k, srcd, HC, WD, [(0, 512, 0), (512, 1024, 512)])
        gtb = [k.sb("gtb%d" % v, [128, D], F32) for v in range(2)]
        for v in range(2 if l == 0 else 1):
            k.dma(gtb[v][:], bcast_row(g.modrow, v, l * 6144 + 5 * D, D), w=[gtb[v].b])
        ht = [k.sb("ht%d" % i, [128, 8, 512], BF16) for i in range(2)]
        actT = [k.sb("actT%d" % i, [128, HC, 512], BF16) for i in range(2)]
        t1 = [k.sb("t1%d" % i, [128, 512], F32) for i in range(2)]
        t2 = [k.sb("t2%d" % i, [128, 512], F32) for i in range(2)]
        ca = [k.sb("ca%d" % i, [128, 512], F32) for i in range(2)]
        cg = [k.sb("cg%d" % i, [128, 512], F32) for i in range(2)]
        sa = [k.sb("sa%d" % i, [128, 512], F32) for i in range(2)]
        xt = [k.sb("xt%d" % i, [128, D], F32) for i in range(8)]
        xo = [k.sb("xo%d" % i, [128, D], F32) for i in range(2)]
        tmp = [k.sb("tmp%d" % i, [128, 512], F32) for i in range(2)]
        pa = [k.ps("pa%d" % i) for i in range(2)]
        pg = [k.ps("pg%d" % i) for i in range(2)]
        po = [k.ps("po%d" % i) for i in range(3)]
        xsrc = g.XN if hf == 0 else g.XP
        tiles = p3_tiles(l)
        cnt = {"x": 0, "o": 0, "po": 0, "c": 0, "tmp": 0}

        def load(ti):
            s0, n, var = tiles[ti]
            h = ht[ti % 2]
            lo, hi = (0, S) if var == 0 else (S, S + CT)
            a, b_ = max(s0 - 1, lo), min(s0 + n + 1, hi)
            c0 = a - (s0 - 1)
            if c0 > 0:
                k.memset("pool", h[:, :, 0:c0], 0.0, w=[h.b])
            if b_ < s0 + n + 1:
                k.memset("pool", h[:, :, n + 1:n + 2], 0.0, w=[h.b])
            k.dma(h[:, :, c0:c0 + (b_ - a)], g.HT2[:, :, a:b_].rearrange("c p t -> p c t"), w=[h.b])

        def load_x(ti):
            s0, n, var = tiles[ti]
            nsub = (n + 127) // 128
            bufs = []
            for j in range(nsub):
                m = min(128, n - j * 128)
                r0 = s0 + j * 128
                x_ = xt[cnt["x"] % 8]
                cnt["x"] += 1
                k.dma(x_[0:m, :], xsrc[r0:r0 + m, :], w=[x_.b])
                bufs.append(x_)
            return bufs

        def up_chunk(ti, c):
            s0, n, var = tiles[ti]
            h = ht[ti % 2]
            aT = actT[ti % 2]
            cols = n + 2
            i = cnt["c"] % 2
            cnt["c"] += 1
            for (pp, wc0) in ((pa[i], c * 128), (pg[i], HC * 128 + c * 128)):
                wb = WUS.bufs(wc0, wc0 + 128)
                for kc in range(8):
                    k.mm(pp[:, 0:cols], WU[:, kc, wc0:wc0 + 128], h[:, kc, 0:cols], start=(kc == 0),
                         r=wb + [h.b], w=[pp.b])
            for (pp, dst, ci) in ((pa[i], ca[i], hf * HC + c), (pg[i], cg[i], 22 + hf * HC + c)):
                w3 = g.convf[:, l, ci, :]
                k.act(t1[i][:, 0:n], pp[:, 1:n + 1], AF.Copy, r=[g.convf.b], w=[pp.b, t1[i].b], scale=w3[:, 1:2])
                k.stt("dve", t2[i][:, 0:n], pp[:, 0:n], w3[:, 0:1], t1[i][:, 0:n], ALU.mult, ALU.add,
                      r=[g.convf.b, t1[i].b], w=[pp.b, t2[i].b])
                k.stt("dve", dst[:, 0:n], pp[:, 2:n + 2], w3[:, 2:3], t2[i][:, 0:n], ALU.mult, ALU.add,
                      r=[g.convf.b, t2[i].b], w=[pp.b, dst.b])
            k.act(sa[i][:, 0:n], ca[i][:, 0:n], AF.Silu, r=[ca[i].b], w=[sa[i].b])
            k.tt("pool", aT[:, c, 0:n], sa[i][:, 0:n], cg[i][:, 0:n], ALU.mult, r=[sa[i].b, cg[i].b], w=[aT.b])

        def down(ti, xbufs):
            s0, n, var = tiles[ti]
            aT = actT[ti % 2]
            nsub = (n + 127) // 128
            for j in range(nsub):
                m = min(128, n - j * 128)
                r0 = s0 + j * 128
                x_ = xbufs[j]
                o_ = xo[cnt["o"] % 2]
                cnt["o"] += 1
                for hh in range(2):
                    p_ = po[cnt["po"] % 3]
                    cnt["po"] += 1
                    wb = WDS.bufs(hh * 512, (hh + 1) * 512)
                    for hc in range(HC):
                        k.mm(p_[0:m, :], aT[:, hc, j * 128:j * 128 + m], WD[:, hc, hh * 512:(hh + 1) * 512],
                             start=(hc == 0), r=[aT.b] + wb, w=[p_.b])
                    t_ = tmp[cnt["tmp"] % 2]
                    cnt["tmp"] += 1
                    k.tt("dve", t_[0:m, :], p_[0:m, :], gtb[var][0:m, hh * 512:(hh + 1) * 512], ALU.mult,
                         r=[gtb[var].b], w=[p_.b, t_.b])
                    k.tt("pool", o_[0:m, hh * 512:(hh + 1) * 512], x_[0:m, hh * 512:(hh + 1) * 512], t_[0:m, :], ALU.add,
                         r=[x_.b, t_.b], w=[o_.b])
                if hf == 0:
                    dst = g.XP[r0:r0 + m, :]
                elif l == L - 1:
                    dst = g.out[r0:r0 + m, :]
                else:
                    dst = g.XM[r0:r0 + m, :]
                k.dma(dst, o_[0:m, :], r=[o_.b])

        NPRE = 2
        load(0)
        for c in range(HC):
            up_chunk(0, c)
        for ti in range(len(tiles)):
            if ti + 1 < len(tiles):
                load(ti + 1)
            xbufs = load_x(ti)
            if ti + 1 < len(tiles):
                for c in range(NPRE):
                    up_chunk(ti + 1, c)
            down(ti, xbufs)
            if ti + 1 < len(tiles):
                for c in range(NPRE, HC):
                    up_chunk(ti + 1, c)

WARM_N = 0
WARM_EVERY = 1
KC0 = (0, 8, 24, 32)


def na_rowcfgs(a):
    if a == 0:
        return [(0, 3), (1, 4)]
    if a == 15:
        return [(14, 5), (15, 6)]
    return [(a - 1, 0), (a, 1), (a + 1, 2)]


def p2_mixers(k, g, l):
    din = g.din
    N = 256
    with phase(k):
        WO = k.sb("wo", [128, 8, D], BF16)
        NAB = k.sb("nab", [128, 8, NTILE * 8], BF16)
        stg = [k.sb("stg%d" % i, [128, 8, 128], F32) for i in range(2)]
        def load_p2_weights():
            wload(k, stg, din["nab"][:, l, :].rearrange("p (a c) -> p a c", a=8), 8, NTILE * 8, [(0, NTILE * 8, NAB, 0)],
                  blk=128, func=AF.Exp)
            wload_cast(k, din["w_o"][l].rearrange("(kc p) n -> p kc n", p=128), 8, [(0, D, WO, 0)])
        nabf = NAB[:, :, :].rearrange("p a c -> p (a c)")
        nvar = 2 if l == 0 else 1
        gtb = [k.sb("gtb%d" % v, [128, D], F32) for v in range(nvar)]
        for v in range(nvar):
            k.dma(gtb[v][:], bcast_row(g.modrow, v, l * 6144 + 2 * D, D), w=[gtb[v].b])
        es = k.sb("es", [128, 6], F32)
        k.act(es[:], g.sink[:, l, :], AF.Exp, r=[g.sink.b], w=[es.b])
        KAc = k.sb("kac", [128, 3, CT], BF16)
        KBc = k.sb("kbc", [128, 2, CT], BF16)
        k.dma(KAc[:], g.QK[3:6, :, S:S + CT].rearrange("c p t -> p c t"), w=[KAc.b])
        k.dma(KBc[:], g.QK[9:11, :, S:S + CT].rearrange("c p t -> p c t"), w=[KBc.b])
        VAc = [k.sb("vac%d" % i, [128, 6, 128], BF16) for i in range(2)]
        VBc = [k.sb("vbc%d" % i, [128, 2, 128], BF16) for i in range(2)]
        for ct in range(2):
            k.dma(VAc[ct][:], g.VA[S + ct * 128:S + (ct + 1) * 128, :, :], w=[VAc[ct].b])
            k.dma(VBc[ct][:], g.VB[S + ct * 128:S + (ct + 1) * 128, :, :], w=[VBc[ct].b])
        KAn = [k.sb("kan%d" % i, [128, 3, 256], BF16) for i in range(2)]
        KAg = [[k.sb("kag%d_%d" % (i, j), [128, 3, 128], BF16) for j in range(4)] for i in range(4)]
        VAr = [[k.sb("var%d_%d" % (i, j), [128, 6, 128], BF16) for j in range(4)] for i in range(4)]
        KBr = [k.sb("kbr%d" % i, [128, 2, 128], BF16) for i in range(6)]
        VBr = [k.sb("vbr%d" % i, [128, 2, 128], BF16) for i in range(6)]

        def load_group(b):
            if b < 0 or b > 15:
                return
            sl = b % 4
            t0 = b * 256
            kn = KAn[b % 2]
            k.dma(kn[:], g.QK[3:6, :, t0:t0 + 256].rearrange("c p t -> p c t"), w=[kn.b])
            for j in range(4):
                for c in range(3):
                    k.cp("pool", KAg[sl][j][:, c, :].rearrange("p (r x) -> p r x", x=32),
                         kn[:, c, :].rearrange("p (r x) -> p r x", x=64)[:, :, KC0[j]:KC0[j] + 32],
                         r=[kn.b], w=[KAg[sl][j].b])
                for kr in range(4):
                    r0 = t0 + kr * 64 + KC0[j]
                    k.dma(VAr[sl][j][kr * 32:(kr + 1) * 32, :, :], g.VA[r0:r0 + 32, :, :], w=[VAr[sl][j].b])

        def load_kt(kt):
            if kt < 0 or kt > 31:
                return
            sl = kt % 6
            t0 = kt * 128
            k.dma(KBr[sl][:], g.QK[9:11, :, t0:t0 + 128].rearrange("c p t -> p c t"), w=[KBr[sl].b])
            k.dma(VBr[sl][:], g.VB[t0:t0 + 128, :, :], w=[VBr[sl].b])

        q = [k.sb("q%d" % i, [128, 6, N], BF16) for i in range(2)]
        cu = [k.sb("cu%d" % i, [128, 2, N + 2], F32) for i in range(2)]
        bg = [k.sb("bg%d" % i, [128, 2, N], F32) for i in range(2)]
        P = [k.sb("P%d" % i, [128, 512], BF16) for i in range(6)]
        YT = k.sb("YT", [128, 8, N], BF16)
        rc = [k.sb("rc%d" % i, [128, N], F32) for i in range(2)]
        c1 = [k.sb("c1%d" % i, [128, N], F32) for i in range(2)]
        c2 = [k.sb("c2%d" % i, [128, N], F32) for i in range(2)]
        xt = [k.sb("xt%d" % i, [128, D], F32) for i in range(4)]
        xo = [k.sb("xo%d" % i, [128, D], F32) for i in range(2)]
        tmp = [k.sb("tmp%d" % i, [128, 512], F32) for i in range(2)]
        xn = k.sb("xn", [128, 2, D], BF16)
        junk = k.sb("junk", [128, D], BF16)
        ss = k.sb("ss", [128, 2], F32)
        rs = k.sb("rs", [128, 2], F32)
        h2o = k.sb("h2o", [128, 8, N], BF16)
        pS = [k.ps("pS%d" % i) for i in range(6)]
        pO = [k.ps("pO%d" % i) for i in range(2)]
        cnt = {"pS": 0, "pO": 0, "P": 0, "rc": 0, "x": 0, "o": 0, "po": 0, "tmp": 0, "c": 0}

        tiles = [(i * N, N, 0) for i in range(S // N)] + ([(S, CT, 1)] if l == 0 else [])

        def load_tile(ti):
            tok0, n, var = tiles[ti]
            b_ = ti % 2
            k.dma(q[b_][:, 0:3, 0:n], g.QK[0:3, :, tok0:tok0 + n].rearrange("c p t -> p c t"), w=[q[b_].b])
            k.dma(q[b_][:, 3:6, 0:n], g.QK[6:9, :, tok0:tok0 + n].rearrange("c p t -> p c t"), w=[q[b_].b])
            lo, hi = (0, S) if var == 0 else (S, S + CT)
            a, e = max(tok0 - 1, lo), min(tok0 + n + 1, hi)
            c0 = a - (tok0 - 1)
            if c0 > 0:
                k.memset("pool", cu[b_][:, :, 0:1], 0.0, w=[cu[b_].b])
            if e < tok0 + n + 1:
                k.memset("pool", cu[b_][:, :, n + 1:n + 2], 0.0, w=[cu[b_].b])
            k.dma(cu[b_][:, :, c0:c0 + (e - a)], g.CU[:, :, a:e].rearrange("c p t -> p c t"), w=[cu[b_].b])
            k.dma(bg[b_][:, :, 0:n], g.BG[:, :, tok0:tok0 + n].rearrange("c p t -> p c t"), w=[bg[b_].b])

        def next_ps(name, arr):
            p = arr[cnt[name] % len(arr)]
            cnt[name] += 1
            return p

        def ctx_part(qh, qbuf, Kc, kch, pb, Vc, vh, n, pOut):
            for ct in range(2):
                ps_ = next_ps("pS", pS)
                k.mm(ps_[:, 0:n], Kc[pb:pb + 64, kch, ct * 128:(ct + 1) * 128], qh, start=True,
                     r=[Kc.b, qbuf], w=[ps_.b])
                pp = next_ps("P", P)
                k.act(pp[:, 0:n], ps_[:, 0:n], AF.Exp, w=[ps_.b, pp.b])
                k.mm(pOut[:, 0:n], Vc[ct][:, vh, :], pp[:, 0:n], start=(ct == 0), r=[Vc[ct].b, pp.b], w=[pOut.b])

        wmask = {}

        def get_mask(pattern):
            if pattern not in wmask:
                t = k.sb("wm%d" % len(wmask), [128, len(pattern) * 128], BF16)
                for i, mk in enumerate(pattern):
                    k.cp("pool", t[:, i * 128:(i + 1) * 128], g.wgm[:, mk, :], r=[g.wgm.b], w=[t.b])
                wmask[pattern] = t
            return wmask[pattern]

        def blk(ap):
            return ap.rearrange("p (j r c) -> p j r c", j=4, r=4, c=16)

        def finalize(pOut, n, h_extra, dst, blocked):
            r_ = rc[cnt["rc"] % 2]
            cnt["rc"] += 1
            if h_extra is None:
                k.act(r_[64:128, 0:n], pOut[64:128, 0:n], AF.Ln, w=[pOut.b, r_.b])
            else:
                k.act(r_[64:128, 0:n], pOut[64:128, 0:n], AF.Ln, r=[es.b], w=[pOut.b, r_.b],
                      bias=es[64:128, h_extra:h_extra + 1])
            k.act(r_[64:128, 0:n], r_[64:128, 0:n], AF.Exp, r=[r_.b], w=[r_.b], scale=-1.0)
            if blocked:
                k.tt("dve", dst.rearrange("p (r j c) -> p j r c", r=4, j=4, c=16), blk(pOut[0:64, 0:n]),
                     blk(r_[64:128, 0:n]), ALU.mult, r=[r_.b], w=[pOut.b, YT.b])
            else:
                k.tt("dve", dst, pOut[0:64, 0:n], r_[64:128, 0:n], ALU.mult, r=[r_.b], w=[pOut.b, YT.b])

        class U:
            __slots__ = ("pre", "qk", "ex", "pv", "fin", "post")

            def __init__(self):
                self.pre = []
                self.qk = self.ex = self.pv = self.fin = None
                self.post = []

        def make_tile_units(ti):
            tok0, n, var = tiles[ti]
            b_ = ti % 2
            a = ti
            qq = q[b_]
            units = []

            def ctx_units(qh, Kc, kch, pb, Vc, vh, pO_cell, blocked=False):
                u = U()
                cell = {}

                def qk(cell=cell):
                    ps_ = next_ps("pS", pS)
                    cell["ps"] = ps_
                    for ct in range(2):
                        k.mm(ps_[:, ct * n:(ct + 1) * n], Kc[pb:pb + 64, kch, ct * 128:(ct + 1) * 128], qh, start=(ct == 0),
                             r=[Kc.b, qq.b], w=[ps_.b])

                def ex(cell=cell):
                    pp = next_ps("P", P)
                    cell["pp"] = pp
                    if blocked:
                        for ct in range(2):
                            k.act(pp[:, ct * n:(ct + 1) * n].rearrange("p (j r c) -> p r j c", j=4, r=4, c=16),
                                  cell["ps"][:, ct * n:(ct + 1) * n].rearrange("p (r j c) -> p r j c", r=4, j=4, c=16),
                                  AF.Exp, w=[cell["ps"].b, pp.b])
                    else:
                        k.act(pp[:, 0:2 * n], cell["ps"][:, 0:2 * n], AF.Exp, w=[cell["ps"].b, pp.b])

                def pv(cell=cell):
                    pO_cell["p"] = next_ps("pO", pO)
                    pOut = pO_cell["p"]
                    pp = cell["pp"]
                    for ct in range(2):
                        k.mm(pOut[:, 0:n], Vc[ct][:, vh, :], pp[:, ct * n:(ct + 1) * n], start=(ct == 0),
                             r=[Vc[ct].b, pp.b], w=[pOut.b])

                u.qk, u.ex, u.pv = qk, ex, pv
                units.append(u)

            for h in range(6):
                ch, pb = h // 2, 64 * (h % 2)
                qh = qq[pb:pb + 64, ch, 0:n]
                pO_cell = {}
                ctx_units(qh, KAc, ch, pb, VAc, h, pO_cell, blocked=(var == 0))
                if var == 0:
                    q3 = qq[pb:pb + 64, ch, :].rearrange("p (r c) -> p r c", c=64)
                    slots = []
                    for (b, rcfg) in na_rowcfgs(a):
                        for j in range(4):
                            slots.append((b, j, (h * 7 + rcfg) * 4 + j))
                    for s0 in range(0, len(slots), 8):
                        grp = slots[s0:s0 + 8]
                        u = U()
                        cell = {}

                        def qk(grp=grp, cell=cell, q3=q3, pb=pb, ch=ch):
                            ps_ = next_ps("pS", pS)
                            cell["ps"] = ps_
                            for i, (b, j, tix) in enumerate(grp):
                                kt_ = KAg[b % 4][j]
                                k.mm(ps_[:, i * 64:(i + 1) * 64], kt_[pb:pb + 64, ch, :], q3[:, :, 16 * j:16 * j + 16],
                                     start=(i == 0), r=[kt_.b, qq.b], w=[ps_.b])

                        def ex(grp=grp, cell=cell):
                            pp = next_ps("P", P)
                            cell["pp"] = pp
                            w_ = len(grp) * 64
                            k.act(pp[:, 0:w_], cell["ps"][:, 0:w_], AF.Exp, w=[cell["ps"].b, pp.b])
                            t0_ = grp[0][2] * 64
                            k.tt("dve", pp[:, 0:w_], pp[:, 0:w_], nabf[:, t0_:t0_ + w_], ALU.mult, r=[pp.b, NAB.b], w=[pp.b])

                        def pv(grp=grp, cell=cell, pO_cell=pO_cell, h=h):
                            pOut = pO_cell["p"]
                            pp = cell["pp"]
                            for i, (b, j, tix) in enumerate(grp):
                                vt_ = VAr[b % 4][j]
                                k.mm(pOut[:, j * 64:(j + 1) * 64], vt_[:, h, :], pp[:, i * 64:(i + 1) * 64],
                                     start=False, r=[vt_.b, pp.b], w=[pOut.b])

                        u.qk, u.ex, u.pv = qk, ex, pv
                        units.append(u)
                units[-1].fin = (lambda pO_cell=pO_cell, pb=pb, ch=ch: finalize(pO_cell["p"], n, None, YT[pb:pb + 64, ch, 0:n],
                                                                                    var == 0))
            for h in range(6):
                ch, pb, kv = 3 + h // 2, 64 * (h % 2), h // 3
                qh = qq[pb:pb + 64, ch, 0:n]
                pO_cell = {}
                ctx_units(qh, KBc, kv, pb, VBc, kv, pO_cell)
                if var == 0:
                    slots = []
                    for t in range(2):
                        i_ = 2 * a + t
                        for kt in (i_ - 1, i_, i_ + 1):
                            if 0 <= kt <= 31:
                                slots.append((t, kt, 0 if kt == i_ else (1 if kt < i_ else 2)))
                    for s0 in range(0, len(slots), 4):
                        grp = slots[s0:s0 + 4]
                        u = U()
                        cell = {}

                        def qk(grp=grp, cell=cell, pb=pb, ch=ch, kv=kv):
                            ps_ = next_ps("pS", pS)
                            cell["ps"] = ps_
                            for i, (t, kt, mk) in enumerate(grp):
                                kt_ = KBr[kt % 6]
                                k.mm(ps_[:, i * 128:(i + 1) * 128], kt_[pb:pb + 64, kv, :],
                                     qq[pb:pb + 64, ch, t * 128:(t + 1) * 128],
                                     start=(i == 0), r=[kt_.b, qq.b], w=[ps_.b])

                        def ex(grp=grp, cell=cell):
                            pp = next_ps("P", P)
                            cell["pp"] = pp
                            w_ = len(grp) * 128
                            k.act(pp[:, 0:w_], cell["ps"][:, 0:w_], AF.Exp, w=[cell["ps"].b, pp.b])
                            pat = tuple(mk for (_, _, mk) in grp)
                            if any(pat):
                                mt = get_mask(pat)
                                k.tt("dve", pp[:, 0:w_], pp[:, 0:w_], mt[:, 0:w_], ALU.mult, r=[pp.b, mt.b], w=[pp.b])

                        def pv(grp=grp, cell=cell, pO_cell=pO_cell, kv=kv):
                            pOut = pO_cell["p"]
                            pp = cell["pp"]
                            for i, (t, kt, mk) in enumerate(grp):
                                vt_ = VBr[kt % 6]
                                k.mm(pOut[:, t * 128:(t + 1) * 128], vt_[:, kv, :], pp[:, i * 128:(i + 1) * 128],
                                     start=False, r=[vt_.b, pp.b], w=[pOut.b])

                        u.qk, u.ex, u.pv = qk, ex, pv
                        units.append(u)
                units[-1].fin = (lambda pO_cell=pO_cell, pb=pb, ch=ch, h=h: finalize(pO_cell["p"], n, h, YT[pb:pb + 64, ch, 0:n],
                                                                                         False))

            xcell = {}

            def prefetch():
                if var == 0:
                    xsrc = (din["x"] if l == 0 else g.XM[:])[tok0:tok0 + n, :]
                else:
                    xsrc = din["ctx"]
                xcell["b"] = []
                for s in range(n // 128):
                    x_ = xt[cnt["x"] % 4]
                    cnt["x"] += 1
                    k.dma(x_[:], xsrc[s * 128:(s + 1) * 128, :], w=[x_.b])
                    xcell["b"].append(x_)
                if ti + 1 < len(tiles):
                    load_tile(ti + 1)
                if var == 0:
                    load_group(a + 2)
                    load_kt(2 * a + 3); load_kt(2 * a + 4)

            def conv():
                for c in range(2):
                    i = cnt["c"] % 2
                    cnt["c"] += 1
                    w3 = g.convc[:, l, c, :]
                    cuc = cu[b_]
                    k.act(c1[i][:, 0:n], cuc[:, c, 1:n + 1], AF.Copy, r=[cuc.b, g.convc.b], w=[c1[i].b], scale=w3[:, 1:2])
                    k.stt("dve", c2[i][:, 0:n], cuc[:, c, 0:n], w3[:, 0:1], c1[i][:, 0:n], ALU.mult, ALU.add,
                          r=[cuc.b, c1[i].b, g.convc.b], w=[c2[i].b])
                    k.stt("dve", c1[i][:, 0:n], cuc[:, c, 2:n + 2], w3[:, 2:3], c2[i][:, 0:n], ALU.mult, ALU.add,
                          r=[cuc.b, c2[i].b, g.convc.b], w=[c1[i].b])
                    k.tt("pool", YT[:, 6 + c, 0:n], c1[i][:, 0:n], bg[b_][:, c, 0:n], ALU.mult,
                         r=[c1[i].b, bg[b_].b], w=[YT.b])

            def wo_epi():
                nsub = n // 128
                for s in range(nsub):
                    x_ = xcell["b"][s]
                    o_ = xo[cnt["o"] % 2]
                    cnt["o"] += 1
                    for hh in range(2):
                        p_ = next_ps("pS", pS)
                        for kc in range(8):
                            k.mm(p_[:, :], YT[:, kc, s * 128:(s + 1) * 128], WO[:, kc, hh * 512:(hh + 1) * 512],
                                 start=(kc == 0), r=[YT.b, WO.b], w=[p_.b])
                        t_ = tmp[cnt["tmp"] % 2]
                        cnt["tmp"] += 1
                        k.tt("dve", t_[:], p_[:, :], gtb[var][:, hh * 512:(hh + 1) * 512], ALU.mult,
                             r=[gtb[var].b], w=[p_.b, t_.b])
                        k.tt("pool", o_[:, hh * 512:(hh + 1) * 512], x_[:, hh * 512:(hh + 1) * 512], t_[:], ALU.add,
                             r=[x_.b, t_.b], w=[o_.b])
                    k.dma(g.XN[tok0 + s * 128:tok0 + (s + 1) * 128, :], o_[:], r=[o_.b])
                    norm_rows(k, o_[:], s, ss, junk, r=[o_.b])
                    k.rsqrt(rs[:, s:s + 1], ss[:, s:s + 1], g.epsb, r=[ss.b], w=[rs.b])
                    k.ts("dve", xn[:, s, :], o_[:], rs[:, s:s + 1], None, ALU.mult, r=[o_.b, rs.b], w=[xn.b])

            def transposes():
                nsub = n // 128
                for kc in range(8):
                    pt_ = next_ps("pS", pS)
                    ptb = pt_.t.bitcast(BF16)
                    for s in range(nsub):
                        k.tr(ptb[:, s * 128:(s + 1) * 128], xn[:, s, kc * 128:(kc + 1) * 128], g.identb[:],
                             r=[xn.b, g.identb.b], w=[pt_.b])
                    k.act(h2o[:, kc, 0:n], ptb[:, 0:n], AF.Identity, r=[g.AB.b], w=[pt_.b, h2o.b],
                          scale=abv(g, l, var, 2)[:, kc:kc + 1], bias=abv(g, l, var, 3)[:, kc:kc + 1])
                k.dma(g.HT2[:, :, tok0:tok0 + n].rearrange("c p t -> p c t"), h2o[:, :, 0:n], r=[h2o.b])

            def warm():
                pw = pT.t.bitcast(F32)
                for i in range(WARM_N):
                    k.mm(pw[:, 0:512], WO[:, i % 8, 0:128], WO[:, (i + 1) % 8, 0:512], start=True, r=[WO.b], w=[pT.b])

            if WARM_N and (ti % WARM_EVERY == 0):
                units[0].pre.append(warm)
            units[min(7, len(units) - 1)].pre.append(prefetch)
            units[min(8, len(units) - 1)].pre.append(conv)
            units[-1].post.append(wo_epi)
            return units, transposes

        load_tile(0)
        load_group(0)
        for kt in range(0, 3):
            load_kt(kt)
        load_p2_weights()
        load_group(1)
        allu = []
        pending_tr = None
        for ti in range(len(tiles)):
            us, trf = make_tile_units(ti)
            if pending_tr is not None:
                us[min(12, len(us) - 1)].pre.append(pending_tr)
            pending_tr = trf
            allu.extend(us)
        SK = 4
        FD = 1
        due = {}
        for i in range(len(allu) + SK + FD + 1):
            if i < len(allu):
                u = allu[i]
                for f in u.pre:
                    f()
                u.qk()
                u.ex()
            j = i - SK
            if 0 <= j < len(allu):
                u = allu[j]
                u.pv()
                fl = []
                if u.fin is not None:
                    fl.append(u.fin)
                fl.extend(u.post)
                if fl:
                    due.setdefault(i + FD, []).extend(fl)
            for f in due.pop(i, []):
                f()
        assert not due
        pending_tr()

def _shared_inputs(inp, consts):
    m = _core_inputs(0, inp, consts)
    for kx in ("x", "ctx", "cvt"):
        m.pop(kx)
    return m


def kernel(**inputs):
    consts = _consts()
    shared = _shared_inputs(inputs, consts)
    in_maps = [_core_inputs(b, inputs, consts, shared) for b in range(NCORES)]
    nc = build()
    res = run_bass_kernel_spmd(nc, in_maps, core_ids=list(range(NCORES)))
    return np.stack([np.asarray(r["out"], dtype=np.float32) for r in res.results], axis=0)
```

```python
import numpy as np
import ml_dtypes
import concourse.bass as bass
import concourse.mybir as mybir
from concourse.bass_utils import run_bass_kernel_spmd

F32 = mybir.dt.float32
BF16 = mybir.dt.bfloat16
AF = mybir.ActivationFunctionType
ALU = mybir.AluOpType

D = 1024
S = 4096
CT = 256
TT = S + CT
L = 2
DFF = 2816
INW = 2560
EPS = 1e-6
NEGM = -30000.0
NCORES = 8

ROWCFG = [(5, 4), (5, 5), (5, 6), (0, 0), (0, 1), (15, 14), (15, 15)]
NTILE = 6 * 7 * 4


class Buf:
    __slots__ = ("name",)

    def __init__(self, name):
        self.name = name


class Op:
    __slots__ = ("eng", "fn", "deps", "dma", "sig", "sem", "val", "inc")

    def __init__(self, eng, fn, dma):
        self.eng = eng
        self.fn = fn
        self.deps = []
        self.dma = dma
        self.sig = dma
        self.sem = None
        self.val = 0
        self.inc = 1


class Sched:
    NDMA = 12
    ENGS = ["pe", "act", "dve", "pool", "sp"]

    def __init__(self, nc, stack):
        self.nc = nc
        self.csem = {e: stack.enter_context(nc.semaphore("c_" + e)) for e in self.ENGS}
        self.dsem = {e: [stack.enter_context(nc.semaphore("d_%s_%d" % (e, i))) for i in range(self.NDMA)]
                     for e in ("sp", "pool")}
        self.cnt = {e: 0 for e in self.ENGS}
        self.dcnt = {e: 0 for e in self.dsem}
        self.dtot = {}
        self.seen = {e: {} for e in self.ENGS}
        self.nphase = 0
        self.reset()

    def reset(self):
        self.ops = []
        self.last_w = {}
        self.readers = {}
        self.dma_hist = {}

    def add(self, eng, fn, r=(), w=(), dma=False):
        op = Op(eng, fn, dma)
        deps = {}
        for b in r:
            lw = self.last_w.get(b)
            if lw is not None:
                deps[id(lw)] = (lw, 0)
        for b in w:
            lw = self.last_w.get(b)
            if lw is not None and id(lw) not in deps:
                deps[id(lw)] = (lw, 1)
            for rd in self.readers.get(b, ()):
                if id(rd) not in deps:
                    deps[id(rd)] = (rd, 1)
        for p, kind in deps.values():
            if (not p.dma) and (not dma) and p.eng == eng and kind == 1 and eng == "pe":
                continue
            op.deps.append(p)
            p.sig = True
        if dma:
            h = self.dma_hist.setdefault(eng, [])
            if len(h) >= self.NDMA:
                op.deps.append(h[len(h) - self.NDMA])
            h.append(op)
        for b in r:
            self.readers.setdefault(b, []).append(op)
        for b in w:
            self.last_w[b] = op
            self.readers[b] = []
        self.ops.append(op)
        return op

    def emit_phase(self):
        nc = self.nc
        per = {e: [o for o in self.ops if o.eng == e] for e in self.ENGS}
        bar = [(self.csem[e], self.cnt[e]) for e in self.ENGS if self.cnt[e] > 0]
        for e in self.dsem:
            for s in self.dsem[e]:
                if self.dtot.get(id(s), 0) > 0:
                    bar.append((s, self.dtot[id(s)]))
        for e in self.ENGS:
            comp = [o for o in per[e] if not o.dma]
            if comp:
                comp[-1].sig = True
        for op in self.ops:
            if op.dma:
                i = self.dcnt[op.eng]
                self.dcnt[op.eng] = i + 1
                sm = self.dsem[op.eng][i % self.NDMA]
                t = self.dtot.get(id(sm), 0) + 16
                self.dtot[id(sm)] = t
                op.sem, op.val, op.inc = sm, t, 16
            elif op.sig:
                self.cnt[op.eng] += 1
                op.sem, op.val, op.inc = self.csem[op.eng], self.cnt[op.eng], 1
        first = self.nphase == 0
        self.nphase += 1

        def run(e, eng):
            seen = self.seen[e]
            if not first:
                for sm, v in bar:
                    if seen.get(id(sm), 0) < v:
                        eng.wait_ge(sm, v)
                        seen[id(sm)] = v
            for op in per[e]:
                for p in op.deps:
                    k = id(p.sem)
                    if seen.get(k, 0) < p.val:
                        eng.wait_ge(p.sem, p.val)
                        seen[k] = p.val
                ins = op.fn(eng)
                if op.sig:
                    ins.then_inc(op.sem, op.inc)

        with nc.Block() as block:
            @block.tensor
            def _(eng):
                run("pe", eng)

            @block.scalar
            def _(eng):
                run("act", eng)

            @block.vector
            def _(eng):
                run("dve", eng)

            @block.gpsimd
            def _(eng):
                run("pool", eng)

            @block.sync
            def _(eng):
                run("sp", eng)
        self.reset()

    def emit_final(self):
        nc = self.nc
        bar = [(self.csem[e], self.cnt[e]) for e in self.ENGS if self.cnt[e] > 0]
        for e in self.dsem:
            for s in self.dsem[e]:
                if self.dtot.get(id(s), 0) > 0:
                    bar.append((s, self.dtot[id(s)]))
        with nc.Block() as block:
            @block.sync
            def _(eng):
                for sm, v in bar:
                    eng.wait_ge(sm, v)


class Tn:
    def __init__(self, t, name):
        self.t = t
        self.b = Buf(name)

    def __getitem__(self, k):
        return self.t[k]


class K:
    def __init__(self, nc, stack, dbg=None):
        self.nc = nc
        self.st = stack
        self.s = Sched(nc, stack)
        self.dbg = dbg or set()
        self.gst = stack

    def sb(self, name, shape, dt):
        self.nn = getattr(self, "nn", 0) + 1
        t = self.st.enter_context(self.nc.sbuf_tensor("s%d_%s" % (self.nn, name), list(shape), dt))
        return Tn(t, name)

    def ps(self, name, dt=F32, cols=512):
        self.nn = getattr(self, "nn", 0) + 1
        t = self.st.enter_context(self.nc.psum_tensor("p%d_%s" % (self.nn, name), [128, cols], dt))
        return Tn(t, name)

    def dram(self, name, shape, dt, kind="Internal"):
        if name in self.dbg:
            kind = "ExternalOutput"
        if name in getattr(self, "dbg_in", ()):
            kind = "ExternalInput"
        t = self.nc.dram_tensor(name, list(shape), dt, kind=kind)
        d = Tn(t.ap(), name)
        d.h = t
        return d

    def dma(self, out, in_, r=(), w=(), q="sp", **kw):
        return self.s.add(q, lambda e: e.dma_start(out=out, in_=in_, **kw), r=r, w=w, dma=True)

    def mm(self, out, lhsT, rhs, start, stop=True, r=(), w=()):
        return self.s.add(
            "pe",
            lambda e: e.matmul(out, lhsT, rhs, start=start, stop=stop, skip_group_check=True),
            r=r, w=w)

    def tr(self, out, in_, ident, r=(), w=()):
        return self.s.add("pe", lambda e: e.transpose(out, in_, ident), r=r, w=w)

    def act(self, out, in_, func, r=(), w=(), eng="act", **kw):
        return self.s.add(eng, lambda e: e.activation(out=out, in_=in_, func=func, **kw), r=r, w=w)

    def tt(self, eng, out, in0, in1, op, r=(), w=()):
        return self.s.add(eng, lambda e: e.tensor_tensor(out=out, in0=in0, in1=in1, op=op), r=r, w=w)

    def ts(self, eng, out, in0, s1, s2, op0, op1=None, r=(), w=()):
        if op1 is None:
            return self.s.add(eng, lambda e: e.tensor_scalar(out=out, in0=in0, scalar1=s1, scalar2=None, op0=op0), r=r, w=w)
        return self.s.add(eng, lambda e: e.tensor_scalar(out=out, in0=in0, scalar1=s1, scalar2=s2, op0=op0, op1=op1), r=r, w=w)

    def stt(self, eng, out, in0, scalar, in1, op0, op1, r=(), w=()):
        return self.s.add(eng, lambda e: e.scalar_tensor_tensor(out=out, in0=in0, scalar=scalar, in1=in1, op0=op0, op1=op1), r=r, w=w)

    def cp(self, eng, out, in_, r=(), w=()):
        if eng == "act":
            return self.s.add(eng, lambda e: e.copy(out=out, in_=in_), r=r, w=w)
        return self.s.add(eng, lambda e: e.tensor_copy(out=out, in_=in_), r=r, w=w)

    def recip(self, out, in_, r=(), w=()):
        return self.s.add("dve", lambda e: e.reciprocal(out=out, in_=in_), r=r, w=w)

    def rsqrt(self, out, in_, epsb, r=(), w=(), inw=()):
        self.act(out, in_, AF.Ln, r=list(r) + [epsb.b], w=list(inw) + list(w), bias=epsb[:, 0:1])
        return self.act(out, out, AF.Exp, r=list(w), w=list(w), scale=-0.5)

    def memset(self, eng, ap, val, w=()):
        return self.s.add(eng, lambda e: e.memset(ap, val), w=w)


def _na_index():
    kr_in = np.arange(128) // 32
    kc_in = np.arange(128) % 32
    r_in = np.arange(64) // 16
    c_in = np.arange(64) % 16
    drow = np.zeros((7, 128, 64), np.int64)
    rok = np.zeros((7, 128, 64), bool)
    for i, (a, b) in enumerate(ROWCFG):
        r = 4 * a + r_in[None, :]
        kr = 4 * b + kr_in[:, None]
        r0 = np.clip(r - 4, 0, 56)
        rok[i] = (kr >= r0) & (kr < r0 + 8)
        drow[i] = np.clip(kr - r + 7, 0, 14)
    dcol = np.zeros((4, 128, 64), np.int64)
    cok = np.zeros((4, 128, 64), bool)
    for i, j in enumerate((0, 1, 2, 3)):
        kc0 = int(np.clip(16 * j - 8, 0, 32))
        c = 16 * j + c_in[None, :]
        kc = kc0 + kc_in[:, None]
        c0 = np.clip(c - 8, 0, 48)
        cok[i] = (kc >= c0) & (kc < c0 + 16)
        dcol[i] = np.clip(kc - c + 15, 0, 30)
    return drow, rok, dcol, cok


def _consts():
    c = {}
    c["identb"] = np.eye(128, dtype=np.float32).astype(ml_dtypes.bfloat16)
    c["identf"] = np.eye(128, dtype=np.float32)
    bm = np.zeros((128, 128), np.float32)
    bm[:64, :64] = 1.0 / 64
    bm[64:, 64:] = 1.0 / 64
    c["bm"] = bm.astype(ml_dtypes.bfloat16)
    pm = np.zeros((128, 128), np.float32)
    for m in range(128):
        k = m + 32 if (m % 64) < 32 else m - 32
        pm[k, m] = 1.0
    c["pm"] = pm
    t = np.arange(S)
    row = (t // 64).astype(np.float32)
    col = (t % 64).astype(np.float32)
    inv = (np.float32(10000.0) ** (-np.arange(16, dtype=np.float32) / np.float32(16))).astype(np.float32)
    ang = np.concatenate([row[:, None] * inv, col[:, None] * inv], axis=-1).astype(np.float32)
    cos = np.cos(ang).astype(np.float32)
    sin = np.sin(ang).astype(np.float32)
    d = np.arange(128) % 64
    cosT = cos[:, d % 32].T
    sgn = np.where(d < 32, -1.0, 1.0).astype(np.float32)
    sinT = sin[:, d % 32].T * sgn[:, None]
    c["rope"] = np.ascontiguousarray(np.stack([cosT, sinT], axis=1)).astype(np.float32)
    ki = np.arange(128)[:, None]
    qi = np.arange(128)[None, :]
    mprev = np.where(qi <= ki, 1.0, 0.0)
    mnext = np.where(ki <= qi, 1.0, 0.0)
    c["wgm"] = np.stack([np.ones_like(mprev), mprev, mnext], axis=1).astype(np.float32).astype(ml_dtypes.bfloat16)
    return c


def _core_inputs(b, inp, consts, shared=None):
    f = lambda a: np.ascontiguousarray(np.asarray(a, dtype=np.float32))
    m = {}
    m["x"] = f(inp["x"][b])
    m["ctx"] = f(inp["ctx"][b])
    cvec = np.stack([np.asarray(inp["c"][b]), np.asarray(inp["c_ctx"])], 0)
    m["cvt"] = f(cvec.reshape(2, 8, 128).transpose(2, 1, 0))
    if shared is not None:
        m.update(shared)
        return m
    m["w_ada"] = f(inp["w_ada"])
    m["b_ada"] = f(inp["b_ada"])
    gt = lambda g: np.asarray(g).reshape(L, 8, 128).transpose(2, 0, 1)
    m["gT"] = f(np.stack([gt(inp["g_attn"]), gt(inp["g_ffn"])], axis=2))
    m["w_in"] = f(inp["w_in"])
    qkg = np.stack([np.asarray(inp[k]) for k in ("qn_a", "kn_a", "qn_b", "kn_b")], axis=-1)
    m["qkg"] = f(np.concatenate([qkg, qkg], axis=1).transpose(1, 0, 2))
    drow, rok, dcol, cok = _na_index()
    rpb = np.asarray(inp["rpb_a"], dtype=np.float32)
    g = rpb[:, :, drow[:, None], dcol[None, :]]
    ok = (rok[:, None] & cok[None, :])[None, None]
    nab = np.where(ok, g, np.float32(NEGM)).astype(np.float32)
    m["nab"] = f(nab.transpose(4, 0, 1, 2, 3, 5).reshape(128, L, NTILE * 64))
    m["sink"] = f(np.broadcast_to(np.asarray(inp["sink_b"])[None], (128, L, 6)))
    m["convc"] = f(np.asarray(inp["conv_c"]).reshape(L, 3, 2, 128).transpose(3, 0, 2, 1))
    m["w_o"] = f(inp["w_o"])
    m["w_up"] = f(inp["w_up"])
    m["convf"] = f(np.asarray(inp["conv_ffn"]).reshape(L, 3, 44, 128).transpose(3, 0, 2, 1))
    m["w_down"] = f(inp["w_down"])
    m.update(consts)
    return m


from contextlib import ExitStack, contextmanager


@contextmanager
def phase(k):
    old = k.st
    with ExitStack() as st:
        k.st = st
        yield
        k.s.emit_phase()
    k.st = old


def fence(k, eng, r, w):
    d = k.dummy
    return k.s.add(eng, lambda e: e.memset(d[0:1, 0:1], 0.0), r=r, w=list(w) + [d.b])


def wload_cast(k, src, nkc, segs, nsplit=4):
    subs = {}
    step = (nkc + nsplit - 1) // nsplit
    for (s0, s1, dst, d0) in segs:
        for k0 in range(0, nkc, step):
            k1 = min(nkc, k0 + step)
            sb_ = Buf("sub")
            subs.setdefault(id(dst), (dst, []))[1].append(sb_)
            k.dma(dst[:, k0:k1, d0:d0 + (s1 - s0)], src[:, k0:k1, s0:s1], w=[sb_], q="pool")
    for dst, bl in subs.values():
        fence(k, "pool", r=bl, w=[dst.b])


class WSegs:
    def __init__(self):
        self.rng = []

    def bufs(self, c0, c1):
        return [b for (d0, d1, b) in self.rng if d0 < c1 and c0 < d1]


def wload_segs(k, src, nkc, dst, segs):
    ws = WSegs()
    for (s0, s1, d0) in segs:
        b = Buf("wseg")
        k.dma(dst[:, 0:nkc, d0:d0 + (s1 - s0)], src[:, :, s0:s1], w=[b], q="pool")
        ws.rng.append((d0, d0 + (s1 - s0), b))
    return ws


def wload(k, stg, src, nkc, ncols, segs, engs=("dve", "pool", "act"), blk=256, func=None):
    subs = {}
    ci = 0
    for bi, c0 in enumerate(range(0, ncols, blk)):
        c1 = min(ncols, c0 + blk)
        sg = stg[bi % len(stg)]
        k.dma(sg[:, 0:nkc, 0:c1 - c0], src[:, :, c0:c1], w=[sg.b])
        for (s0, s1, dst, d0) in segs:
            lo, hi = max(s0, c0), min(s1, c1)
            if lo >= hi:
                continue
            sb_ = Buf("sub")
            subs.setdefault(id(dst), (dst, []))[1].append(sb_)
            if func is None:
                k.cp(engs[ci % len(engs)], dst[:, 0:nkc, d0 + lo - s0:d0 + hi - s0], sg[:, 0:nkc, lo - c0:hi - c0],
                     r=[sg.b], w=[sb_])
            else:
                k.act(dst[:, 0:nkc, d0 + lo - s0:d0 + hi - s0], sg[:, 0:nkc, lo - c0:hi - c0], func, r=[sg.b], w=[sb_])
            ci += 1
    for dst, bl in subs.values():
        fence(k, "pool", r=bl, w=[dst.b])


class G:
    pass


def build(nlayers=L, dbg=(), stop_after=None, dbg_in=(), only=None):
    nc = bass.Bass("TRN2", target_bir_lowering=False)
    gst = ExitStack()
    with gst:
        k = K(nc, gst, set(dbg))
        k.dbg_in = set(dbg_in)
        g = G()
        din = {}

        def inp(name, shape, dt=F32):
            din[name] = nc.dram_tensor(name, list(shape), dt, kind="ExternalInput").ap()

        inp("x", [S, D]); inp("ctx", [CT, D]); inp("cvt", [128, 8, 2])
        inp("w_ada", [L, D, 6 * D]); inp("b_ada", [L, 6 * D]); inp("gT", [128, L, 2, 8])
        inp("w_in", [L, D, INW]); inp("qkg", [128, L, 4]); inp("nab", [128, L, NTILE * 64])
        inp("sink", [128, L, 6]); inp("convc", [128, L, 2, 3]); inp("w_o", [L, D, D])
        inp("w_up", [L, D, 2 * DFF]); inp("convf", [128, L, 44, 3]); inp("w_down", [L, DFF, D])
        inp("identb", [128, 128], BF16); inp("identf", [128, 128]); inp("bm", [128, 128], BF16)
        inp("pm", [128, 128]); inp("rope", [128, 2, S]); inp("wgm", [128, 3, 128], BF16)
        g.din = din
        g.out = nc.dram_tensor("out", [S, D], F32, kind="ExternalOutput").ap()
        g.modrow = k.dram("modrow", [2, L * 6 * D], F32)
        g.QK = k.dram("QK", [11, 128, TT], BF16)
        g.VA = k.dram("VA", [TT, 6, 128], BF16)
        g.VB = k.dram("VB", [TT, 2, 128], BF16)
        g.CU = k.dram("CU", [2, 128, TT], F32)
        g.BG = k.dram("BG", [2, 128, TT], F32)
        g.XN = k.dram("XN", [TT, D], F32)
        g.XP = k.dram("XP", [TT, D], F32)
        g.XM = k.dram("XM", [TT, D], F32)
        g.HT2 = k.dram("HT2", [8, 128, TT], BF16)
        g.identb = k.sb("identb", [128, 128], BF16)
        g.identf = k.sb("identf", [128, 128], F32)
        g.bm = k.sb("bm", [128, 128], BF16)
        g.pm = k.sb("pm", [128, 128], F32)
        g.wgm = k.sb("wgm", [128, 3, 128], BF16)
        g.modT = k.sb("modT", [128, L, 96], F32)
        g.AB = k.sb("AB", [128, L * 2 * 4, 8], F32)
        g.gT = k.sb("gT", [128, L, 2, 8], F32)
        g.qkg = k.sb("qkg", [128, L, 4], F32)
        g.sink = k.sb("sink", [128, L, 6], F32)
        g.convc = k.sb("convc", [128, L, 2, 3], F32)
        g.convf = k.sb("convf", [128, L, 44, 3], F32)
        k.dummy = k.sb("dummy", [128, 4], F32)
        g.epsb = k.sb("epsb", [128, 1], F32)

        p0_mods(k, g)
        if stop_after == "p0":
            k.s.emit_final()
            return nc
        for l in range(nlayers):
            if only is None or "p1" in only:
                p1_inproj(k, g, l)
            if stop_after == "p1":
                break
            if only is None or "p2" in only:
                p2_mixers(k, g, l)
            if stop_after == "p2":
                break
            if only is None or "p3" in only:
                p3_ffn(k, g, l, 0)
                p3_ffn(k, g, l, 1)
        k.s.emit_final()
    return nc


def abv(g, l, var, which):
    return g.AB[:, (l * 2 + var) * 4 + which, :]


def p0_mods(k, g):
    din = g.din
    with phase(k):
        for nm in ("identb", "identf", "bm", "pm", "wgm", "gT", "qkg", "sink", "convc", "convf"):
            t = getattr(g, nm)
            k.dma(t[:], din[nm], w=[t.b])
        k.memset("dve", g.epsb[:], EPS, w=[g.epsb.b])
        for gi in (0, 2):
            k.ts("dve", g.qkg[:, :, gi:gi + 1], g.qkg[:, :, gi:gi + 1], 0.125, None, ALU.mult, r=[g.qkg.b], w=[g.qkg.b])
        cvt = k.sb("cvt", [128, 8, 2], F32)
        sct = k.sb("sct", [128, 8, 2], F32)
        k.dma(cvt[:], din["cvt"], w=[cvt.b])
        k.act(sct[:], cvt[:], AF.Silu, r=[cvt.b], w=[sct.b])
        NB0 = 6
        wst = [k.sb("wst%d" % i, [128, 8, 512], F32) for i in range(NB0)]
        bad = [k.sb("bad%d" % i, [2, 512], F32) for i in range(NB0)]
        mrow = [k.sb("mrow%d" % i, [2, 512], F32) for i in range(NB0)]
        pmm = [k.ps("p0m%d" % i) for i in range(NB0)]
        pT = k.ps("p0T")
        chunks = [(l_, n_) for l_ in range(L) for n_ in range(12)]

        def p0_load(ci):
            l_, n_ = chunks[ci]
            i = ci % NB0
            wv = din["w_ada"][l_].rearrange("(kc p) n -> p kc n", p=128)
            k.dma(wst[i][:], wv[:, :, n_ * 512:(n_ + 1) * 512], w=[wst[i].b])
            for r_ in range(2):
                k.dma(bad[i][r_:r_ + 1, :], din["b_ada"][l_:l_ + 1, n_ * 512:(n_ + 1) * 512], w=[bad[i].b])

        PF = NB0 - 2
        for ci in range(PF):
            p0_load(ci)
        for l in range(L):
            for n in range(12):
                ci = l * 12 + n
                i = ci % NB0
                if ci + PF < len(chunks):
                    p0_load(ci + PF)
                for kc in range(8):
                    k.mm(pmm[i][0:2, :], sct[:, kc, :], wst[i][:, kc, :], start=(kc == 0),
                         r=[sct.b, wst[i].b], w=[pmm[i].b])
                k.tt("dve", mrow[i][:], pmm[i][0:2, :], bad[i][:], ALU.add, r=[bad[i].b], w=[pmm[i].b, mrow[i].b])
                k.dma(g.modrow[:, l * 6144 + n * 512:l * 6144 + (n + 1) * 512], mrow[i][:], r=[mrow[i].b])
                for j in range(4):
                    idx = n * 4 + j
                    k.mm(pT[:, idx * 2:idx * 2 + 2], mrow[i][0:2, j * 128:(j + 1) * 128], g.identf[0:2, 0:2],
                         start=(idx == 0), r=[mrow[i].b, g.identf.b], w=[pT.b])
            k.cp("dve", g.modT[:, l, :], pT[:, 0:96], w=[pT.b, g.modT.b])
            mv = g.modT[:, l, :].rearrange("p (c v) -> p c v", v=2)
            for var in range(2):
                k.stt("dve", abv(g, l, var, 0), mv[:, 8:16, var], 1.0, g.gT[:, l, 0, :], ALU.add, ALU.mult,
                      r=[g.modT.b, g.gT.b], w=[g.AB.b])
                k.cp("dve", abv(g, l, var, 1), mv[:, 0:8, var], r=[g.modT.b], w=[g.AB.b])
                k.stt("dve", abv(g, l, var, 2), mv[:, 32:40, var], 1.0, g.gT[:, l, 1, :], ALU.add, ALU.mult,
                      r=[g.modT.b, g.gT.b], w=[g.AB.b])
                k.cp("dve", abv(g, l, var, 3), mv[:, 24:32, var], r=[g.modT.b], w=[g.AB.b])


def norm_rows(k, xt, s, ss, junk, r=()):
    k.act(junk[:], xt, AF.Square, r=list(r), w=[junk.b, ss.b], accum_out=ss[:, s:s + 1], scale=1.0 / 32.0)


W1_QA, W1_KA, W1_QB, W1_KBD, W1_U, W1_CG, W1_BG, W1_V = 0, 384, 768, 1152, 1408, 1664, 1920, 2176
W1_N = 2688


def p1_inproj(k, g, l):
    din = g.din
    with phase(k):
        W = k.sb("w1", [128, 8, W1_N], BF16)
        src = din["w_in"][l].rearrange("(kc p) n -> p kc n", p=128)
        segs = [(0, 128, W1_QA), (128, 384, W1_QA + 128), (384, 768, W1_KA), (1152, 1536, W1_QB),
                (1536, 1600, W1_KBD), (1536, 1600, W1_KBD + 64),
                (1600, 1664, W1_KBD + 128), (1600, 1664, W1_KBD + 192),
                (1792, 2048, W1_U), (2304, 2560, W1_CG), (2048, 2304, W1_BG),
                (768, 1152, W1_V), (1664, 1792, W1_V + 384)]
        WS = wload_segs(k, src, 8, W, segs)

        xt = [k.sb("xt%d" % i, [128, 4, D], F32) for i in range(2)]
        cs = [k.sb("cs%d" % i, [128, 2, 512], F32) for i in range(2)]
        xn = [k.sb("xn%d" % i, [128, 4, D], BF16) for i in range(2)]
        junk = k.sb("junk", [128, D], BF16)
        ss = [k.sb("ss%d" % i, [128, 4], F32) for i in range(2)]
        rs = [k.sb("rs%d" % i, [128, 4], F32) for i in range(2)]
        hTs = [k.sb("hT%d" % i, [128, 8, 512], BF16) for i in range(2)]
        sq = [k.sb("sq%d" % i, [128, 512], BF16) for i in range(3)]
        rstd = [k.sb("rstd%d" % i, [128, 512], F32) for i in range(3)]
        qn = [k.sb("qn%d" % i, [128, 512], F32) for i in range(3)]
        t1 = [k.sb("t1%d" % i, [128, 512], F32) for i in range(3)]
        t2 = [k.sb("t2%d" % i, [128, 512], F32) for i in range(3)]
        ob = [k.sb("ob%d" % i, [128, 512], BF16) for i in range(3)]
        usb = [k.sb("usb%d" % i, [128, 512], F32) for i in range(2)]
        of = [k.sb("of%d" % i, [128, 512], F32) for i in range(3)]
        vt = [k.sb("vt%d" % i, [128, 8, 128], BF16) for i in range(2)]
        for v_ in vt:
            k.memset("pool", v_[:, :, 64:128], 1.0, w=[v_.b])
        pT = [k.ps("pT%d" % i, BF16, 1024) for i in range(2)]
        pq = [k.ps("pq%d" % i) for i in range(4)]
        pmn = k.ps("pmn")
        pr = k.ps("pr")

        tiles = [(i * 512, 512, 0) for i in range(8)] + [(S, CT, 1)]
        cnt = {"ob": 0, "of": 0, "pq": 0, "a": 0, "vt": 0, "pT": 0}

        def load(ti):
            tok0, n, var = tiles[ti]
            nsub = n // 128
            b = ti % 2
            if var == 0:
                srcx = (din["x"] if l == 0 else g.XM[:])[tok0:tok0 + n, :]
                rd = [] if l == 0 else [g.XM.b]
            else:
                srcx = din["ctx"] if l == 0 else g.XM[S:S + CT, :]
                rd = [] if l == 0 else [g.XM.b]
            k.dma(xt[b][:, 0:nsub, :], srcx.rearrange("(s p) f -> p s f", p=128), w=[xt[b].b])

        def load_cs(ti):
            tok0, n, var = tiles[ti]
            b = ti % 2
            if var == 0:
                k.dma(cs[b][:, :, 0:n], din["rope"][:, :, tok0:tok0 + n], w=[cs[b].b])

        def norm(ti):
            tok0, n, var = tiles[ti]
            nsub = n // 128
            b = ti % 2
            xn_ = xn[b]
            for s in range(nsub):
                norm_rows(k, xt[b][:, s, :], s, ss[b], junk, r=[xt[b].b])
            k.rsqrt(rs[b][:, 0:nsub], ss[b][:, 0:nsub], g.epsb, r=[ss[b].b], w=[rs[b].b])
            for s in range(nsub):
                if s % 2 == 0:
                    k.ts("dve", xn_[:, s, :], xt[b][:, s, :], rs[b][:, s:s + 1], None, ALU.mult,
                         r=[xt[b].b, rs[b].b], w=[xn_.b])
                else:
                    k.act(xn_[:, s, :], xt[b][:, s, :], AF.Copy, r=[xt[b].b, rs[b].b], w=[xn_.b], scale=rs[b][:, s:s + 1])

        def trans(ti):
            tok0, n, var = tiles[ti]
            nsub = n // 128
            xn_ = xn[ti % 2]
            hT_ = hTs[ti % 2]
            for kc in range(8):
                p = pT[cnt["pT"] % 2]
                cnt["pT"] += 1
                for s in range(nsub):
                    k.tr(p[:, s * 128:(s + 1) * 128], xn_[:, s, kc * 128:(kc + 1) * 128], g.identb[:],
                         r=[xn_.b, g.identb.b], w=[p.b])
                k.act(hT_[:, kc, 0:n], p[:, 0:n], AF.Identity, r=[g.AB.b], w=[p.b, hT_.b],
                      scale=abv(g, l, var, 0)[:, kc:kc + 1], bias=abv(g, l, var, 1)[:, kc:kc + 1])

        load(0)
        load_cs(0)
        load(1)
        norm(0)
        trans(0)
        chunks = []

        def add_chunk(A, B=None, C=None, pre=None):
            chunks.append((A, B, C, pre))

        def make_tile(ti):
            tok0, n, var = tiles[ti]
            nsub = n // 128
            b = ti % 2
            hT = hTs[b]

            def proj(wc0):
                p = pq[cnt["pq"] % 4]
                cnt["pq"] += 1
                wb = WS.bufs(wc0, wc0 + 128)
                for kc in range(8):
                    k.mm(p[:, 0:n], W[:, kc, wc0:wc0 + 128], hT[:, kc, 0:n], start=(kc == 0), r=wb + [hT.b], w=[p.b])
                return p

            def qk_chunk(wc0, gi, rope, qkidx, pre=None):
                cell = {}

                def A():
                    p = proj(wc0)
                    a = cnt["a"] % 3
                    cnt["a"] += 1
                    cell["p"], cell["a"] = p, a
                    k.act(sq[a][:, 0:n], p[:, 0:n], AF.Square, w=[p.b, sq[a].b])

                def B():
                    p, a = cell["p"], cell["a"]
                    k.mm(pmn[:, 0:n], g.bm[:], sq[a][:, 0:n], start=True, r=[g.bm.b, sq[a].b], w=[pmn.b])
                    k.rsqrt(rstd[a][:, 0:n], pmn[:, 0:n], g.epsb, w=[rstd[a].b], inw=[pmn.b])
                    if not rope:
                        o = ob[cnt["ob"] % 3]
                        cnt["ob"] += 1
                        k.stt("dve", o[:, 0:n], p[:, 0:n], g.qkg[:, l, gi:gi + 1], rstd[a][:, 0:n], ALU.mult, ALU.mult,
                              r=[rstd[a].b, g.qkg.b], w=[p.b, o.b])
                        k.dma(g.QK[qkidx, :, tok0:tok0 + n], o[:, 0:n], r=[o.b])
                    else:
                        k.stt("dve", qn[a][:, 0:n], p[:, 0:n], g.qkg[:, l, gi:gi + 1], rstd[a][:, 0:n], ALU.mult, ALU.mult,
                              r=[rstd[a].b, g.qkg.b], w=[p.b, qn[a].b])

                def C():
                    a = cell["a"]
                    o = ob[cnt["ob"] % 3]
                    cnt["ob"] += 1
                    k.mm(pr[:, 0:n], g.pm[:], qn[a][:, 0:n], start=True, r=[g.pm.b, qn[a].b], w=[pr.b])
                    k.tt("pool", t1[a][:, 0:n], qn[a][:, 0:n], cs[b][:, 0, 0:n], ALU.mult, r=[qn[a].b, cs[b].b], w=[t1[a].b])
                    k.tt("dve", t2[a][:, 0:n], pr[:, 0:n], cs[b][:, 1, 0:n], ALU.mult, r=[cs[b].b], w=[pr.b, t2[a].b])
                    k.tt("pool", o[:, 0:n], t1[a][:, 0:n], t2[a][:, 0:n], ALU.add, r=[t1[a].b, t2[a].b], w=[o.b])
                    k.dma(g.QK[qkidx, :, tok0:tok0 + n], o[:, 0:n], r=[o.b])

                add_chunk(A, B, C if rope else None, pre)

            def tile_pre():
                if ti + 1 < len(tiles):
                    norm(ti + 1)
                if ti + 2 < len(tiles):
                    load(ti + 2)

            def mid_pre():
                if ti + 1 < len(tiles):
                    trans(ti + 1)
                    load_cs(ti + 1)

            for c in range(3):
                qk_chunk(W1_QA + c * 128, 0, False, c, pre=tile_pre if c == 0 else None)
            for c in range(3):
                qk_chunk(W1_KA + c * 128, 1, False, 3 + c)
            for c in range(3):
                qk_chunk(W1_QB + c * 128, 2, var == 0, 6 + c, pre=mid_pre if c == 0 else None)
            for c in range(2):
                qk_chunk(W1_KBD + c * 128, 3, var == 0, 9 + c)
            for c in range(2):
                ucell = {}

                def A_u(c=c, ucell=ucell):
                    pu = proj(W1_U + c * 128)
                    a = cnt["u"] % 2
                    cnt["u"] += 1
                    ucell["a"] = a
                    k.cp("act", usb[a][:, 0:n], pu[:, 0:n], w=[pu.b, usb[a].b])

                def A_cg(c=c, ucell=ucell):
                    a = ucell["a"]
                    pc = proj(W1_CG + c * 128)
                    o = of[cnt["of"] % 3]
                    cnt["of"] += 1
                    k.tt("dve", o[:, 0:n], pc[:, 0:n], usb[a][:, 0:n], ALU.mult, r=[usb[a].b], w=[pc.b, o.b])
                    k.dma(g.CU[c, :, tok0:tok0 + n], o[:, 0:n], r=[o.b])

                def A_bg(c=c):
                    pb = proj(W1_BG + c * 128)
                    o = of[cnt["of"] % 3]
                    cnt["of"] += 1
                    k.cp("act", o[:, 0:n], pb[:, 0:n], w=[pb.b, o.b])
                    k.dma(g.BG[c, :, tok0:tok0 + n], o[:, 0:n], r=[o.b])

                add_chunk(A_u)
                add_chunk(A_cg)
                add_chunk(A_bg)
            for s in range(nsub):
                def A_v(s=s):
                    pv_ = pq[cnt["pq"] % 4]
                    cnt["pq"] += 1
                    wb = WS.bufs(W1_V, W1_V + 512)
                    for kc in range(8):
                        k.mm(pv_[:, :], hT[:, kc, s * 128:(s + 1) * 128], W[:, kc, W1_V:W1_V + 512], start=(kc == 0),
                             r=wb + [hT.b], w=[pv_.b])
                    v = vt[cnt["vt"] % 2]
                    cnt["vt"] += 1
                    k.cp("act" if s % 2 == 0 else "dve", v[:, :, 0:64], pv_[:, :].rearrange("p (h d) -> p h d", d=64),
                         w=[pv_.b, v.b])
                    k.dma(g.VA[tok0 + s * 128:tok0 + (s + 1) * 128, :, :], v[:, 0:6, :], r=[v.b])
                    k.dma(g.VB[tok0 + s * 128:tok0 + (s + 1) * 128, :, :], v[:, 6:8, :], r=[v.b])

                add_chunk(A_v)

        cnt["u"] = 0
        for ti in range(len(tiles)):
            make_tile(ti)
        nch = len(chunks)
        for i in range(nch + 2):
            if i < nch:
                A, B, C, pre = chunks[i]
                if pre is not None:
                    pre()
                A()
            if 0 <= i - 1 < nch and chunks[i - 1][1] is not None:
                chunks[i - 1][1]()
            if 0 <= i - 2 < nch and chunks[i - 2][2] is not None:
                chunks[i - 2][2]()

def bcast_row(dt_, row, off, n):
    ncols = dt_.t.shape[1]
    return bass.AP(dt_.h, row * ncols + off, [[0, 128], [1, n]])


def p3_tiles(l):
    t = []
    s0 = 0
    while s0 < S:
        n = min(510, S - s0)
        t.append((s0, n, 0))
        s0 += n
    if l == 0:
        t.append((S, CT, 1))
    return t


def p3_ffn(k, g, l, hf):
    din = g.din
    HC = 11
    with phase(k):
        WU = k.sb("wu", [128, 8, 2 * HC * 128], BF16)
        WD = k.sb("wd", [128, HC, D], BF16)
        srcu = din["w_up"][l].rearrange("(kc p) n -> p kc n", p=128)
        a0 = hf * HC * 128
        CG = [(0, 1), (1, 2), (2, 4), (4, 7), (7, HC)]
        usegs = []
        for (c0, c1) in CG:
            usegs.append((a0 + c0 * 128, a0 + c1 * 128, c0 * 128))
            usegs.append((DFF + a0 + c0 * 128, DFF + a0 + c1 * 128, HC * 128 + c0 * 128))
        WUS = wload_segs(k, srcu, 8, WU, usegs)
        srcd = din["w_down"][l][a0:a0 + HC * 128, :].rearrange("(hc p) n -> p hc n", p=128)
        WDS = wload_segs(k, srcd, HC, WD, [(0, 512, 0), (512, 1024, 512)])
        gtb = [k.sb("gtb%d" % v, [128, D], F32) for v in range(2)]
        for v in range(2 if l == 0 else 1):
            k.dma(gtb[v][:], bcast_row(g.modrow, v, l * 6144 + 5 * D, D), w=[gtb[v].b])
        ht = [k.sb("ht%d" % i, [128, 8, 512], BF16) for i in range(2)]
        actT = [k.sb("actT%d" % i, [128, HC, 512], BF16) for i in range(2)]
        t1 = [k.sb("t1%d" % i, [128, 512], F32) for i in range(2)]
        t2 = [k.sb("t2%d" % i, [128, 512], F32) for i in range(2)]
        ca = [k.sb("ca%d" % i, [128, 512], F32) for i in range(2)]
        cg = [k.sb("cg%d" % i, [128, 512], F32) for i in range(2)]
        sa = [k.sb("sa%d" % i, [128, 512], F32) for i in range(2)]
        xt = [k.sb("xt%d" % i, [128, D], F32) for i in range(8)]
        xo = [k.sb("xo%d" % i, [128, D], F32) for i in range(2)]
        tmp = [k.sb("tmp%d" % i, [128, 512], F32) for i in range(2)]
        pa = [k.ps("pa%d" % i) for i in range(2)]
        pg = [k.ps("pg%d" % i) for i in range(2)]
        po = [k.ps("po%d" % i) for i in range(3)]
        xsrc = g.XN if hf == 0 else g.XP
        tiles = p3_tiles(l)
        cnt = {"x": 0, "o": 0, "po": 0, "c": 0, "tmp": 0}

        def load(ti):
            s0, n, var = tiles[ti]
            h = ht[ti % 2]
            lo, hi = (0, S) if var == 0 else (S, S + CT)
            a, b_ = max(s0 - 1, lo), min(s0 + n + 1, hi)
            c0 = a - (s0 - 1)
            if c0 > 0:
                k.memset("pool", h[:, :, 0:c0], 0.0, w=[h.b])
            if b_ < s0 + n + 1:
                k.memset("pool", h[:, :, n + 1:n + 2], 0.0, w=[h.b])
            k.dma(h[:, :, c0:c0 + (b_ - a)], g.HT2[:, :, a:b_].rearrange("c p t -> p c t"), w=[h.b])

        def load_x(ti):
            s0, n, var = tiles[ti]
            nsub = (n + 127) // 128
            bufs = []
            for j in range(nsub):
                m = min(128, n - j * 128)
                r0 = s0 + j * 128
                x_ = xt[cnt["x"] % 8]
                cnt["x"] += 1
                k.dma(x_[0:m, :], xsrc[r0:r0 + m, :], w=[x_.b])
                bufs.append(x_)
            return bufs

        def up_chunk(ti, c):
            s0, n, var = tiles[ti]
            h = ht[ti % 2]
            aT = actT[ti % 2]
            cols = n + 2
            i = cnt["c"] % 2
            cnt["c"] += 1
            for (pp, wc0) in ((pa[i], c * 128), (pg[i], HC * 128 + c * 128)):
                wb = WUS.bufs(wc0, wc0 + 128)
                for kc in range(8):
                    k.mm(pp[:, 0:cols], WU[:, kc, wc0:wc0 + 128], h[:, kc, 0:cols], start=(kc == 0),
                         r=wb + [h.b], w=[pp.b])
            for (pp, dst, ci) in ((pa[i], ca[i], hf * HC + c), (pg[i], cg[i], 22 + hf * HC + c)):
                w3 = g.convf[:, l, ci, :]
                k.act(t1[i][:, 0:n], pp[:, 1:n + 1], AF.Copy, r=[g.convf.b], w=[pp.b, t1[i].b], scale=w3[:, 1:2])
                k.stt("dve", t2[i][:, 0:n], pp[:, 0:n], w3[:, 0:1], t1[i][:, 0:n], ALU.mult, ALU.add,
                      r=[g.convf.b, t1[i].b], w=[pp.b, t2[i].b])
                k.stt("dve", dst[:, 0:n], pp[:, 2:n + 2], w3[:, 2:3], t2[i][:, 0:n], ALU.mult, ALU.add,
                      r=[g.convf.b, t2[i].b], w=[pp.b, dst.b])
            k.act(sa[i][:, 0:n], ca[i][:, 0:n], AF.Silu, r=[ca[i].b], w=[sa[i].b])
            k.tt("pool", aT[:, c, 0:n], sa[i][:, 0:n], cg[i][:, 0:n], ALU.mult, r=[sa[i].b, cg[i].b], w=[aT.b])

        def down(ti, xbufs):
            s0, n, var = tiles[ti]
            aT = actT[ti % 2]
            nsub = (n + 127) // 128
            for j in range(nsub):
                m = min(128, n - j * 128)
                r0 = s0 + j * 128
                x_ = xbufs[j]
                o_ = xo[cnt["o"] % 2]
                cnt["o"] += 1
                for hh in range(2):
                    p_ = po[cnt["po"] % 3]
                    cnt["po"] += 1
                    wb = WDS.bufs(hh * 512, (hh + 1) * 512)
                    for hc in range(HC):
                        k.mm(p_[0:m, :], aT[:, hc, j * 128:j * 128 + m], WD[:, hc, hh * 512:(hh + 1) * 512],
                             start=(hc == 0), r=[aT.b] + wb, w=[p_.b])
                    t_ = tmp[cnt["tmp"] % 2]
                    cnt["tmp"] += 1
                    k.tt("dve", t_[0:m, :], p_[0:m, :], gtb[var][0:m, hh * 512:(hh + 1) * 512], ALU.mult,
                         r=[gtb[var].b], w=[p_.b, t_.b])
                    k.tt("pool", o_[0:m, hh * 512:(hh + 1) * 512], x_[0:m, hh * 512:(hh + 1) * 512], t_[0:m, :], ALU.add,
                         r=[x_.b, t_.b], w=[o_.b])
                if hf == 0:
                    dst = g.XP[r0:r0 + m, :]
                elif l == L - 1:
                    dst = g.out[r0:r0 + m, :]
                else:
                    dst = g.XM[r0:r0 + m, :]
                k.dma(dst, o_[0:m, :], r=[o_.b])

        NPRE = 2
        load(0)
        for c in range(HC):
            up_chunk(0, c)
        for ti in range(len(tiles)):
            if ti + 1 < len(tiles):
                load(ti + 1)
            xbufs = load_x(ti)
            if ti + 1 < len(tiles):
                for c in range(NPRE):
                    up_chunk(ti + 1, c)
            down(ti, xbufs)
            if ti + 1 < len(tiles):
                for c in range(NPRE, HC):
                    up_chunk(ti + 1, c)

WARM_N = 0
WARM_EVERY = 1
KC0 = (0, 8, 24, 32)


def na_rowcfgs(a):
    if a == 0:
        return [(0, 3), (1, 4)]
    if a == 15:
        return [(14, 5), (15, 6)]
    return [(a - 1, 0), (a, 1), (a + 1, 2)]


def p2_mixers(k, g, l):
    din = g.din
    N = 256
    with phase(k):
        WO = k.sb("wo", [128, 8, D], BF16)
        NAB = k.sb("nab", [128, 8, NTILE * 8], BF16)
        stg = [k.sb("stg%d" % i, [128, 8, 128], F32) for i in range(2)]
        def load_p2_weights():
            wload(k, stg, din["nab"][:, l, :].rearrange("p (a c) -> p a c", a=8), 8, NTILE * 8, [(0, NTILE * 8, NAB, 0)],
                  blk=128, func=AF.Exp)
            wload_cast(k, din["w_o"][l].rearrange("(kc p) n -> p kc n", p=128), 8, [(0, D, WO, 0)])
        nabf = NAB[:, :, :].rearrange("p a c -> p (a c)")
        nvar = 2 if l == 0 else 1
        gtb = [k.sb("gtb%d" % v, [128, D], F32) for v in range(nvar)]
        for v in range(nvar):
            k.dma(gtb[v][:], bcast_row(g.modrow, v, l * 6144 + 2 * D, D), w=[gtb[v].b])
        es = k.sb("es", [128, 6], F32)
        k.act(es[:], g.sink[:, l, :], AF.Exp, r=[g.sink.b], w=[es.b])
        KAc = k.sb("kac", [128, 3, CT], BF16)
        KBc = k.sb("kbc", [128, 2, CT], BF16)
        k.dma(KAc[:], g.QK[3:6, :, S:S + CT].rearrange("c p t -> p c t"), w=[KAc.b])
        k.dma(KBc[:], g.QK[9:11, :, S:S + CT].rearrange("c p t -> p c t"), w=[KBc.b])
        VAc = [k.sb("vac%d" % i, [128, 6, 128], BF16) for i in range(2)]
        VBc = [k.sb("vbc%d" % i, [128, 2, 128], BF16) for i in range(2)]
        for ct in range(2):
            k.dma(VAc[ct][:], g.VA[S + ct * 128:S + (ct + 1) * 128, :, :], w=[VAc[ct].b])
            k.dma(VBc[ct][:], g.VB[S + ct * 128:S + (ct + 1) * 128, :, :], w=[VBc[ct].b])
        KAn = [k.sb("kan%d" % i, [128, 3, 256], BF16) for i in range(2)]
        KAg = [[k.sb("kag%d_%d" % (i, j), [128, 3, 128], BF16) for j in range(4)] for i in range(4)]
        VAr = [[k.sb("var%d_%d" % (i, j), [128, 6, 128], BF16) for j in range(4)] for i in range(4)]
        KBr = [k.sb("kbr%d" % i, [128, 2, 128], BF16) for i in range(6)]
        VBr = [k.sb("vbr%d" % i, [128, 2, 128], BF16) for i in range(6)]

        def load_group(b):
            if b < 0 or b > 15:
                return
            sl = b % 4
            t0 = b * 256
            kn = KAn[b % 2]
            k.dma(kn[:], g.QK[3:6, :, t0:t0 + 256].rearrange("c p t -> p c t"), w=[kn.b])
            for j in range(4):
                for c in range(3):
                    k.cp("pool", KAg[sl][j][:, c, :].rearrange("p (r x) -> p r x", x=32),
                         kn[:, c, :].rearrange("p (r x) -> p r x", x=64)[:, :, KC0[j]:KC0[j] + 32],
                         r=[kn.b], w=[KAg[sl][j].b])
                for kr in range(4):
                    r0 = t0 + kr * 64 + KC0[j]
                    k.dma(VAr[sl][j][kr * 32:(kr + 1) * 32, :, :], g.VA[r0:r0 + 32, :, :], w=[VAr[sl][j].b])

        def load_kt(kt):
            if kt < 0 or kt > 31:
                return
            sl = kt % 6
            t0 = kt * 128
            k.dma(KBr[sl][:], g.QK[9:11, :, t0:t0 + 128].rearrange("c p t -> p c t"), w=[KBr[sl].b])
            k.dma(VBr[sl][:], g.VB[t0:t0 + 128, :, :], w=[VBr[sl].b])

        q = [k.sb("q%d" % i, [128, 6, N], BF16) for i in range(2)]
        cu = [k.sb("cu%d" % i, [128, 2, N + 2], F32) for i in range(2)]
        bg = [k.sb("bg%d" % i, [128, 2, N], F32) for i in range(2)]
        P = [k.sb("P%d" % i, [128, 512], BF16) for i in range(6)]
        YT = k.sb("YT", [128, 8, N], BF16)
        YTc = Buf("YTconv")
        rc = [k.sb("rc%d" % i, [128, N], F32) for i in range(2)]
        c1 = [k.sb("c1%d" % i, [128, N], F32) for i in range(2)]
        c2 = [k.sb("c2%d" % i, [128, N], F32) for i in range(2)]
        xt = [k.sb("xt%d" % i, [128, D], F32) for i in range(4)]
        xo = [k.sb("xo%d" % i, [128, D], F32) for i in range(2)]
        tmp = [k.sb("tmp%d" % i, [128, 512], F32) for i in range(2)]
        xn = k.sb("xn", [128, 2, D], BF16)
        junk = k.sb("junk", [128, D], BF16)
        ss = k.sb("ss", [128, 2], F32)
        rs = k.sb("rs", [128, 2], F32)
        h2o = k.sb("h2o", [128, 8, N], BF16)
        pS = [k.ps("pS%d" % i) for i in range(6)]
        pO = [k.ps("pO%d" % i) for i in range(2)]
        cnt = {"pS": 0, "pO": 0, "P": 0, "rc": 0, "x": 0, "o": 0, "po": 0, "tmp": 0, "c": 0}

        tiles = [(i * N, N, 0) for i in range(S // N)] + ([(S, CT, 1)] if l == 0 else [])

        def load_tile(ti):
            tok0, n, var = tiles[ti]
            b_ = ti % 2
            k.dma(q[b_][:, 0:3, 0:n], g.QK[0:3, :, tok0:tok0 + n].rearrange("c p t -> p c t"), w=[q[b_].b])
            k.dma(q[b_][:, 3:6, 0:n], g.QK[6:9, :, tok0:tok0 + n].rearrange("c p t -> p c t"), w=[q[b_].b])
            lo, hi = (0, S) if var == 0 else (S, S + CT)
            a, e = max(tok0 - 1, lo), min(tok0 + n + 1, hi)
            c0 = a - (tok0 - 1)
            if c0 > 0:
                k.memset("pool", cu[b_][:, :, 0:1], 0.0, w=[cu[b_].b])
            if e < tok0 + n + 1:
                k.memset("pool", cu[b_][:, :, n + 1:n + 2], 0.0, w=[cu[b_].b])
            k.dma(cu[b_][:, :, c0:c0 + (e - a)], g.CU[:, :, a:e].rearrange("c p t -> p c t"), w=[cu[b_].b])
            k.dma(bg[b_][:, :, 0:n], g.BG[:, :, tok0:tok0 + n].rearrange("c p t -> p c t"), w=[bg[b_].b])

        def next_ps(name, arr):
            p = arr[cnt[name] % len(arr)]
            cnt[name] += 1
            return p

        def ctx_part(qh, qbuf, Kc, kch, pb, Vc, vh, n, pOut):
            for ct in range(2):
                ps_ = next_ps("pS", pS)
                k.mm(ps_[:, 0:n], Kc[pb:pb + 64, kch, ct * 128:(ct + 1) * 128], qh, start=True,
                     r=[Kc.b, qbuf], w=[ps_.b])
                pp = next_ps("P", P)
                k.act(pp[:, 0:n], ps_[:, 0:n], AF.Exp, w=[ps_.b, pp.b])
                k.mm(pOut[:, 0:n], Vc[ct][:, vh, :], pp[:, 0:n], start=(ct == 0), r=[Vc[ct].b, pp.b], w=[pOut.b])

        wmask = {}

        def get_mask(pattern):
            if pattern not in wmask:
                t = k.sb("wm%d" % len(wmask), [128, len(pattern) * 128], BF16)
                for i, mk in enumerate(pattern):
                    k.cp("pool", t[:, i * 128:(i + 1) * 128], g.wgm[:, mk, :], r=[g.wgm.b], w=[t.b])
                wmask[pattern] = t
            return wmask[pattern]

        def blk(ap):
            return ap.rearrange("p (j r c) -> p j r c", j=4, r=4, c=16)

        def finalize(pOut, n, h_extra, dst, blocked):
            r_ = rc[cnt["rc"] % 2]
            cnt["rc"] += 1
            if h_extra is None:
                k.act(r_[64:128, 0:n], pOut[64:128, 0:n], AF.Ln, w=[pOut.b, r_.b])
            else:
                k.act(r_[64:128, 0:n], pOut[64:128, 0:n], AF.Ln, r=[es.b], w=[pOut.b, r_.b],
                      bias=es[64:128, h_extra:h_extra + 1])
            k.act(r_[64:128, 0:n], r_[64:128, 0:n], AF.Exp, r=[r_.b], w=[r_.b], scale=-1.0)
            if blocked:
                k.tt("dve", dst.rearrange("p (r j c) -> p j r c", r=4, j=4, c=16), blk(pOut[0:64, 0:n]),
                     blk(r_[64:128, 0:n]), ALU.mult, r=[r_.b], w=[pOut.b, YT.b])
            else:
                k.tt("dve", dst, pOut[0:64, 0:n], r_[64:128, 0:n], ALU.mult, r=[r_.b], w=[pOut.b, YT.b])

        class U:
            __slots__ = ("pre", "qk", "ex", "pv", "fin", "post")

            def __init__(self):
                self.pre = []
                self.qk = self.ex = self.pv = self.fin = None
                self.post = []

        def make_tile_units(ti):
            tok0, n, var = tiles[ti]
            b_ = ti % 2
            a = ti
            qq = q[b_]
            units = []

            def ctx_units(qh, Kc, kch, pb, Vc, vh, pO_cell, blocked=False):
                u = U()
                cell = {}

                def qk(cell=cell):
                    ps_ = next_ps("pS", pS)
                    cell["ps"] = ps_
                    for ct in range(2):
                        k.mm(ps_[:, ct * n:(ct + 1) * n], Kc[pb:pb + 64, kch, ct * 128:(ct + 1) * 128], qh, start=(ct == 0),
                             r=[Kc.b, qq.b], w=[ps_.b])

                def ex(cell=cell):
                    pp = next_ps("P", P)
                    cell["pp"] = pp
                    if blocked:
                        for ct in range(2):
                            k.act(pp[:, ct * n:(ct + 1) * n].rearrange("p (j r c) -> p r j c", j=4, r=4, c=16),
                                  cell["ps"][:, ct * n:(ct + 1) * n].rearrange("p (r j c) -> p r j c", r=4, j=4, c=16),
                                  AF.Exp, w=[cell["ps"].b, pp.b])
                    else:
                        k.act(pp[:, 0:2 * n], cell["ps"][:, 0:2 * n], AF.Exp, w=[cell["ps"].b, pp.b])

                def pv(cell=cell):
                    pO_cell["p"] = next_ps("pO", pO)
                    pOut = pO_cell["p"]
                    pp = cell["pp"]
                    for ct in range(2):
                        k.mm(pOut[:, 0:n], Vc[ct][:, vh, :], pp[:, ct * n:(ct + 1) * n], start=(ct == 0),
                             r=[Vc[ct].b, pp.b], w=[pOut.b])

                u.qk, u.ex, u.pv = qk, ex, pv
                units.append(u)

            for h in range(6):
                ch, pb = h // 2, 64 * (h % 2)
                qh = qq[pb:pb + 64, ch, 0:n]
                pO_cell = {}
                ctx_units(qh, KAc, ch, pb, VAc, h, pO_cell, blocked=(var == 0))
                if var == 0:
                    q3 = qq[pb:pb + 64, ch, :].rearrange("p (r c) -> p r c", c=64)
                    slots = []
                    for (b, rcfg) in na_rowcfgs(a):
                        for j in range(4):
                            slots.append((b, j, (h * 7 + rcfg) * 4 + j))
                    for s0 in range(0, len(slots), 8):
                        grp = slots[s0:s0 + 8]
                        u = U()
                        cell = {}

                        def qk(grp=grp, cell=cell, q3=q3, pb=pb, ch=ch):
                            ps_ = next_ps("pS", pS)
                            cell["ps"] = ps_
                            for i, (b, j, tix) in enumerate(grp):
                                kt_ = KAg[b % 4][j]
                                k.mm(ps_[:, i * 64:(i + 1) * 64], kt_[pb:pb + 64, ch, :], q3[:, :, 16 * j:16 * j + 16],
                                     start=(i == 0), r=[kt_.b, qq.b], w=[ps_.b])

                        def ex(grp=grp, cell=cell):
                            pp = next_ps("P", P)
                            cell["pp"] = pp
                            w_ = len(grp) * 64
                            k.act(pp[:, 0:w_], cell["ps"][:, 0:w_], AF.Exp, w=[cell["ps"].b, pp.b])
                            t0_ = grp[0][2] * 64
                            k.tt("dve", pp[:, 0:w_], pp[:, 0:w_], nabf[:, t0_:t0_ + w_], ALU.mult, r=[pp.b, NAB.b], w=[pp.b])

                        def pv(grp=grp, cell=cell, pO_cell=pO_cell, h=h):
                            pOut = pO_cell["p"]
                            pp = cell["pp"]
                            for i, (b, j, tix) in enumerate(grp):
                                vt_ = VAr[b % 4][j]
                                k.mm(pOut[:, j * 64:(j + 1) * 64], vt_[:, h, :], pp[:, i * 64:(i + 1) * 64],
                                     start=False, r=[vt_.b, pp.b], w=[pOut.b])

                        u.qk, u.ex, u.pv = qk, ex, pv
                        units.append(u)
                units[-1].fin = (lambda pO_cell=pO_cell, pb=pb, ch=ch: finalize(pO_cell["p"], n, None, YT[pb:pb + 64, ch, 0:n],
                                                                                    var == 0))
            for h in range(6):
                ch, pb, kv = 3 + h // 2, 64 * (h % 2), h // 3
                qh = qq[pb:pb + 64, ch, 0:n]
                pO_cell = {}
                ctx_units(qh, KBc, kv, pb, VBc, kv, pO_cell)
                if var == 0:
                    slots = []
                    for t in range(2):
                        i_ = 2 * a + t
                        for kt in (i_ - 1, i_, i_ + 1):
                            if 0 <= kt <= 31:
                                slots.append((t, kt, 0 if kt == i_ else (1 if kt < i_ else 2)))
                    for s0 in range(0, len(slots), 4):
                        grp = slots[s0:s0 + 4]
                        u = U()
                        cell = {}

                        def qk(grp=grp, cell=cell, pb=pb, ch=ch, kv=kv):
                            ps_ = next_ps("pS", pS)
                            cell["ps"] = ps_
                            for i, (t, kt, mk) in enumerate(grp):
                                kt_ = KBr[kt % 6]
                                k.mm(ps_[:, i * 128:(i + 1) * 128], kt_[pb:pb + 64, kv, :],
                                     qq[pb:pb + 64, ch, t * 128:(t + 1) * 128],
                                     start=(i == 0), r=[kt_.b, qq.b], w=[ps_.b])

                        def ex(grp=grp, cell=cell):
                            pp = next_ps("P", P)
                            cell["pp"] = pp
                            w_ = len(grp) * 128
                            k.act(pp[:, 0:w_], cell["ps"][:, 0:w_], AF.Exp, w=[cell["ps"].b, pp.b])
                            pat = tuple(mk for (_, _, mk) in grp)
                            if any(pat):
                                mt = get_mask(pat)
                                k.tt("dve", pp[:, 0:w_], pp[:, 0:w_], mt[:, 0:w_], ALU.mult, r=[pp.b, mt.b], w=[pp.b])

                        def pv(grp=grp, cell=cell, pO_cell=pO_cell, kv=kv):
                            pOut = pO_cell["p"]
                            pp = cell["pp"]
                            for i, (t, kt, mk) in enumerate(grp):
                                vt_ = VBr[kt % 6]
                                k.mm(pOut[:, t * 128:(t + 1) * 128], vt_[:, kv, :], pp[:, i * 128:(i + 1) * 128],
                                     start=False, r=[vt_.b, pp.b], w=[pOut.b])

                        u.qk, u.ex, u.pv = qk, ex, pv
                        units.append(u)
                units[-1].fin = (lambda pO_cell=pO_cell, pb=pb, ch=ch, h=h: finalize(pO_cell["p"], n, h, YT[pb:pb + 64, ch, 0:n],
                                                                                         False))

            xcell = {}

            def prefetch():
                if var == 0:
                    xsrc = (din["x"] if l == 0 else g.XM[:])[tok0:tok0 + n, :]
                else:
                    xsrc = din["ctx"]
                xcell["b"] = []
                for s in range(n // 128):
                    x_ = xt[cnt["x"] % 4]
                    cnt["x"] += 1
                    k.dma(x_[:], xsrc[s * 128:(s + 1) * 128, :], w=[x_.b])
                    xcell["b"].append(x_)
                if ti + 1 < len(tiles):
                    load_tile(ti + 1)
                if var == 0:
                    load_group(a + 2)
                    load_kt(2 * a + 3); load_kt(2 * a + 4)

            def conv():
                for c in range(2):
                    i = cnt["c"] % 2
                    cnt["c"] += 1
                    w3 = g.convc[:, l, c, :]
                    cuc = cu[b_]
                    k.act(c1[i][:, 0:n], cuc[:, c, 1:n + 1], AF.Copy, r=[cuc.b, g.convc.b], w=[c1[i].b], scale=w3[:, 1:2])
                    k.stt("dve", c2[i][:, 0:n], cuc[:, c, 0:n], w3[:, 0:1], c1[i][:, 0:n], ALU.mult, ALU.add,
                          r=[cuc.b, c1[i].b, g.convc.b], w=[c2[i].b])
                    k.stt("dve", c1[i][:, 0:n], cuc[:, c, 2:n + 2], w3[:, 2:3], c2[i][:, 0:n], ALU.mult, ALU.add,
                          r=[cuc.b, c2[i].b, g.convc.b], w=[c1[i].b])
                    k.tt("pool", YT[:, 6 + c, 0:n], c1[i][:, 0:n], bg[b_][:, c, 0:n], ALU.mult,
                         r=[c1[i].b, bg[b_].b], w=[YTc])

            def wo_epi():
                nsub = n // 128
                for s in range(nsub):
                    x_ = xcell["b"][s]
                    o_ = xo[cnt["o"] % 2]
                    cnt["o"] += 1
                    for hh in range(2):
                        p_ = next_ps("pS", pS)
                        for kc in range(8):
                            k.mm(p_[:, :], YT[:, kc, s * 128:(s + 1) * 128], WO[:, kc, hh * 512:(hh + 1) * 512],
                                 start=(kc == 0), r=[YT.b, YTc, WO.b], w=[p_.b])
                        t_ = tmp[cnt["tmp"] % 2]
                        cnt["tmp"] += 1
                        k.tt("dve", t_[:], p_[:, :], gtb[var][:, hh * 512:(hh + 1) * 512], ALU.mult,
                             r=[gtb[var].b], w=[p_.b, t_.b])
                        k.tt("pool", o_[:, hh * 512:(hh + 1) * 512], x_[:, hh * 512:(hh + 1) * 512], t_[:], ALU.add,
                             r=[x_.b, t_.b], w=[o_.b])
                    k.dma(g.XN[tok0 + s * 128:tok0 + (s + 1) * 128, :], o_[:], r=[o_.b])
                    norm_rows(k, o_[:], s, ss, junk, r=[o_.b])
                    k.rsqrt(rs[:, s:s + 1], ss[:, s:s + 1], g.epsb, r=[ss.b], w=[rs.b])
                    k.ts("dve", xn[:, s, :], o_[:], rs[:, s:s + 1], None, ALU.mult, r=[o_.b, rs.b], w=[xn.b])

            def transposes():
                nsub = n // 128
                for kc in range(8):
                    pt_ = next_ps("pS", pS)
                    ptb = pt_.t.bitcast(BF16)
                    for s in range(nsub):
                        k.tr(ptb[:, s * 128:(s + 1) * 128], xn[:, s, kc * 128:(kc + 1) * 128], g.identb[:],
                             r=[xn.b, g.identb.b], w=[pt_.b])
                    k.act(h2o[:, kc, 0:n], ptb[:, 0:n], AF.Identity, r=[g.AB.b], w=[pt_.b, h2o.b],
                          scale=abv(g, l, var, 2)[:, kc:kc + 1], bias=abv(g, l, var, 3)[:, kc:kc + 1])
                k.dma(g.HT2[:, :, tok0:tok0 + n].rearrange("c p t -> p c t"), h2o[:, :, 0:n], r=[h2o.b])

            def warm():
                pw = pT.t.bitcast(F32)
                for i in range(WARM_N):
                    k.mm(pw[:, 0:512], WO[:, i % 8, 0:128], WO[:, (i + 1) % 8, 0:512], start=True, r=[WO.b], w=[pT.b])

            if WARM_N and (ti % WARM_EVERY == 0):
                units[0].pre.append(warm)
            units[min(7, len(units) - 1)].pre.append(prefetch)
            units[min(6, len(units) - 1)].pre.append(conv)
            units[-1].post.append(wo_epi)
            return units, transposes

        load_tile(0)
        load_group(0)
        for kt in range(0, 3):
            load_kt(kt)
        load_p2_weights()
        load_group(1)
        allu = []
        pending_tr = None
        for ti in range(len(tiles)):
            us, trf = make_tile_units(ti)
            if pending_tr is not None:
                us[min(12, len(us) - 1)].pre.append(pending_tr)
            pending_tr = trf
            allu.extend(us)
        SK = 4
        FD = 1
        due = {}
        for i in range(len(allu) + SK + FD + 1):
            if i < len(allu):
                u = allu[i]
                for f in u.pre:
                    f()
                u.qk()
                u.ex()
            j = i - SK
            if 0 <= j < len(allu):
                u = allu[j]
                u.pv()
                fl = []
                if u.fin is not None:
                    fl.append(u.fin)
                fl.extend(u.post)
                if fl:
                    due.setdefault(i + FD, []).extend(fl)
            for f in due.pop(i, []):
                f()
        assert not due
        pending_tr()

def _shared_inputs(inp, consts):
    m = _core_inputs(0, inp, consts)
    for kx in ("x", "ctx", "cvt"):
        m.pop(kx)
    return m


def kernel(**inputs):
    consts = _consts()
    shared = _shared_inputs(inputs, consts)
    in_maps = [_core_inputs(b, inputs, consts, shared) for b in range(NCORES)]
    nc = build()
    res = run_bass_kernel_spmd(nc, in_maps, core_ids=list(range(NCORES)))
    return np.stack([np.asarray(r["out"], dtype=np.float32) for r in res.results], axis=0)
```

```python
import numpy as np
import ml_dtypes
import concourse.bass as bass
import concourse.mybir as mybir
from concourse.bass_utils import run_bass_kernel_spmd

F32 = mybir.dt.float32
BF16 = mybir.dt.bfloat16
AF = mybir.ActivationFunctionType
ALU = mybir.AluOpType

D = 1024
S = 4096
CT = 256
TT = S + CT
L = 2
DFF = 2816
INW = 2560
EPS = 1e-6
NEGM = -30000.0
NCORES = 8

ROWCFG = [(5, 4), (5, 5), (5, 6), (0, 0), (0, 1), (15, 14), (15, 15)]
NTILE = 6 * 7 * 4


class Buf:
    __slots__ = ("name",)

    def __init__(self, name):
        self.name = name


class Op:
    __slots__ = ("eng", "fn", "deps", "dma", "sig", "sem", "val", "inc")

    def __init__(self, eng, fn, dma):
        self.eng = eng
        self.fn = fn
        self.deps = []
        self.dma = dma
        self.sig = dma
        self.sem = None
        self.val = 0
        self.inc = 1


class Sched:
    NDMA = 12
    ENGS = ["pe", "act", "dve", "pool", "sp"]

    def __init__(self, nc, stack):
        self.nc = nc
        self.csem = {e: stack.enter_context(nc.semaphore("c_" + e)) for e in self.ENGS}
        self.dsem = {e: [stack.enter_context(nc.semaphore("d_%s_%d" % (e, i))) for i in range(self.NDMA)]
                     for e in ("sp", "pool")}
        self.cnt = {e: 0 for e in self.ENGS}
        self.dcnt = {e: 0 for e in self.dsem}
        self.dtot = {}
        self.seen = {e: {} for e in self.ENGS}
        self.nphase = 0
        self.reset()

    def reset(self):
        self.ops = []
        self.last_w = {}
        self.readers = {}
        self.dma_hist = {}

    def add(self, eng, fn, r=(), w=(), dma=False):
        op = Op(eng, fn, dma)
        deps = {}
        for b in r:
            lw = self.last_w.get(b)
            if lw is not None:
                deps[id(lw)] = (lw, 0)
        for b in w:
            lw = self.last_w.get(b)
            if lw is not None and id(lw) not in deps:
                deps[id(lw)] = (lw, 1)
            for rd in self.readers.get(b, ()):
                if id(rd) not in deps:
                    deps[id(rd)] = (rd, 1)
        for p, kind in deps.values():
            if (not p.dma) and (not dma) and p.eng == eng and kind == 1 and eng == "pe":
                continue
            op.deps.append(p)
            p.sig = True
        if dma:
            h = self.dma_hist.setdefault(eng, [])
            if len(h) >= self.NDMA:
                op.deps.append(h[len(h) - self.NDMA])
            h.append(op)
        for b in r:
            self.readers.setdefault(b, []).append(op)
        for b in w:
            self.last_w[b] = op
            self.readers[b] = []
        self.ops.append(op)
        return op

    def emit_phase(self):
        nc = self.nc
        per = {e: [o for o in self.ops if o.eng == e] for e in self.ENGS}
        bar = [(self.csem[e], self.cnt[e]) for e in self.ENGS if self.cnt[e] > 0]
        for e in self.dsem:
            for s in self.dsem[e]:
                if self.dtot.get(id(s), 0) > 0:
                    bar.append((s, self.dtot[id(s)]))
        for e in self.ENGS:
            comp = [o for o in per[e] if not o.dma]
            if comp:
                comp[-1].sig = True
        for op in self.ops:
            if op.dma:
                i = self.dcnt[op.eng]
                self.dcnt[op.eng] = i + 1
                sm = self.dsem[op.eng][i % self.NDMA]
                t = self.dtot.get(id(sm), 0) + 16
                self.dtot[id(sm)] = t
                op.sem, op.val, op.inc = sm, t, 16
            elif op.sig:
                self.cnt[op.eng] += 1
                op.sem, op.val, op.inc = self.csem[op.eng], self.cnt[op.eng], 1
        first = self.nphase == 0
        self.nphase += 1

        def run(e, eng):
            seen = self.seen[e]
            if not first:
                for sm, v in bar:
                    if seen.get(id(sm), 0) < v:
                        eng.wait_ge(sm, v)
                        seen[id(sm)] = v
            for op in per[e]:
                for p in op.deps:
                    k = id(p.sem)
                    if seen.get(k, 0) < p.val:
                        eng.wait_ge(p.sem, p.val)
                        seen[k] = p.val
                ins = op.fn(eng)
                if op.sig:
                    ins.then_inc(op.sem, op.inc)

        with nc.Block() as block:
            @block.tensor
            def _(eng):
                run("pe", eng)

            @block.scalar
            def _(eng):
                run("act", eng)

            @block.vector
            def _(eng):
                run("dve", eng)

            @block.gpsimd
            def _(eng):
                run("pool", eng)

            @block.sync
            def _(eng):
                run("sp", eng)
        self.reset()

    def emit_final(self):
        nc = self.nc
        bar = [(self.csem[e], self.cnt[e]) for e in self.ENGS if self.cnt[e] > 0]
        for e in self.dsem:
            for s in self.dsem[e]:
                if self.dtot.get(id(s), 0) > 0:
                    bar.append((s, self.dtot[id(s)]))
        with nc.Block() as block:
            @block.sync
            def _(eng):
                for sm, v in bar:
                    eng.wait_ge(sm, v)


class Tn:
    def __init__(self, t, name):
        self.t = t
        self.b = Buf(name)

    def __getitem__(self, k):
        return self.t[k]


class K:
    def __init__(self, nc, stack, dbg=None):
        self.nc = nc
        self.st = stack
        self.s = Sched(nc, stack)
        self.dbg = dbg or set()
        self.gst = stack

    def sb(self, name, shape, dt):
        self.nn = getattr(self, "nn", 0) + 1
        t = self.st.enter_context(self.nc.sbuf_tensor("s%d_%s" % (self.nn, name), list(shape), dt))
        return Tn(t, name)

    def ps(self, name, dt=F32, cols=512):
        self.nn = getattr(self, "nn", 0) + 1
        t = self.st.enter_context(self.nc.psum_tensor("p%d_%s" % (self.nn, name), [128, cols], dt))
        return Tn(t, name)

    def dram(self, name, shape, dt, kind="Internal"):
        if name in self.dbg:
            kind = "ExternalOutput"
        if name in getattr(self, "dbg_in", ()):
            kind = "ExternalInput"
        t = self.nc.dram_tensor(name, list(shape), dt, kind=kind)
        d = Tn(t.ap(), name)
        d.h = t
        return d

    def dma(self, out, in_, r=(), w=(), q="sp", **kw):
        return self.s.add(q, lambda e: e.dma_start(out=out, in_=in_, **kw), r=r, w=w, dma=True)

    def mm(self, out, lhsT, rhs, start, stop=True, r=(), w=()):
        return self.s.add(
            "pe",
            lambda e: e.matmul(out, lhsT, rhs, start=start, stop=stop, skip_group_check=True),
            r=r, w=w)

    def tr(self, out, in_, ident, r=(), w=()):
        return self.s.add("pe", lambda e: e.transpose(out, in_, ident), r=r, w=w)

    def act(self, out, in_, func, r=(), w=(), eng="act", **kw):
        return self.s.add(eng, lambda e: e.activation(out=out, in_=in_, func=func, **kw), r=r, w=w)

    def tt(self, eng, out, in0, in1, op, r=(), w=()):
        return self.s.add(eng, lambda e: e.tensor_tensor(out=out, in0=in0, in1=in1, op=op), r=r, w=w)

    def ts(self, eng, out, in0, s1, s2, op0, op1=None, r=(), w=()):
        if op1 is None:
            return self.s.add(eng, lambda e: e.tensor_scalar(out=out, in0=in0, scalar1=s1, scalar2=None, op0=op0), r=r, w=w)
        return self.s.add(eng, lambda e: e.tensor_scalar(out=out, in0=in0, scalar1=s1, scalar2=s2, op0=op0, op1=op1), r=r, w=w)

    def stt(self, eng, out, in0, scalar, in1, op0, op1, r=(), w=()):
        return self.s.add(eng, lambda e: e.scalar_tensor_tensor(out=out, in0=in0, scalar=scalar, in1=in1, op0=op0, op1=op1), r=r, w=w)

    def cp(self, eng, out, in_, r=(), w=()):
        if eng == "act":
            return self.s.add(eng, lambda e: e.copy(out=out, in_=in_), r=r, w=w)
        return self.s.add(eng, lambda e: e.tensor_copy(out=out, in_=in_), r=r, w=w)

    def recip(self, out, in_, r=(), w=()):
        return self.s.add("dve", lambda e: e.reciprocal(out=out, in_=in_), r=r, w=w)

    def rsqrt(self, out, in_, epsb, r=(), w=(), inw=()):
        self.act(out, in_, AF.Ln, r=list(r) + [epsb.b], w=list(inw) + list(w), bias=epsb[:, 0:1])
        return self.act(out, out, AF.Exp, r=list(w), w=list(w), scale=-0.5)

    def memset(self, eng, ap, val, w=()):
        return self.s.add(eng, lambda e: e.memset(ap, val), w=w)


def _na_index():
    kr_in = np.arange(128) // 32
    kc_in = np.arange(128) % 32
    r_in = np.arange(64) // 16
    c_in = np.arange(64) % 16
    drow = np.zeros((7, 128, 64), np.int64)
    rok = np.zeros((7, 128, 64), bool)
    for i, (a, b) in enumerate(ROWCFG):
        r = 4 * a + r_in[None, :]
        kr = 4 * b + kr_in[:, None]
        r0 = np.clip(r - 4, 0, 56)
        rok[i] = (kr >= r0) & (kr < r0 + 8)
        drow[i] = np.clip(kr - r + 7, 0, 14)
    dcol = np.zeros((4, 128, 64), np.int64)
    cok = np.zeros((4, 128, 64), bool)
    for i, j in enumerate((0, 1, 2, 3)):
        kc0 = int(np.clip(16 * j - 8, 0, 32))
        c = 16 * j + c_in[None, :]
        kc = kc0 + kc_in[:, None]
        c0 = np.clip(c - 8, 0, 48)
        cok[i] = (kc >= c0) & (kc < c0 + 16)
        dcol[i] = np.clip(kc - c + 15, 0, 30)
    return drow, rok, dcol, cok


def _consts():
    c = {}
    c["identb"] = np.eye(128, dtype=np.float32).astype(ml_dtypes.bfloat16)
    c["identf"] = np.eye(128, dtype=np.float32)
    bm = np.zeros((128, 128), np.float32)
    bm[:64, :64] = 1.0 / 64
    bm[64:, 64:] = 1.0 / 64
    c["bm"] = bm.astype(ml_dtypes.bfloat16)
    pm = np.zeros((128, 128), np.float32)
    for m in range(128):
        k = m + 32 if (m % 64) < 32 else m - 32
        pm[k, m] = 1.0
    c["pm"] = pm
    t = np.arange(S)
    row = (t // 64).astype(np.float32)
    col = (t % 64).astype(np.float32)
    inv = (np.float32(10000.0) ** (-np.arange(16, dtype=np.float32) / np.float32(16))).astype(np.float32)
    ang = np.concatenate([row[:, None] * inv, col[:, None] * inv], axis=-1).astype(np.float32)
    cos = np.cos(ang).astype(np.float32)
    sin = np.sin(ang).astype(np.float32)
    d = np.arange(128) % 64
    cosT = cos[:, d % 32].T
    sgn = np.where(d < 32, -1.0, 1.0).astype(np.float32)
    sinT = sin[:, d % 32].T * sgn[:, None]
    c["rope"] = np.ascontiguousarray(np.stack([cosT, sinT], axis=1)).astype(np.float32)
    ki = np.arange(128)[:, None]
    qi = np.arange(128)[None, :]
    mprev = np.where(qi <= ki, 1.0, 0.0)
    mnext = np.where(ki <= qi, 1.0, 0.0)
    c["wgm"] = np.stack([np.ones_like(mprev), mprev, mnext], axis=1).astype(np.float32).astype(ml_dtypes.bfloat16)
    return c


def _core_inputs(b, inp, consts, shared=None):
    f = lambda a: np.ascontiguousarray(np.asarray(a, dtype=np.float32))
    m = {}
    m["x"] = f(inp["x"][b])
    m["ctx"] = f(inp["ctx"][b])
    cvec = np.stack([np.asarray(inp["c"][b]), np.asarray(inp["c_ctx"])], 0)
    m["cvt"] = f(cvec.reshape(2, 8, 128).transpose(2, 1, 0))
    if shared is not None:
        m.update(shared)
        return m
    m["w_ada"] = f(inp["w_ada"])
    m["b_ada"] = f(inp["b_ada"])
    gt = lambda g: np.asarray(g).reshape(L, 8, 128).transpose(2, 0, 1)
    m["gT"] = f(np.stack([gt(inp["g_attn"]), gt(inp["g_ffn"])], axis=2))
    m["w_in"] = f(inp["w_in"])
    qkg = np.stack([np.asarray(inp[k]) for k in ("qn_a", "kn_a", "qn_b", "kn_b")], axis=-1)
    m["qkg"] = f(np.concatenate([qkg, qkg], axis=1).transpose(1, 0, 2))
    drow, rok, dcol, cok = _na_index()
    rpb = np.asarray(inp["rpb_a"], dtype=np.float32)
    g = rpb[:, :, drow[:, None], dcol[None, :]]
    ok = (rok[:, None] & cok[None, :])[None, None]
    nab = np.where(ok, g, np.float32(NEGM)).astype(np.float32)
    m["nab"] = f(nab.transpose(4, 0, 1, 2, 3, 5).reshape(128, L, NTILE * 64))
    m["sink"] = f(np.broadcast_to(np.asarray(inp["sink_b"])[None], (128, L, 6)))
    m["convc"] = f(np.asarray(inp["conv_c"]).reshape(L, 3, 2, 128).transpose(3, 0, 2, 1))
    m["w_o"] = f(inp["w_o"])
    m["w_up"] = f(inp["w_up"])
    m["convf"] = f(np.asarray(inp["conv_ffn"]).reshape(L, 3, 44, 128).transpose(3, 0, 2, 1))
    m["w_down"] = f(inp["w_down"])
    m.update(consts)
    return m


from contextlib import ExitStack, contextmanager


@contextmanager
def phase(k):
    old = k.st
    with ExitStack() as st:
        k.st = st
        yield
        k.s.emit_phase()
    k.st = old


def fence(k, eng, r, w):
    d = k.dummy
    return k.s.add(eng, lambda e: e.memset(d[0:1, 0:1], 0.0), r=r, w=list(w) + [d.b])


def wload_cast(k, src, nkc, segs, nsplit=4):
    subs = {}
    step = (nkc + nsplit - 1) // nsplit
    for (s0, s1, dst, d0) in segs:
        for k0 in range(0, nkc, step):
            k1 = min(nkc, k0 + step)
            sb_ = Buf("sub")
            subs.setdefault(id(dst), (dst, []))[1].append(sb_)
            k.dma(dst[:, k0:k1, d0:d0 + (s1 - s0)], src[:, k0:k1, s0:s1], w=[sb_], q="pool")
    for dst, bl in subs.values():
        fence(k, "pool", r=bl, w=[dst.b])


class WSegs:
    def __init__(self):
        self.rng = []

    def bufs(self, c0, c1):
        return [b for (d0, d1, b) in self.rng if d0 < c1 and c0 < d1]


def wload_segs(k, src, nkc, dst, segs):
    ws = WSegs()
    for (s0, s1, d0) in segs:
        b = Buf("wseg")
        k.dma(dst[:, 0:nkc, d0:d0 + (s1 - s0)], src[:, :, s0:s1], w=[b], q="pool")
        ws.rng.append((d0, d0 + (s1 - s0), b))
    return ws


def wload(k, stg, src, nkc, ncols, segs, engs=("dve", "pool", "act"), blk=256, func=None):
    subs = {}
    ci = 0
    for bi, c0 in enumerate(range(0, ncols, blk)):
        c1 = min(ncols, c0 + blk)
        sg = stg[bi % len(stg)]
        k.dma(sg[:, 0:nkc, 0:c1 - c0], src[:, :, c0:c1], w=[sg.b])
        for (s0, s1, dst, d0) in segs:
            lo, hi = max(s0, c0), min(s1, c1)
            if lo >= hi:
                continue
            sb_ = Buf("sub")
            subs.setdefault(id(dst), (dst, []))[1].append(sb_)
            if func is None:
                k.cp(engs[ci % len(engs)], dst[:, 0:nkc, d0 + lo - s0:d0 + hi - s0], sg[:, 0:nkc, lo - c0:hi - c0],
                     r=[sg.b], w=[sb_])
            else:
                k.act(dst[:, 0:nkc, d0 + lo - s0:d0 + hi - s0], sg[:, 0:nkc, lo - c0:hi - c0], func, r=[sg.b], w=[sb_])
            ci += 1
    for dst, bl in subs.values():
        fence(k, "pool", r=bl, w=[dst.b])


class G:
    pass


def build(nlayers=L, dbg=(), stop_after=None, dbg_in=(), only=None):
    nc = bass.Bass("TRN2", target_bir_lowering=False)
    gst = ExitStack()
    with gst:
        k = K(nc, gst, set(dbg))
        k.dbg_in = set(dbg_in)
        g = G()
        din = {}

        def inp(name, shape, dt=F32):
            din[name] = nc.dram_tensor(name, list(shape), dt, kind="ExternalInput").ap()

        inp("x", [S, D]); inp("ctx", [CT, D]); inp("cvt", [128, 8, 2])
        inp("w_ada", [L, D, 6 * D]); inp("b_ada", [L, 6 * D]); inp("gT", [128, L, 2, 8])
        inp("w_in", [L, D, INW]); inp("qkg", [128, L, 4]); inp("nab", [128, L, NTILE * 64])
        inp("sink", [128, L, 6]); inp("convc", [128, L, 2, 3]); inp("w_o", [L, D, D])
        inp("w_up", [L, D, 2 * DFF]); inp("convf", [128, L, 44, 3]); inp("w_down", [L, DFF, D])
        inp("identb", [128, 128], BF16); inp("identf", [128, 128]); inp("bm", [128, 128], BF16)
        inp("pm", [128, 128]); inp("rope", [128, 2, S]); inp("wgm", [128, 3, 128], BF16)
        g.din = din
        g.out = nc.dram_tensor("out", [S, D], F32, kind="ExternalOutput").ap()
        g.modrow = k.dram("modrow", [2, L * 6 * D], F32)
        g.QK = k.dram("QK", [11, 128, TT], BF16)
        g.VA = k.dram("VA", [TT, 6, 128], BF16)
        g.VB = k.dram("VB", [TT, 2, 128], BF16)
        g.CU = k.dram("CU", [2, 128, TT], F32)
        g.BG = k.dram("BG", [2, 128, TT], F32)
        g.XN = k.dram("XN", [TT, D], F32)
        g.XP = k.dram("XP", [TT, D], F32)
        g.XM = k.dram("XM", [TT, D], F32)
        g.HT2 = k.dram("HT2", [8, 128, TT], BF16)
        g.identb = k.sb("identb", [128, 128], BF16)
        g.identf = k.sb("identf", [128, 128], F32)
        g.bm = k.sb("bm", [128, 128], BF16)
        g.pm = k.sb("pm", [128, 128], F32)
        g.wgm = k.sb("wgm", [128, 3, 128], BF16)
        g.modT = k.sb("modT", [128, L, 96], F32)
        g.AB = k.sb("AB", [128, L * 2 * 4, 8], F32)
        g.gT = k.sb("gT", [128, L, 2, 8], F32)
        g.qkg = k.sb("qkg", [128, L, 4], F32)
        g.sink = k.sb("sink", [128, L, 6], F32)
        g.convc = k.sb("convc", [128, L, 2, 3], F32)
        g.convf = k.sb("convf", [128, L, 44, 3], F32)
        k.dummy = k.sb("dummy", [128, 4], F32)
        g.epsb = k.sb("epsb", [128, 1], F32)

        p0_mods(k, g)
        if stop_after == "p0":
            k.s.emit_final()
            return nc
        for l in range(nlayers):
            if only is None or "p1" in only:
                p1_inproj(k, g, l)
            if stop_after == "p1":
                break
            if only is None or "p2" in only:
                p2_mixers(k, g, l)
            if stop_after == "p2":
                break
            if only is None or "p3" in only:
                p3_ffn(k, g, l, 0)
                p3_ffn(k, g, l, 1)
        k.s.emit_final()
    return nc


def abv(g, l, var, which):
    return g.AB[:, (l * 2 + var) * 4 + which, :]


def p0_mods(k, g):
    din = g.din
    with phase(k):
        for nm in ("identb", "identf", "bm", "pm", "wgm", "gT", "qkg", "sink", "convc", "convf"):
            t = getattr(g, nm)
            k.dma(t[:], din[nm], w=[t.b])
        k.memset("dve", g.epsb[:], EPS, w=[g.epsb.b])
        for gi in (0, 2):
            k.ts("dve", g.qkg[:, :, gi:gi + 1], g.qkg[:, :, gi:gi + 1], 0.125, None, ALU.mult, r=[g.qkg.b], w=[g.qkg.b])
        cvt = k.sb("cvt", [128, 8, 2], F32)
        sct = k.sb("sct", [128, 8, 2], F32)
        k.dma(cvt[:], din["cvt"], w=[cvt.b])
        k.act(sct[:], cvt[:], AF.Silu, r=[cvt.b], w=[sct.b])
        NB0 = 6
        wst = [k.sb("wst%d" % i, [128, 8, 512], F32) for i in range(NB0)]
        bad = [k.sb("bad%d" % i, [2, 512], F32) for i in range(NB0)]
        mrow = [k.sb("mrow%d" % i, [2, 512], F32) for i in range(NB0)]
        pmm = [k.ps("p0m%d" % i) for i in range(NB0)]
        pT = k.ps("p0T")
        chunks = [(l_, n_) for l_ in range(L) for n_ in range(12)]

        def p0_load(ci):
            l_, n_ = chunks[ci]
            i = ci % NB0
            wv = din["w_ada"][l_].rearrange("(kc p) n -> p kc n", p=128)
            k.dma(wst[i][:], wv[:, :, n_ * 512:(n_ + 1) * 512], w=[wst[i].b])
            for r_ in range(2):
                k.dma(bad[i][r_:r_ + 1, :], din["b_ada"][l_:l_ + 1, n_ * 512:(n_ + 1) * 512], w=[bad[i].b])

        PF = NB0 - 2
        for ci in range(PF):
            p0_load(ci)
        for l in range(L):
            for n in range(12):
                ci = l * 12 + n
                i = ci % NB0
                if ci + PF < len(chunks):
                    p0_load(ci + PF)
                for kc in range(8):
                    k.mm(pmm[i][0:2, :], sct[:, kc, :], wst[i][:, kc, :], start=(kc == 0),
                         r=[sct.b, wst[i].b], w=[pmm[i].b])
                k.tt("dve", mrow[i][:], pmm[i][0:2, :], bad[i][:], ALU.add, r=[bad[i].b], w=[pmm[i].b, mrow[i].b])
                k.dma(g.modrow[:, l * 6144 + n * 512:l * 6144 + (n + 1) * 512], mrow[i][:], r=[mrow[i].b])
                for j in range(4):
                    idx = n * 4 + j
                    k.mm(pT[:, idx * 2:idx * 2 + 2], mrow[i][0:2, j * 128:(j + 1) * 128], g.identf[0:2, 0:2],
                         start=(idx == 0), r=[mrow[i].b, g.identf.b], w=[pT.b])
            k.cp("dve", g.modT[:, l, :], pT[:, 0:96], w=[pT.b, g.modT.b])
            mv = g.modT[:, l, :].rearrange("p (c v) -> p c v", v=2)
            for var in range(2):
                k.stt("dve", abv(g, l, var, 0), mv[:, 8:16, var], 1.0, g.gT[:, l, 0, :], ALU.add, ALU.mult,
                      r=[g.modT.b, g.gT.b], w=[g.AB.b])
                k.cp("dve", abv(g, l, var, 1), mv[:, 0:8, var], r=[g.modT.b], w=[g.AB.b])
                k.stt("dve", abv(g, l, var, 2), mv[:, 32:40, var], 1.0, g.gT[:, l, 1, :], ALU.add, ALU.mult,
                      r=[g.modT.b, g.gT.b], w=[g.AB.b])
                k.cp("dve", abv(g, l, var, 3), mv[:, 24:32, var], r=[g.modT.b], w=[g.AB.b])


def norm_rows(k, xt, s, ss, junk, r=()):
    k.act(junk[:], xt, AF.Square, r=list(r), w=[junk.b, ss.b], accum_out=ss[:, s:s + 1], scale=1.0 / 32.0)


W1_QA, W1_KA, W1_QB, W1_KBD, W1_U, W1_CG, W1_BG, W1_V = 0, 384, 768, 1152, 1408, 1664, 1920, 2176
W1_N = 2688


def p1_inproj(k, g, l):
    din = g.din
    with phase(k):
        W = k.sb("w1", [128, 8, W1_N], BF16)
        src = din["w_in"][l].rearrange("(kc p) n -> p kc n", p=128)
        segs = [(0, 128, W1_QA), (128, 384, W1_QA + 128), (384, 768, W1_KA), (1152, 1536, W1_QB),
                (1536, 1600, W1_KBD), (1536, 1600, W1_KBD + 64),
                (1600, 1664, W1_KBD + 128), (1600, 1664, W1_KBD + 192),
                (1792, 2048, W1_U), (2304, 2560, W1_CG), (2048, 2304, W1_BG),
                (768, 1152, W1_V), (1664, 1792, W1_V + 384)]
        WS = wload_segs(k, src, 8, W, segs)

        xt = [k.sb("xt%d" % i, [128, 4, D], F32) for i in range(2)]
        cs = [k.sb("cs%d" % i, [128, 2, 512], F32) for i in range(2)]
        xn = [k.sb("xn%d" % i, [128, 4, D], BF16) for i in range(2)]
        junk = k.sb("junk", [128, D], BF16)
        ss = [k.sb("ss%d" % i, [128, 4], F32) for i in range(2)]
        rs = [k.sb("rs%d" % i, [128, 4], F32) for i in range(2)]
        hTs = [k.sb("hT%d" % i, [128, 8, 512], BF16) for i in range(2)]
        sq = [k.sb("sq%d" % i, [128, 512], BF16) for i in range(3)]
        rstd = [k.sb("rstd%d" % i, [128, 512], F32) for i in range(3)]
        qn = [k.sb("qn%d" % i, [128, 512], F32) for i in range(3)]
        t1 = [k.sb("t1%d" % i, [128, 512], F32) for i in range(3)]
        t2 = [k.sb("t2%d" % i, [128, 512], F32) for i in range(3)]
        ob = [k.sb("ob%d" % i, [128, 512], BF16) for i in range(3)]
        usb = [k.sb("usb%d" % i, [128, 512], F32) for i in range(2)]
        of = [k.sb("of%d" % i, [128, 512], F32) for i in range(3)]
        vt = [k.sb("vt%d" % i, [128, 8, 128], BF16) for i in range(2)]
        for v_ in vt:
            k.memset("pool", v_[:, :, 64:128], 1.0, w=[v_.b])
        pT = [k.ps("pT%d" % i, BF16, 1024) for i in range(2)]
        pq = [k.ps("pq%d" % i) for i in range(4)]
        pmn = k.ps("pmn")
        pr = k.ps("pr")

        tiles = [(i * 512, 512, 0) for i in range(8)] + [(S, CT, 1)]
        cnt = {"ob": 0, "of": 0, "pq": 0, "a": 0, "vt": 0, "pT": 0}

        def load(ti):
            tok0, n, var = tiles[ti]
            nsub = n // 128
            b = ti % 2
            if var == 0:
                srcx = (din["x"] if l == 0 else g.XM[:])[tok0:tok0 + n, :]
                rd = [] if l == 0 else [g.XM.b]
            else:
                srcx = din["ctx"] if l == 0 else g.XM[S:S + CT, :]
                rd = [] if l == 0 else [g.XM.b]
            k.dma(xt[b][:, 0:nsub, :], srcx.rearrange("(s p) f -> p s f", p=128), w=[xt[b].b])

        def load_cs(ti):
            tok0, n, var = tiles[ti]
            b = ti % 2
            if var == 0:
                k.dma(cs[b][:, :, 0:n], din["rope"][:, :, tok0:tok0 + n], w=[cs[b].b])

        def norm(ti):
            tok0, n, var = tiles[ti]
            nsub = n // 128
            b = ti % 2
            xn_ = xn[b]
            for s in range(nsub):
                norm_rows(k, xt[b][:, s, :], s, ss[b], junk, r=[xt[b].b])
            k.rsqrt(rs[b][:, 0:nsub], ss[b][:, 0:nsub], g.epsb, r=[ss[b].b], w=[rs[b].b])
            for s in range(nsub):
                if s % 2 == 0:
                    k.ts("dve", xn_[:, s, :], xt[b][:, s, :], rs[b][:, s:s + 1], None, ALU.mult,
                         r=[xt[b].b, rs[b].b], w=[xn_.b])
                else:
                    k.act(xn_[:, s, :], xt[b][:, s, :], AF.Copy, r=[xt[b].b, rs[b].b], w=[xn_.b], scale=rs[b][:, s:s + 1])

        def trans(ti):
            tok0, n, var = tiles[ti]
            nsub = n // 128
            xn_ = xn[ti % 2]
            hT_ = hTs[ti % 2]
            for kc in range(8):
                p = pT[cnt["pT"] % 2]
                cnt["pT"] += 1
                for s in range(nsub):
                    k.tr(p[:, s * 128:(s + 1) * 128], xn_[:, s, kc * 128:(kc + 1) * 128], g.identb[:],
                         r=[xn_.b, g.identb.b], w=[p.b])
                k.act(hT_[:, kc, 0:n], p[:, 0:n], AF.Identity, r=[g.AB.b], w=[p.b, hT_.b],
                      scale=abv(g, l, var, 0)[:, kc:kc + 1], bias=abv(g, l, var, 1)[:, kc:kc + 1])

        load(0)
        load_cs(0)
        load(1)
        norm(0)
        trans(0)
        chunks = []

        def add_chunk(A, B=None, C=None, pre=None):
            chunks.append((A, B, C, pre))

        def make_tile(ti):
            tok0, n, var = tiles[ti]
            nsub = n // 128
            b = ti % 2
            hT = hTs[b]

            def proj(wc0):
                p = pq[cnt["pq"] % 4]
                cnt["pq"] += 1
                wb = WS.bufs(wc0, wc0 + 128)
                for kc in range(8):
                    k.mm(p[:, 0:n], W[:, kc, wc0:wc0 + 128], hT[:, kc, 0:n], start=(kc == 0), r=wb + [hT.b], w=[p.b])
                return p

            def qk_chunk(wc0, gi, rope, qkidx, pre=None):
                cell = {}

                def A():
                    p = proj(wc0)
                    a = cnt["a"] % 3
                    cnt["a"] += 1
                    cell["p"], cell["a"] = p, a
                    k.act(sq[a][:, 0:n], p[:, 0:n], AF.Square, w=[p.b, sq[a].b])

                def B():
                    p, a = cell["p"], cell["a"]
                    k.mm(pmn[:, 0:n], g.bm[:], sq[a][:, 0:n], start=True, r=[g.bm.b, sq[a].b], w=[pmn.b])
                    k.rsqrt(rstd[a][:, 0:n], pmn[:, 0:n], g.epsb, w=[rstd[a].b], inw=[pmn.b])
                    if not rope:
                        o = ob[cnt["ob"] % 3]
                        cnt["ob"] += 1
                        k.stt("dve", o[:, 0:n], p[:, 0:n], g.qkg[:, l, gi:gi + 1], rstd[a][:, 0:n], ALU.mult, ALU.mult,
                              r=[rstd[a].b, g.qkg.b], w=[p.b, o.b])
                        k.dma(g.QK[qkidx, :, tok0:tok0 + n], o[:, 0:n], r=[o.b])
                    else:
                        k.stt("dve", qn[a][:, 0:n], p[:, 0:n], g.qkg[:, l, gi:gi + 1], rstd[a][:, 0:n], ALU.mult, ALU.mult,
                              r=[rstd[a].b, g.qkg.b], w=[p.b, qn[a].b])

                def C():
                    a = cell["a"]
                    o = ob[cnt["ob"] % 3]
                    cnt["ob"] += 1
                    k.mm(pr[:, 0:n], g.pm[:], qn[a][:, 0:n], start=True, r=[g.pm.b, qn[a].b], w=[pr.b])
                    k.tt("pool", t1[a][:, 0:n], qn[a][:, 0:n], cs[b][:, 0, 0:n], ALU.mult, r=[qn[a].b, cs[b].b], w=[t1[a].b])
                    k.tt("dve", t2[a][:, 0:n], pr[:, 0:n], cs[b][:, 1, 0:n], ALU.mult, r=[cs[b].b], w=[pr.b, t2[a].b])
                    k.tt("pool", o[:, 0:n], t1[a][:, 0:n], t2[a][:, 0:n], ALU.add, r=[t1[a].b, t2[a].b], w=[o.b])
                    k.dma(g.QK[qkidx, :, tok0:tok0 + n], o[:, 0:n], r=[o.b])

                add_chunk(A, B, C if rope else None, pre)

            def tile_pre():
                if ti + 1 < len(tiles):
                    norm(ti + 1)
                if ti + 2 < len(tiles):
                    load(ti + 2)

            def mid_pre():
                if ti + 1 < len(tiles):
                    trans(ti + 1)
                    load_cs(ti + 1)

            for c in range(3):
                qk_chunk(W1_QA + c * 128, 0, False, c, pre=tile_pre if c == 0 else None)
            for c in range(3):
                qk_chunk(W1_KA + c * 128, 1, False, 3 + c)
            for c in range(3):
                qk_chunk(W1_QB + c * 128, 2, var == 0, 6 + c, pre=mid_pre if c == 0 else None)
            for c in range(2):
                qk_chunk(W1_KBD + c * 128, 3, var == 0, 9 + c)
            for c in range(2):
                ucell = {}

                def A_u(c=c, ucell=ucell):
                    pu = proj(W1_U + c * 128)
                    a = cnt["u"] % 2
                    cnt["u"] += 1
                    ucell["a"] = a
                    k.cp("act", usb[a][:, 0:n], pu[:, 0:n], w=[pu.b, usb[a].b])

                def A_cg(c=c, ucell=ucell):
                    a = ucell["a"]
                    pc = proj(W1_CG + c * 128)
                    o = of[cnt["of"] % 3]
                    cnt["of"] += 1
                    k.tt("dve", o[:, 0:n], pc[:, 0:n], usb[a][:, 0:n], ALU.mult, r=[usb[a].b], w=[pc.b, o.b])
                    k.dma(g.CU[c, :, tok0:tok0 + n], o[:, 0:n], r=[o.b])

                def A_bg(c=c):
                    pb = proj(W1_BG + c * 128)
                    o = of[cnt["of"] % 3]
                    cnt["of"] += 1
                    k.cp("act", o[:, 0:n], pb[:, 0:n], w=[pb.b, o.b])
                    k.dma(g.BG[c, :, tok0:tok0 + n], o[:, 0:n], r=[o.b])

                add_chunk(A_u)
                add_chunk(A_cg)
                add_chunk(A_bg)
            for s in range(nsub):
                def A_v(s=s):
                    pv_ = pq[cnt["pq"] % 4]
                    cnt["pq"] += 1
                    wb = WS.bufs(W1_V, W1_V + 512)
                    for kc in range(8):
                        k.mm(pv_[:, :], hT[:, kc, s * 128:(s + 1) * 128], W[:, kc, W1_V:W1_V + 512], start=(kc == 0),
                             r=wb + [hT.b], w=[pv_.b])
                    v = vt[cnt["vt"] % 2]
                    cnt["vt"] += 1
                    k.cp("act" if s % 2 == 0 else "dve", v[:, :, 0:64], pv_[:, :].rearrange("p (h d) -> p h d", d=64),
                         w=[pv_.b, v.b])
                    k.dma(g.VA[tok0 + s * 128:tok0 + (s + 1) * 128, :, :], v[:, 0:6, :], r=[v.b])
                    k.dma(g.VB[tok0 + s * 128:tok0 + (s + 1) * 128, :, :], v[:, 6:8, :], r=[v.b])

                add_chunk(A_v)

        cnt["u"] = 0
        for ti in range(len(tiles)):
            make_tile(ti)
        nch = len(chunks)
        for i in range(nch + 2):
            if i < nch:
                A, B, C, pre = chunks[i]
                if pre is not None:
                    pre()
                A()
            if 0 <= i - 1 < nch and chunks[i - 1][1] is not None:
                chunks[i - 1][1]()
            if 0 <= i - 2 < nch and chunks[i - 2][2] is not None:
                chunks[i - 2][2]()

def bcast_row(dt_, row, off, n):
    ncols = dt_.t.shape[1]
    return bass.AP(dt_.h, row * ncols + off, [[0, 128], [1, n]])


def p3_tiles(l):
    t = []
    s0 = 0
    while s0 < S:
        n = min(510, S - s0)
        t.append((s0, n, 0))
        s0 += n
    if l == 0:
        t.append((S, CT, 1))
    return t


def p3_ffn(k, g, l, hf):
    din = g.din
    HC = 11
    with phase(k):
        WU = k.sb("wu", [128, 8, 2 * HC * 128], BF16)
        WD = k.sb("wd", [128, HC, D], BF16)
        srcu = din["w_up"][l].rearrange("(kc p) n -> p kc n", p=128)
        a0 = hf * HC * 128
        CG = [(0, 1), (1, 2), (2, 4), (4, 7), (7, HC)]
        usegs = []
        for (c0, c1) in CG:
            usegs.append((a0 + c0 * 128, a0 + c1 * 128, c0 * 128))
            usegs.append((DFF + a0 + c0 * 128, DFF + a0 + c1 * 128, HC * 128 + c0 * 128))
        WUS = wload_segs(k, srcu, 8, WU, usegs)
        srcd = din["w_down"][l][a0:a0 + HC * 128, :].rearrange("(hc p) n -> p hc n", p=128)
        WDS = wload_segs(k, srcd, HC, WD, [(0, 512, 0), (512, 1024, 512)])
        gtb = [k.sb("gtb%d" % v, [128, D], F32) for v in range(2)]
        for v in range(2 if l == 0 else 1):
            k.dma(gtb[v][:], bcast_row(g.modrow, v, l * 6144 + 5 * D, D), w=[gtb[v].b])
        ht = [k.sb("ht%d" % i, [128, 8, 512], BF16) for i in range(2)]
        actT = [k.sb("actT%d" % i, [128, HC, 512], BF16) for i in range(2)]
        t1 = [k.sb("t1%d" % i, [128, 512], F32) for i in range(2)]
        t2 = [k.sb("t2%d" % i, [128, 512], F32) for i in range(2)]
        ca = [k.sb("ca%d" % i, [128, 512], F32) for i in range(2)]
        cg = [k.sb("cg%d" % i, [128, 512], F32) for i in range(2)]
        sa = [k.sb("sa%d" % i, [128, 512], F32) for i in range(2)]
        xt = [k.sb("xt%d" % i, [128, D], F32) for i in range(8)]
        xo = [k.sb("xo%d" % i, [128, D], F32) for i in range(2)]
        tmp = [k.sb("tmp%d" % i, [128, 512], F32) for i in range(2)]
        pa = [k.ps("pa%d" % i) for i in range(2)]
        pg = [k.ps("pg%d" % i) for i in range(2)]
        po = [k.ps("po%d" % i) for i in range(3)]
        xsrc = g.XN if hf == 0 else g.XP
        tiles = p3_tiles(l)
        cnt = {"x": 0, "o": 0, "po": 0, "c": 0, "tmp": 0}

        def load(ti):
            s0, n, var = tiles[ti]
            h = ht[ti % 2]
            lo, hi = (0, S) if var == 0 else (S, S + CT)
            a, b_ = max(s0 - 1, lo), min(s0 + n + 1, hi)
            c0 = a - (s0 - 1)
            if c0 > 0:
                k.memset("pool", h[:, :, 0:c0], 0.0, w=[h.b])
            if b_ < s0 + n + 1:
                k.memset("pool", h[:, :, n + 1:n + 2], 0.0, w=[h.b])
            k.dma(h[:, :, c0:c0 + (b_ - a)], g.HT2[:, :, a:b_].rearrange("c p t -> p c t"), w=[h.b])

        def load_x(ti):
            s0, n, var = tiles[ti]
            nsub = (n + 127) // 128
            bufs = []
            for j in range(nsub):
                m = min(128, n - j * 128)
                r0 = s0 + j * 128
                x_ = xt[cnt["x"] % 8]
                cnt["x"] += 1
                k.dma(x_[0:m, :], xsrc[r0:r0 + m, :], w=[x_.b])
                bufs.append(x_)
            return bufs

        def up_chunk(ti, c):
            s0, n, var = tiles[ti]
            h = ht[ti % 2]
            aT = actT[ti % 2]
            cols = n + 2
            i = cnt["c"] % 2
            cnt["c"] += 1
            for (pp, wc0) in ((pa[i], c * 128), (pg[i], HC * 128 + c * 128)):
                wb = WUS.bufs(wc0, wc0 + 128)
                for kc in range(8):
                    k.mm(pp[:, 0:cols], WU[:, kc, wc0:wc0 + 128], h[:, kc, 0:cols], start=(kc == 0),
                         r=wb + [h.b], w=[pp.b])
            for (pp, dst, ci) in ((pa[i], ca[i], hf * HC + c), (pg[i], cg[i], 22 + hf * HC + c)):
                w3 = g.convf[:, l, ci, :]
                k.act(t1[i][:, 0:n], pp[:, 1:n + 1], AF.Copy, r=[g.convf.b], w=[pp.b, t1[i].b], scale=w3[:, 1:2])
                k.stt("dve", t2[i][:, 0:n], pp[:, 0:n], w3[:, 0:1], t1[i][:, 0:n], ALU.mult, ALU.add,
                      r=[g.convf.b, t1[i].b], w=[pp.b, t2[i].b])
                k.stt("dve", dst[:, 0:n], pp[:, 2:n + 2], w3[:, 2:3], t2[i][:, 0:n], ALU.mult, ALU.add,
                      r=[g.convf.b, t2[i].b], w=[pp.b, dst.b])
            k.act(sa[i][:, 0:n], ca[i][:, 0:n], AF.Silu, r=[ca[i].b], w=[sa[i].b])
            k.tt("pool", aT[:, c, 0:n], sa[i][:, 0:n], cg[i][:, 0:n], ALU.mult, r=[sa[i].b, cg[i].b], w=[aT.b])

        def down(ti, xbufs):
            s0, n, var = tiles[ti]
            aT = actT[ti % 2]
            nsub = (n + 127) // 128
            for j in range(nsub):
                m = min(128, n - j * 128)
                r0 = s0 + j * 128
                x_ = xbufs[j]
                o_ = xo[cnt["o"] % 2]
                cnt["o"] += 1
                for hh in range(2):
                    p_ = po[cnt["po"] % 3]
                    cnt["po"] += 1
                    wb = WDS.bufs(hh * 512, (hh + 1) * 512)
                    for hc in range(HC):
                        k.mm(p_[0:m, :], aT[:, hc, j * 128:j * 128 + m], WD[:, hc, hh * 512:(hh + 1) * 512],
                             start=(hc == 0), r=[aT.b] + wb, w=[p_.b])
                    t_ = tmp[cnt["tmp"] % 2]
                    cnt["tmp"] += 1
                    k.tt("dve", t_[0:m, :], p_[0:m, :], gtb[var][0:m, hh * 512:(hh + 1) * 512], ALU.mult,
                         r=[gtb[var].b], w=[p_.b, t_.b])
                    k.tt("pool", o_[0:m, hh * 512:(hh + 1) * 512], x_[0:m, hh * 512:(hh + 1) * 512], t_[0:m, :], ALU.add,
                         r=[x_.b, t_.b], w=[o_.b])
                if hf == 0:
                    dst = g.XP[r0:r0 + m, :]
                elif l == L - 1:
                    dst = g.out[r0:r0 + m, :]
                else:
                    dst = g.XM[r0:r0 + m, :]
                k.dma(dst, o_[0:m, :], r=[o_.b])

        NPRE = 2
        load(0)
        for c in range(HC):
            up_chunk(0, c)
        for ti in range(len(tiles)):
            if ti + 1 < len(tiles):
                load(ti + 1)
            xbufs = load_x(ti)
            if ti + 1 < len(tiles):
                for c in range(NPRE):
                    up_chunk(ti + 1, c)
            down(ti, xbufs)
            if ti + 1 < len(tiles):
                for c in range(NPRE, HC):
                    up_chunk(ti + 1, c)

WARM_N = 0
WARM_EVERY = 1
KC0 = (0, 8, 24, 32)


def na_rowcfgs(a):
    if a == 0:
        return [(0, 3), (1, 4)]
    if a == 15:
        return [(14, 5), (15, 6)]
    return [(a - 1, 0), (a, 1), (a + 1, 2)]


def p2_mixers(k, g, l):
    din = g.din
    N = 256
    with phase(k):
        WO = k.sb("wo", [128, 8, D], BF16)
        NAB = k.sb("nab", [128, 8, NTILE * 8], BF16)
        stg = [k.sb("stg%d" % i, [128, 8, 128], F32) for i in range(2)]
        def load_p2_weights():
            wload(k, stg, din["nab"][:, l, :].rearrange("p (a c) -> p a c", a=8), 8, NTILE * 8, [(0, NTILE * 8, NAB, 0)],
                  blk=128, func=AF.Exp)
            wload_cast(k, din["w_o"][l].rearrange("(kc p) n -> p kc n", p=128), 8, [(0, D, WO, 0)])
        nabf = NAB[:, :, :].rearrange("p a c -> p (a c)")
        nvar = 2 if l == 0 else 1
        gtb = [k.sb("gtb%d" % v, [128, D], F32) for v in range(nvar)]
        for v in range(nvar):
            k.dma(gtb[v][:], bcast_row(g.modrow, v, l * 6144 + 2 * D, D), w=[gtb[v].b])
        es = k.sb("es", [128, 6], F32)
        k.act(es[:], g.sink[:, l, :], AF.Exp, r=[g.sink.b], w=[es.b])
        KAc = k.sb("kac", [128, 3, CT], BF16)
        KBc = k.sb("kbc", [128, 2, CT], BF16)
        k.dma(KAc[:], g.QK[3:6, :, S:S + CT].rearrange("c p t -> p c t"), w=[KAc.b])
        k.dma(KBc[:], g.QK[9:11, :, S:S + CT].rearrange("c p t -> p c t"), w=[KBc.b])
        VAc = [k.sb("vac%d" % i, [128, 6, 128], BF16) for i in range(2)]
        VBc = [k.sb("vbc%d" % i, [128, 2, 128], BF16) for i in range(2)]
        for ct in range(2):
            k.dma(VAc[ct][:], g.VA[S + ct * 128:S + (ct + 1) * 128, :, :], w=[VAc[ct].b])
            k.dma(VBc[ct][:], g.VB[S + ct * 128:S + (ct + 1) * 128, :, :], w=[VBc[ct].b])
        KAn = [k.sb("kan%d" % i, [128, 3, 256], BF16) for i in range(2)]
        KAg = [[k.sb("kag%d_%d" % (i, j), [128, 3, 128], BF16) for j in range(4)] for i in range(4)]
        VAr = [[k.sb("var%d_%d" % (i, j), [128, 6, 128], BF16) for j in range(4)] for i in range(4)]
        KBr = [k.sb("kbr%d" % i, [128, 2, 128], BF16) for i in range(6)]
        VBr = [k.sb("vbr%d" % i, [128, 2, 128], BF16) for i in range(6)]

        def load_group(b):
            if b < 0 or b > 15:
                return
            sl = b % 4
            t0 = b * 256
            kn = KAn[b % 2]
            k.dma(kn[:], g.QK[3:6, :, t0:t0 + 256].rearrange("c p t -> p c t"), w=[kn.b])
            for j in range(4):
                for c in range(3):
                    k.cp("pool", KAg[sl][j][:, c, :].rearrange("p (r x) -> p r x", x=32),
                         kn[:, c, :].rearrange("p (r x) -> p r x", x=64)[:, :, KC0[j]:KC0[j] + 32],
                         r=[kn.b], w=[KAg[sl][j].b])
                for kr in range(4):
                    r0 = t0 + kr * 64 + KC0[j]
                    k.dma(VAr[sl][j][kr * 32:(kr + 1) * 32, :, :], g.VA[r0:r0 + 32, :, :], w=[VAr[sl][j].b])

        def load_kt(kt):
            if kt < 0 or kt > 31:
                return
            sl = kt % 6
            t0 = kt * 128
            k.dma(KBr[sl][:], g.QK[9:11, :, t0:t0 + 128].rearrange("c p t -> p c t"), w=[KBr[sl].b])
            k.dma(VBr[sl][:], g.VB[t0:t0 + 128, :, :], w=[VBr[sl].b])

        q = [k.sb("q%d" % i, [128, 6, N], BF16) for i in range(2)]
        cu = [k.sb("cu%d" % i, [128, 2, N + 2], F32) for i in range(2)]
        bg = [k.sb("bg%d" % i, [128, 2, N], F32) for i in range(2)]
        P = [k.sb("P%d" % i, [128, 512], BF16) for i in range(6)]
        YT = k.sb("YT", [128, 8, N], BF16)
        YTc = Buf("YTconv")
        rc = [k.sb("rc%d" % i, [128, N], F32) for i in range(2)]
        c1 = [k.sb("c1%d" % i, [128, N], F32) for i in range(2)]
        c2 = [k.sb("c2%d" % i, [128, N], F32) for i in range(2)]
        xt = [k.sb("xt%d" % i, [128, D], F32) for i in range(4)]
        xo = [k.sb("xo%d" % i, [128, D], F32) for i in range(2)]
        tmp = [k.sb("tmp%d" % i, [128, 512], F32) for i in range(2)]
        xn = k.sb("xn", [128, 2, D], BF16)
        junk = k.sb("junk", [128, D], BF16)
        ss = k.sb("ss", [128, 2], F32)
        rs = k.sb("rs", [128, 2], F32)
        h2o = k.sb("h2o", [128, 8, N], BF16)
        pS = [k.ps("pS%d" % i) for i in range(6)]
        pO = [k.ps("pO%d" % i) for i in range(2)]
        cnt = {"pS": 0, "pO": 0, "P": 0, "rc": 0, "x": 0, "o": 0, "po": 0, "tmp": 0, "c": 0}

        tiles = [(i * N, N, 0) for i in range(S // N)] + ([(S, CT, 1)] if l == 0 else [])

        def load_tile(ti):
            tok0, n, var = tiles[ti]
            b_ = ti % 2
            k.dma(q[b_][:, 0:3, 0:n], g.QK[0:3, :, tok0:tok0 + n].rearrange("c p t -> p c t"), w=[q[b_].b])
            k.dma(q[b_][:, 3:6, 0:n], g.QK[6:9, :, tok0:tok0 + n].rearrange("c p t -> p c t"), w=[q[b_].b])
            lo, hi = (0, S) if var == 0 else (S, S + CT)
            a, e = max(tok0 - 1, lo), min(tok0 + n + 1, hi)
            c0 = a - (tok0 - 1)
            if c0 > 0:
                k.memset("pool", cu[b_][:, :, 0:1], 0.0, w=[cu[b_].b])
            if e < tok0 + n + 1:
                k.memset("pool", cu[b_][:, :, n + 1:n + 2], 0.0, w=[cu[b_].b])
            k.dma(cu[b_][:, :, c0:c0 + (e - a)], g.CU[:, :, a:e].rearrange("c p t -> p c t"), w=[cu[b_].b])
            k.dma(bg[b_][:, :, 0:n], g.BG[:, :, tok0:tok0 + n].rearrange("c p t -> p c t"), w=[bg[b_].b])

        def next_ps(name, arr):
            p = arr[cnt[name] % len(arr)]
            cnt[name] += 1
            return p

        def ctx_part(qh, qbuf, Kc, kch, pb, Vc, vh, n, pOut):
            for ct in range(2):
                ps_ = next_ps("pS", pS)
                k.mm(ps_[:, 0:n], Kc[pb:pb + 64, kch, ct * 128:(ct + 1) * 128], qh, start=True,
                     r=[Kc.b, qbuf], w=[ps_.b])
                pp = next_ps("P", P)
                k.act(pp[:, 0:n], ps_[:, 0:n], AF.Exp, w=[ps_.b, pp.b])
                k.mm(pOut[:, 0:n], Vc[ct][:, vh, :], pp[:, 0:n], start=(ct == 0), r=[Vc[ct].b, pp.b], w=[pOut.b])

        wmask = {}

        def get_mask(pattern):
            if pattern not in wmask:
                t = k.sb("wm%d" % len(wmask), [128, len(pattern) * 128], BF16)
                for i, mk in enumerate(pattern):
                    k.cp("pool", t[:, i * 128:(i + 1) * 128], g.wgm[:, mk, :], r=[g.wgm.b], w=[t.b])
                wmask[pattern] = t
            return wmask[pattern]

        def blk(ap):
            return ap.rearrange("p (j r c) -> p j r c", j=4, r=4, c=16)

        def finalize(pOut, n, h_extra, dst, blocked):
            r_ = rc[cnt["rc"] % 2]
            cnt["rc"] += 1
            if h_extra is None:
                k.act(r_[64:128, 0:n], pOut[64:128, 0:n], AF.Ln, w=[pOut.b, r_.b])
            else:
                k.act(r_[64:128, 0:n], pOut[64:128, 0:n], AF.Ln, r=[es.b], w=[pOut.b, r_.b],
                      bias=es[64:128, h_extra:h_extra + 1])
            k.act(r_[64:128, 0:n], r_[64:128, 0:n], AF.Exp, r=[r_.b], w=[r_.b], scale=-1.0)
            if blocked:
                k.tt("dve", dst.rearrange("p (r j c) -> p j r c", r=4, j=4, c=16), blk(pOut[0:64, 0:n]),
                     blk(r_[64:128, 0:n]), ALU.mult, r=[r_.b], w=[pOut.b, YT.b])
            else:
                k.tt("dve", dst, pOut[0:64, 0:n], r_[64:128, 0:n], ALU.mult, r=[r_.b], w=[pOut.b, YT.b])

        class U:
            __slots__ = ("pre", "qk", "ex", "pv", "fin", "post", "post2")

            def __init__(self):
                self.pre = []
                self.qk = self.ex = self.pv = self.fin = None
                self.post = []
                self.post2 = []

        def make_tile_units(ti):
            tok0, n, var = tiles[ti]
            b_ = ti % 2
            a = ti
            qq = q[b_]
            units = []

            def ctx_units(qh, Kc, kch, pb, Vc, vh, pO_cell, blocked=False):
                u = U()
                cell = {}

                def qk(cell=cell):
                    ps_ = next_ps("pS", pS)
                    cell["ps"] = ps_
                    for ct in range(2):
                        k.mm(ps_[:, ct * n:(ct + 1) * n], Kc[pb:pb + 64, kch, ct * 128:(ct + 1) * 128], qh, start=(ct == 0),
                             r=[Kc.b, qq.b], w=[ps_.b])

                def ex(cell=cell):
                    pp = next_ps("P", P)
                    cell["pp"] = pp
                    if blocked:
                        for ct in range(2):
                            k.act(pp[:, ct * n:(ct + 1) * n].rearrange("p (j r c) -> p r j c", j=4, r=4, c=16),
                                  cell["ps"][:, ct * n:(ct + 1) * n].rearrange("p (r j c) -> p r j c", r=4, j=4, c=16),
                                  AF.Exp, w=[cell["ps"].b, pp.b])
                    else:
                        k.act(pp[:, 0:2 * n], cell["ps"][:, 0:2 * n], AF.Exp, w=[cell["ps"].b, pp.b])

                def pv(cell=cell):
                    pO_cell["p"] = next_ps("pO", pO)
                    pOut = pO_cell["p"]
                    pp = cell["pp"]
                    for ct in range(2):
                        k.mm(pOut[:, 0:n], Vc[ct][:, vh, :], pp[:, ct * n:(ct + 1) * n], start=(ct == 0),
                             r=[Vc[ct].b, pp.b], w=[pOut.b])

                u.qk, u.ex, u.pv = qk, ex, pv
                units.append(u)

            for h in range(6):
                ch, pb = h // 2, 64 * (h % 2)
                qh = qq[pb:pb + 64, ch, 0:n]
                pO_cell = {}
                ctx_units(qh, KAc, ch, pb, VAc, h, pO_cell, blocked=(var == 0))
                if var == 0:
                    q3 = qq[pb:pb + 64, ch, :].rearrange("p (r c) -> p r c", c=64)
                    slots = []
                    for (b, rcfg) in na_rowcfgs(a):
                        for j in range(4):
                            slots.append((b, j, (h * 7 + rcfg) * 4 + j))
                    for s0 in range(0, len(slots), 8):
                        grp = slots[s0:s0 + 8]
                        u = U()
                        cell = {}

                        def qk(grp=grp, cell=cell, q3=q3, pb=pb, ch=ch):
                            ps_ = next_ps("pS", pS)
                            cell["ps"] = ps_
                            for i, (b, j, tix) in enumerate(grp):
                                kt_ = KAg[b % 4][j]
                                k.mm(ps_[:, i * 64:(i + 1) * 64], kt_[pb:pb + 64, ch, :], q3[:, :, 16 * j:16 * j + 16],
                                     start=(i == 0), r=[kt_.b, qq.b], w=[ps_.b])

                        def ex(grp=grp, cell=cell):
                            pp = next_ps("P", P)
                            cell["pp"] = pp
                            w_ = len(grp) * 64
                            k.act(pp[:, 0:w_], cell["ps"][:, 0:w_], AF.Exp, w=[cell["ps"].b, pp.b])
                            t0_ = grp[0][2] * 64
                            k.tt("dve", pp[:, 0:w_], pp[:, 0:w_], nabf[:, t0_:t0_ + w_], ALU.mult, r=[pp.b, NAB.b], w=[pp.b])

                        def pv(grp=grp, cell=cell, pO_cell=pO_cell, h=h):
                            pOut = pO_cell["p"]
                            pp = cell["pp"]
                            for i, (b, j, tix) in enumerate(grp):
                                vt_ = VAr[b % 4][j]
                                k.mm(pOut[:, j * 64:(j + 1) * 64], vt_[:, h, :], pp[:, i * 64:(i + 1) * 64],
                                     start=False, r=[vt_.b, pp.b], w=[pOut.b])

                        u.qk, u.ex, u.pv = qk, ex, pv
                        units.append(u)
                units[-1].fin = (lambda pO_cell=pO_cell, pb=pb, ch=ch: finalize(pO_cell["p"], n, None, YT[pb:pb + 64, ch, 0:n],
                                                                                    var == 0))
            for h in range(6):
                ch, pb, kv = 3 + h // 2, 64 * (h % 2), h // 3
                qh = qq[pb:pb + 64, ch, 0:n]
                pO_cell = {}
                ctx_units(qh, KBc, kv, pb, VBc, kv, pO_cell)
                if var == 0:
                    slots = []
                    for t in range(2):
                        i_ = 2 * a + t
                        for kt in (i_ - 1, i_, i_ + 1):
                            if 0 <= kt <= 31:
                                slots.append((t, kt, 0 if kt == i_ else (1 if kt < i_ else 2)))
                    for s0 in range(0, len(slots), 4):
                        grp = slots[s0:s0 + 4]
                        u = U()
                        cell = {}

                        def qk(grp=grp, cell=cell, pb=pb, ch=ch, kv=kv):
                            ps_ = next_ps("pS", pS)
                            cell["ps"] = ps_
                            for i, (t, kt, mk) in enumerate(grp):
                                kt_ = KBr[kt % 6]
                                k.mm(ps_[:, i * 128:(i + 1) * 128], kt_[pb:pb + 64, kv, :],
                                     qq[pb:pb + 64, ch, t * 128:(t + 1) * 128],
                                     start=(i == 0), r=[kt_.b, qq.b], w=[ps_.b])

                        def ex(grp=grp, cell=cell):
                            pp = next_ps("P", P)
                            cell["pp"] = pp
                            w_ = len(grp) * 128
                            k.act(pp[:, 0:w_], cell["ps"][:, 0:w_], AF.Exp, w=[cell["ps"].b, pp.b])
                            pat = tuple(mk for (_, _, mk) in grp)
                            if any(pat):
                                mt = get_mask(pat)
                                k.tt("dve", pp[:, 0:w_], pp[:, 0:w_], mt[:, 0:w_], ALU.mult, r=[pp.b, mt.b], w=[pp.b])

                        def pv(grp=grp, cell=cell, pO_cell=pO_cell, kv=kv):
                            pOut = pO_cell["p"]
                            pp = cell["pp"]
                            for i, (t, kt, mk) in enumerate(grp):
                                vt_ = VBr[kt % 6]
                                k.mm(pOut[:, t * 128:(t + 1) * 128], vt_[:, kv, :], pp[:, i * 128:(i + 1) * 128],
                                     start=False, r=[vt_.b, pp.b], w=[pOut.b])

                        u.qk, u.ex, u.pv = qk, ex, pv
                        units.append(u)
                units[-1].fin = (lambda pO_cell=pO_cell, pb=pb, ch=ch, h=h: finalize(pO_cell["p"], n, h, YT[pb:pb + 64, ch, 0:n],
                                                                                         False))

            xcell = {}

            def prefetch():
                if var == 0:
                    xsrc = (din["x"] if l == 0 else g.XM[:])[tok0:tok0 + n, :]
                else:
                    xsrc = din["ctx"]
                xcell["b"] = []
                for s in range(n // 128):
                    x_ = xt[cnt["x"] % 4]
                    cnt["x"] += 1
                    k.dma(x_[:], xsrc[s * 128:(s + 1) * 128, :], w=[x_.b])
                    xcell["b"].append(x_)
                if ti + 1 < len(tiles):
                    load_tile(ti + 1)
                if var == 0:
                    load_group(a + 2)
                    load_kt(2 * a + 3); load_kt(2 * a + 4)

            def conv():
                for c in range(2):
                    i = cnt["c"] % 2
                    cnt["c"] += 1
                    w3 = g.convc[:, l, c, :]
                    cuc = cu[b_]
                    k.act(c1[i][:, 0:n], cuc[:, c, 1:n + 1], AF.Copy, r=[cuc.b, g.convc.b], w=[c1[i].b], scale=w3[:, 1:2])
                    k.stt("dve", c2[i][:, 0:n], cuc[:, c, 0:n], w3[:, 0:1], c1[i][:, 0:n], ALU.mult, ALU.add,
                          r=[cuc.b, c1[i].b, g.convc.b], w=[c2[i].b])
                    k.stt("dve", c1[i][:, 0:n], cuc[:, c, 2:n + 2], w3[:, 2:3], c2[i][:, 0:n], ALU.mult, ALU.add,
                          r=[cuc.b, c2[i].b, g.convc.b], w=[c1[i].b])
                    k.tt("pool", YT[:, 6 + c, 0:n], c1[i][:, 0:n], bg[b_][:, c, 0:n], ALU.mult,
                         r=[c1[i].b, bg[b_].b], w=[YTc])

            ocell = []

            def wo_epi():
                nsub = n // 128
                for s in range(nsub):
                    x_ = xcell["b"][s]
                    o_ = xo[cnt["o"] % 2]
                    cnt["o"] += 1
                    for hh in range(2):
                        p_ = next_ps("pS", pS)
                        for kc in range(8):
                            k.mm(p_[:, :], YT[:, kc, s * 128:(s + 1) * 128], WO[:, kc, hh * 512:(hh + 1) * 512],
                                 start=(kc == 0), r=[YT.b, YTc, WO.b], w=[p_.b])
                        t_ = tmp[cnt["tmp"] % 2]
                        cnt["tmp"] += 1
                        k.tt("dve", t_[:], p_[:, :], gtb[var][:, hh * 512:(hh + 1) * 512], ALU.mult,
                             r=[gtb[var].b], w=[p_.b, t_.b])
                        k.tt("pool", o_[:, hh * 512:(hh + 1) * 512], x_[:, hh * 512:(hh + 1) * 512], t_[:], ALU.add,
                             r=[x_.b, t_.b], w=[o_.b])
                    k.dma(g.XN[tok0 + s * 128:tok0 + (s + 1) * 128, :], o_[:], r=[o_.b])
                    ocell.append(o_)

            def wo_norm():
                for s, o_ in enumerate(ocell):
                    norm_rows(k, o_[:], s, ss, junk, r=[o_.b])
                    k.rsqrt(rs[:, s:s + 1], ss[:, s:s + 1], g.epsb, r=[ss.b], w=[rs.b])
                    k.ts("dve", xn[:, s, :], o_[:], rs[:, s:s + 1], None, ALU.mult, r=[o_.b, rs.b], w=[xn.b])

            def transposes():
                nsub = n // 128
                for kc in range(8):
                    pt_ = next_ps("pS", pS)
                    ptb = pt_.t.bitcast(BF16)
                    for s in range(nsub):
                        k.tr(ptb[:, s * 128:(s + 1) * 128], xn[:, s, kc * 128:(kc + 1) * 128], g.identb[:],
                             r=[xn.b, g.identb.b], w=[pt_.b])
                    k.act(h2o[:, kc, 0:n], ptb[:, 0:n], AF.Identity, r=[g.AB.b], w=[pt_.b, h2o.b],
                          scale=abv(g, l, var, 2)[:, kc:kc + 1], bias=abv(g, l, var, 3)[:, kc:kc + 1])
                k.dma(g.HT2[:, :, tok0:tok0 + n].rearrange("c p t -> p c t"), h2o[:, :, 0:n], r=[h2o.b])

            def warm():
                pw = pT.t.bitcast(F32)
                for i in range(WARM_N):
                    k.mm(pw[:, 0:512], WO[:, i % 8, 0:128], WO[:, (i + 1) % 8, 0:512], start=True, r=[WO.b], w=[pT.b])

            if WARM_N and (ti % WARM_EVERY == 0):
                units[0].pre.append(warm)
            units[min(7, len(units) - 1)].pre.append(prefetch)
            units[min(6, len(units) - 1)].pre.append(conv)
            units[-1].post.append(wo_epi)
            units[-1].post2.append(wo_norm)
            return units, transposes

        load_tile(0)
        load_group(0)
        for kt in range(0, 3):
            load_kt(kt)
        load_p2_weights()
        load_group(1)
        allu = []
        pending_tr = None
        for ti in range(len(tiles)):
            us, trf = make_tile_units(ti)
            if pending_tr is not None:
                us[min(12, len(us) - 1)].pre.append(pending_tr)
            pending_tr = trf
            allu.extend(us)
        SK = 4
        FD = 1
        due = {}
        for i in range(len(allu) + SK + FD + 5):
            if i < len(allu):
                u = allu[i]
                for f in u.pre:
                    f()
                u.qk()
                u.ex()
            j = i - SK
            if 0 <= j < len(allu):
                u = allu[j]
                u.pv()
                fl = []
                if u.fin is not None:
                    fl.append(u.fin)
                fl.extend(u.post)
                if fl:
                    due.setdefault(i + FD, []).extend(fl)
                if u.post2:
                    due.setdefault(i + FD + 3, []).extend(u.post2)
            for f in due.pop(i, []):
                f()
        assert not due
        pending_tr()

def _shared_inputs(inp, consts):
    m = _core_inputs(0, inp, consts)
    for kx in ("x", "ctx", "cvt"):
        m.pop(kx)
    return m


def kernel(**inputs):
    consts = _consts()
    shared = _shared_inputs(inputs, consts)
    in_maps = [_core_inputs(b, inputs, consts, shared) for b in range(NCORES)]
    nc = build()
    res = run_bass_kernel_spmd(nc, in_maps, core_ids=list(range(NCORES)))
    return np.stack([np.asarray(r["out"], dtype=np.float32) for r in res.results], axis=0)
```

```python
import numpy as np
import ml_dtypes
import concourse.bass as bass
import concourse.mybir as mybir
from concourse.bass_utils import run_bass_kernel_spmd

F32 = mybir.dt.float32
BF16 = mybir.dt.bfloat16
AF = mybir.ActivationFunctionType
ALU = mybir.AluOpType

D = 1024
S = 4096
CT = 256
TT = S + CT
L = 2
DFF = 2816
INW = 2560
EPS = 1e-6
NEGM = -30000.0
NCORES = 8

ROWCFG = [(5, 4), (5, 5), (5, 6), (0, 0), (0, 1), (15, 14), (15, 15)]
NTILE = 6 * 7 * 4


class Buf:
    __slots__ = ("name",)

    def __init__(self, name):
        self.name = name


class Op:
    __slots__ = ("eng", "fn", "deps", "dma", "sig", "sem", "val", "inc")

    def __init__(self, eng, fn, dma):
        self.eng = eng
        self.fn = fn
        self.deps = []
        self.dma = dma
        self.sig = dma
        self.sem = None
        self.val = 0
        self.inc = 1


class Sched:
    NDMA = 12
    ENGS = ["pe", "act", "dve", "pool", "sp"]

    def __init__(self, nc, stack):
        self.nc = nc
        self.csem = {e: stack.enter_context(nc.semaphore("c_" + e)) for e in self.ENGS}
        self.dsem = {e: [stack.enter_context(nc.semaphore("d_%s_%d" % (e, i))) for i in range(self.NDMA)]
                     for e in ("sp", "pool")}
        self.cnt = {e: 0 for e in self.ENGS}
        self.dcnt = {e: 0 for e in self.dsem}
        self.dtot = {}
        self.seen = {e: {} for e in self.ENGS}
        self.nphase = 0
        self.reset()

    def reset(self):
        self.ops = []
        self.last_w = {}
        self.readers = {}
        self.dma_hist = {}

    def add(self, eng, fn, r=(), w=(), dma=False):
        op = Op(eng, fn, dma)
        deps = {}
        for b in r:
            lw = self.last_w.get(b)
            if lw is not None:
                deps[id(lw)] = (lw, 0)
        for b in w:
            lw = self.last_w.get(b)
            if lw is not None and id(lw) not in deps:
                deps[id(lw)] = (lw, 1)
            for rd in self.readers.get(b, ()):
                if id(rd) not in deps:
                    deps[id(rd)] = (rd, 1)
        for p, kind in deps.values():
            if (not p.dma) and (not dma) and p.eng == eng and kind == 1 and eng == "pe":
                continue
            op.deps.append(p)
            p.sig = True
        if dma:
            h = self.dma_hist.setdefault(eng, [])
            if len(h) >= self.NDMA:
                op.deps.append(h[len(h) - self.NDMA])
            h.append(op)
        for b in r:
            self.readers.setdefault(b, []).append(op)
        for b in w:
            self.last_w[b] = op
            self.readers[b] = []
        self.ops.append(op)
        return op

    def emit_phase(self):
        nc = self.nc
        per = {e: [o for o in self.ops if o.eng == e] for e in self.ENGS}
        bar = [(self.csem[e], self.cnt[e]) for e in self.ENGS if self.cnt[e] > 0]
        for e in self.dsem:
            for s in self.dsem[e]:
                if self.dtot.get(id(s), 0) > 0:
                    bar.append((s, self.dtot[id(s)]))
        for e in self.ENGS:
            comp = [o for o in per[e] if not o.dma]
            if comp:
                comp[-1].sig = True
        for op in self.ops:
            if op.dma:
                i = self.dcnt[op.eng]
                self.dcnt[op.eng] = i + 1
                sm = self.dsem[op.eng][i % self.NDMA]
                t = self.dtot.get(id(sm), 0) + 16
                self.dtot[id(sm)] = t
                op.sem, op.val, op.inc = sm, t, 16
            elif op.sig:
                self.cnt[op.eng] += 1
                op.sem, op.val, op.inc = self.csem[op.eng], self.cnt[op.eng], 1
        first = self.nphase == 0
        self.nphase += 1

        def run(e, eng):
            seen = self.seen[e]
            if not first:
                for sm, v in bar:
                    if seen.get(id(sm), 0) < v:
                        eng.wait_ge(sm, v)
                        seen[id(sm)] = v
            for op in per[e]:
                for p in op.deps:
                    k = id(p.sem)
                    if seen.get(k, 0) < p.val:
                        eng.wait_ge(p.sem, p.val)
                        seen[k] = p.val
                ins = op.fn(eng)
                if op.sig:
                    ins.then_inc(op.sem, op.inc)

        with nc.Block() as block:
            @block.tensor
            def _(eng):
                run("pe", eng)

            @block.scalar
            def _(eng):
                run("act", eng)

            @block.vector
            def _(eng):
                run("dve", eng)

            @block.gpsimd
            def _(eng):
                run("pool", eng)

            @block.sync
            def _(eng):
                run("sp", eng)
        self.reset()

    def emit_final(self):
        nc = self.nc
        bar = [(self.csem[e], self.cnt[e]) for e in self.ENGS if self.cnt[e] > 0]
        for e in self.dsem:
            for s in self.dsem[e]:
                if self.dtot.get(id(s), 0) > 0:
                    bar.append((s, self.dtot[id(s)]))
        with nc.Block() as block:
            @block.sync
            def _(eng):
                for sm, v in bar:
                    eng.wait_ge(sm, v)


class Tn:
    def __init__(self, t, name):
        self.t = t
        self.b = Buf(name)

    def __getitem__(self, k):
        return self.t[k]


class K:
    def __init__(self, nc, stack, dbg=None):
        self.nc = nc
        self.st = stack
        self.s = Sched(nc, stack)
        self.dbg = dbg or set()
        self.gst = stack

    def sb(self, name, shape, dt):
        self.nn = getattr(self, "nn", 0) + 1
        t = self.st.enter_context(self.nc.sbuf_tensor("s%d_%s" % (self.nn, name), list(shape), dt))
        return Tn(t, name)

    def ps(self, name, dt=F32, cols=512):
        self.nn = getattr(self, "nn", 0) + 1
        t = self.st.enter_context(self.nc.psum_tensor("p%d_%s" % (self.nn, name), [128, cols], dt))
        return Tn(t, name)

    def dram(self, name, shape, dt, kind="Internal"):
        if name in self.dbg:
            kind = "ExternalOutput"
        if name in getattr(self, "dbg_in", ()):
            kind = "ExternalInput"
        t = self.nc.dram_tensor(name, list(shape), dt, kind=kind)
        d = Tn(t.ap(), name)
        d.h = t
        return d

    def dma(self, out, in_, r=(), w=(), q="sp", **kw):
        return self.s.add(q, lambda e: e.dma_start(out=out, in_=in_, **kw), r=r, w=w, dma=True)

    def mm(self, out, lhsT, rhs, start, stop=True, r=(), w=()):
        return self.s.add(
            "pe",
            lambda e: e.matmul(out, lhsT, rhs, start=start, stop=stop, skip_group_check=True),
            r=r, w=w)

    def tr(self, out, in_, ident, r=(), w=()):
        return self.s.add("pe", lambda e: e.transpose(out, in_, ident), r=r, w=w)

    def act(self, out, in_, func, r=(), w=(), eng="act", **kw):
        return self.s.add(eng, lambda e: e.activation(out=out, in_=in_, func=func, **kw), r=r, w=w)

    def tt(self, eng, out, in0, in1, op, r=(), w=()):
        return self.s.add(eng, lambda e: e.tensor_tensor(out=out, in0=in0, in1=in1, op=op), r=r, w=w)

    def ts(self, eng, out, in0, s1, s2, op0, op1=None, r=(), w=()):
        if op1 is None:
            return self.s.add(eng, lambda e: e.tensor_scalar(out=out, in0=in0, scalar1=s1, scalar2=None, op0=op0), r=r, w=w)
        return self.s.add(eng, lambda e: e.tensor_scalar(out=out, in0=in0, scalar1=s1, scalar2=s2, op0=op0, op1=op1), r=r, w=w)

    def stt(self, eng, out, in0, scalar, in1, op0, op1, r=(), w=()):
        return self.s.add(eng, lambda e: e.scalar_tensor_tensor(out=out, in0=in0, scalar=scalar, in1=in1, op0=op0, op1=op1), r=r, w=w)

    def cp(self, eng, out, in_, r=(), w=()):
        if eng == "act":
            return self.s.add(eng, lambda e: e.copy(out=out, in_=in_), r=r, w=w)
        return self.s.add(eng, lambda e: e.tensor_copy(out=out, in_=in_), r=r, w=w)

    def recip(self, out, in_, r=(), w=()):
        return self.s.add("dve", lambda e: e.reciprocal(out=out, in_=in_), r=r, w=w)

    def rsqrt(self, out, in_, epsb, r=(), w=(), inw=()):
        self.act(out, in_, AF.Ln, r=list(r) + [epsb.b], w=list(inw) + list(w), bias=epsb[:, 0:1])
        return self.act(out, out, AF.Exp, r=list(w), w=list(w), scale=-0.5)

    def memset(self, eng, ap, val, w=()):
        return self.s.add(eng, lambda e: e.memset(ap, val), w=w)


def _na_index():
    kr_in = np.arange(128) // 32
    kc_in = np.arange(128) % 32
    r_in = np.arange(64) // 16
    c_in = np.arange(64) % 16
    drow = np.zeros((7, 128, 64), np.int64)
    rok = np.zeros((7, 128, 64), bool)
    for i, (a, b) in enumerate(ROWCFG):
        r = 4 * a + r_in[None, :]
        kr = 4 * b + kr_in[:, None]
        r0 = np.clip(r - 4, 0, 56)
        rok[i] = (kr >= r0) & (kr < r0 + 8)
        drow[i] = np.clip(kr - r + 7, 0, 14)
    dcol = np.zeros((4, 128, 64), np.int64)
    cok = np.zeros((4, 128, 64), bool)
    for i, j in enumerate((0, 1, 2, 3)):
        kc0 = int(np.clip(16 * j - 8, 0, 32))
        c = 16 * j + c_in[None, :]
        kc = kc0 + kc_in[:, None]
        c0 = np.clip(c - 8, 0, 48)
        cok[i] = (kc >= c0) & (kc < c0 + 16)
        dcol[i] = np.clip(kc - c + 15, 0, 30)
    return drow, rok, dcol, cok


def _consts():
    c = {}
    c["identb"] = np.eye(128, dtype=np.float32).astype(ml_dtypes.bfloat16)
    c["identf"] = np.eye(128, dtype=np.float32)
    bm = np.zeros((128, 128), np.float32)
    bm[:64, :64] = 1.0 / 64
    bm[64:, 64:] = 1.0 / 64
    c["bm"] = bm.astype(ml_dtypes.bfloat16)
    pm = np.zeros((128, 128), np.float32)
    for m in range(128):
        k = m + 32 if (m % 64) < 32 else m - 32
        pm[k, m] = 1.0
    c["pm"] = pm
    t = np.arange(S)
    row = (t // 64).astype(np.float32)
    col = (t % 64).astype(np.float32)
    inv = (np.float32(10000.0) ** (-np.arange(16, dtype=np.float32) / np.float32(16))).astype(np.float32)
    ang = np.concatenate([row[:, None] * inv, col[:, None] * inv], axis=-1).astype(np.float32)
    cos = np.cos(ang).astype(np.float32)
    sin = np.sin(ang).astype(np.float32)
    d = np.arange(128) % 64
    cosT = cos[:, d % 32].T
    sgn = np.where(d < 32, -1.0, 1.0).astype(np.float32)
    sinT = sin[:, d % 32].T * sgn[:, None]
    c["rope"] = np.ascontiguousarray(np.stack([cosT, sinT], axis=1)).astype(np.float32)
    ki = np.arange(128)[:, None]
    qi = np.arange(128)[None, :]
    mprev = np.where(qi <= ki, 1.0, 0.0)
    mnext = np.where(ki <= qi, 1.0, 0.0)
    c["wgm"] = np.stack([np.ones_like(mprev), mprev, mnext], axis=1).astype(np.float32).astype(ml_dtypes.bfloat16)
    return c


def _core_inputs(b, inp, consts, shared=None):
    f = lambda a: np.ascontiguousarray(np.asarray(a, dtype=np.float32))
    m = {}
    m["x"] = f(inp["x"][b])
    m["ctx"] = f(inp["ctx"][b])
    cvec = np.stack([np.asarray(inp["c"][b]), np.asarray(inp["c_ctx"])], 0)
    m["cvt"] = f(cvec.reshape(2, 8, 128).transpose(2, 1, 0))
    if shared is not None:
        m.update(shared)
        return m
    m["w_ada"] = f(inp["w_ada"])
    m["b_ada"] = f(inp["b_ada"])
    gt = lambda g: np.asarray(g).reshape(L, 8, 128).transpose(2, 0, 1)
    m["gT"] = f(np.stack([gt(inp["g_attn"]), gt(inp["g_ffn"])], axis=2))
    m["w_in"] = f(inp["w_in"])
    qkg = np.stack([np.asarray(inp[k]) for k in ("qn_a", "kn_a", "qn_b", "kn_b")], axis=-1)
    m["qkg"] = f(np.concatenate([qkg, qkg], axis=1).transpose(1, 0, 2))
    drow, rok, dcol, cok = _na_index()
    rpb = np.asarray(inp["rpb_a"], dtype=np.float32)
    g = rpb[:, :, drow[:, None], dcol[None, :]]
    ok = (rok[:, None] & cok[None, :])[None, None]
    nab = np.where(ok, g, np.float32(NEGM)).astype(np.float32)
    m["nab"] = f(nab.transpose(4, 0, 1, 2, 3, 5).reshape(128, L, NTILE * 64))
    m["sink"] = f(np.broadcast_to(np.asarray(inp["sink_b"])[None], (128, L, 6)))
    m["convc"] = f(np.asarray(inp["conv_c"]).reshape(L, 3, 2, 128).transpose(3, 0, 2, 1))
    m["w_o"] = f(inp["w_o"])
    m["w_up"] = f(inp["w_up"])
    m["convf"] = f(np.asarray(inp["conv_ffn"]).reshape(L, 3, 44, 128).transpose(3, 0, 2, 1))
    m["w_down"] = f(inp["w_down"])
    m.update(consts)
    return m


from contextlib import ExitStack, contextmanager


@contextmanager
def phase(k):
    old = k.st
    with ExitStack() as st:
        k.st = st
        yield
        k.s.emit_phase()
    k.st = old


def fence(k, eng, r, w):
    d = k.dummy
    return k.s.add(eng, lambda e: e.memset(d[0:1, 0:1], 0.0), r=r, w=list(w) + [d.b])


def wload_cast(k, src, nkc, segs, nsplit=4):
    subs = {}
    step = (nkc + nsplit - 1) // nsplit
    for (s0, s1, dst, d0) in segs:
        for k0 in range(0, nkc, step):
            k1 = min(nkc, k0 + step)
            sb_ = Buf("sub")
            subs.setdefault(id(dst), (dst, []))[1].append(sb_)
            k.dma(dst[:, k0:k1, d0:d0 + (s1 - s0)], src[:, k0:k1, s0:s1], w=[sb_], q="pool")
    for dst, bl in subs.values():
        fence(k, "pool", r=bl, w=[dst.b])


class WSegs:
    def __init__(self):
        self.rng = []

    def bufs(self, c0, c1):
        return [b for (d0, d1, b) in self.rng if d0 < c1 and c0 < d1]


def wload_segs(k, src, nkc, dst, segs):
    ws = WSegs()
    for (s0, s1, d0) in segs:
        b = Buf("wseg")
        k.dma(dst[:, 0:nkc, d0:d0 + (s1 - s0)], src[:, :, s0:s1], w=[b], q="pool")
        ws.rng.append((d0, d0 + (s1 - s0), b))
    return ws


def wload(k, stg, src, nkc, ncols, segs, engs=("dve", "pool", "act"), blk=256, func=None):
    subs = {}
    ci = 0
    for bi, c0 in enumerate(range(0, ncols, blk)):
        c1 = min(ncols, c0 + blk)
        sg = stg[bi % len(stg)]
        k.dma(sg[:, 0:nkc, 0:c1 - c0], src[:, :, c0:c1], w=[sg.b])
        for (s0, s1, dst, d0) in segs:
            lo, hi = max(s0, c0), min(s1, c1)
            if lo >= hi:
                continue
            sb_ = Buf("sub")
            subs.setdefault(id(dst), (dst, []))[1].append(sb_)
            if func is None:
                k.cp(engs[ci % len(engs)], dst[:, 0:nkc, d0 + lo - s0:d0 + hi - s0], sg[:, 0:nkc, lo - c0:hi - c0],
                     r=[sg.b], w=[sb_])
            else:
                k.act(dst[:, 0:nkc, d0 + lo - s0:d0 + hi - s0], sg[:, 0:nkc, lo - c0:hi - c0], func, r=[sg.b], w=[sb_])
            ci += 1
    for dst, bl in subs.values():
        fence(k, "pool", r=bl, w=[dst.b])


class G:
    pass


def build(nlayers=L, dbg=(), stop_after=None, dbg_in=(), only=None):
    nc = bass.Bass("TRN2", target_bir_lowering=False)
    gst = ExitStack()
    with gst:
        k = K(nc, gst, set(dbg))
        k.dbg_in = set(dbg_in)
        g = G()
        din = {}

        def inp(name, shape, dt=F32):
            din[name] = nc.dram_tensor(name, list(shape), dt, kind="ExternalInput").ap()

        inp("x", [S, D]); inp("ctx", [CT, D]); inp("cvt", [128, 8, 2])
        inp("w_ada", [L, D, 6 * D]); inp("b_ada", [L, 6 * D]); inp("gT", [128, L, 2, 8])
        inp("w_in", [L, D, INW]); inp("qkg", [128, L, 4]); inp("nab", [128, L, NTILE * 64])
        inp("sink", [128, L, 6]); inp("convc", [128, L, 2, 3]); inp("w_o", [L, D, D])
        inp("w_up", [L, D, 2 * DFF]); inp("convf", [128, L, 44, 3]); inp("w_down", [L, DFF, D])
        inp("identb", [128, 128], BF16); inp("identf", [128, 128]); inp("bm", [128, 128], BF16)
        inp("pm", [128, 128]); inp("rope", [128, 2, S]); inp("wgm", [128, 3, 128], BF16)
        g.din = din
        g.out = nc.dram_tensor("out", [S, D], F32, kind="ExternalOutput").ap()
        g.modrow = k.dram("modrow", [2, L * 6 * D], F32)
        g.QK = k.dram("QK", [11, 128, TT], BF16)
        g.VA = k.dram("VA", [TT, 6, 128], BF16)
        g.VB = k.dram("VB", [TT, 2, 128], BF16)
        g.CU = k.dram("CU", [2, 128, TT], F32)
        g.BG = k.dram("BG", [2, 128, TT], F32)
        g.XN = k.dram("XN", [TT, D], F32)
        g.XP = k.dram("XP", [TT, D], F32)
        g.XM = k.dram("XM", [TT, D], F32)
        g.HT2 = k.dram("HT2", [8, 128, TT], BF16)
        g.identb = k.sb("identb", [128, 128], BF16)
        g.identf = k.sb("identf", [128, 128], F32)
        g.bm = k.sb("bm", [128, 128], BF16)
        g.pm = k.sb("pm", [128, 128], F32)
        g.wgm = k.sb("wgm", [128, 3, 128], BF16)
        g.modT = k.sb("modT", [128, L, 96], F32)
        g.AB = k.sb("AB", [128, L * 2 * 4, 8], F32)
        g.gT = k.sb("gT", [128, L, 2, 8], F32)
        g.qkg = k.sb("qkg", [128, L, 4], F32)
        g.sink = k.sb("sink", [128, L, 6], F32)
        g.convc = k.sb("convc", [128, L, 2, 3], F32)
        g.convf = k.sb("convf", [128, L, 44, 3], F32)
        k.dummy = k.sb("dummy", [128, 4], F32)
        g.epsb = k.sb("epsb", [128, 1], F32)

        p0_mods(k, g)
        if stop_after == "p0":
            k.s.emit_final()
            return nc
        for l in range(nlayers):
            if only is None or "p1" in only:
                p1_inproj(k, g, l)
            if stop_after == "p1":
                break
            if only is None or "p2" in only:
                p2_mixers(k, g, l)
            if stop_after == "p2":
                break
            if only is None or "p3" in only:
                p3_ffn(k, g, l, 0)
                p3_ffn(k, g, l, 1)
        k.s.emit_final()
    return nc


def abv(g, l, var, which):
    return g.AB[:, (l * 2 + var) * 4 + which, :]


def p0_mods(k, g):
    din = g.din
    with phase(k):
        for nm in ("identb", "identf", "bm", "pm", "wgm", "gT", "qkg", "sink", "convc", "convf"):
            t = getattr(g, nm)
            k.dma(t[:], din[nm], w=[t.b])
        k.memset("dve", g.epsb[:], EPS, w=[g.epsb.b])
        for gi in (0, 2):
            k.ts("dve", g.qkg[:, :, gi:gi + 1], g.qkg[:, :, gi:gi + 1], 0.125, None, ALU.mult, r=[g.qkg.b], w=[g.qkg.b])
        cvt = k.sb("cvt", [128, 8, 2], F32)
        sct = k.sb("sct", [128, 8, 2], F32)
        k.dma(cvt[:], din["cvt"], w=[cvt.b])
        k.act(sct[:], cvt[:], AF.Silu, r=[cvt.b], w=[sct.b])
        NB0 = 6
        wst = [k.sb("wst%d" % i, [128, 8, 512], F32) for i in range(NB0)]
        bad = [k.sb("bad%d" % i, [2, 512], F32) for i in range(NB0)]
        mrow = [k.sb("mrow%d" % i, [2, 512], F32) for i in range(NB0)]
        pmm = [k.ps("p0m%d" % i) for i in range(NB0)]
        pT = k.ps("p0T")
        chunks = [(l_, n_) for l_ in range(L) for n_ in range(12)]

        def p0_load(ci):
            l_, n_ = chunks[ci]
            i = ci % NB0
            wv = din["w_ada"][l_].rearrange("(kc p) n -> p kc n", p=128)
            k.dma(wst[i][:], wv[:, :, n_ * 512:(n_ + 1) * 512], w=[wst[i].b])
            for r_ in range(2):
                k.dma(bad[i][r_:r_ + 1, :], din["b_ada"][l_:l_ + 1, n_ * 512:(n_ + 1) * 512], w=[bad[i].b])

        PF = NB0 - 2
        for ci in range(PF):
            p0_load(ci)
        for l in range(L):
            for n in range(12):
                ci = l * 12 + n
                i = ci % NB0
                if ci + PF < len(chunks):
                    p0_load(ci + PF)
                for kc in range(8):
                    k.mm(pmm[i][0:2, :], sct[:, kc, :], wst[i][:, kc, :], start=(kc == 0),
                         r=[sct.b, wst[i].b], w=[pmm[i].b])
                k.tt("dve", mrow[i][:], pmm[i][0:2, :], bad[i][:], ALU.add, r=[bad[i].b], w=[pmm[i].b, mrow[i].b])
                k.dma(g.modrow[:, l * 6144 + n * 512:l * 6144 + (n + 1) * 512], mrow[i][:], r=[mrow[i].b])
                for j in range(4):
                    idx = n * 4 + j
                    k.mm(pT[:, idx * 2:idx * 2 + 2], mrow[i][0:2, j * 128:(j + 1) * 128], g.identf[0:2, 0:2],
                         start=(idx == 0), r=[mrow[i].b, g.identf.b], w=[pT.b])
            k.cp("dve", g.modT[:, l, :], pT[:, 0:96], w=[pT.b, g.modT.b])
            mv = g.modT[:, l, :].rearrange("p (c v) -> p c v", v=2)
            for var in range(2):
                k.stt("dve", abv(g, l, var, 0), mv[:, 8:16, var], 1.0, g.gT[:, l, 0, :], ALU.add, ALU.mult,
                      r=[g.modT.b, g.gT.b], w=[g.AB.b])
                k.cp("dve", abv(g, l, var, 1), mv[:, 0:8, var], r=[g.modT.b], w=[g.AB.b])
                k.stt("dve", abv(g, l, var, 2), mv[:, 32:40, var], 1.0, g.gT[:, l, 1, :], ALU.add, ALU.mult,
                      r=[g.modT.b, g.gT.b], w=[g.AB.b])
                k.cp("dve", abv(g, l, var, 3), mv[:, 24:32, var], r=[g.modT.b], w=[g.AB.b])


def norm_rows(k, xt, s, ss, junk, r=()):
    k.act(junk[:], xt, AF.Square, r=list(r), w=[junk.b, ss.b], accum_out=ss[:, s:s + 1], scale=1.0 / 32.0)


W1_QA, W1_KA, W1_QB, W1_KBD, W1_U, W1_CG, W1_BG, W1_V = 0, 384, 768, 1152, 1408, 1664, 1920, 2176
W1_N = 2688


def p1_inproj(k, g, l):
    din = g.din
    with phase(k):
        W = k.sb("w1", [128, 8, W1_N], BF16)
        src = din["w_in"][l].rearrange("(kc p) n -> p kc n", p=128)
        segs = [(0, 128, W1_QA), (128, 384, W1_QA + 128), (384, 768, W1_KA), (1152, 1536, W1_QB),
                (1536, 1600, W1_KBD), (1536, 1600, W1_KBD + 64),
                (1600, 1664, W1_KBD + 128), (1600, 1664, W1_KBD + 192),
                (1792, 2048, W1_U), (2304, 2560, W1_CG), (2048, 2304, W1_BG),
                (768, 1152, W1_V), (1664, 1792, W1_V + 384)]
        WS = wload_segs(k, src, 8, W, segs)

        xt = [k.sb("xt%d" % i, [128, 4, D], F32) for i in range(2)]
        cs = [k.sb("cs%d" % i, [128, 2, 512], F32) for i in range(2)]
        xn = [k.sb("xn%d" % i, [128, 4, D], BF16) for i in range(2)]
        junk = k.sb("junk", [128, D], BF16)
        ss = [k.sb("ss%d" % i, [128, 4], F32) for i in range(2)]
        rs = [k.sb("rs%d" % i, [128, 4], F32) for i in range(2)]
        hTs = [k.sb("hT%d" % i, [128, 8, 512], BF16) for i in range(2)]
        sq = [k.sb("sq%d" % i, [128, 512], BF16) for i in range(3)]
        rstd = [k.sb("rstd%d" % i, [128, 512], F32) for i in range(3)]
        qn = [k.sb("qn%d" % i, [128, 512], F32) for i in range(3)]
        t1 = [k.sb("t1%d" % i, [128, 512], F32) for i in range(3)]
        t2 = [k.sb("t2%d" % i, [128, 512], F32) for i in range(3)]
        ob = [k.sb("ob%d" % i, [128, 512], BF16) for i in range(3)]
        usb = [k.sb("usb%d" % i, [128, 512], F32) for i in range(2)]
        of = [k.sb("of%d" % i, [128, 512], F32) for i in range(3)]
        vt = [k.sb("vt%d" % i, [128, 8, 128], BF16) for i in range(2)]
        for v_ in vt:
            k.memset("pool", v_[:, :, 64:128], 1.0, w=[v_.b])
        pT = [k.ps("pT%d" % i, BF16, 1024) for i in range(2)]
        pq = [k.ps("pq%d" % i) for i in range(4)]
        pmn = k.ps("pmn")
        pr = k.ps("pr")

        tiles = [(i * 512, 512, 0) for i in range(8)] + [(S, CT, 1)]
        cnt = {"ob": 0, "of": 0, "pq": 0, "a": 0, "vt": 0, "pT": 0}

        def load(ti):
            tok0, n, var = tiles[ti]
            nsub = n // 128
            b = ti % 2
            if var == 0:
                srcx = (din["x"] if l == 0 else g.XM[:])[tok0:tok0 + n, :]
                rd = [] if l == 0 else [g.XM.b]
            else:
                srcx = din["ctx"] if l == 0 else g.XM[S:S + CT, :]
                rd = [] if l == 0 else [g.XM.b]
            k.dma(xt[b][:, 0:nsub, :], srcx.rearrange("(s p) f -> p s f", p=128), w=[xt[b].b])

        def load_cs(ti):
            tok0, n, var = tiles[ti]
            b = ti % 2
            if var == 0:
                k.dma(cs[b][:, :, 0:n], din["rope"][:, :, tok0:tok0 + n], w=[cs[b].b])

        def norm(ti):
            tok0, n, var = tiles[ti]
            nsub = n // 128
            b = ti % 2
            xn_ = xn[b]
            for s in range(nsub):
                norm_rows(k, xt[b][:, s, :], s, ss[b], junk, r=[xt[b].b])
            k.rsqrt(rs[b][:, 0:nsub], ss[b][:, 0:nsub], g.epsb, r=[ss[b].b], w=[rs[b].b])
            for s in range(nsub):
                if s % 2 == 0:
                    k.ts("dve", xn_[:, s, :], xt[b][:, s, :], rs[b][:, s:s + 1], None, ALU.mult,
                         r=[xt[b].b, rs[b].b], w=[xn_.b])
                else:
                    k.act(xn_[:, s, :], xt[b][:, s, :], AF.Copy, r=[xt[b].b, rs[b].b], w=[xn_.b], scale=rs[b][:, s:s + 1])

        def trans(ti):
            tok0, n, var = tiles[ti]
            nsub = n // 128
            xn_ = xn[ti % 2]
            hT_ = hTs[ti % 2]
            for kc in range(8):
                p = pT[cnt["pT"] % 2]
                cnt["pT"] += 1
                for s in range(nsub):
                    k.tr(p[:, s * 128:(s + 1) * 128], xn_[:, s, kc * 128:(kc + 1) * 128], g.identb[:],
                         r=[xn_.b, g.identb.b], w=[p.b])
                k.act(hT_[:, kc, 0:n], p[:, 0:n], AF.Identity, r=[g.AB.b], w=[p.b, hT_.b],
                      scale=abv(g, l, var, 0)[:, kc:kc + 1], bias=abv(g, l, var, 1)[:, kc:kc + 1])

        load(0)
        load_cs(0)
        load(1)
        norm(0)
        trans(0)
        chunks = []

        def add_chunk(A, B=None, C=None, pre=None):
            chunks.append((A, B, C, pre))

        def make_tile(ti):
            tok0, n, var = tiles[ti]
            nsub = n // 128
            b = ti % 2
            hT = hTs[b]

            def proj(wc0):
                p = pq[cnt["pq"] % 4]
                cnt["pq"] += 1
                wb = WS.bufs(wc0, wc0 + 128)
                for kc in range(8):
                    k.mm(p[:, 0:n], W[:, kc, wc0:wc0 + 128], hT[:, kc, 0:n], start=(kc == 0), r=wb + [hT.b], w=[p.b])
                return p

            def qk_chunk(wc0, gi, rope, qkidx, pre=None):
                cell = {}

                def A():
                    p = proj(wc0)
                    a = cnt["a"] % 3
                    cnt["a"] += 1
                    cell["p"], cell["a"] = p, a
                    k.act(sq[a][:, 0:n], p[:, 0:n], AF.Square, w=[p.b, sq[a].b])

                def B():
                    p, a = cell["p"], cell["a"]
                    k.mm(pmn[:, 0:n], g.bm[:], sq[a][:, 0:n], start=True, r=[g.bm.b, sq[a].b], w=[pmn.b])
                    k.rsqrt(rstd[a][:, 0:n], pmn[:, 0:n], g.epsb, w=[rstd[a].b], inw=[pmn.b])
                    if not rope:
                        o = ob[cnt["ob"] % 3]
                        cnt["ob"] += 1
                        k.stt("dve", o[:, 0:n], p[:, 0:n], g.qkg[:, l, gi:gi + 1], rstd[a][:, 0:n], ALU.mult, ALU.mult,
                              r=[rstd[a].b, g.qkg.b], w=[p.b, o.b])
                        k.dma(g.QK[qkidx, :, tok0:tok0 + n], o[:, 0:n], r=[o.b])
                    else:
                        k.stt("dve", qn[a][:, 0:n], p[:, 0:n], g.qkg[:, l, gi:gi + 1], rstd[a][:, 0:n], ALU.mult, ALU.mult,
                              r=[rstd[a].b, g.qkg.b], w=[p.b, qn[a].b])

                def C():
                    a = cell["a"]
                    o = ob[cnt["ob"] % 3]
                    cnt["ob"] += 1
                    k.mm(pr[:, 0:n], g.pm[:], qn[a][:, 0:n], start=True, r=[g.pm.b, qn[a].b], w=[pr.b])
                    k.tt("pool", t1[a][:, 0:n], qn[a][:, 0:n], cs[b][:, 0, 0:n], ALU.mult, r=[qn[a].b, cs[b].b], w=[t1[a].b])
                    k.tt("dve", t2[a][:, 0:n], pr[:, 0:n], cs[b][:, 1, 0:n], ALU.mult, r=[cs[b].b], w=[pr.b, t2[a].b])
                    k.tt("pool", o[:, 0:n], t1[a][:, 0:n], t2[a][:, 0:n], ALU.add, r=[t1[a].b, t2[a].b], w=[o.b])
                    k.dma(g.QK[qkidx, :, tok0:tok0 + n], o[:, 0:n], r=[o.b])

                add_chunk(A, B, C if rope else None, pre)

            def tile_pre():
                if ti + 1 < len(tiles):
                    norm(ti + 1)
                if ti + 2 < len(tiles):
                    load(ti + 2)

            def mid_pre():
                if ti + 1 < len(tiles):
                    trans(ti + 1)
                    load_cs(ti + 1)

            for c in range(3):
                qk_chunk(W1_QA + c * 128, 0, False, c, pre=tile_pre if c == 0 else None)
            for c in range(3):
                qk_chunk(W1_KA + c * 128, 1, False, 3 + c)
            for c in range(3):
                qk_chunk(W1_QB + c * 128, 2, var == 0, 6 + c, pre=mid_pre if c == 0 else None)
            for c in range(2):
                qk_chunk(W1_KBD + c * 128, 3, var == 0, 9 + c)
            for c in range(2):
                ucell = {}

                def A_u(c=c, ucell=ucell):
                    pu = proj(W1_U + c * 128)
                    a = cnt["u"] % 2
                    cnt["u"] += 1
                    ucell["a"] = a
                    k.cp("act", usb[a][:, 0:n], pu[:, 0:n], w=[pu.b, usb[a].b])

                def A_cg(c=c, ucell=ucell):
                    a = ucell["a"]
                    pc = proj(W1_CG + c * 128)
                    o = of[cnt["of"] % 3]
                    cnt["of"] += 1
                    k.tt("dve", o[:, 0:n], pc[:, 0:n], usb[a][:, 0:n], ALU.mult, r=[usb[a].b], w=[pc.b, o.b])
                    k.dma(g.CU[c, :, tok0:tok0 + n], o[:, 0:n], r=[o.b])

                def A_bg(c=c):
                    pb = proj(W1_BG + c * 128)
                    o = of[cnt["of"] % 3]
                    cnt["of"] += 1
                    k.cp("act", o[:, 0:n], pb[:, 0:n], w=[pb.b, o.b])
                    k.dma(g.BG[c, :, tok0:tok0 + n], o[:, 0:n], r=[o.b])

                add_chunk(A_u)
                add_chunk(A_cg)
                add_chunk(A_bg)
            for s in range(nsub):
                def A_v(s=s):
                    pv_ = pq[cnt["pq"] % 4]
                    cnt["pq"] += 1
                    wb = WS.bufs(W1_V, W1_V + 512)
                    for kc in range(8):
                        k.mm(pv_[:, :], hT[:, kc, s * 128:(s + 1) * 128], W[:, kc, W1_V:W1_V + 512], start=(kc == 0),
                             r=wb + [hT.b], w=[pv_.b])
                    v = vt[cnt["vt"] % 2]
                    cnt["vt"] += 1
                    k.cp("act" if s % 2 == 0 else "dve", v[:, :, 0:64], pv_[:, :].rearrange("p (h d) -> p h d", d=64),
                         w=[pv_.b, v.b])
                    k.dma(g.VA[tok0 + s * 128:tok0 + (s + 1) * 128, :, :], v[:, 0:6, :], r=[v.b])
                    k.dma(g.VB[tok0 + s * 128:tok0 + (s + 1) * 128, :, :], v[:, 6:8, :], r=[v.b])

                add_chunk(A_v)

        cnt["u"] = 0
        for ti in range(len(tiles)):
            make_tile(ti)
        nch = len(chunks)
        for i in range(nch + 2):
            if i < nch:
                A, B, C, pre = chunks[i]
                if pre is not None:
                    pre()
                A()
            if 0 <= i - 1 < nch and chunks[i - 1][1] is not None:
                chunks[i - 1][1]()
            if 0 <= i - 2 < nch and chunks[i - 2][2] is not None:
                chunks[i - 2][2]()

def bcast_row(dt_, row, off, n):
    ncols = dt_.t.shape[1]
    return bass.AP(dt_.h, row * ncols + off, [[0, 128], [1, n]])


def p3_tiles(l):
    t = []
    s0 = 0
    while s0 < S:
        n = min(510, S - s0)
        t.append((s0, n, 0))
        s0 += n
    if l == 0:
        t.append((S, CT, 1))
    return t


def p3_ffn(k, g, l, hf):
    din = g.din
    HC = 11
    with phase(k):
        WU = k.sb("wu", [128, 8, 2 * HC * 128], BF16)
        WD = k.sb("wd", [128, HC, D], BF16)
        srcu = din["w_up"][l].rearrange("(kc p) n -> p kc n", p=128)
        a0 = hf * HC * 128
        CG = [(0, 1), (1, 2), (2, 4), (4, 7), (7, HC)]
        usegs = []
        for (c0, c1) in CG:
            usegs.append((a0 + c0 * 128, a0 + c1 * 128, c0 * 128))
            usegs.append((DFF + a0 + c0 * 128, DFF + a0 + c1 * 128, HC * 128 + c0 * 128))
        WUS = wload_segs(k, srcu, 8, WU, usegs)
        srcd = din["w_down"][l][a0:a0 + HC * 128, :].rearrange("(hc p) n -> p hc n", p=128)
        WDS = wload_segs(k, srcd, HC, WD, [(0, 512, 0), (512, 1024, 512)])
        gtb = [k.sb("gtb%d" % v, [128, D], F32) for v in range(2)]
        for v in range(2 if l == 0 else 1):
            k.dma(gtb[v][:], bcast_row(g.modrow, v, l * 6144 + 5 * D, D), w=[gtb[v].b])
        ht = [k.sb("ht%d" % i, [128, 8, 512], BF16) for i in range(2)]
        actT = [k.sb("actT%d" % i, [128, HC, 512], BF16) for i in range(2)]
        t1 = [k.sb("t1%d" % i, [128, 512], F32) for i in range(2)]
        t2 = [k.sb("t2%d" % i, [128, 512], F32) for i in range(2)]
        ca = [k.sb("ca%d" % i, [128, 512], F32) for i in range(2)]
        cg = [k.sb("cg%d" % i, [128, 512], F32) for i in range(2)]
        sa = [k.sb("sa%d" % i, [128, 512], F32) for i in range(2)]
        xt = [k.sb("xt%d" % i, [128, D], F32) for i in range(8)]
        xo = [k.sb("xo%d" % i, [128, D], F32) for i in range(2)]
        tmp = [k.sb("tmp%d" % i, [128, 512], F32) for i in range(2)]
        pa = [k.ps("pa%d" % i) for i in range(2)]
        pg = [k.ps("pg%d" % i) for i in range(2)]
        po = [k.ps("po%d" % i) for i in range(3)]
        xsrc = g.XN if hf == 0 else g.XP
        tiles = p3_tiles(l)
        cnt = {"x": 0, "o": 0, "po": 0, "c": 0, "tmp": 0}

        def load(ti):
            s0, n, var = tiles[ti]
            h = ht[ti % 2]
            lo, hi = (0, S) if var == 0 else (S, S + CT)
            a, b_ = max(s0 - 1, lo), min(s0 + n + 1, hi)
            c0 = a - (s0 - 1)
            if c0 > 0:
                k.memset("pool", h[:, :, 0:c0], 0.0, w=[h.b])
            if b_ < s0 + n + 1:
                k.memset("pool", h[:, :, n + 1:n + 2], 0.0, w=[h.b])
            k.dma(h[:, :, c0:c0 + (b_ - a)], g.HT2[:, :, a:b_].rearrange("c p t -> p c t"), w=[h.b])

        def load_x(ti):
            s0, n, var = tiles[ti]
            nsub = (n + 127) // 128
            bufs = []
            for j in range(nsub):
                m = min(128, n - j * 128)
                r0 = s0 + j * 128
                x_ = xt[cnt["x"] % 8]
                cnt["x"] += 1
                k.dma(x_[0:m, :], xsrc[r0:r0 + m, :], w=[x_.b])
                bufs.append(x_)
            return bufs

        def up_chunk(ti, c):
            s0, n, var = tiles[ti]
            h = ht[ti % 2]
            aT = actT[ti % 2]
            cols = n + 2
            i = cnt["c"] % 2
            cnt["c"] += 1
            for (pp, wc0) in ((pa[i], c * 128), (pg[i], HC * 128 + c * 128)):
                wb = WUS.bufs(wc0, wc0 + 128)
                for kc in range(8):
                    k.mm(pp[:, 0:cols], WU[:, kc, wc0:wc0 + 128], h[:, kc, 0:cols], start=(kc == 0),
                         r=wb + [h.b], w=[pp.b])
            for (pp, dst, ci) in ((pa[i], ca[i], hf * HC + c), (pg[i], cg[i], 22 + hf * HC + c)):
                w3 = g.convf[:, l, ci, :]
                k.act(t1[i][:, 0:n], pp[:, 1:n + 1], AF.Copy, r=[g.convf.b], w=[pp.b, t1[i].b], scale=w3[:, 1:2])
                k.stt("dve", t2[i][:, 0:n], pp[:, 0:n], w3[:, 0:1], t1[i][:, 0:n], ALU.mult, ALU.add,
                      r=[g.convf.b, t1[i].b], w=[pp.b, t2[i].b])
                k.stt("dve", dst[:, 0:n], pp[:, 2:n + 2], w3[:, 2:3], t2[i][:, 0:n], ALU.mult, ALU.add,
                      r=[g.convf.b, t2[i].b], w=[pp.b, dst.b])
            k.act(sa[i][:, 0:n], ca[i][:, 0:n], AF.Silu, r=[ca[i].b], w=[sa[i].b])
            k.tt("pool", aT[:, c, 0:n], sa[i][:, 0:n], cg[i][:, 0:n], ALU.mult, r=[sa[i].b, cg[i].b], w=[aT.b])

        def down(ti, xbufs):
            s0, n, var = tiles[ti]
            aT = actT[ti % 2]
            nsub = (n + 127) // 128
            for j in range(nsub):
                m = min(128, n - j * 128)
                r0 = s0 + j * 128
                x_ = xbufs[j]
                o_ = xo[cnt["o"] % 2]
                cnt["o"] += 1
                for hh in range(2):
                    p_ = po[cnt["po"] % 3]
                    cnt["po"] += 1
                    wb = WDS.bufs(hh * 512, (hh + 1) * 512)
                    for hc in range(HC):
                        k.mm(p_[0:m, :], aT[:, hc, j * 128:j * 128 + m], WD[:, hc, hh * 512:(hh + 1) * 512],
                             start=(hc == 0), r=[aT.b] + wb, w=[p_.b])
                    t_ = tmp[cnt["tmp"] % 2]
                    cnt["tmp"] += 1
                    k.tt("dve", t_[0:m, :], p_[0:m, :], gtb[var][0:m, hh * 512:(hh + 1) * 512], ALU.mult,
                         r=[gtb[var].b], w=[p_.b, t_.b])
                    k.tt("pool", o_[0:m, hh * 512:(hh + 1) * 512], x_[0:m, hh * 512:(hh + 1) * 512], t_[0:m, :], ALU.add,
                         r=[x_.b, t_.b], w=[o_.b])
                if hf == 0:
                    dst = g.XP[r0:r0 + m, :]
                elif l == L - 1:
                    dst = g.out[r0:r0 + m, :]
                else:
                    dst = g.XM[r0:r0 + m, :]
                k.dma(dst, o_[0:m, :], r=[o_.b])

        NPRE = 2
        load(0)
        for c in range(HC):
            up_chunk(0, c)
        for ti in range(len(tiles)):
            if ti + 1 < len(tiles):
                load(ti + 1)
            xbufs = load_x(ti)
            if ti + 1 < len(tiles):
                for c in range(NPRE):
                    up_chunk(ti + 1, c)
            down(ti, xbufs)
            if ti + 1 < len(tiles):
                for c in range(NPRE, HC):
                    up_chunk(ti + 1, c)

WARM_N = 0
WARM_EVERY = 1
KC0 = (0, 8, 24, 32)


def na_rowcfgs(a):
    if a == 0:
        return [(0, 3), (1, 4)]
    if a == 15:
        return [(14, 5), (15, 6)]
    return [(a - 1, 0), (a, 1), (a + 1, 2)]


def p2_mixers(k, g, l):
    din = g.din
    N = 256
    with phase(k):
        WO = k.sb("wo", [128, 8, D], BF16)
        NAB = k.sb("nab", [128, 8, NTILE * 8], BF16)
        stg = [k.sb("stg%d" % i, [128, 8, 128], F32) for i in range(2)]
        def load_p2_weights():
            wload(k, stg, din["nab"][:, l, :].rearrange("p (a c) -> p a c", a=8), 8, NTILE * 8, [(0, NTILE * 8, NAB, 0)],
                  blk=128, func=AF.Exp)
            wload_cast(k, din["w_o"][l].rearrange("(kc p) n -> p kc n", p=128), 8, [(0, D, WO, 0)])
        nabf = NAB[:, :, :].rearrange("p a c -> p (a c)")
        nvar = 2 if l == 0 else 1
        gtb = [k.sb("gtb%d" % v, [128, D], F32) for v in range(nvar)]
        for v in range(nvar):
            k.dma(gtb[v][:], bcast_row(g.modrow, v, l * 6144 + 2 * D, D), w=[gtb[v].b])
        es = k.sb("es", [128, 6], F32)
        k.act(es[:], g.sink[:, l, :], AF.Exp, r=[g.sink.b], w=[es.b])
        KAc = k.sb("kac", [128, 3, CT], BF16)
        KBc = k.sb("kbc", [128, 2, CT], BF16)
        k.dma(KAc[:], g.QK[3:6, :, S:S + CT].rearrange("c p t -> p c t"), w=[KAc.b])
        k.dma(KBc[:], g.QK[9:11, :, S:S + CT].rearrange("c p t -> p c t"), w=[KBc.b])
        VAc = [k.sb("vac%d" % i, [128, 6, 128], BF16) for i in range(2)]
        VBc = [k.sb("vbc%d" % i, [128, 2, 128], BF16) for i in range(2)]
        for ct in range(2):
            k.dma(VAc[ct][:], g.VA[S + ct * 128:S + (ct + 1) * 128, :, :], w=[VAc[ct].b])
            k.dma(VBc[ct][:], g.VB[S + ct * 128:S + (ct + 1) * 128, :, :], w=[VBc[ct].b])
        KAn = [k.sb("kan%d" % i, [128, 3, 256], BF16) for i in range(2)]
        KAg = [[k.sb("kag%d_%d" % (i, j), [128, 3, 128], BF16) for j in range(4)] for i in range(4)]
        VAr = [[k.sb("var%d_%d" % (i, j), [128, 6, 128], BF16) for j in range(4)] for i in range(4)]
        KBr = [k.sb("kbr%d" % i, [128, 2, 128], BF16) for i in range(6)]
        VBr = [k.sb("vbr%d" % i, [128, 2, 128], BF16) for i in range(6)]

        def load_group(b):
            if b < 0 or b > 15:
                return
            sl = b % 4
            t0 = b * 256
            kn = KAn[b % 2]
            k.dma(kn[:], g.QK[3:6, :, t0:t0 + 256].rearrange("c p t -> p c t"), w=[kn.b])
            for j in range(4):
                for c in range(3):
                    k.cp("pool", KAg[sl][j][:, c, :].rearrange("p (r x) -> p r x", x=32),
                         kn[:, c, :].rearrange("p (r x) -> p r x", x=64)[:, :, KC0[j]:KC0[j] + 32],
                         r=[kn.b], w=[KAg[sl][j].b])
                for kr in range(4):
                    r0 = t0 + kr * 64 + KC0[j]
                    k.dma(VAr[sl][j][kr * 32:(kr + 1) * 32, :, :], g.VA[r0:r0 + 32, :, :], w=[VAr[sl][j].b])

        def load_kt(kt):
            if kt < 0 or kt > 31:
                return
            sl = kt % 6
            t0 = kt * 128
            k.dma(KBr[sl][:], g.QK[9:11, :, t0:t0 + 128].rearrange("c p t -> p c t"), w=[KBr[sl].b])
            k.dma(VBr[sl][:], g.VB[t0:t0 + 128, :, :], w=[VBr[sl].b])

        q = [k.sb("q%d" % i, [128, 6, N], BF16) for i in range(2)]
        cu = [k.sb("cu%d" % i, [128, 2, N + 2], F32) for i in range(2)]
        bg = [k.sb("bg%d" % i, [128, 2, N], F32) for i in range(2)]
        P = [k.sb("P%d" % i, [128, 512], BF16) for i in range(6)]
        YT = k.sb("YT", [128, 8, N], BF16)
        YTc = Buf("YTconv")
        rc = [k.sb("rc%d" % i, [128, N], F32) for i in range(2)]
        c1 = [k.sb("c1%d" % i, [128, N], F32) for i in range(2)]
        c2 = [k.sb("c2%d" % i, [128, N], F32) for i in range(2)]
        xt = [k.sb("xt%d" % i, [128, D], F32) for i in range(4)]
        xo = [k.sb("xo%d" % i, [128, D], F32) for i in range(2)]
        tmp = [k.sb("tmp%d" % i, [128, 512], F32) for i in range(2)]
        xn = k.sb("xn", [128, 2, D], BF16)
        junk = k.sb("junk", [128, D], BF16)
        ss = k.sb("ss", [128, 2], F32)
        rs = k.sb("rs", [128, 2], F32)
        h2o = k.sb("h2o", [128, 8, N], BF16)
        pS = [k.ps("pS%d" % i) for i in range(6)]
        pO = [k.ps("pO%d" % i) for i in range(2)]
        cnt = {"pS": 0, "pO": 0, "P": 0, "rc": 0, "x": 0, "o": 0, "po": 0, "tmp": 0, "c": 0}

        tiles = [(i * N, N, 0) for i in range(S // N)] + ([(S, CT, 1)] if l == 0 else [])

        def load_tile(ti):
            tok0, n, var = tiles[ti]
            b_ = ti % 2
            k.dma(q[b_][:, 0:3, 0:n], g.QK[0:3, :, tok0:tok0 + n].rearrange("c p t -> p c t"), w=[q[b_].b])
            k.dma(q[b_][:, 3:6, 0:n], g.QK[6:9, :, tok0:tok0 + n].rearrange("c p t -> p c t"), w=[q[b_].b])
            lo, hi = (0, S) if var == 0 else (S, S + CT)
            a, e = max(tok0 - 1, lo), min(tok0 + n + 1, hi)
            c0 = a - (tok0 - 1)
            if c0 > 0:
                k.memset("pool", cu[b_][:, :, 0:1], 0.0, w=[cu[b_].b])
            if e < tok0 + n + 1:
                k.memset("pool", cu[b_][:, :, n + 1:n + 2], 0.0, w=[cu[b_].b])
            k.dma(cu[b_][:, :, c0:c0 + (e - a)], g.CU[:, :, a:e].rearrange("c p t -> p c t"), w=[cu[b_].b])
            k.dma(bg[b_][:, :, 0:n], g.BG[:, :, tok0:tok0 + n].rearrange("c p t -> p c t"), w=[bg[b_].b])

        def next_ps(name, arr):
            p = arr[cnt[name] % len(arr)]
            cnt[name] += 1
            return p

        def ctx_part(qh, qbuf, Kc, kch, pb, Vc, vh, n, pOut):
            for ct in range(2):
                ps_ = next_ps("pS", pS)
                k.mm(ps_[:, 0:n], Kc[pb:pb + 64, kch, ct * 128:(ct + 1) * 128], qh, start=True,
                     r=[Kc.b, qbuf], w=[ps_.b])
                pp = next_ps("P", P)
                k.act(pp[:, 0:n], ps_[:, 0:n], AF.Exp, w=[ps_.b, pp.b])
                k.mm(pOut[:, 0:n], Vc[ct][:, vh, :], pp[:, 0:n], start=(ct == 0), r=[Vc[ct].b, pp.b], w=[pOut.b])

        wmask = {}

        def get_mask(pattern):
            if pattern not in wmask:
                t = k.sb("wm%d" % len(wmask), [128, len(pattern) * 128], BF16)
                for i, mk in enumerate(pattern):
                    k.cp("pool", t[:, i * 128:(i + 1) * 128], g.wgm[:, mk, :], r=[g.wgm.b], w=[t.b])
                wmask[pattern] = t
            return wmask[pattern]

        def blk(ap):
            return ap.rearrange("p (j r c) -> p j r c", j=4, r=4, c=16)

        def finalize(pOut, n, h_extra, dst, blocked):
            r_ = rc[cnt["rc"] % 2]
            cnt["rc"] += 1
            if h_extra is None:
                k.act(r_[64:128, 0:n], pOut[64:128, 0:n], AF.Ln, w=[pOut.b, r_.b])
            else:
                k.act(r_[64:128, 0:n], pOut[64:128, 0:n], AF.Ln, r=[es.b], w=[pOut.b, r_.b],
                      bias=es[64:128, h_extra:h_extra + 1])
            k.act(r_[64:128, 0:n], r_[64:128, 0:n], AF.Exp, r=[r_.b], w=[r_.b], scale=-1.0)
            if blocked:
                k.tt("dve", dst.rearrange("p (r j c) -> p j r c", r=4, j=4, c=16), blk(pOut[0:64, 0:n]),
                     blk(r_[64:128, 0:n]), ALU.mult, r=[r_.b], w=[pOut.b, YT.b])
            else:
                k.tt("dve", dst, pOut[0:64, 0:n], r_[64:128, 0:n], ALU.mult, r=[r_.b], w=[pOut.b, YT.b])

        class U:
            __slots__ = ("pre", "qk", "ex", "pv", "fin", "post", "post2")

            def __init__(self):
                self.pre = []
                self.qk = self.ex = self.pv = self.fin = None
                self.post = []
                self.post2 = []

        def make_tile_units(ti):
            tok0, n, var = tiles[ti]
            b_ = ti % 2
            a = ti
            qq = q[b_]
            units = []

            def ctx_units(qh, Kc, kch, pb, Vc, vh, pO_cell, blocked=False):
                u = U()
                cell = {}

                def qk(cell=cell):
                    ps_ = next_ps("pS", pS)
                    cell["ps"] = ps_
                    for ct in range(2):
                        k.mm(ps_[:, ct * n:(ct + 1) * n], Kc[pb:pb + 64, kch, ct * 128:(ct + 1) * 128], qh, start=(ct == 0),
                             r=[Kc.b, qq.b], w=[ps_.b])

                def ex(cell=cell):
                    pp = next_ps("P", P)
                    cell["pp"] = pp
                    if blocked:
                        for ct in range(2):
                            k.act(pp[:, ct * n:(ct + 1) * n].rearrange("p (j r c) -> p r j c", j=4, r=4, c=16),
                                  cell["ps"][:, ct * n:(ct + 1) * n].rearrange("p (r j c) -> p r j c", r=4, j=4, c=16),
                                  AF.Exp, w=[cell["ps"].b, pp.b])
                    else:
                        k.act(pp[:, 0:2 * n], cell["ps"][:, 0:2 * n], AF.Exp, w=[cell["ps"].b, pp.b])

                def pv(cell=cell):
                    pO_cell["p"] = next_ps("pO", pO)
                    pOut = pO_cell["p"]
                    pp = cell["pp"]
                    for ct in range(2):
                        k.mm(pOut[:, 0:n], Vc[ct][:, vh, :], pp[:, ct * n:(ct + 1) * n], start=(ct == 0),
                             r=[Vc[ct].b, pp.b], w=[pOut.b])

                u.qk, u.ex, u.pv = qk, ex, pv
                units.append(u)

            for h in range(6):
                ch, pb = h // 2, 64 * (h % 2)
                qh = qq[pb:pb + 64, ch, 0:n]
                pO_cell = {}
                ctx_units(qh, KAc, ch, pb, VAc, h, pO_cell, blocked=(var == 0))
                if var == 0:
                    q3 = qq[pb:pb + 64, ch, :].rearrange("p (r c) -> p r c", c=64)
                    slots = []
                    for (b, rcfg) in na_rowcfgs(a):
                        for j in range(4):
                            slots.append((b, j, (h * 7 + rcfg) * 4 + j))
                    for s0 in range(0, len(slots), 8):
                        grp = slots[s0:s0 + 8]
                        u = U()
                        cell = {}

                        def qk(grp=grp, cell=cell, q3=q3, pb=pb, ch=ch):
                            ps_ = next_ps("pS", pS)
                            cell["ps"] = ps_
                            for i, (b, j, tix) in enumerate(grp):
                                kt_ = KAg[b % 4][j]
                                k.mm(ps_[:, i * 64:(i + 1) * 64], kt_[pb:pb + 64, ch, :], q3[:, :, 16 * j:16 * j + 16],
                                     start=(i == 0), r=[kt_.b, qq.b], w=[ps_.b])

                        def ex(grp=grp, cell=cell):
                            pp = next_ps("P", P)
                            cell["pp"] = pp
                            w_ = len(grp) * 64
                            k.act(pp[:, 0:w_], cell["ps"][:, 0:w_], AF.Exp, w=[cell["ps"].b, pp.b])
                            t0_ = grp[0][2] * 64
                            k.tt("dve", pp[:, 0:w_], pp[:, 0:w_], nabf[:, t0_:t0_ + w_], ALU.mult, r=[pp.b, NAB.b], w=[pp.b])

                        def pv(grp=grp, cell=cell, pO_cell=pO_cell, h=h):
                            pOut = pO_cell["p"]
                            pp = cell["pp"]
                            for i, (b, j, tix) in enumerate(grp):
                                vt_ = VAr[b % 4][j]
                                k.mm(pOut[:, j * 64:(j + 1) * 64], vt_[:, h, :], pp[:, i * 64:(i + 1) * 64],
                                     start=False, r=[vt_.b, pp.b], w=[pOut.b])

                        u.qk, u.ex, u.pv = qk, ex, pv
                        units.append(u)
                units[-1].fin = (lambda pO_cell=pO_cell, pb=pb, ch=ch: finalize(pO_cell["p"], n, None, YT[pb:pb + 64, ch, 0:n],
                                                                                    var == 0))
            for h in range(6):
                ch, pb, kv = 3 + h // 2, 64 * (h % 2), h // 3
                qh = qq[pb:pb + 64, ch, 0:n]
                pO_cell = {}
                ctx_units(qh, KBc, kv, pb, VBc, kv, pO_cell)
                if var == 0:
                    slots = []
                    for t in range(2):
                        i_ = 2 * a + t
                        for kt in (i_ - 1, i_, i_ + 1):
                            if 0 <= kt <= 31:
                                slots.append((t, kt, 0 if kt == i_ else (1 if kt < i_ else 2)))
                    for s0 in range(0, len(slots), 4):
                        grp = slots[s0:s0 + 4]
                        u = U()
                        cell = {}

                        def qk(grp=grp, cell=cell, pb=pb, ch=ch, kv=kv):
                            ps_ = next_ps("pS", pS)
                            cell["ps"] = ps_
                            for i, (t, kt, mk) in enumerate(grp):
                                kt_ = KBr[kt % 6]
                                k.mm(ps_[:, i * 128:(i + 1) * 128], kt_[pb:pb + 64, kv, :],
                                     qq[pb:pb + 64, ch, t * 128:(t + 1) * 128],
                                     start=(i == 0), r=[kt_.b, qq.b], w=[ps_.b])

                        def ex(grp=grp, cell=cell):
                            pp = next_ps("P", P)
                            cell["pp"] = pp
                            w_ = len(grp) * 128
                            k.act(pp[:, 0:w_], cell["ps"][:, 0:w_], AF.Exp, w=[cell["ps"].b, pp.b])
                            pat = tuple(mk for (_, _, mk) in grp)
                            if any(pat):
                                mt = get_mask(pat)
                                k.tt("dve", pp[:, 0:w_], pp[:, 0:w_], mt[:, 0:w_], ALU.mult, r=[pp.b, mt.b], w=[pp.b])

                        def pv(grp=grp, cell=cell, pO_cell=pO_cell, kv=kv):
                            pOut = pO_cell["p"]
                            pp = cell["pp"]
                            for i, (t, kt, mk) in enumerate(grp):
                                vt_ = VBr[kt % 6]
                                k.mm(pOut[:, t * 128:(t + 1) * 128], vt_[:, kv, :], pp[:, i * 128:(i + 1) * 128],
                                     start=False, r=[vt_.b, pp.b], w=[pOut.b])

                        u.qk, u.ex, u.pv = qk, ex, pv
                        units.append(u)
                units[-1].fin = (lambda pO_cell=pO_cell, pb=pb, ch=ch, h=h: finalize(pO_cell["p"], n, h, YT[pb:pb + 64, ch, 0:n],
                                                                                         False))

            xcell = {}

            def prefetch():
                if var == 0:
                    xsrc = (din["x"] if l == 0 else g.XM[:])[tok0:tok0 + n, :]
                else:
                    xsrc = din["ctx"]
                xcell["b"] = []
                for s in range(n // 128):
                    x_ = xt[cnt["x"] % 4]
                    cnt["x"] += 1
                    k.dma(x_[:], xsrc[s * 128:(s + 1) * 128, :], w=[x_.b])
                    xcell["b"].append(x_)
                if ti + 1 < len(tiles):
                    load_tile(ti + 1)
                if var == 0:
                    load_group(a + 2)
                    load_kt(2 * a + 3); load_kt(2 * a + 4)

            def conv():
                for c in range(2):
                    i = cnt["c"] % 2
                    cnt["c"] += 1
                    w3 = g.convc[:, l, c, :]
                    cuc = cu[b_]
                    k.act(c1[i][:, 0:n], cuc[:, c, 1:n + 1], AF.Copy, r=[cuc.b, g.convc.b], w=[c1[i].b], scale=w3[:, 1:2])
                    k.stt("dve", c2[i][:, 0:n], cuc[:, c, 0:n], w3[:, 0:1], c1[i][:, 0:n], ALU.mult, ALU.add,
                          r=[cuc.b, c1[i].b, g.convc.b], w=[c2[i].b])
                    k.stt("dve", c1[i][:, 0:n], cuc[:, c, 2:n + 2], w3[:, 2:3], c2[i][:, 0:n], ALU.mult, ALU.add,
                          r=[cuc.b, c2[i].b, g.convc.b], w=[c1[i].b])
                    k.tt("pool", YT[:, 6 + c, 0:n], c1[i][:, 0:n], bg[b_][:, c, 0:n], ALU.mult,
                         r=[c1[i].b, bg[b_].b], w=[YTc])

            ocell = []

            def wo_epi():
                nsub = n // 128
                for s in range(nsub):
                    x_ = xcell["b"][s]
                    o_ = xo[cnt["o"] % 2]
                    cnt["o"] += 1
                    for hh in range(2):
                        p_ = next_ps("pS", pS)
                        for kc in range(8):
                            k.mm(p_[:, :], YT[:, kc, s * 128:(s + 1) * 128], WO[:, kc, hh * 512:(hh + 1) * 512],
                                 start=(kc == 0), r=[YT.b, YTc, WO.b], w=[p_.b])
                        t_ = tmp[cnt["tmp"] % 2]
                        cnt["tmp"] += 1
                        k.tt("dve", t_[:], p_[:, :], gtb[var][:, hh * 512:(hh + 1) * 512], ALU.mult,
                             r=[gtb[var].b], w=[p_.b, t_.b])
                        k.tt("pool", o_[:, hh * 512:(hh + 1) * 512], x_[:, hh * 512:(hh + 1) * 512], t_[:], ALU.add,
                             r=[x_.b, t_.b], w=[o_.b])
                    k.dma(g.XN[tok0 + s * 128:tok0 + (s + 1) * 128, :], o_[:], r=[o_.b])
                    ocell.append(o_)

            def wo_sq(s):
                def f():
                    o_ = ocell[s]
                    norm_rows(k, o_[:], s, ss, junk, r=[o_.b])
                return f

            def wo_norm():
                nsub = len(ocell)
                k.rsqrt(rs[:, 0:nsub], ss[:, 0:nsub], g.epsb, r=[ss.b], w=[rs.b])
                for s, o_ in enumerate(ocell):
                    k.ts("dve", xn[:, s, :], o_[:], rs[:, s:s + 1], None, ALU.mult, r=[o_.b, rs.b], w=[xn.b])

            def transposes():
                nsub = n // 128
                for kc in range(8):
                    pt_ = next_ps("pS", pS)
                    ptb = pt_.t.bitcast(BF16)
                    for s in range(nsub):
                        k.tr(ptb[:, s * 128:(s + 1) * 128], xn[:, s, kc * 128:(kc + 1) * 128], g.identb[:],
                             r=[xn.b, g.identb.b], w=[pt_.b])
                    k.act(h2o[:, kc, 0:n], ptb[:, 0:n], AF.Identity, r=[g.AB.b], w=[pt_.b, h2o.b],
                          scale=abv(g, l, var, 2)[:, kc:kc + 1], bias=abv(g, l, var, 3)[:, kc:kc + 1])
                k.dma(g.HT2[:, :, tok0:tok0 + n].rearrange("c p t -> p c t"), h2o[:, :, 0:n], r=[h2o.b])

            def warm():
                pw = pT.t.bitcast(F32)
                for i in range(WARM_N):
                    k.mm(pw[:, 0:512], WO[:, i % 8, 0:128], WO[:, (i + 1) % 8, 0:512], start=True, r=[WO.b], w=[pT.b])

            if WARM_N and (ti % WARM_EVERY == 0):
                units[0].pre.append(warm)
            units[min(7, len(units) - 1)].pre.append(prefetch)
            units[min(6, len(units) - 1)].pre.append(conv)
            units[-1].post.append(wo_epi)
            units[-1].post2 = [(3, wo_sq(0)), (5, wo_sq(1)), (6, wo_norm)]
            return units, transposes

        load_tile(0)
        load_group(0)
        for kt in range(0, 3):
            load_kt(kt)
        load_p2_weights()
        load_group(1)
        allu = []
        pending_tr = None
        for ti in range(len(tiles)):
            us, trf = make_tile_units(ti)
            if pending_tr is not None:
                us[min(12, len(us) - 1)].pre.append(pending_tr)
            pending_tr = trf
            allu.extend(us)
        SK = 4
        FD = 1
        due = {}
        for i in range(len(allu) + SK + FD + 8):
            if i < len(allu):
                u = allu[i]
                for f in u.pre:
                    f()
                u.qk()
                u.ex()
            j = i - SK
            if 0 <= j < len(allu):
                u = allu[j]
                u.pv()
                fl = []
                if u.fin is not None:
                    fl.append(u.fin)
                fl.extend(u.post)
                if fl:
                    due.setdefault(i + FD, []).extend(fl)
                for (dl, f2) in u.post2:
                    due.setdefault(i + FD + dl, []).append(f2)
            for f in due.pop(i, []):
                f()
        assert not due
        pending_tr()

def _shared_inputs(inp, consts):
    m = _core_inputs(0, inp, consts)
    for kx in ("x", "ctx", "cvt"):
        m.pop(kx)
    return m


def kernel(**inputs):
    consts = _consts()
    shared = _shared_inputs(inputs, consts)
    in_maps = [_core_inputs(b, inputs, consts, shared) for b in range(NCORES)]
    nc = build()
    res = run_bass_kernel_spmd(nc, in_maps, core_ids=list(range(NCORES)))
    return np.stack([np.asarray(r["out"], dtype=np.float32) for r in res.results], axis=0)
```

```python
import numpy as np
import ml_dtypes
import concourse.bass as bass
import concourse.mybir as mybir
from concourse.bass_utils import run_bass_kernel_spmd

F32 = mybir.dt.float32
BF16 = mybir.dt.bfloat16
AF = mybir.ActivationFunctionType
ALU = mybir.AluOpType

D = 1024
S = 4096
CT = 256
TT = S + CT
L = 2
DFF = 2816
INW = 2560
EPS = 1e-6
NEGM = -30000.0
NCORES = 8

ROWCFG = [(5, 4), (5, 5), (5, 6), (0, 0), (0, 1), (15, 14), (15, 15)]
NTILE = 6 * 7 * 4


class Buf:
    __slots__ = ("name",)

    def __init__(self, name):
        self.name = name


class Op:
    __slots__ = ("eng", "fn", "deps", "dma", "sig", "sem", "val", "inc")

    def __init__(self, eng, fn, dma):
        self.eng = eng
        self.fn = fn
        self.deps = []
        self.dma = dma
        self.sig = dma
        self.sem = None
        self.val = 0
        self.inc = 1


class Sched:
    NDMA = 12
    ENGS = ["pe", "act", "dve", "pool", "sp"]

    def __init__(self, nc, stack):
        self.nc = nc
        self.csem = {e: stack.enter_context(nc.semaphore("c_" + e)) for e in self.ENGS}
        self.dsem = {e: [stack.enter_context(nc.semaphore("d_%s_%d" % (e, i))) for i in range(self.NDMA)]
                     for e in ("sp", "pool")}
        self.cnt = {e: 0 for e in self.ENGS}
        self.dcnt = {e: 0 for e in self.dsem}
        self.dtot = {}
        self.seen = {e: {} for e in self.ENGS}
        self.nphase = 0
        self.reset()

    def reset(self):
        self.ops = []
        self.last_w = {}
        self.readers = {}
        self.dma_hist = {}

    def add(self, eng, fn, r=(), w=(), dma=False):
        op = Op(eng, fn, dma)
        deps = {}
        for b in r:
            lw = self.last_w.get(b)
            if lw is not None:
                deps[id(lw)] = (lw, 0)
        for b in w:
            lw = self.last_w.get(b)
            if lw is not None and id(lw) not in deps:
                deps[id(lw)] = (lw, 1)
            for rd in self.readers.get(b, ()):
                if id(rd) not in deps:
                    deps[id(rd)] = (rd, 1)
        for p, kind in deps.values():
            if (not p.dma) and (not dma) and p.eng == eng and kind == 1 and eng == "pe":
                continue
            op.deps.append(p)
            p.sig = True
        if dma:
            h = self.dma_hist.setdefault(eng, [])
            if len(h) >= self.NDMA:
                op.deps.append(h[len(h) - self.NDMA])
            h.append(op)
        for b in r:
            self.readers.setdefault(b, []).append(op)
        for b in w:
            self.last_w[b] = op
            self.readers[b] = []
        self.ops.append(op)
        return op

    def emit_phase(self):
        nc = self.nc
        per = {e: [o for o in self.ops if o.eng == e] for e in self.ENGS}
        bar = [(self.csem[e], self.cnt[e]) for e in self.ENGS if self.cnt[e] > 0]
        for e in self.dsem:
            for s in self.dsem[e]:
                if self.dtot.get(id(s), 0) > 0:
                    bar.append((s, self.dtot[id(s)]))
        for e in self.ENGS:
            comp = [o for o in per[e] if not o.dma]
            if comp:
                comp[-1].sig = True
        for op in self.ops:
            if op.dma:
                i = self.dcnt[op.eng]
                self.dcnt[op.eng] = i + 1
                sm = self.dsem[op.eng][i % self.NDMA]
                t = self.dtot.get(id(sm), 0) + 16
                self.dtot[id(sm)] = t
                op.sem, op.val, op.inc = sm, t, 16
            elif op.sig:
                self.cnt[op.eng] += 1
                op.sem, op.val, op.inc = self.csem[op.eng], self.cnt[op.eng], 1
        first = self.nphase == 0
        self.nphase += 1

        def run(e, eng):
            seen = self.seen[e]
            if not first:
                for sm, v in bar:
                    if seen.get(id(sm), 0) < v:
                        eng.wait_ge(sm, v)
                        seen[id(sm)] = v
            for op in per[e]:
                for p in op.deps:
                    k = id(p.sem)
                    if seen.get(k, 0) < p.val:
                        eng.wait_ge(p.sem, p.val)
                        seen[k] = p.val
                ins = op.fn(eng)
                if op.sig:
                    ins.then_inc(op.sem, op.inc)

        with nc.Block() as block:
            @block.tensor
            def _(eng):
                run("pe", eng)

            @block.scalar
            def _(eng):
                run("act", eng)

            @block.vector
            def _(eng):
                run("dve", eng)

            @block.gpsimd
            def _(eng):
                run("pool", eng)

            @block.sync
            def _(eng):
                run("sp", eng)
        self.reset()

    def emit_final(self):
        nc = self.nc
        bar = [(self.csem[e], self.cnt[e]) for e in self.ENGS if self.cnt[e] > 0]
        for e in self.dsem:
            for s in self.dsem[e]:
                if self.dtot.get(id(s), 0) > 0:
                    bar.append((s, self.dtot[id(s)]))
        with nc.Block() as block:
            @block.sync
            def _(eng):
                for sm, v in bar:
                    eng.wait_ge(sm, v)


class Tn:
    def __init__(self, t, name):
        self.t = t
        self.b = Buf(name)

    def __getitem__(self, k):
        return self.t[k]


class K:
    def __init__(self, nc, stack, dbg=None):
        self.nc = nc
        self.st = stack
        self.s = Sched(nc, stack)
        self.dbg = dbg or set()
        self.gst = stack

    def sb(self, name, shape, dt):
        self.nn = getattr(self, "nn", 0) + 1
        t = self.st.enter_context(self.nc.sbuf_tensor("s%d_%s" % (self.nn, name), list(shape), dt))
        return Tn(t, name)

    def ps(self, name, dt=F32, cols=512):
        self.nn = getattr(self, "nn", 0) + 1
        t = self.st.enter_context(self.nc.psum_tensor("p%d_%s" % (self.nn, name), [128, cols], dt))
        return Tn(t, name)

    def dram(self, name, shape, dt, kind="Internal"):
        if name in self.dbg:
            kind = "ExternalOutput"
        if name in getattr(self, "dbg_in", ()):
            kind = "ExternalInput"
        t = self.nc.dram_tensor(name, list(shape), dt, kind=kind)
        d = Tn(t.ap(), name)
        d.h = t
        return d

    def dma(self, out, in_, r=(), w=(), q="sp", **kw):
        return self.s.add(q, lambda e: e.dma_start(out=out, in_=in_, **kw), r=r, w=w, dma=True)

    def mm(self, out, lhsT, rhs, start, stop=True, r=(), w=()):
        return self.s.add(
            "pe",
            lambda e: e.matmul(out, lhsT, rhs, start=start, stop=stop, skip_group_check=True),
            r=r, w=w)

    def tr(self, out, in_, ident, r=(), w=()):
        return self.s.add("pe", lambda e: e.transpose(out, in_, ident), r=r, w=w)

    def act(self, out, in_, func, r=(), w=(), eng="act", **kw):
        return self.s.add(eng, lambda e: e.activation(out=out, in_=in_, func=func, **kw), r=r, w=w)

    def tt(self, eng, out, in0, in1, op, r=(), w=()):
        return self.s.add(eng, lambda e: e.tensor_tensor(out=out, in0=in0, in1=in1, op=op), r=r, w=w)

    def ts(self, eng, out, in0, s1, s2, op0, op1=None, r=(), w=()):
        if op1 is None:
            return self.s.add(eng, lambda e: e.tensor_scalar(out=out, in0=in0, scalar1=s1, scalar2=None, op0=op0), r=r, w=w)
        return self.s.add(eng, lambda e: e.tensor_scalar(out=out, in0=in0, scalar1=s1, scalar2=s2, op0=op0, op1=op1), r=r, w=w)

    def stt(self, eng, out, in0, scalar, in1, op0, op1, r=(), w=()):
        return self.s.add(eng, lambda e: e.scalar_tensor_tensor(out=out, in0=in0, scalar=scalar, in1=in1, op0=op0, op1=op1), r=r, w=w)

    def cp(self, eng, out, in_, r=(), w=()):
        if eng == "act":
            return self.s.add(eng, lambda e: e.copy(out=out, in_=in_), r=r, w=w)
        return self.s.add(eng, lambda e: e.tensor_copy(out=out, in_=in_), r=r, w=w)

    def recip(self, out, in_, r=(), w=()):
        return self.s.add("dve", lambda e: e.reciprocal(out=out, in_=in_), r=r, w=w)

    def rsqrt(self, out, in_, epsb, r=(), w=(), inw=()):
        self.act(out, in_, AF.Ln, r=list(r) + [epsb.b], w=list(inw) + list(w), bias=epsb[:, 0:1])
        return self.act(out, out, AF.Exp, r=list(w), w=list(w), scale=-0.5)

    def memset(self, eng, ap, val, w=()):
        return self.s.add(eng, lambda e: e.memset(ap, val), w=w)


def _na_index():
    kr_in = np.arange(128) // 32
    kc_in = np.arange(128) % 32
    r_in = np.arange(64) // 16
    c_in = np.arange(64) % 16
    drow = np.zeros((7, 128, 64), np.int64)
    rok = np.zeros((7, 128, 64), bool)
    for i, (a, b) in enumerate(ROWCFG):
        r = 4 * a + r_in[None, :]
        kr = 4 * b + kr_in[:, None]
        r0 = np.clip(r - 4, 0, 56)
        rok[i] = (kr >= r0) & (kr < r0 + 8)
        drow[i] = np.clip(kr - r + 7, 0, 14)
    dcol = np.zeros((4, 128, 64), np.int64)
    cok = np.zeros((4, 128, 64), bool)
    for i, j in enumerate((0, 1, 2, 3)):
        kc0 = int(np.clip(16 * j - 8, 0, 32))
        c = 16 * j + c_in[None, :]
        kc = kc0 + kc_in[:, None]
        c0 = np.clip(c - 8, 0, 48)
        cok[i] = (kc >= c0) & (kc < c0 + 16)
        dcol[i] = np.clip(kc - c + 15, 0, 30)
    return drow, rok, dcol, cok


def _consts():
    c = {}
    c["identb"] = np.eye(128, dtype=np.float32).astype(ml_dtypes.bfloat16)
    c["identf"] = np.eye(128, dtype=np.float32)
    bm = np.zeros((128, 128), np.float32)
    bm[:64, :64] = 1.0 / 64
    bm[64:, 64:] = 1.0 / 64
    c["bm"] = bm.astype(ml_dtypes.bfloat16)
    pm = np.zeros((128, 128), np.float32)
    for m in range(128):
        k = m + 32 if (m % 64) < 32 else m - 32
        pm[k, m] = 1.0
    c["pm"] = pm
    t = np.arange(S)
    row = (t // 64).astype(np.float32)
    col = (t % 64).astype(np.float32)
    inv = (np.float32(10000.0) ** (-np.arange(16, dtype=np.float32) / np.float32(16))).astype(np.float32)
    ang = np.concatenate([row[:, None] * inv, col[:, None] * inv], axis=-1).astype(np.float32)
    cos = np.cos(ang).astype(np.float32)
    sin = np.sin(ang).astype(np.float32)
    d = np.arange(128) % 64
    cosT = cos[:, d % 32].T
    sgn = np.where(d < 32, -1.0, 1.0).astype(np.float32)
    sinT = sin[:, d % 32].T * sgn[:, None]
    c["rope"] = np.ascontiguousarray(np.stack([cosT, sinT], axis=1)).astype(np.float32)
    ki = np.arange(128)[:, None]
    qi = np.arange(128)[None, :]
    mprev = np.where(qi <= ki, 1.0, 0.0)
    mnext = np.where(ki <= qi, 1.0, 0.0)
    c["wgm"] = np.stack([np.ones_like(mprev), mprev, mnext], axis=1).astype(np.float32).astype(ml_dtypes.bfloat16)
    return c


def _core_inputs(b, inp, consts, shared=None):
    f = lambda a: np.ascontiguousarray(np.asarray(a, dtype=np.float32))
    m = {}
    m["x"] = f(inp["x"][b])
    m["ctx"] = f(inp["ctx"][b])
    cvec = np.stack([np.asarray(inp["c"][b]), np.asarray(inp["c_ctx"])], 0)
    m["cvt"] = f(cvec.reshape(2, 8, 128).transpose(2, 1, 0))
    if shared is not None:
        m.update(shared)
        return m
    m["w_ada"] = f(inp["w_ada"])
    m["b_ada"] = f(inp["b_ada"])
    gt = lambda g: np.asarray(g).reshape(L, 8, 128).transpose(2, 0, 1)
    m["gT"] = f(np.stack([gt(inp["g_attn"]), gt(inp["g_ffn"])], axis=2))
    m["w_in"] = f(inp["w_in"])
    qkg = np.stack([np.asarray(inp[k]) for k in ("qn_a", "kn_a", "qn_b", "kn_b")], axis=-1)
    m["qkg"] = f(np.concatenate([qkg, qkg], axis=1).transpose(1, 0, 2))
    drow, rok, dcol, cok = _na_index()
    rpb = np.asarray(inp["rpb_a"], dtype=np.float32)
    g = rpb[:, :, drow[:, None], dcol[None, :]]
    ok = (rok[:, None] & cok[None, :])[None, None]
    nab = np.where(ok, g, np.float32(NEGM)).astype(np.float32)
    m["nab"] = f(nab.transpose(4, 0, 1, 2, 3, 5).reshape(128, L, NTILE * 64))
    m["sink"] = f(np.broadcast_to(np.asarray(inp["sink_b"])[None], (128, L, 6)))
    m["convc"] = f(np.asarray(inp["conv_c"]).reshape(L, 3, 2, 128).transpose(3, 0, 2, 1))
    m["w_o"] = f(inp["w_o"])
    m["w_up"] = f(inp["w_up"])
    m["convf"] = f(np.asarray(inp["conv_ffn"]).reshape(L, 3, 44, 128).transpose(3, 0, 2, 1))
    m["w_down"] = f(inp["w_down"])
    m.update(consts)
    return m


from contextlib import ExitStack, contextmanager


@contextmanager
def phase(k):
    old = k.st
    with ExitStack() as st:
        k.st = st
        yield
        k.s.emit_phase()
    k.st = old


def fence(k, eng, r, w):
    d = k.dummy
    return k.s.add(eng, lambda e: e.memset(d[0:1, 0:1], 0.0), r=r, w=list(w) + [d.b])


def wload_cast(k, src, nkc, segs, nsplit=4):
    subs = {}
    step = (nkc + nsplit - 1) // nsplit
    for (s0, s1, dst, d0) in segs:
        for k0 in range(0, nkc, step):
            k1 = min(nkc, k0 + step)
            sb_ = Buf("sub")
            subs.setdefault(id(dst), (dst, []))[1].append(sb_)
            k.dma(dst[:, k0:k1, d0:d0 + (s1 - s0)], src[:, k0:k1, s0:s1], w=[sb_], q="pool")
    for dst, bl in subs.values():
        fence(k, "pool", r=bl, w=[dst.b])


class WSegs:
    def __init__(self):
        self.rng = []

    def bufs(self, c0, c1):
        return [b for (d0, d1, b) in self.rng if d0 < c1 and c0 < d1]


def wload_segs(k, src, nkc, dst, segs):
    ws = WSegs()
    for (s0, s1, d0) in segs:
        b = Buf("wseg")
        k.dma(dst[:, 0:nkc, d0:d0 + (s1 - s0)], src[:, :, s0:s1], w=[b], q="pool")
        ws.rng.append((d0, d0 + (s1 - s0), b))
    return ws


def wload(k, stg, src, nkc, ncols, segs, engs=("dve", "pool", "act"), blk=256, func=None):
    subs = {}
    ci = 0
    for bi, c0 in enumerate(range(0, ncols, blk)):
        c1 = min(ncols, c0 + blk)
        sg = stg[bi % len(stg)]
        k.dma(sg[:, 0:nkc, 0:c1 - c0], src[:, :, c0:c1], w=[sg.b])
        for (s0, s1, dst, d0) in segs:
            lo, hi = max(s0, c0), min(s1, c1)
            if lo >= hi:
                continue
            sb_ = Buf("sub")
            subs.setdefault(id(dst), (dst, []))[1].append(sb_)
            if func is None:
                k.cp(engs[ci % len(engs)], dst[:, 0:nkc, d0 + lo - s0:d0 + hi - s0], sg[:, 0:nkc, lo - c0:hi - c0],
                     r=[sg.b], w=[sb_])
            else:
                k.act(dst[:, 0:nkc, d0 + lo - s0:d0 + hi - s0], sg[:, 0:nkc, lo - c0:hi - c0], func, r=[sg.b], w=[sb_])
            ci += 1
    for dst, bl in subs.values():
        fence(k, "pool", r=bl, w=[dst.b])


class G:
    pass


def build(nlayers=L, dbg=(), stop_after=None, dbg_in=(), only=None):
    nc = bass.Bass("TRN2", target_bir_lowering=False)
    gst = ExitStack()
    with gst:
        k = K(nc, gst, set(dbg))
        k.dbg_in = set(dbg_in)
        g = G()
        din = {}

        def inp(name, shape, dt=F32):
            din[name] = nc.dram_tensor(name, list(shape), dt, kind="ExternalInput").ap()

        inp("x", [S, D]); inp("ctx", [CT, D]); inp("cvt", [128, 8, 2])
        inp("w_ada", [L, D, 6 * D]); inp("b_ada", [L, 6 * D]); inp("gT", [128, L, 2, 8])
        inp("w_in", [L, D, INW]); inp("qkg", [128, L, 4]); inp("nab", [128, L, NTILE * 64])
        inp("sink", [128, L, 6]); inp("convc", [128, L, 2, 3]); inp("w_o", [L, D, D])
        inp("w_up", [L, D, 2 * DFF]); inp("convf", [128, L, 44, 3]); inp("w_down", [L, DFF, D])
        inp("identb", [128, 128], BF16); inp("identf", [128, 128]); inp("bm", [128, 128], BF16)
        inp("pm", [128, 128]); inp("rope", [128, 2, S]); inp("wgm", [128, 3, 128], BF16)
        g.din = din
        g.out = nc.dram_tensor("out", [S, D], F32, kind="ExternalOutput").ap()
        g.modrow = k.dram("modrow", [2, L * 6 * D], F32)
        g.QK = k.dram("QK", [11, 128, TT], BF16)
        g.VA = k.dram("VA", [TT, 6, 128], BF16)
        g.VB = k.dram("VB", [TT, 2, 128], BF16)
        g.CU = k.dram("CU", [2, 128, TT], F32)
        g.BG = k.dram("BG", [2, 128, TT], F32)
        g.XN = k.dram("XN", [TT, D], F32)
        g.XP = k.dram("XP", [TT, D], F32)
        g.XM = k.dram("XM", [TT, D], F32)
        g.HT2 = k.dram("HT2", [8, 128, TT], BF16)
        g.identb = k.sb("identb", [128, 128], BF16)
        g.identf = k.sb("identf", [128, 128], F32)
        g.bm = k.sb("bm", [128, 128], BF16)
        g.pm = k.sb("pm", [128, 128], F32)
        g.wgm = k.sb("wgm", [128, 3, 128], BF16)
        g.modT = k.sb("modT", [128, L, 96], F32)
        g.AB = k.sb("AB", [128, L * 2 * 4, 8], F32)
        g.gT = k.sb("gT", [128, L, 2, 8], F32)
        g.qkg = k.sb("qkg", [128, L, 4], F32)
        g.sink = k.sb("sink", [128, L, 6], F32)
        g.convc = k.sb("convc", [128, L, 2, 3], F32)
        g.convf = k.sb("convf", [128, L, 44, 3], F32)
        k.dummy = k.sb("dummy", [128, 4], F32)
        g.epsb = k.sb("epsb", [128, 1], F32)

        p0_mods(k, g)
        if stop_after == "p0":
            k.s.emit_final()
            return nc
        for l in range(nlayers):
            if only is None or "p1" in only:
                p1_inproj(k, g, l)
            if stop_after == "p1":
                break
            if only is None or "p2" in only:
                p2_mixers(k, g, l)
            if stop_after == "p2":
                break
            if only is None or "p3" in only:
                p3_ffn(k, g, l, 0)
                p3_ffn(k, g, l, 1)
        k.s.emit_final()
    return nc


def abv(g, l, var, which):
    return g.AB[:, (l * 2 + var) * 4 + which, :]


def p0_mods(k, g):
    din = g.din
    with phase(k):
        for nm in ("identb", "identf", "bm", "pm", "wgm", "gT", "qkg", "sink", "convc", "convf"):
            t = getattr(g, nm)
            k.dma(t[:], din[nm], w=[t.b])
        k.memset("dve", g.epsb[:], EPS, w=[g.epsb.b])
        for gi in (0, 2):
            k.ts("dve", g.qkg[:, :, gi:gi + 1], g.qkg[:, :, gi:gi + 1], 0.125, None, ALU.mult, r=[g.qkg.b], w=[g.qkg.b])
        cvt = k.sb("cvt", [128, 8, 2], F32)
        sct = k.sb("sct", [128, 8, 2], F32)
        k.dma(cvt[:], din["cvt"], w=[cvt.b])
        k.act(sct[:], cvt[:], AF.Silu, r=[cvt.b], w=[sct.b])
        NB0 = 6
        wst = [k.sb("wst%d" % i, [128, 8, 512], F32) for i in range(NB0)]
        bad = [k.sb("bad%d" % i, [2, 512], F32) for i in range(NB0)]
        mrow = [k.sb("mrow%d" % i, [2, 512], F32) for i in range(NB0)]
        pmm = [k.ps("p0m%d" % i) for i in range(NB0)]
        pT = k.ps("p0T")
        chunks = [(l_, n_) for l_ in range(L) for n_ in range(12)]

        def p0_load(ci):
            l_, n_ = chunks[ci]
            i = ci % NB0
            wv = din["w_ada"][l_].rearrange("(kc p) n -> p kc n", p=128)
            k.dma(wst[i][:], wv[:, :, n_ * 512:(n_ + 1) * 512], w=[wst[i].b])
            for r_ in range(2):
                k.dma(bad[i][r_:r_ + 1, :], din["b_ada"][l_:l_ + 1, n_ * 512:(n_ + 1) * 512], w=[bad[i].b])

        PF = NB0 - 2
        for ci in range(PF):
            p0_load(ci)
        for l in range(L):
            for n in range(12):
                ci = l * 12 + n
                i = ci % NB0
                if ci + PF < len(chunks):
                    p0_load(ci + PF)
                for kc in range(8):
                    k.mm(pmm[i][0:2, :], sct[:, kc, :], wst[i][:, kc, :], start=(kc == 0),
                         r=[sct.b, wst[i].b], w=[pmm[i].b])
                k.tt("dve", mrow[i][:], pmm[i][0:2, :], bad[i][:], ALU.add, r=[bad[i].b], w=[pmm[i].b, mrow[i].b])
                k.dma(g.modrow[:, l * 6144 + n * 512:l * 6144 + (n + 1) * 512], mrow[i][:], r=[mrow[i].b])
                for j in range(4):
                    idx = n * 4 + j
                    k.mm(pT[:, idx * 2:idx * 2 + 2], mrow[i][0:2, j * 128:(j + 1) * 128], g.identf[0:2, 0:2],
                         start=(idx == 0), r=[mrow[i].b, g.identf.b], w=[pT.b])
            k.cp("dve", g.modT[:, l, :], pT[:, 0:96], w=[pT.b, g.modT.b])
            mv = g.modT[:, l, :].rearrange("p (c v) -> p c v", v=2)
            for var in range(2):
                k.stt("dve", abv(g, l, var, 0), mv[:, 8:16, var], 1.0, g.gT[:, l, 0, :], ALU.add, ALU.mult,
                      r=[g.modT.b, g.gT.b], w=[g.AB.b])
                k.cp("dve", abv(g, l, var, 1), mv[:, 0:8, var], r=[g.modT.b], w=[g.AB.b])
                k.stt("dve", abv(g, l, var, 2), mv[:, 32:40, var], 1.0, g.gT[:, l, 1, :], ALU.add, ALU.mult,
                      r=[g.modT.b, g.gT.b], w=[g.AB.b])
                k.cp("dve", abv(g, l, var, 3), mv[:, 24:32, var], r=[g.modT.b], w=[g.AB.b])


def norm_rows(k, xt, s, ss, junk, r=()):
    k.act(junk[:], xt, AF.Square, r=list(r), w=[junk.b, ss.b], accum_out=ss[:, s:s + 1], scale=1.0 / 32.0)


W1_QA, W1_KA, W1_QB, W1_KBD, W1_U, W1_CG, W1_BG, W1_V = 0, 384, 768, 1152, 1408, 1664, 1920, 2176
W1_N = 2688


def p1_inproj(k, g, l):
    din = g.din
    with phase(k):
        W = k.sb("w1", [128, 8, W1_N], BF16)
        src = din["w_in"][l].rearrange("(kc p) n -> p kc n", p=128)
        segs = [(0, 128, W1_QA), (128, 384, W1_QA + 128), (384, 768, W1_KA), (1152, 1536, W1_QB),
                (1536, 1600, W1_KBD), (1536, 1600, W1_KBD + 64),
                (1600, 1664, W1_KBD + 128), (1600, 1664, W1_KBD + 192),
                (1792, 2048, W1_U), (2304, 2560, W1_CG), (2048, 2304, W1_BG),
                (768, 1152, W1_V), (1664, 1792, W1_V + 384)]
        WS = wload_segs(k, src, 8, W, segs)

        xt = [k.sb("xt%d" % i, [128, 4, D], F32) for i in range(2)]
        cs = [k.sb("cs%d" % i, [128, 2, 512], F32) for i in range(2)]
        xn = [k.sb("xn%d" % i, [128, 4, D], BF16) for i in range(2)]
        junk = k.sb("junk", [128, D], BF16)
        ss = [k.sb("ss%d" % i, [128, 4], F32) for i in range(2)]
        rs = [k.sb("rs%d" % i, [128, 4], F32) for i in range(2)]
        hTs = [k.sb("hT%d" % i, [128, 8, 512], BF16) for i in range(2)]
        sq = [k.sb("sq%d" % i, [128, 512], BF16) for i in range(3)]
        rstd = [k.sb("rstd%d" % i, [128, 512], F32) for i in range(3)]
        qn = [k.sb("qn%d" % i, [128, 512], F32) for i in range(3)]
        t1 = [k.sb("t1%d" % i, [128, 512], F32) for i in range(3)]
        t2 = [k.sb("t2%d" % i, [128, 512], F32) for i in range(3)]
        ob = [k.sb("ob%d" % i, [128, 512], BF16) for i in range(3)]
        usb = [k.sb("usb%d" % i, [128, 512], F32) for i in range(2)]
        of = [k.sb("of%d" % i, [128, 512], F32) for i in range(3)]
        vt = [k.sb("vt%d" % i, [128, 8, 128], BF16) for i in range(2)]
        for v_ in vt:
            k.memset("pool", v_[:, :, 64:128], 1.0, w=[v_.b])
        pT = [k.ps("pT%d" % i, BF16, 1024) for i in range(2)]
        pq = [k.ps("pq%d" % i) for i in range(4)]
        pmn = k.ps("pmn")
        pr = k.ps("pr")

        tiles = [(i * 512, 512, 0) for i in range(8)] + [(S, CT, 1)]
        cnt = {"ob": 0, "of": 0, "pq": 0, "a": 0, "vt": 0, "pT": 0}

        def load(ti):
            tok0, n, var = tiles[ti]
            nsub = n // 128
            b = ti % 2
            if var == 0:
                srcx = (din["x"] if l == 0 else g.XM[:])[tok0:tok0 + n, :]
                rd = [] if l == 0 else [g.XM.b]
            else:
                srcx = din["ctx"] if l == 0 else g.XM[S:S + CT, :]
                rd = [] if l == 0 else [g.XM.b]
            k.dma(xt[b][:, 0:nsub, :], srcx.rearrange("(s p) f -> p s f", p=128), w=[xt[b].b])

        def load_cs(ti):
            tok0, n, var = tiles[ti]
            b = ti % 2
            if var == 0:
                k.dma(cs[b][:, :, 0:n], din["rope"][:, :, tok0:tok0 + n], w=[cs[b].b])

        def norm(ti):
            tok0, n, var = tiles[ti]
            nsub = n // 128
            b = ti % 2
            xn_ = xn[b]
            for s in range(nsub):
                norm_rows(k, xt[b][:, s, :], s, ss[b], junk, r=[xt[b].b])
            k.rsqrt(rs[b][:, 0:nsub], ss[b][:, 0:nsub], g.epsb, r=[ss[b].b], w=[rs[b].b])
            for s in range(nsub):
                if s % 2 == 0:
                    k.ts("dve", xn_[:, s, :], xt[b][:, s, :], rs[b][:, s:s + 1], None, ALU.mult,
                         r=[xt[b].b, rs[b].b], w=[xn_.b])
                else:
                    k.act(xn_[:, s, :], xt[b][:, s, :], AF.Copy, r=[xt[b].b, rs[b].b], w=[xn_.b], scale=rs[b][:, s:s + 1])

        def trans(ti):
            tok0, n, var = tiles[ti]
            nsub = n // 128
            xn_ = xn[ti % 2]
            hT_ = hTs[ti % 2]
            for kc in range(8):
                p = pT[cnt["pT"] % 2]
                cnt["pT"] += 1
                for s in range(nsub):
                    k.tr(p[:, s * 128:(s + 1) * 128], xn_[:, s, kc * 128:(kc + 1) * 128], g.identb[:],
                         r=[xn_.b, g.identb.b], w=[p.b])
                k.act(hT_[:, kc, 0:n], p[:, 0:n], AF.Identity, r=[g.AB.b], w=[p.b, hT_.b],
                      scale=abv(g, l, var, 0)[:, kc:kc + 1], bias=abv(g, l, var, 1)[:, kc:kc + 1])

        load(0)
        load_cs(0)
        load(1)
        norm(0)
        trans(0)
        chunks = []

        def add_chunk(A, B=None, C=None, pre=None):
            chunks.append((A, B, C, pre))

        def make_tile(ti):
            tok0, n, var = tiles[ti]
            nsub = n // 128
            b = ti % 2
            hT = hTs[b]

            def proj(wc0):
                p = pq[cnt["pq"] % 4]
                cnt["pq"] += 1
                wb = WS.bufs(wc0, wc0 + 128)
                for kc in range(8):
                    k.mm(p[:, 0:n], W[:, kc, wc0:wc0 + 128], hT[:, kc, 0:n], start=(kc == 0), r=wb + [hT.b], w=[p.b])
                return p

            def qk_chunk(wc0, gi, rope, qkidx, pre=None):
                cell = {}

                def A():
                    p = proj(wc0)
                    a = cnt["a"] % 3
                    cnt["a"] += 1
                    cell["p"], cell["a"] = p, a
                    k.act(sq[a][:, 0:n], p[:, 0:n], AF.Square, w=[p.b, sq[a].b])

                def B():
                    p, a = cell["p"], cell["a"]
                    k.mm(pmn[:, 0:n], g.bm[:], sq[a][:, 0:n], start=True, r=[g.bm.b, sq[a].b], w=[pmn.b])
                    k.rsqrt(rstd[a][:, 0:n], pmn[:, 0:n], g.epsb, w=[rstd[a].b], inw=[pmn.b])
                    if not rope:
                        o = ob[cnt["ob"] % 3]
                        cnt["ob"] += 1
                        k.stt("dve", o[:, 0:n], p[:, 0:n], g.qkg[:, l, gi:gi + 1], rstd[a][:, 0:n], ALU.mult, ALU.mult,
                              r=[rstd[a].b, g.qkg.b], w=[p.b, o.b])
                        k.dma(g.QK[qkidx, :, tok0:tok0 + n], o[:, 0:n], r=[o.b])
                    else:
                        k.stt("dve", qn[a][:, 0:n], p[:, 0:n], g.qkg[:, l, gi:gi + 1], rstd[a][:, 0:n], ALU.mult, ALU.mult,
                              r=[rstd[a].b, g.qkg.b], w=[p.b, qn[a].b])

                def C():
                    a = cell["a"]
                    o = ob[cnt["ob"] % 3]
                    cnt["ob"] += 1
                    k.mm(pr[:, 0:n], g.pm[:], qn[a][:, 0:n], start=True, r=[g.pm.b, qn[a].b], w=[pr.b])
                    k.tt("pool", t1[a][:, 0:n], qn[a][:, 0:n], cs[b][:, 0, 0:n], ALU.mult, r=[qn[a].b, cs[b].b], w=[t1[a].b])
                    k.tt("dve", t2[a][:, 0:n], pr[:, 0:n], cs[b][:, 1, 0:n], ALU.mult, r=[cs[b].b], w=[pr.b, t2[a].b])
                    k.tt("pool", o[:, 0:n], t1[a][:, 0:n], t2[a][:, 0:n], ALU.add, r=[t1[a].b, t2[a].b], w=[o.b])
                    k.dma(g.QK[qkidx, :, tok0:tok0 + n], o[:, 0:n], r=[o.b])

                add_chunk(A, B, C if rope else None, pre)

            def tile_pre():
                if ti + 1 < len(tiles):
                    norm(ti + 1)
                if ti + 2 < len(tiles):
                    load(ti + 2)

            def mid_pre():
                if ti + 1 < len(tiles):
                    trans(ti + 1)
                    load_cs(ti + 1)

            for c in range(3):
                qk_chunk(W1_QA + c * 128, 0, False, c, pre=tile_pre if c == 0 else None)
            for c in range(3):
                qk_chunk(W1_KA + c * 128, 1, False, 3 + c)
            for c in range(3):
                qk_chunk(W1_QB + c * 128, 2, var == 0, 6 + c, pre=mid_pre if c == 0 else None)
            for c in range(2):
                qk_chunk(W1_KBD + c * 128, 3, var == 0, 9 + c)
            for c in range(2):
                ucell = {}

                def A_u(c=c, ucell=ucell):
                    pu = proj(W1_U + c * 128)
                    a = cnt["u"] % 2
                    cnt["u"] += 1
                    ucell["a"] = a
                    k.cp("act", usb[a][:, 0:n], pu[:, 0:n], w=[pu.b, usb[a].b])

                def A_cg(c=c, ucell=ucell):
                    a = ucell["a"]
                    pc = proj(W1_CG + c * 128)
                    o = of[cnt["of"] % 3]
                    cnt["of"] += 1
                    k.tt("dve", o[:, 0:n], pc[:, 0:n], usb[a][:, 0:n], ALU.mult, r=[usb[a].b], w=[pc.b, o.b])
                    k.dma(g.CU[c, :, tok0:tok0 + n], o[:, 0:n], r=[o.b])

                def A_bg(c=c):
                    pb = proj(W1_BG + c * 128)
                    o = of[cnt["of"] % 3]
                    cnt["of"] += 1
                    k.cp("act", o[:, 0:n], pb[:, 0:n], w=[pb.b, o.b])
                    k.dma(g.BG[c, :, tok0:tok0 + n], o[:, 0:n], r=[o.b])

                add_chunk(A_u)
                add_chunk(A_cg)
                add_chunk(A_bg)
            for s in range(nsub):
                def A_v(s=s):
                    pv_ = pq[cnt["pq"] % 4]
                    cnt["pq"] += 1
                    wb = WS.bufs(W1_V, W1_V + 512)
                    for kc in range(8):
                        k.mm(pv_[:, :], hT[:, kc, s * 128:(s + 1) * 128], W[:, kc, W1_V:W1_V + 512], start=(kc == 0),
                             r=wb + [hT.b], w=[pv_.b])
                    v = vt[cnt["vt"] % 2]
                    cnt["vt"] += 1
                    k.cp("act" if s % 2 == 0 else "dve", v[:, :, 0:64], pv_[:, :].rearrange("p (h d) -> p h d", d=64),
                         w=[pv_.b, v.b])
                    k.dma(g.VA[tok0 + s * 128:tok0 + (s + 1) * 128, :, :], v[:, 0:6, :], r=[v.b])
                    k.dma(g.VB[tok0 + s * 128:tok0 + (s + 1) * 128, :, :], v[:, 6:8, :], r=[v.b])

                add_chunk(A_v)

        cnt["u"] = 0
        for ti in range(len(tiles)):
            make_tile(ti)
        nch = len(chunks)
        for i in range(nch + 2):
            if i < nch:
                A, B, C, pre = chunks[i]
                if pre is not None:
                    pre()
                A()
            if 0 <= i - 1 < nch and chunks[i - 1][1] is not None:
                chunks[i - 1][1]()
            if 0 <= i - 2 < nch and chunks[i - 2][2] is not None:
                chunks[i - 2][2]()

def bcast_row(dt_, row, off, n):
    ncols = dt_.t.shape[1]
    return bass.AP(dt_.h, row * ncols + off, [[0, 128], [1, n]])


def p3_tiles(l):
    t = []
    s0 = 0
    while s0 < S:
        n = min(510, S - s0)
        t.append((s0, n, 0))
        s0 += n
    if l == 0:
        t.append((S, CT, 1))
    return t


def p3_ffn(k, g, l, hf):
    din = g.din
    HC = 11
    with phase(k):
        WU = k.sb("wu", [128, 8, 2 * HC * 128], BF16)
        WD = k.sb("wd", [128, HC, D], BF16)
        srcu = din["w_up"][l].rearrange("(kc p) n -> p kc n", p=128)
        a0 = hf * HC * 128
        CG = [(0, 1), (1, 2), (2, 4), (4, 7), (7, HC)]
        usegs = []
        for (c0, c1) in CG:
            usegs.append((a0 + c0 * 128, a0 + c1 * 128, c0 * 128))
            usegs.append((DFF + a0 + c0 * 128, DFF + a0 + c1 * 128, HC * 128 + c0 * 128))
        WUS = wload_segs(k, srcu, 8, WU, usegs)
        srcd = din["w_down"][l][a0:a0 + HC * 128, :].rearrange("(hc p) n -> p hc n", p=128)
        WDS = wload_segs(k, srcd, HC, WD, [(0, 512, 0), (512, 1024, 512)])
        gtb = [k.sb("gtb%d" % v, [128, D], F32) for v in range(2)]
        for v in range(2 if l == 0 else 1):
            k.dma(gtb[v][:], bcast_row(g.modrow, v, l * 6144 + 5 * D, D), w=[gtb[v].b])
        ht = [k.sb("ht%d" % i, [128, 8, 512], BF16) for i in range(2)]
        actT = [k.sb("actT%d" % i, [128, HC, 512], BF16) for i in range(2)]
        t1 = [k.sb("t1%d" % i, [128, 512], F32) for i in range(2)]
        t2 = [k.sb("t2%d" % i, [128, 512], F32) for i in range(2)]
        ca = [k.sb("ca%d" % i, [128, 512], F32) for i in range(2)]
        cg = [k.sb("cg%d" % i, [128, 512], F32) for i in range(2)]
        sa = [k.sb("sa%d" % i, [128, 512], F32) for i in range(2)]
        xt = [k.sb("xt%d" % i, [128, D], F32) for i in range(8)]
        xo = [k.sb("xo%d" % i, [128, D], F32) for i in range(2)]
        tmp = [k.sb("tmp%d" % i, [128, 512], F32) for i in range(2)]
        pa = [k.ps("pa%d" % i) for i in range(2)]
        pg = [k.ps("pg%d" % i) for i in range(2)]
        po = [k.ps("po%d" % i) for i in range(3)]
        xsrc = g.XN if hf == 0 else g.XP
        tiles = p3_tiles(l)
        cnt = {"x": 0, "o": 0, "po": 0, "c": 0, "tmp": 0}

        def load(ti):
            s0, n, var = tiles[ti]
            h = ht[ti % 2]
            lo, hi = (0, S) if var == 0 else (S, S + CT)
            a, b_ = max(s0 - 1, lo), min(s0 + n + 1, hi)
            c0 = a - (s0 - 1)
            if c0 > 0:
                k.memset("pool", h[:, :, 0:c0], 0.0, w=[h.b])
            if b_ < s0 + n + 1:
                k.memset("pool", h[:, :, n + 1:n + 2], 0.0, w=[h.b])
            k.dma(h[:, :, c0:c0 + (b_ - a)], g.HT2[:, :, a:b_].rearrange("c p t -> p c t"), w=[h.b])

        def load_x(ti):
            s0, n, var = tiles[ti]
            nsub = (n + 127) // 128
            bufs = []
            for j in range(nsub):
                m = min(128, n - j * 128)
                r0 = s0 + j * 128
                x_ = xt[cnt["x"] % 8]
                cnt["x"] += 1
                k.dma(x_[0:m, :], xsrc[r0:r0 + m, :], w=[x_.b])
                bufs.append(x_)
            return bufs

        def up_chunk(ti, c):
            s0, n, var = tiles[ti]
            h = ht[ti % 2]
            aT = actT[ti % 2]
            cols = n + 2
            i = cnt["c"] % 2
            cnt["c"] += 1
            for (pp, wc0) in ((pa[i], c * 128), (pg[i], HC * 128 + c * 128)):
                wb = WUS.bufs(wc0, wc0 + 128)
                for kc in range(8):
                    k.mm(pp[:, 0:cols], WU[:, kc, wc0:wc0 + 128], h[:, kc, 0:cols], start=(kc == 0),
                         r=wb + [h.b], w=[pp.b])
            for (pp, dst, ci) in ((pa[i], ca[i], hf * HC + c), (pg[i], cg[i], 22 + hf * HC + c)):
                w3 = g.convf[:, l, ci, :]
                k.act(t1[i][:, 0:n], pp[:, 1:n + 1], AF.Copy, r=[g.convf.b], w=[pp.b, t1[i].b], scale=w3[:, 1:2])
                k.stt("dve", t2[i][:, 0:n], pp[:, 0:n], w3[:, 0:1], t1[i][:, 0:n], ALU.mult, ALU.add,
                      r=[g.convf.b, t1[i].b], w=[pp.b, t2[i].b])
                k.stt("dve", dst[:, 0:n], pp[:, 2:n + 2], w3[:, 2:3], t2[i][:, 0:n], ALU.mult, ALU.add,
                      r=[g.convf.b, t2[i].b], w=[pp.b, dst.b])
            k.act(sa[i][:, 0:n], ca[i][:, 0:n], AF.Silu, r=[ca[i].b], w=[sa[i].b])
            k.tt("pool", aT[:, c, 0:n], sa[i][:, 0:n], cg[i][:, 0:n], ALU.mult, r=[sa[i].b, cg[i].b], w=[aT.b])

        def down(ti, xbufs):
            s0, n, var = tiles[ti]
            aT = actT[ti % 2]
            nsub = (n + 127) // 128
            for j in range(nsub):
                m = min(128, n - j * 128)
                r0 = s0 + j * 128
                x_ = xbufs[j]
                o_ = xo[cnt["o"] % 2]
                cnt["o"] += 1
                for hh in range(2):
                    p_ = po[cnt["po"] % 3]
                    cnt["po"] += 1
                    wb = WDS.bufs(hh * 512, (hh + 1) * 512)
                    for hc in range(HC):
                        k.mm(p_[0:m, :], aT[:, hc, j * 128:j * 128 + m], WD[:, hc, hh * 512:(hh + 1) * 512],
                             start=(hc == 0), r=[aT.b] + wb, w=[p_.b])
                    t_ = tmp[cnt["tmp"] % 2]
                    cnt["tmp"] += 1
                    k.tt("dve", t_[0:m, :], p_[0:m, :], gtb[var][0:m, hh * 512:(hh + 1) * 512], ALU.mult,
                         r=[gtb[var].b], w=[p_.b, t_.b])
                    k.tt("pool", o_[0:m, hh * 512:(hh + 1) * 512], x_[0:m, hh * 512:(hh + 1) * 512], t_[0:m, :], ALU.add,
                         r=[x_.b, t_.b], w=[o_.b])
                if hf == 0:
                    dst = g.XP[r0:r0 + m, :]
                elif l == L - 1:
                    dst = g.out[r0:r0 + m, :]
                else:
                    dst = g.XM[r0:r0 + m, :]
                k.dma(dst, o_[0:m, :], r=[o_.b])

        NPRE = 2
        load(0)
        for c in range(HC):
            up_chunk(0, c)
        for ti in range(len(tiles)):
            if ti + 1 < len(tiles):
                load(ti + 1)
            xbufs = load_x(ti)
            if ti + 1 < len(tiles):
                for c in range(NPRE):
                    up_chunk(ti + 1, c)
            down(ti, xbufs)
            if ti + 1 < len(tiles):
                for c in range(NPRE, HC):
                    up_chunk(ti + 1, c)

WARM_N = 0
WARM_EVERY = 1
KC0 = (0, 8, 24, 32)


def na_rowcfgs(a):
    if a == 0:
        return [(0, 3), (1, 4)]
    if a == 15:
        return [(14, 5), (15, 6)]
    return [(a - 1, 0), (a, 1), (a + 1, 2)]


def p2_mixers(k, g, l):
    din = g.din
    N = 256
    with phase(k):
        WO = k.sb("wo", [128, 8, D], BF16)
        NAB = k.sb("nab", [128, 8, NTILE * 8], BF16)
        stg = [k.sb("stg%d" % i, [128, 8, 128], F32) for i in range(2)]
        def load_p2_weights():
            wload(k, stg, din["nab"][:, l, :].rearrange("p (a c) -> p a c", a=8), 8, NTILE * 8, [(0, NTILE * 8, NAB, 0)],
                  blk=128, func=AF.Exp)
            wload_cast(k, din["w_o"][l].rearrange("(kc p) n -> p kc n", p=128), 8, [(0, D, WO, 0)])
        nabf = NAB[:, :, :].rearrange("p a c -> p (a c)")
        nvar = 2 if l == 0 else 1
        gtb = [k.sb("gtb%d" % v, [128, D], F32) for v in range(nvar)]
        for v in range(nvar):
            k.dma(gtb[v][:], bcast_row(g.modrow, v, l * 6144 + 2 * D, D), w=[gtb[v].b])
        es = k.sb("es", [128, 6], F32)
        k.act(es[:], g.sink[:, l, :], AF.Exp, r=[g.sink.b], w=[es.b])
        KAc = k.sb("kac", [128, 3, CT], BF16)
        KBc = k.sb("kbc", [128, 2, CT], BF16)
        k.dma(KAc[:], g.QK[3:6, :, S:S + CT].rearrange("c p t -> p c t"), w=[KAc.b])
        k.dma(KBc[:], g.QK[9:11, :, S:S + CT].rearrange("c p t -> p c t"), w=[KBc.b])
        VAc = [k.sb("vac%d" % i, [128, 6, 128], BF16) for i in range(2)]
        VBc = [k.sb("vbc%d" % i, [128, 2, 128], BF16) for i in range(2)]
        for ct in range(2):
            k.dma(VAc[ct][:], g.VA[S + ct * 128:S + (ct + 1) * 128, :, :], w=[VAc[ct].b])
            k.dma(VBc[ct][:], g.VB[S + ct * 128:S + (ct + 1) * 128, :, :], w=[VBc[ct].b])
        KAn = [k.sb("kan%d" % i, [128, 3, 256], BF16) for i in range(2)]
        KAg = [[k.sb("kag%d_%d" % (i, j), [128, 3, 128], BF16) for j in range(4)] for i in range(4)]
        VAr = [[k.sb("var%d_%d" % (i, j), [128, 6, 128], BF16) for j in range(4)] for i in range(4)]
        KBr = [k.sb("kbr%d" % i, [128, 2, 128], BF16) for i in range(6)]
        VBr = [k.sb("vbr%d" % i, [128, 2, 128], BF16) for i in range(6)]

        def load_group(b):
            if b < 0 or b > 15:
                return
            sl = b % 4
            t0 = b * 256
            kn = KAn[b % 2]
            k.dma(kn[:], g.QK[3:6, :, t0:t0 + 256].rearrange("c p t -> p c t"), w=[kn.b])
            for j in range(4):
                for c in range(3):
                    k.cp("pool", KAg[sl][j][:, c, :].rearrange("p (r x) -> p r x", x=32),
                         kn[:, c, :].rearrange("p (r x) -> p r x", x=64)[:, :, KC0[j]:KC0[j] + 32],
                         r=[kn.b], w=[KAg[sl][j].b])
                for kr in range(4):
                    r0 = t0 + kr * 64 + KC0[j]
                    k.dma(VAr[sl][j][kr * 32:(kr + 1) * 32, :, :], g.VA[r0:r0 + 32, :, :], w=[VAr[sl][j].b])

        def load_kt(kt):
            if kt < 0 or kt > 31:
                return
            sl = kt % 6
            t0 = kt * 128
            k.dma(KBr[sl][:], g.QK[9:11, :, t0:t0 + 128].rearrange("c p t -> p c t"), w=[KBr[sl].b])
            k.dma(VBr[sl][:], g.VB[t0:t0 + 128, :, :], w=[VBr[sl].b])

        q = [k.sb("q%d" % i, [128, 6, N], BF16) for i in range(2)]
        cu = [k.sb("cu%d" % i, [128, 2, N + 2], F32) for i in range(2)]
        bg = [k.sb("bg%d" % i, [128, 2, N], F32) for i in range(2)]
        P = [k.sb("P%d" % i, [128, 512], BF16) for i in range(6)]
        YT = k.sb("YT", [128, 8, N], BF16)
        YTc = Buf("YTconv")
        rc = [k.sb("rc%d" % i, [128, N], F32) for i in range(2)]
        c1 = [k.sb("c1%d" % i, [128, N], F32) for i in range(2)]
        c2 = [k.sb("c2%d" % i, [128, N], F32) for i in range(2)]
        xt = [k.sb("xt%d" % i, [128, D], F32) for i in range(4)]
        xo = [k.sb("xo%d" % i, [128, D], F32) for i in range(2)]
        tmp = [k.sb("tmp%d" % i, [128, 512], F32) for i in range(2)]
        xn = k.sb("xn", [128, 2, D], BF16)
        junk = k.sb("junk", [128, D], BF16)
        ss = k.sb("ss", [128, 2], F32)
        rs = k.sb("rs", [128, 2], F32)
        h2o = k.sb("h2o", [128, 8, N], BF16)
        pS = [k.ps("pS%d" % i) for i in range(6)]
        pO = [k.ps("pO%d" % i) for i in range(2)]
        cnt = {"pS": 0, "pO": 0, "P": 0, "rc": 0, "x": 0, "o": 0, "po": 0, "tmp": 0, "c": 0}

        tiles = [(i * N, N, 0) for i in range(S // N)] + ([(S, CT, 1)] if l == 0 else [])

        def load_tile(ti):
            tok0, n, var = tiles[ti]
            b_ = ti % 2
            k.dma(q[b_][:, 0:3, 0:n], g.QK[0:3, :, tok0:tok0 + n].rearrange("c p t -> p c t"), w=[q[b_].b])
            k.dma(q[b_][:, 3:6, 0:n], g.QK[6:9, :, tok0:tok0 + n].rearrange("c p t -> p c t"), w=[q[b_].b])
            lo, hi = (0, S) if var == 0 else (S, S + CT)
            a, e = max(tok0 - 1, lo), min(tok0 + n + 1, hi)
            c0 = a - (tok0 - 1)
            if c0 > 0:
                k.memset("pool", cu[b_][:, :, 0:1], 0.0, w=[cu[b_].b])
            if e < tok0 + n + 1:
                k.memset("pool", cu[b_][:, :, n + 1:n + 2], 0.0, w=[cu[b_].b])
            k.dma(cu[b_][:, :, c0:c0 + (e - a)], g.CU[:, :, a:e].rearrange("c p t -> p c t"), w=[cu[b_].b])
            k.dma(bg[b_][:, :, 0:n], g.BG[:, :, tok0:tok0 + n].rearrange("c p t -> p c t"), w=[bg[b_].b])

        def next_ps(name, arr):
            p = arr[cnt[name] % len(arr)]
            cnt[name] += 1
            return p

        def ctx_part(qh, qbuf, Kc, kch, pb, Vc, vh, n, pOut):
            for ct in range(2):
                ps_ = next_ps("pS", pS)
                k.mm(ps_[:, 0:n], Kc[pb:pb + 64, kch, ct * 128:(ct + 1) * 128], qh, start=True,
                     r=[Kc.b, qbuf], w=[ps_.b])
                pp = next_ps("P", P)
                k.act(pp[:, 0:n], ps_[:, 0:n], AF.Exp, w=[ps_.b, pp.b])
                k.mm(pOut[:, 0:n], Vc[ct][:, vh, :], pp[:, 0:n], start=(ct == 0), r=[Vc[ct].b, pp.b], w=[pOut.b])

        wmask = {}

        def get_mask(pattern):
            if pattern not in wmask:
                t = k.sb("wm%d" % len(wmask), [128, len(pattern) * 128], BF16)
                for i, mk in enumerate(pattern):
                    k.cp("pool", t[:, i * 128:(i + 1) * 128], g.wgm[:, mk, :], r=[g.wgm.b], w=[t.b])
                wmask[pattern] = t
            return wmask[pattern]

        def blk(ap):
            return ap.rearrange("p (j r c) -> p j r c", j=4, r=4, c=16)

        def finalize(pOut, n, h_extra, dst, blocked):
            r_ = rc[cnt["rc"] % 2]
            cnt["rc"] += 1
            if h_extra is None:
                k.act(r_[64:128, 0:n], pOut[64:128, 0:n], AF.Ln, w=[pOut.b, r_.b])
            else:
                k.act(r_[64:128, 0:n], pOut[64:128, 0:n], AF.Ln, r=[es.b], w=[pOut.b, r_.b],
                      bias=es[64:128, h_extra:h_extra + 1])
            k.act(r_[64:128, 0:n], r_[64:128, 0:n], AF.Exp, r=[r_.b], w=[r_.b], scale=-1.0)
            if blocked:
                k.tt("dve", dst.rearrange("p (r j c) -> p j r c", r=4, j=4, c=16), blk(pOut[0:64, 0:n]),
                     blk(r_[64:128, 0:n]), ALU.mult, r=[r_.b], w=[pOut.b, YT.b])
            else:
                k.tt("dve", dst, pOut[0:64, 0:n], r_[64:128, 0:n], ALU.mult, r=[r_.b], w=[pOut.b, YT.b])

        class U:
            __slots__ = ("pre", "qk", "ex", "pv", "fin", "post", "post2")

            def __init__(self):
                self.pre = []
                self.qk = self.ex = self.pv = self.fin = None
                self.post = []
                self.post2 = []

        def make_tile_units(ti):
            tok0, n, var = tiles[ti]
            b_ = ti % 2
            a = ti
            qq = q[b_]
            units = []

            def ctx_units(qh, Kc, kch, pb, Vc, vh, pO_cell, blocked=False):
                u = U()
                cell = {}

                def qk(cell=cell):
                    ps_ = next_ps("pS", pS)
                    cell["ps"] = ps_
                    for ct in range(2):
                        k.mm(ps_[:, ct * n:(ct + 1) * n], Kc[pb:pb + 64, kch, ct * 128:(ct + 1) * 128], qh, start=(ct == 0),
                             r=[Kc.b, qq.b], w=[ps_.b])

                def ex(cell=cell):
                    pp = next_ps("P", P)
                    cell["pp"] = pp
                    if blocked:
                        for ct in range(2):
                            k.act(pp[:, ct * n:(ct + 1) * n].rearrange("p (j r c) -> p r j c", j=4, r=4, c=16),
                                  cell["ps"][:, ct * n:(ct + 1) * n].rearrange("p (r j c) -> p r j c", r=4, j=4, c=16),
                                  AF.Exp, w=[cell["ps"].b, pp.b])
                    else:
                        k.act(pp[:, 0:2 * n], cell["ps"][:, 0:2 * n], AF.Exp, w=[cell["ps"].b, pp.b])

                def pv(cell=cell):
                    pO_cell["p"] = next_ps("pO", pO)
                    pOut = pO_cell["p"]
                    pp = cell["pp"]
                    for ct in range(2):
                        k.mm(pOut[:, 0:n], Vc[ct][:, vh, :], pp[:, ct * n:(ct + 1) * n], start=(ct == 0),
                             r=[Vc[ct].b, pp.b], w=[pOut.b])

                u.qk, u.ex, u.pv = qk, ex, pv
                units.append(u)

            for h in range(6):
                ch, pb = h // 2, 64 * (h % 2)
                qh = qq[pb:pb + 64, ch, 0:n]
                pO_cell = {}
                ctx_units(qh, KAc, ch, pb, VAc, h, pO_cell, blocked=(var == 0))
                if var == 0:
                    q3 = qq[pb:pb + 64, ch, :].rearrange("p (r c) -> p r c", c=64)
                    slots = []
                    for (b, rcfg) in na_rowcfgs(a):
                        for j in range(4):
                            slots.append((b, j, (h * 7 + rcfg) * 4 + j))
                    for s0 in range(0, len(slots), 8):
                        grp = slots[s0:s0 + 8]
                        u = U()
                        cell = {}

                        def qk(grp=grp, cell=cell, q3=q3, pb=pb, ch=ch):
                            ps_ = next_ps("pS", pS)
                            cell["ps"] = ps_
                            for i, (b, j, tix) in enumerate(grp):
                                kt_ = KAg[b % 4][j]
                                k.mm(ps_[:, i * 64:(i + 1) * 64], kt_[pb:pb + 64, ch, :], q3[:, :, 16 * j:16 * j + 16],
                                     start=(i == 0), r=[kt_.b, qq.b], w=[ps_.b])

                        def ex(grp=grp, cell=cell):
                            pp = next_ps("P", P)
                            cell["pp"] = pp
                            w_ = len(grp) * 64
                            k.act(pp[:, 0:w_], cell["ps"][:, 0:w_], AF.Exp, w=[cell["ps"].b, pp.b])
                            t0_ = grp[0][2] * 64
                            k.tt("dve", pp[:, 0:w_], pp[:, 0:w_], nabf[:, t0_:t0_ + w_], ALU.mult, r=[pp.b, NAB.b], w=[pp.b])

                        def pv(grp=grp, cell=cell, pO_cell=pO_cell, h=h):
                            pOut = pO_cell["p"]
                            pp = cell["pp"]
                            for i, (b, j, tix) in enumerate(grp):
                                vt_ = VAr[b % 4][j]
                                k.mm(pOut[:, j * 64:(j + 1) * 64], vt_[:, h, :], pp[:, i * 64:(i + 1) * 64],
                                     start=False, r=[vt_.b, pp.b], w=[pOut.b])

                        u.qk, u.ex, u.pv = qk, ex, pv
                        units.append(u)
                units[-1].fin = (lambda pO_cell=pO_cell, pb=pb, ch=ch: finalize(pO_cell["p"], n, None, YT[pb:pb + 64, ch, 0:n],
                                                                                    var == 0))
            for h in range(6):
                ch, pb, kv = 3 + h // 2, 64 * (h % 2), h // 3
                qh = qq[pb:pb + 64, ch, 0:n]
                pO_cell = {}
                ctx_units(qh, KBc, kv, pb, VBc, kv, pO_cell)
                if var == 0:
                    slots = []
                    for t in range(2):
                        i_ = 2 * a + t
                        for kt in (i_ - 1, i_, i_ + 1):
                            if 0 <= kt <= 31:
                                slots.append((t, kt, 0 if kt == i_ else (1 if kt < i_ else 2)))
                    for s0 in range(0, len(slots), 4):
                        grp = slots[s0:s0 + 4]
                        u = U()
                        cell = {}

                        def qk(grp=grp, cell=cell, pb=pb, ch=ch, kv=kv):
                            ps_ = next_ps("pS", pS)
                            cell["ps"] = ps_
                            for i, (t, kt, mk) in enumerate(grp):
                                kt_ = KBr[kt % 6]
                                k.mm(ps_[:, i * 128:(i + 1) * 128], kt_[pb:pb + 64, kv, :],
                                     qq[pb:pb + 64, ch, t * 128:(t + 1) * 128],
                                     start=(i == 0), r=[kt_.b, qq.b], w=[ps_.b])

                        def ex(grp=grp, cell=cell):
                            pp = next_ps("P", P)
                            cell["pp"] = pp
                            w_ = len(grp) * 128
                            k.act(pp[:, 0:w_], cell["ps"][:, 0:w_], AF.Exp, w=[cell["ps"].b, pp.b])
                            pat = tuple(mk for (_, _, mk) in grp)
                            if any(pat):
                                mt = get_mask(pat)
                                k.tt("dve", pp[:, 0:w_], pp[:, 0:w_], mt[:, 0:w_], ALU.mult, r=[pp.b, mt.b], w=[pp.b])

                        def pv(grp=grp, cell=cell, pO_cell=pO_cell, kv=kv):
                            pOut = pO_cell["p"]
                            pp = cell["pp"]
                            for i, (t, kt, mk) in enumerate(grp):
                                vt_ = VBr[kt % 6]
                                k.mm(pOut[:, t * 128:(t + 1) * 128], vt_[:, kv, :], pp[:, i * 128:(i + 1) * 128],
                                     start=False, r=[vt_.b, pp.b], w=[pOut.b])

                        u.qk, u.ex, u.pv = qk, ex, pv
                        units.append(u)
                units[-1].fin = (lambda pO_cell=pO_cell, pb=pb, ch=ch, h=h: finalize(pO_cell["p"], n, h, YT[pb:pb + 64, ch, 0:n],
                                                                                         False))

            xcell = {}

            def prefetch():
                if var == 0:
                    xsrc = (din["x"] if l == 0 else g.XM[:])[tok0:tok0 + n, :]
                else:
                    xsrc = din["ctx"]
                xcell["b"] = []
                for s in range(n // 128):
                    x_ = xt[cnt["x"] % 4]
                    cnt["x"] += 1
                    k.dma(x_[:], xsrc[s * 128:(s + 1) * 128, :], w=[x_.b])
                    xcell["b"].append(x_)
                if ti + 1 < len(tiles):
                    load_tile(ti + 1)
                if var == 0:
                    load_group(a + 2)
                    load_kt(2 * a + 3); load_kt(2 * a + 4)

            def conv():
                for c in range(2):
                    i = cnt["c"] % 2
                    cnt["c"] += 1
                    w3 = g.convc[:, l, c, :]
                    cuc = cu[b_]
                    k.act(c1[i][:, 0:n], cuc[:, c, 1:n + 1], AF.Copy, r=[cuc.b, g.convc.b], w=[c1[i].b], scale=w3[:, 1:2])
                    k.stt("dve", c2[i][:, 0:n], cuc[:, c, 0:n], w3[:, 0:1], c1[i][:, 0:n], ALU.mult, ALU.add,
                          r=[cuc.b, c1[i].b, g.convc.b], w=[c2[i].b])
                    k.stt("dve", c1[i][:, 0:n], cuc[:, c, 2:n + 2], w3[:, 2:3], c2[i][:, 0:n], ALU.mult, ALU.add,
                          r=[cuc.b, c2[i].b, g.convc.b], w=[c1[i].b])
                    k.tt("pool", YT[:, 6 + c, 0:n], c1[i][:, 0:n], bg[b_][:, c, 0:n], ALU.mult,
                         r=[c1[i].b, bg[b_].b], w=[YTc])

            ocell = []

            def wo_epi():
                nsub = n // 128
                for s in range(nsub):
                    x_ = xcell["b"][s]
                    o_ = xo[cnt["o"] % 2]
                    cnt["o"] += 1
                    for hh in range(2):
                        p_ = next_ps("pS", pS)
                        for kc in range(8):
                            k.mm(p_[:, :], YT[:, kc, s * 128:(s + 1) * 128], WO[:, kc, hh * 512:(hh + 1) * 512],
                                 start=(kc == 0), r=[YT.b, YTc, WO.b], w=[p_.b])
                        t_ = tmp[cnt["tmp"] % 2]
                        cnt["tmp"] += 1
                        k.tt("dve", t_[:], p_[:, :], gtb[var][:, hh * 512:(hh + 1) * 512], ALU.mult,
                             r=[gtb[var].b], w=[p_.b, t_.b])
                        k.tt("pool", o_[:, hh * 512:(hh + 1) * 512], x_[:, hh * 512:(hh + 1) * 512], t_[:], ALU.add,
                             r=[x_.b, t_.b], w=[o_.b])
                    k.dma(g.XN[tok0 + s * 128:tok0 + (s + 1) * 128, :], o_[:], r=[o_.b])
                    ocell.append(o_)

            def wo_sq(s):
                def f():
                    o_ = ocell[s]
                    norm_rows(k, o_[:], s, ss, junk, r=[o_.b])
                return f

            def wo_norm():
                nsub = len(ocell)
                k.rsqrt(rs[:, 0:nsub], ss[:, 0:nsub], g.epsb, r=[ss.b], w=[rs.b])
                for s, o_ in enumerate(ocell):
                    k.ts("dve", xn[:, s, :], o_[:], rs[:, s:s + 1], None, ALU.mult, r=[o_.b, rs.b], w=[xn.b])

            def transposes():
                nsub = n // 128
                for kc in range(8):
                    pt_ = next_ps("pS", pS)
                    ptb = pt_.t.bitcast(BF16)
                    for s in range(nsub):
                        k.tr(ptb[:, s * 128:(s + 1) * 128], xn[:, s, kc * 128:(kc + 1) * 128], g.identb[:],
                             r=[xn.b, g.identb.b], w=[pt_.b])
                    k.act(h2o[:, kc, 0:n], ptb[:, 0:n], AF.Identity, r=[g.AB.b], w=[pt_.b, h2o.b],
                          scale=abv(g, l, var, 2)[:, kc:kc + 1], bias=abv(g, l, var, 3)[:, kc:kc + 1])
                k.dma(g.HT2[:, :, tok0:tok0 + n].rearrange("c p t -> p c t"), h2o[:, :, 0:n], r=[h2o.b])

            def warm():
                pw = pT.t.bitcast(F32)
                for i in range(WARM_N):
                    k.mm(pw[:, 0:512], WO[:, i % 8, 0:128], WO[:, (i + 1) % 8, 0:512], start=True, r=[WO.b], w=[pT.b])

            if WARM_N and (ti % WARM_EVERY == 0):
                units[0].pre.append(warm)
            units[min(7, len(units) - 1)].pre.append(prefetch)
            units[min(6, len(units) - 1)].pre.append(conv)
            units[-1].post.append(wo_epi)
            units[-1].post2 = [(3, wo_sq(0)), (5, wo_sq(1)), (6, wo_norm)]
            return units, transposes

        load_tile(0)
        load_group(0)
        for kt in range(0, 3):
            load_kt(kt)
        load_p2_weights()
        load_group(1)
        allu = []
        pending_tr = None
        for ti in range(len(tiles)):
            us, trf = make_tile_units(ti)
            if pending_tr is not None:
                us[-1].post.insert(0, pending_tr)
            pending_tr = trf
            allu.extend(us)
        SK = 4
        FD = 1
        due = {}
        for i in range(len(allu) + SK + FD + 8):
            if i < len(allu):
                u = allu[i]
                for f in u.pre:
                    f()
                u.qk()
                u.ex()
            j = i - SK
            if 0 <= j < len(allu):
                u = allu[j]
                u.pv()
                fl = []
                if u.fin is not None:
                    fl.append(u.fin)
                fl.extend(u.post)
                if fl:
                    due.setdefault(i + FD, []).extend(fl)
                for (dl, f2) in u.post2:
                    due.setdefault(i + FD + dl, []).append(f2)
            for f in due.pop(i, []):
                f()
        assert not due
        pending_tr()

def _shared_inputs(inp, consts):
    m = _core_inputs(0, inp, consts)
    for kx in ("x", "ctx", "cvt"):
        m.pop(kx)
    return m


def kernel(**inputs):
    consts = _consts()
    shared = _shared_inputs(inputs, consts)
    in_maps = [_core_inputs(b, inputs, consts, shared) for b in range(NCORES)]
    nc = build()
    res = run_bass_kernel_spmd(nc, in_maps, core_ids=list(range(NCORES)))
    return np.stack([np.asarray(r["out"], dtype=np.float32) for r in res.results], axis=0)
```
